# Optimizing a Trainium2 kernel written in Bass

```python
import jax, jax.numpy as jnp
from jax import lax
import numpy as np

D_MODEL = 1024
BATCH = 8
SEQ = 2048
DEPTH = 1

ATTN_HEADS = 16
HEAD_DIM = 64
ATTN_WIDTH = ATTN_HEADS * HEAD_DIM
KV_HEADS = 4
Q_PER_KV = ATTN_HEADS // KV_HEADS
KV_WIDTH = KV_HEADS * HEAD_DIM
ROPE_DIM = HEAD_DIM // 4
ROPE_THETA = 500000.0
CMP_BLOCK = 32
CMP_STRIDE = 16
CMP_HIDDEN = 256
SLC_BLOCK = 64
SLC_TOPN = 16
WINDOW = 512
N_BRANCH = 3
Q_BLOCK = 128
SLC_Q_BLOCK = 32

SSM_HEADS = 16
SSM_HEAD_DIM = 64
SSM_WIDTH = SSM_HEADS * SSM_HEAD_DIM
SSM_GROUPS = 4
SSM_STATE = 128
CONV_WIDTH = 4
CONV_CH = SSM_WIDTH + 2 * SSM_GROUPS * SSM_STATE
SSM_CHUNK = 256

MIX_WIDTH = ATTN_WIDTH + SSM_WIDTH
IN_WIDTH = (2 * ATTN_WIDTH + 6 * KV_WIDTH + ATTN_HEADS * N_BRANCH
            + 2 * SSM_WIDTH + 2 * SSM_GROUPS * SSM_STATE + SSM_HEADS)
EPS = 1e-6
NEG = -1e30
BIG = 1e30

kernel_name = "nsa_ssd_parallel_hybrid"


def _in_split_points():
    gb = SSM_GROUPS * SSM_STATE
    sizes = [ATTN_WIDTH, KV_WIDTH, KV_WIDTH, KV_WIDTH, KV_WIDTH, KV_WIDTH, KV_WIDTH,
             ATTN_HEADS * N_BRANCH, ATTN_WIDTH, SSM_WIDTH, SSM_WIDTH, gb, gb, SSM_HEADS]
    return np.cumsum(sizes)[:-1].tolist()


def rms_norm(x, w):
    xf = x.astype(jnp.float32)
    y = xf * lax.rsqrt(jnp.mean(xf * xf, axis=-1, keepdims=True) + EPS)
    return (y * w.astype(jnp.float32)).astype(x.dtype)


def partial_rope(t, pos):
    half = ROPE_DIM // 2
    inv_freq = ROPE_THETA ** (-jnp.arange(half, dtype=jnp.float32) * 2.0 / ROPE_DIM)
    ang = pos.astype(jnp.float32)[:, None] * inv_freq[None, :]
    cos = jnp.cos(ang)[None, :, None, :].astype(t.dtype)
    sin = jnp.sin(ang)[None, :, None, :].astype(t.dtype)
    t1, t2, rest = t[..., :half], t[..., half:ROPE_DIM], t[..., ROPE_DIM:]
    return jnp.concatenate([t1 * cos - t2 * sin, t2 * cos + t1 * sin, rest], axis=-1)


def masked_softmax(s, mask):
    s = jnp.where(mask, s.astype(jnp.float32), NEG)
    p = jax.nn.softmax(s, axis=-1)
    return jnp.where(mask, p, 0.0)


def compress_tokens(t, pos_emb, w1, b1, w2, b2):
    bsz, g, s, dh = t.shape
    nc = (s - CMP_BLOCK) // CMP_STRIDE + 1
    idx = np.arange(nc)[:, None] * CMP_STRIDE + np.arange(CMP_BLOCK)[None, :]
    blk = t[:, :, idx] + pos_emb
    h = jax.nn.silu(blk.reshape(bsz, g, nc, CMP_BLOCK * dh) @ w1 + b1)
    return h @ w2 + b2


def nsa_compressed(q, kc, vc):
    s_len = q.shape[3]
    nc = kc.shape[2]
    pos = jnp.arange(s_len)
    blk_end = jnp.arange(nc) * CMP_STRIDE + CMP_BLOCK - 1
    mask = blk_end[None, :] <= pos[:, None]
    s = jnp.einsum('bgrsd,bgcd->bgrsc', q, kc) * (HEAD_DIM ** -0.5)
    p = masked_softmax(s, mask)
    o = jnp.einsum('bgrsc,bgcd->bgrsd', p.astype(vc.dtype), vc)
    return o, p


def select_blocks(p_cmp):
    s_len, nc = p_cmp.shape[3], p_cmp.shape[4]
    nb = s_len // SLC_BLOCK
    n_sel = min(SLC_TOPN, nb)
    ci = np.arange(nc)[:, None] * CMP_STRIDE
    bj = np.arange(nb)[None, :] * SLC_BLOCK
    overlap = jnp.asarray((ci <= bj + SLC_BLOCK - 1) & (ci + CMP_BLOCK - 1 >= bj), jnp.float32)
    imp = jnp.einsum('bgrsc,cj->bgsj', p_cmp, overlap)
    j = jnp.arange(nb)[None, :]
    cur = (jnp.arange(s_len) // SLC_BLOCK)[:, None]
    forced = (j == 0) | (j == cur) | (j == cur - 1)
    imp = jnp.where(forced, BIG, imp)
    imp = jnp.where(j > cur, -BIG, imp)
    _, idx = lax.top_k(imp, n_sel)
    return idx


def nsa_selected(q, k, v, blk_idx):
    bsz, g, r, s_len, dh = q.shape
    nb = s_len // SLC_BLOCK
    n_sel = blk_idx.shape[-1]
    nq = s_len // SLC_Q_BLOCK
    kb = k.reshape(bsz, g, nb, SLC_BLOCK, dh)
    vb = v.reshape(bsz, g, nb, SLC_BLOCK, dh)
    gather = jax.vmap(jax.vmap(lambda tb, ib: tb[ib]))
    qs = jnp.moveaxis(q.reshape(bsz, g, r, nq, SLC_Q_BLOCK, dh), 3, 0)
    ids = jnp.moveaxis(blk_idx.reshape(bsz, g, nq, SLC_Q_BLOCK, n_sel), 2, 0)
    ps = jnp.arange(s_len).reshape(nq, SLC_Q_BLOCK)
    offs = jnp.arange(SLC_BLOCK)
    n_keys = n_sel * SLC_BLOCK

    def block(args):
        qc, ic, pc = args
        kg = gather(kb, ic).reshape(bsz, g, SLC_Q_BLOCK, n_keys, dh)
        vg = gather(vb, ic).reshape(bsz, g, SLC_Q_BLOCK, n_keys, dh)
        kpos = (ic[..., None] * SLC_BLOCK + offs).reshape(bsz, g, SLC_Q_BLOCK, n_keys)
        mask = (kpos <= pc[None, None, :, None])[:, :, None]
        s = jnp.einsum('bgrqd,bgqkd->bgrqk', qc, kg) * (HEAD_DIM ** -0.5)
        p = masked_softmax(s, mask)
        return jnp.einsum('bgrqk,bgqkd->bgrqd', p.astype(vg.dtype), vg)

    o = lax.map(block, (qs, ids, ps))
    return jnp.moveaxis(o, 0, 3).reshape(bsz, g, r, s_len, dh)


def nsa_window(q, k, v):
    bsz, g, r, s_len, dh = q.shape
    nq = s_len // Q_BLOCK
    span = WINDOW + Q_BLOCK
    kp = jnp.pad(k, ((0, 0), (0, 0), (WINDOW, 0), (0, 0)))
    vp = jnp.pad(v, ((0, 0), (0, 0), (WINDOW, 0), (0, 0)))
    qs = jnp.moveaxis(q.reshape(bsz, g, r, nq, Q_BLOCK, dh), 3, 0)

    def block(args):
        qc, qb = args
        start = qb * Q_BLOCK
        kc = lax.dynamic_slice_in_dim(kp, start, span, axis=2)
        vc = lax.dynamic_slice_in_dim(vp, start, span, axis=2)
        qpos = start + jnp.arange(Q_BLOCK)
        kpos = start - WINDOW + jnp.arange(span)
        diff = qpos[:, None] - kpos[None, :]
        mask = (diff >= 0) & (diff < WINDOW) & (kpos[None, :] >= 0)
        s = jnp.einsum('bgrqd,bgkd->bgrqk', qc, kc) * (HEAD_DIM ** -0.5)
        p = masked_softmax(s, mask)
        return jnp.einsum('bgrqk,bgkd->bgrqd', p.astype(vc.dtype), vc)

    o = lax.map(block, (qs, jnp.arange(nq)))
    return jnp.moveaxis(o, 0, 3).reshape(bsz, g, r, s_len, dh)


def causal_depthwise_conv(u, w, b):
    out = lax.conv_general_dilated(
        u, w[:, None, :].astype(u.dtype), window_strides=(1,),
        padding=[(CONV_WIDTH - 1, 0)], dimension_numbers=('NWC', 'WIO', 'NWC'),
        feature_group_count=u.shape[-1])
    return out + b


def ssd_scan(xh, dt, a_neg, bm, cm):
    bsz, s_len, nh, hp = xh.shape
    f32 = jnp.float32
    pad = (-s_len) % SSM_CHUNK
    sp = s_len + pad
    nc = sp // SSM_CHUNK
    r = nh // SSM_GROUPS
    xdt = xh.astype(f32) * dt[..., None]
    a = dt * a_neg

    def padf(t):
        return jnp.pad(t, ((0, 0), (0, pad)) + ((0, 0),) * (t.ndim - 2))

    xc = padf(xdt).reshape(bsz, nc, SSM_CHUNK, SSM_GROUPS, r, hp)
    bc = padf(bm.astype(f32)).reshape(bsz, nc, SSM_CHUNK, SSM_GROUPS, SSM_STATE)
    cc = padf(cm.astype(f32)).reshape(bsz, nc, SSM_CHUNK, SSM_GROUPS, SSM_STATE)
    ac = padf(a).reshape(bsz, nc, SSM_CHUNK, SSM_GROUPS, r).transpose(0, 3, 4, 1, 2)
    a_cs = jnp.cumsum(ac, axis=-1)
    tril = np.tril(np.ones((SSM_CHUNK, SSM_CHUNK), dtype=bool))
    decay_in = jnp.exp(jnp.where(tril, a_cs[..., :, None] - a_cs[..., None, :], -jnp.inf))
    cb = jnp.einsum('bclgn,bcsgn->bgcls', cc, bc)
    y_diag = jnp.einsum('bgrcls,bcsgrp->bclgrp', cb[:, :, None] * decay_in, xc)
    decay_to_end = jnp.exp(a_cs[..., -1:] - a_cs)
    states = jnp.einsum('bclgn,bgrcl,bclgrp->cbgrpn', bc, decay_to_end, xc)
    chunk_decay = jnp.moveaxis(jnp.exp(a_cs[..., -1]), -1, 0)

    def step(h, inp):
        st, dec = inp
        return h * dec[..., None, None] + st, h

    h0 = jnp.zeros((bsz, SSM_GROUPS, r, hp, SSM_STATE), f32)
    _, h_in = lax.scan(step, h0, (states, chunk_decay))
    y_off = jnp.einsum('bclgn,cbgrpn,bgrcl->bclgrp', cc, h_in, jnp.exp(a_cs))
    return (y_diag + y_off).reshape(bsz, sp, nh, hp)[:, :s_len]


def hybrid_layer(x, w_in, w_out, pre_w, post_w, cmp_pos, cmp_w1, cmp_b1, cmp_w2, cmp_b2,
                 gate_b, conv_w, conv_b, dt_bias, a_log, d_skip, ssm_norm_w):
    bsz, s_len, _ = x.shape
    pos = jnp.arange(s_len)
    h = rms_norm(x, pre_w)
    proj = h @ w_in
    (q, k_cm, v_cm, k_sl, v_sl, k_wn, v_wn, g_log, z_att,
     z_ssm, x_ssm, b_ssm, c_ssm, dt_raw) = jnp.split(proj, _in_split_points(), axis=-1)

    def heads(t, n):
        return t.reshape(bsz, s_len, n, HEAD_DIM)

    def kv_layout(t):
        return t.transpose(0, 2, 1, 3)

    def q_layout(t):
        return t.reshape(bsz, s_len, KV_HEADS, Q_PER_KV, HEAD_DIM).transpose(0, 2, 3, 1, 4)

    qh = heads(q, ATTN_HEADS)
    q_raw = q_layout(qh)
    q_rot = q_layout(partial_rope(qh, pos))
    kc = compress_tokens(kv_layout(heads(k_cm, KV_HEADS)), cmp_pos[0], cmp_w1[0], cmp_b1[0], cmp_w2[0], cmp_b2[0])
    vc = compress_tokens(kv_layout(heads(v_cm, KV_HEADS)), cmp_pos[1], cmp_w1[1], cmp_b1[1], cmp_w2[1], cmp_b2[1])
    o_cmp, p_cmp = nsa_compressed(q_raw, kc, vc)
    blk_idx = select_blocks(p_cmp)
    o_slc = nsa_selected(q_rot, kv_layout(partial_rope(heads(k_sl, KV_HEADS), pos)),
                         kv_layout(heads(v_sl, KV_HEADS)), blk_idx)
    o_win = nsa_window(q_rot, kv_layout(partial_rope(heads(k_wn, KV_HEADS), pos)),
                       kv_layout(heads(v_wn, KV_HEADS)))
    gates = jax.nn.sigmoid(g_log + gate_b).reshape(bsz, s_len, KV_HEADS, Q_PER_KV, N_BRANCH)
    gates = gates.transpose(0, 2, 3, 1, 4)
    o_att = gates[..., 0:1] * o_cmp + gates[..., 1:2] * o_slc + gates[..., 2:3] * o_win
    attn_out = o_att.transpose(0, 3, 1, 2, 4).reshape(bsz, s_len, ATTN_WIDTH) * jax.nn.silu(z_att)

    xbc = jnp.concatenate([x_ssm, b_ssm, c_ssm], axis=-1)
    xbc = jax.nn.silu(causal_depthwise_conv(xbc, conv_w, conv_b))
    x_c, b_c, c_c = jnp.split(xbc, [SSM_WIDTH, SSM_WIDTH + SSM_GROUPS * SSM_STATE], axis=-1)
    dt = jax.nn.softplus(dt_raw.astype(jnp.float32) + dt_bias.astype(jnp.float32))
    a_neg = -jnp.exp(a_log.astype(jnp.float32))
    xh = x_c.reshape(bsz, s_len, SSM_HEADS, SSM_HEAD_DIM)
    y = ssd_scan(xh, dt, a_neg,
                 b_c.reshape(bsz, s_len, SSM_GROUPS, SSM_STATE),
                 c_c.reshape(bsz, s_len, SSM_GROUPS, SSM_STATE))
    y = y + d_skip.astype(jnp.float32)[:, None] * xh.astype(jnp.float32)
    y = y.reshape(bsz, s_len, SSM_WIDTH) * jax.nn.silu(z_ssm.astype(jnp.float32))
    ssm_out = rms_norm(y, ssm_norm_w).astype(x.dtype)

    mix = jnp.concatenate([attn_out.astype(x.dtype), ssm_out], axis=-1)
    return x + rms_norm(mix @ w_out, post_w)


def setup_inputs(seed: int = 0) -> dict:
    key = jax.random.key(seed)
    ks = jax.random.split(key, 20)
    f32 = jnp.float32
    nrm = lambda k, shape, scale: jax.random.normal(k, shape, f32) * scale
    dt0 = jnp.exp(jax.random.uniform(ks[14], (DEPTH, SSM_HEADS), f32)
                  * (jnp.log(0.1) - jnp.log(0.001)) + jnp.log(0.001))
    return {
        "x": nrm(ks[0], (BATCH, SEQ, D_MODEL), 1.0),
        "w_in": nrm(ks[1], (DEPTH, D_MODEL, IN_WIDTH), D_MODEL ** -0.5),
        "w_out": nrm(ks[2], (DEPTH, MIX_WIDTH, D_MODEL), MIX_WIDTH ** -0.5),
        "pre_norm_w": 1.0 + nrm(ks[3], (DEPTH, D_MODEL), 0.02),
        "post_norm_w": 1.0 + nrm(ks[4], (DEPTH, D_MODEL), 0.02),
        "cmp_pos": nrm(ks[5], (DEPTH, 2, CMP_BLOCK, HEAD_DIM), 0.02),
        "cmp_w1": nrm(ks[6], (DEPTH, 2, CMP_BLOCK * HEAD_DIM, CMP_HIDDEN), (CMP_BLOCK * HEAD_DIM) ** -0.5),
        "cmp_b1": nrm(ks[7], (DEPTH, 2, CMP_HIDDEN), 0.01),
        "cmp_w2": nrm(ks[8], (DEPTH, 2, CMP_HIDDEN, HEAD_DIM), CMP_HIDDEN ** -0.5),
        "cmp_b2": nrm(ks[9], (DEPTH, 2, HEAD_DIM), 0.01),
        "gate_b": nrm(ks[10], (DEPTH, ATTN_HEADS * N_BRANCH), 0.01),
        "conv_w": nrm(ks[11], (DEPTH, CONV_WIDTH, CONV_CH), CONV_WIDTH ** -0.5),
        "conv_b": nrm(ks[12], (DEPTH, CONV_CH), 0.01),
        "dt_bias": dt0 + jnp.log(-jnp.expm1(-dt0)),
        "a_log": jnp.log(jax.random.uniform(ks[15], (DEPTH, SSM_HEADS), f32, 1.0, 16.0)),
        "d_skip": 1.0 + nrm(ks[16], (DEPTH, SSM_HEADS), 0.02),
        "ssm_norm_w": 1.0 + nrm(ks[17], (DEPTH, SSM_WIDTH), 0.02),
    }


def reference(x, w_in, w_out, pre_norm_w, post_norm_w, cmp_pos, cmp_w1, cmp_b1, cmp_w2, cmp_b2,
              gate_b, conv_w, conv_b, dt_bias, a_log, d_skip, ssm_norm_w):
    for l in range(DEPTH):
        x = hybrid_layer(x, w_in[l], w_out[l], pre_norm_w[l], post_norm_w[l],
                         cmp_pos[l], cmp_w1[l], cmp_b1[l], cmp_w2[l], cmp_b2[l],
                         gate_b[l], conv_w[l], conv_b[l], dt_bias[l], a_log[l],
                         d_skip[l], ssm_norm_w[l])
    return x
```

```python
import numpy as np
import concourse.bass as bass
import concourse.mybir as mybir
from concourse.bass_utils import run_bass_kernel_spmd

F32 = mybir.dt.float32
BF16 = mybir.dt.bfloat16
AF = mybir.ActivationFunctionType
ALU = mybir.AluOpType

S_LEN = 2048
D = 1024
NT = 16
NEGB = -1.0e9
EPS = 1e-6
SAME_ENGINE_SYNC = True

O_Q, O_KCM, O_VCM, O_KSL, O_VSL, O_KWN, O_VWN, O_GLOG, O_ZATT, O_ZSSM, O_XSSM, O_B, O_C, O_DT = (
    0, 1024, 1280, 1536, 1792, 2048, 2304, 2560, 2608, 3632, 4656, 5680, 6192, 6704)


class Res:
    __slots__ = ("name", "last_write", "readers", "excl", "rg")

    def __init__(self, name, excl=False):
        self.name = name
        self.last_write = None
        self.readers = {}
        self.excl = excl
        self.rg = None


class Sched:
    ENGS = ("pe", "act", "dve", "pool", "sp")

    def __init__(self, nc, n_dma_sems=8):
        self.nc = nc
        self.eng = {"pe": nc.tensor, "act": nc.scalar, "dve": nc.vector, "pool": nc.gpsimd, "sp": nc.sync}
        self.sem = {e: nc.alloc_semaphore("s_" + e) for e in ("pe", "act", "dve", "pool")}
        self.cnt = {e: 0 for e in ("pe", "act", "dve", "pool")}
        self.dsem = [nc.alloc_semaphore("s_dma%d" % i) for i in range(n_dma_sems)]
        self.dcnt = [0] * n_dma_sems
        half = n_dma_sems // 2
        self.dpool = {"sp": list(range(0, half)), "pool": list(range(half, n_dma_sems))}
        self.dnext = {"sp": 0, "pool": 0}
        self.seen = {e: {} for e in self.ENGS}
        self.nops = 0

    def _semof(self, f):
        return self.dsem[f] if isinstance(f, int) else self.sem[f]

    def _collect(self, e, reads, writes, rg=None):
        waits = {}

        def need(ev, force=False):
            f, c = ev
            if f == e and (e == "pe" or not SAME_ENGINE_SYNC) and not force:
                return
            if self.seen[e].get(f, 0) >= c:
                return
            if waits.get(f, 0) < c:
                waits[f] = c

        for r in reads:
            if r.excl:
                if r.last_write:
                    need(r.last_write)
                for ev in r.readers.items():
                    need(ev)
            elif r.last_write:
                need(r.last_write)
        for w in writes:
            if w.last_write:
                force = False
                if e == "pe" and w.excl and rg is not None and w.rg is not None:
                    if rg[1] <= w.rg[0] or w.rg[1] <= rg[0]:
                        force = True
                need(w.last_write, force)
            for ev in w.readers.items():
                need(ev)
        return waits

    def _emit_waits(self, e, waits):
        for f, c in waits.items():
            self.seen[e][f] = c
            self.eng[e].wait_ge(self._semof(f), c)

    def op(self, e, fn, reads=(), writes=(), rg=None):
        waits = self._collect(e, reads, writes, rg)
        self._emit_waits(e, waits)
        self.cnt[e] += 1
        c = self.cnt[e]
        fn(self.eng[e]).then_inc(self.sem[e], 1)
        self.nops += 1
        for r in reads:
            if r.excl:
                r.last_write = (e, c)
                r.readers = {}
            elif r.readers.get(e, 0) < c:
                r.readers[e] = c
        for w in writes:
            w.last_write = (e, c)
            w.readers = {}
            if e == "pe" and w.excl:
                w.rg = rg
        return (e, c)

    def dma(self, fn, reads=(), writes=(), q="sp"):
        waits = self._collect(q, reads, writes)
        pool_ = self.dpool[q]
        s = pool_[self.dnext[q] % len(pool_)]
        self.dnext[q] += 1
        if self.dcnt[s] > 0 and self.seen[q].get(s, 0) < self.dcnt[s]:
            if waits.get(s, 0) < self.dcnt[s]:
                waits[s] = self.dcnt[s]
        self._emit_waits(q, waits)
        self.dcnt[s] += 16
        c = self.dcnt[s]
        fn(self.eng[q]).then_inc(self.dsem[s], 16)
        self.nops += 1
        for r in reads:
            if r.readers.get(s, 0) < c:
                r.readers[s] = c
        for w in writes:
            w.last_write = (s, c)
            w.readers = {}
        return (s, c)

    def barrier(self):
        for e in self.ENGS:
            waits = {}
            for f in ("pe", "act", "dve", "pool"):
                if f != e and self.cnt[f] > self.seen[e].get(f, 0):
                    waits[f] = self.cnt[f]
            for s in range(len(self.dsem)):
                if self.dcnt[s] > self.seen[e].get(s, 0):
                    waits[s] = self.dcnt[s]
            self._emit_waits(e, waits)

    def finish(self):
        waits = {}
        for s in range(len(self.dsem)):
            if self.dcnt[s] > self.seen["sp"].get(s, 0):
                waits[s] = self.dcnt[s]
        self._emit_waits("sp", waits)


class Arena:
    def __init__(self, nc):
        self.nc = nc
        self.base = (nc.sbuf_base + 63) // 64 * 64
        self.top = nc.sbuf_top
        self.cur = self.base
        self.n = 0

    def alloc(self, name, shape, dt):
        esz = 2 if dt == BF16 else 4
        per = esz
        for s in shape[1:]:
            per *= s
        off = self.cur
        self.cur = (off + per + 63) // 64 * 64
        assert self.cur <= self.top, "SBUF overflow at %s: %d > %d" % (name, self.cur, self.top)
        self.n += 1
        self.last_off = off
        return self.nc.alloc_sbuf_tensor_at("%s_%d" % (name, self.n), list(shape), dt, offset=off)

    def mark(self):
        return self.cur

    def reset(self, m):
        self.cur = m


def _tile_cols():
    def partner(d):
        return d + 8 if d < 8 else (d - 8 if d < 16 else -1)

    tiles = []
    tmg = [O_GLOG + i for i in range(48)] + [O_DT + i for i in range(16)] + [-1] * 64
    tiles.append(("tmg", tmg))
    for g in range(4):
        tiles.append(("kvcm%d" % g, [O_KCM + g * 64 + d for d in range(64)] + [O_VCM + g * 64 + d for d in range(64)]))
    for g in range(4):
        for j in range(2):
            hs = (4 * g + 2 * j, 4 * g + 2 * j + 1)
            tiles.append(("q", [O_Q + h * 64 + d for h in hs for d in range(64)]))
        for off in (O_KSL, O_KWN):
            tiles.append(("k", [off + g * 64 + d for d in range(64)] * 2))
        for j in range(2):
            hs = (4 * g + 2 * j, 4 * g + 2 * j + 1)
            tiles.append(("z", [O_ZATT + h * 64 + d for h in hs for d in range(64)]))
        tiles.append(("tmv", [O_VSL + g * 64 + d for d in range(64)] + [O_VWN + g * 64 + d for d in range(64)]))
    for g in range(4):
        for off in (O_ZSSM, O_XSSM):
            for j in range(2):
                h0 = 4 * g + 2 * j
                tiles.append(("s", [off + h0 * 64 + i for i in range(128)]))
        tiles.append(("b", [O_B + g * 128 + i for i in range(128)]))
        tiles.append(("c", [O_C + g * 128 + i for i in range(128)]))
    return tiles


N_WT = 1 + 4 + 4 * 7 + 4 * 6


def host_consts():
    c = {}
    half = 8
    inv_freq = (500000.0 ** (-(np.arange(half, dtype=np.float32) * 2.0 / 16))).astype(np.float32)
    pos = np.arange(S_LEN, dtype=np.float32)
    ang = pos[:, None] * inv_freq[None, :]
    cos = np.cos(ang).astype(np.float32).T
    sin = np.sin(ang).astype(np.float32).T
    C = np.ones((128, S_LEN), np.float32)
    Sg = np.zeros((128, S_LEN), np.float32)
    for hp in range(2):
        for d in range(16):
            p = hp * 64 + d
            C[p] = cos[d % 8]
            Sg[p] = -sin[d % 8] if d < 8 else sin[d % 8]
    c["ropeC"] = C
    c["ropeS"] = Sg
    k = np.arange(128)[:, None]
    q = np.arange(128)[None, :]
    c["causalb"] = np.where(k <= q, 0.0, NEGB).astype(np.float32)
    c["antib"] = np.where(k > q, 0.0, NEGB).astype(np.float32)
    c["trimask"] = (k <= q).astype(np.float32)
    pm = np.zeros((128, 128), np.float32)
    for dst in range(128):
        dl = dst % 64
        if dl < 16:
            pm[(dst // 64) * 64 + (dl + 8 if dl < 8 else dl - 8), dst] = 1.0
    c["permm"] = pm
    c["antimask"] = (k > q).astype(np.float32)
    cc = np.arange(128)[:, None]
    qq = np.arange(S_LEN)[None, :]
    c["cmpbias"] = np.where((16 * cc + 31 <= qq) & (cc < 127), 0.0, NEGB).astype(np.float32)
    E = np.zeros((128, 16, 128), np.float32)
    for kt in range(16):
        for kk in range(128):
            j = 2 * kt + kk // 64
            E[j, kt, kk] = 1.0
            E[64 + j, kt, kk] = 1.0
    c["Eall"] = E.reshape(128, 16 * 128)
    c["ident"] = np.eye(128, dtype=np.float32)
    tok = np.arange(S_LEN)
    cur = tok // 64
    j = np.arange(32)[None, :]
    A = np.full((S_LEN, 32), -3.0e38, np.float32)
    A[np.arange(S_LEN), np.maximum(cur - 1, 0)] = 1.0e30
    A[np.arange(S_LEN), cur] = 2.0e30
    A[:, 0] = 3.0e30
    Bm = np.where(j > cur[:, None], -1.0e30, 3.0e38).astype(np.float32)
    c["topkA"] = A.reshape(16, 128, 32).transpose(1, 0, 2).reshape(128, 512).copy()
    c["topkB"] = Bm.reshape(16, 128, 32).transpose(1, 0, 2).reshape(128, 512).copy()
    ci = np.arange(128)[:, None] * 16
    bj = np.arange(32)[None, :] * 64
    ov = ((ci <= bj + 63) & (ci + 31 >= bj)).astype(np.float32)
    ov[127] = 0.0
    vx = np.zeros((128, 33), np.float32)
    vx[:, 0] = 1.0
    vx[:, 1:] = ov
    c["vcext"] = vx
    tri = np.zeros((128, 256), np.float32)
    tri[:, 0:128] = (k <= q)
    tri[:, 128:256] = 1.0
    c["tri2"] = tri
    return c


CONST_SHAPES = {"ropeC": [128, 2048], "ropeS": [128, 2048], "causalb": [128, 128], "antib": [128, 128],
                "trimask": [128, 128], "permm": [128, 128], "antimask": [128, 128], "cmpbias": [128, 2048], "Eall": [128, 2048], "ident": [128, 128],
                "topkA": [128, 512], "topkB": [128, 512], "vcext": [128, 33], "tri2": [128, 256]}

SMALL_SHAPES = {"prew": [1, 1024], "postw": [1, 1024], "posT": [128, 32], "b1T": [128, 4], "w2k": [128, 256],
                "w2v": [128, 128], "b2k": [128, 1], "b2v": [1, 64], "gateb": [1, 48], "convw": [128, 64],
                "convb": [128, 16], "dtb": [1, 16], "alog": [1, 16], "dskip": [128, 8], "snw": [128, 8]}


def host_small(inp):
    s = {}
    s["prew"] = inp["pre_norm_w"][0][None, :]
    s["postw"] = inp["post_norm_w"][0][None, :]
    pos = inp["cmp_pos"][0]
    s["posT"] = np.concatenate([pos[0].T, pos[1].T], 0)
    b1 = inp["cmp_b1"][0]
    s["b1T"] = np.stack([b1[kv, hc * 128:(hc + 1) * 128] for kv in range(2) for hc in range(2)], 1)
    w2 = inp["cmp_w2"][0]
    w2k = w2[0].reshape(2, 128, 64).transpose(1, 0, 2)
    s["w2k"] = np.concatenate([w2k, w2k], 2).reshape(128, 256)
    s["w2v"] = w2[1].reshape(2, 128, 64).transpose(1, 0, 2).reshape(128, 128)
    b2 = inp["cmp_b2"][0]
    s["b2k"] = np.concatenate([b2[0], b2[0]])[:, None]
    s["b2v"] = b2[1][None, :]
    s["gateb"] = inp["gate_b"][0][None, :]
    cw = inp["conv_w"][0]
    s["convw"] = cw.T.reshape(16, 128, 4).transpose(1, 0, 2).reshape(128, 64)
    s["convb"] = inp["conv_b"][0].reshape(16, 128).T
    s["dtb"] = inp["dt_bias"][0][None, :]
    s["alog"] = inp["a_log"][0][None, :]
    s["dskip"] = np.repeat(inp["d_skip"][0], 64).reshape(8, 128).T
    s["snw"] = inp["ssm_norm_w"][0].reshape(8, 128).T
    return {k: np.ascontiguousarray(v, dtype=np.float32) for k, v in s.items()}


def host_wtiles(w_in):
    tiles = _tile_cols()
    assert len(tiles) == N_WT
    out = np.zeros((N_WT, 128, 8, 128), np.float32)
    for t, (_, cols) in enumerate(tiles):
        cols = np.asarray(cols)
        wc = np.zeros((1024, 128), np.float32)
        m = cols >= 0
        wc[:, m] = w_in[:, cols[m]]
        out[t] = wc.reshape(8, 128, 128).transpose(1, 0, 2)
    return out.reshape(N_WT, 128, 1024)


class _Stop(Exception):
    pass


def build(debug=(), stop=None):
    nc = bass.Bass("TRN2", target_bir_lowering=False)
    S = Sched(nc, n_dma_sems=16)
    dbg_outs = {}
    try:
        _build_body(nc, S, dbg_outs, debug, stop)
    except _Stop:
        pass
    S.finish()
    return nc, dbg_outs


def _build_body(nc, S, dbg_outs, debug, stop):
    AR = Arena(nc)

    def chk(name):
        if stop == name:
            raise _Stop()

    def din(name, shape):
        return nc.dram_tensor(name, list(shape), F32, kind="ExternalInput").ap()

    x_d = din("x", [S_LEN, D])
    wt_d = din("wt", [N_WT, 128, 1024])
    w1_d = din("w1", [2, 2048, 256])
    wout_d = din("wout", [2048, 1024])
    cst_d = {k: din(k, v) for k, v in CONST_SHAPES.items()}
    sm_d = {k: din(k, v) for k, v in SMALL_SHAPES.items()}
    out_d = nc.dram_tensor("out", [S_LEN, D], F32, kind="ExternalOutput").ap()

    def dbg(name, ap, res, shape, dt=BF16):
        if name not in debug:
            return
        d = nc.dram_tensor("dbg_" + name, list(shape), dt, kind="ExternalOutput").ap()
        dbg_outs[name] = shape
        S.dma(lambda e: e.dma_start(out=d, in_=ap), reads=[res])

    pb = [nc.alloc_psum_tensor("pb%d" % i, [128, 512], F32) for i in range(7)]
    rpb = [Res("pb%d" % i, excl=True) for i in range(7)]
    pbt = nc.alloc_psum_tensor("pbt", [128, 1024], BF16)
    rpbt = Res("pbt", excl=True)

    def G(name, shape, dt):
        return AR.alloc(name, shape, dt), Res(name)

    hT, r_hT = G("hT", [128, 8, S_LEN], BF16)
    r_hTc = [Res("hTc%d" % i) for i in range(4)]
    woutb = nc.alloc_sbuf_tensor_at("woutb_alias", [128, 16, 1024], BF16, offset=AR.last_off)
    r_woutb = Res("woutb")
    mixT, r_mix = G("mixT", [128, 16, S_LEN], BF16)
    r_mixt = [Res("mix%d" % i) for i in range(16)]
    ropeC, r_ropeC = G("ropeC", [128, S_LEN], BF16)
    ropeS, r_ropeS = G("ropeS", [128, S_LEN], BF16)
    cmpbias, r_cmpb = G("cmpbias", [128, S_LEN], BF16)
    ident, r_id = G("ident", [128, 128], BF16)
    causalb, r_cb = G("causalb", [128, 128], BF16)
    antib, r_ab = G("antib", [128, 128], BF16)
    trimask, r_tm = G("trimask", [128, 128], BF16)
    permm, r_pm = G("permm", [128, 128], BF16)
    antimask, r_am = G("antimask", [128, 128], BF16)
    topkA, r_tA = G("topkA", [128, 16, 32], BF16)
    topkB, r_tB = G("topkB", [128, 16, 32], BF16)
    onesf, r_ones = G("onesf", [128, 1], F32)
    wbf = [G("wbf%d" % i, [128, 8, 128], BF16) for i in range(4)]
    gates, r_gates = G("gates", [128, 16, 48], F32)
    dtv, r_dtv = G("dtv", [128, 16, 16], F32)
    a_tm, r_atm = G("a_tm", [128, 16, 16], F32)
    a3 = [G("a3_%d" % i, [128, 16, 16], BF16) for i in range(3)]
    tri2b, r_trib = G("tri2b", [128, 256], BF16)
    acs_tm, r_acs = G("acs_tm", [128, 16, 16], F32)
    ssq, r_ssq = G("ssq", [128, 16], F32)
    kcT2, r_kc = G("kcT2", [128, 4, 128], BF16)
    vcaug, r_vc = G("vcaug", [128, 4, 97], BF16)
    sm = {}
    for k, shp in SMALL_SHAPES.items():
        if shp[0] == 128:
            sm[k] = G("sm_" + k, shp, F32)
    bc = {}
    for k in ("b2v", "gateb", "dtb", "alog"):
        bc[k] = G("bc_" + k, [128, SMALL_SHAPES[k][1]], F32)
    aneg, r_aneg = G("aneg", [128, 16], F32)
    bias1, r_bias1 = G("bias1", [128, 4], F32)
    w2kb, r_w2kb = G("w2kb", [128, 2, 128], BF16)
    w2vb, r_w2vb = G("w2vb", [128, 2, 64], BF16)
    posTb, r_posTb = G("posTb", [128, 32], BF16)
    PHASE = AR.mark()
    tri2, r_tri = G("tri2", [128, 256], F32)
    cstg, r_cstg = G("cstg", [128, 64], F32)

    def load_const(name, dst, rdst, ncols, view=None):
        o = dst[:] if view is None else view
        S.dma(lambda e: e.dma_start(out=o, in_=cst_d[name]), writes=[rdst], q="pool")

    load_const("ropeC", ropeC, r_ropeC, 2048)
    load_const("ropeS", ropeS, r_ropeS, 2048)
    load_const("cmpbias", cmpbias, r_cmpb, 2048)
    load_const("ident", ident, r_id, 128)
    load_const("causalb", causalb, r_cb, 128)
    load_const("antib", antib, r_ab, 128)
    load_const("trimask", trimask, r_tm, 128)
    load_const("permm", permm, r_pm, 128)
    load_const("antimask", antimask, r_am, 128)
    S.dma(lambda e: e.dma_start(out=topkA[:].rearrange("p a b -> p (a b)"), in_=cst_d["topkA"]), writes=[r_tA], q="pool")
    S.dma(lambda e: e.dma_start(out=topkB[:].rearrange("p a b -> p (a b)"), in_=cst_d["topkB"]), writes=[r_tB], q="pool")
    S.dma(lambda e: e.dma_start(out=tri2[:], in_=cst_d["tri2"]), writes=[r_tri])
    S.dma(lambda e: e.dma_start(out=tri2b[:], in_=cst_d["tri2"]), writes=[r_trib], q="pool")
    S.op("pool", lambda e: e.memset(onesf[:], 1.0), writes=[r_ones])
    for k in sm:
        t, r = sm[k]
        S.dma(lambda e, t=t, k=k: e.dma_start(out=t[:], in_=sm_d[k]), writes=[r])
    for k in bc:
        t, r = bc[k]
        S.dma(lambda e, t=t, k=k: e.dma_start(out=t[:], in_=sm_d[k].partition_broadcast(128)), writes=[r])
    S.dma(lambda e: e.dma_start(out=cstg[:, 0:33], in_=cst_d["vcext"]), writes=[r_cstg])
    for g in range(4):
        S.op("pool", lambda e, g=g: e.tensor_copy(out=vcaug[:, g, 64:97], in_=cstg[:, 0:33]), reads=[r_cstg], writes=[r_vc])
    S.op("act", lambda e: e.activation(out=aneg[:], in_=bc["alog"][0][:], func=AF.Exp), reads=[bc["alog"][1]], writes=[r_aneg])
    S.op("dve", lambda e: e.tensor_scalar(out=aneg[:], in0=aneg[:], scalar1=-1.0, scalar2=None, op0=ALU.mult), reads=[r_aneg], writes=[r_aneg])
    S.op("pool", lambda e: e.tensor_copy(out=w2kb[:].rearrange("p a b -> p (a b)"), in_=sm["w2k"][0][:]), reads=[sm["w2k"][1]], writes=[r_w2kb])
    S.op("pool", lambda e: e.tensor_copy(out=w2vb[:].rearrange("p a b -> p (a b)"), in_=sm["w2v"][0][:]), reads=[sm["w2v"][1]], writes=[r_w2vb])
    S.op("pool", lambda e: e.tensor_copy(out=posTb[:], in_=sm["posT"][0][:]), reads=[sm["posT"][1]], writes=[r_posTb])
    S.op("pool", lambda e: e.memset(ssq[:], 0.0), writes=[r_ssq])

    wstate = {"next": 0, "issued": 0}
    WPF = 2

    def load_wtile():
        t = wstate["next"]
        wstate["next"] += 1
        while wstate["issued"] <= min(t + WPF, N_WT - 1):
            ti = wstate["issued"]
            wstate["issued"] += 1
            wb_, rwb_ = wbf[ti % 4]
            S.dma(lambda e, ti=ti, wb_=wb_: e.dma_start(out=wb_[:].rearrange("p a b -> p (a b)"), in_=wt_d[ti]), writes=[rwb_], q="pool")
        return wbf[t % 4]

    bank_rr = {"i": 0}

    def next_bank(lo=0, hi=2):
        i = lo + bank_rr["i"] % (hi - lo)
        bank_rr["i"] += 1
        return i

    def fm_matmuls(wb, rwb, tc, bi):
        for kc in range(8):
            S.op("pe", lambda e, kc=kc: e.matmul(pb[bi][:, :], lhsT=wb[:, kc, :], rhs=hT[:, kc, tc * 512:(tc + 1) * 512],
                                                  start=(kc == 0), stop=(kc == 7)),
                 reads=[rwb, r_hTc[tc]], writes=[rpb[bi]], rg=(0, 4))

    xt = [G("xt%d" % i, [128, 1024], F32) for i in range(2)]
    pw, r_pw = G("pw", [128, 1024], F32)
    junk, r_junk = G("junk", [128, 1024], F32)
    hb = [G("hb%d" % i, [128, 1024], BF16) for i in range(2)]
    ss0, r_ss0 = G("ss0", [128, 16], F32)
    S.dma(lambda e: e.dma_start(out=pw[:], in_=sm_d["prew"].partition_broadcast(128)), writes=[r_pw])
    hb3 = hb + [G("hb2", [128, 1024], BF16)]
    xt3 = xt + [G("xt2_", [128, 1024], F32)]

    def p0_a(tt):
        xb, rxb = xt3[tt % 3]
        hbb, rhbb = hb3[tt % 3]
        S.dma(lambda e: e.dma_start(out=xb[:], in_=x_d[tt * 128:(tt + 1) * 128, :]), writes=[rxb])
        S.op("act", lambda e: e.activation(out=junk[:], in_=xb[:], func=AF.Square, accum_out=ss0[:, tt:tt + 1]), reads=[rxb], writes=[r_junk, r_ss0])
        S.op("dve", lambda e: e.tensor_scalar(out=ss0[:, tt:tt + 1], in0=ss0[:, tt:tt + 1], scalar1=1.0 / D, scalar2=EPS, op0=ALU.mult, op1=ALU.add), reads=[r_ss0], writes=[r_ss0])
        S.op("act", lambda e: e.activation(out=ss0[:, tt:tt + 1], in_=ss0[:, tt:tt + 1], func=AF.Sqrt), reads=[r_ss0], writes=[r_ss0])
        S.op("dve", lambda e: e.reciprocal(out=ss0[:, tt:tt + 1], in_=ss0[:, tt:tt + 1]), reads=[r_ss0], writes=[r_ss0])
        S.op("dve", lambda e: e.scalar_tensor_tensor(out=hbb[:], in0=xb[:], scalar=ss0[:, tt:tt + 1], in1=pw[:], op0=ALU.mult, op1=ALU.mult),
             reads=[rxb, r_ss0, r_pw], writes=[rhbb])

    def p0_b(tt):
        hbb, rhbb = hb3[tt % 3]
        for kc in range(8):
            S.op("pe", lambda e, kc=kc: e.transpose(out=pbt[:, kc * 128:(kc + 1) * 128], in_=hbb[:, kc * 128:(kc + 1) * 128], identity=ident[:]),
                 reads=[rhbb, r_id], writes=[rpbt], rg=(0, 4))
        S.op("act", lambda e: e.copy(out=hT[:, 0:4, tt * 128:(tt + 1) * 128], in_=pbt[:, 0:512].rearrange("p (k t) -> p k t", k=4)),
             reads=[rpbt], writes=[r_hTc[tt // 4]])
        S.op("dve", lambda e: e.tensor_copy(out=hT[:, 4:8, tt * 128:(tt + 1) * 128], in_=pbt[:, 512:1024].rearrange("p (k t) -> p k t", k=4)),
             reads=[rpbt], writes=[r_hTc[tt // 4]])

    p0_a(0)
    for tt in range(NT):
        if tt + 1 < NT:
            p0_a(tt + 1)
        p0_b(tt)
    dbg("hT", hT[:, 0, :], r_hTc[3], [128, 2048])
    chk("p0")

    graw, r_graw = G("graw", [128, 16, 64], F32)
    wb, rwb = load_wtile()
    for tt in range(NT):
        bi = next_bank()
        for kc in range(8):
            S.op("pe", lambda e, kc=kc: e.matmul(pb[bi][:, 0:128], lhsT=hT[:, kc, tt * 128:(tt + 1) * 128], rhs=wb[:, kc, :],
                                                  start=(kc == 0), stop=(kc == 7)), reads=[rwb, r_hTc[tt // 4]], writes=[rpb[bi]], rg=(0, 4))
        S.op("dve", lambda e: e.tensor_tensor(out=graw[:, tt, 0:48], in0=pb[bi][:, 0:48], in1=bc["gateb"][0][:], op=ALU.add),
             reads=[rpb[bi], bc["gateb"][1]], writes=[r_graw])
        S.op("dve", lambda e: e.tensor_tensor(out=graw[:, tt, 48:64], in0=pb[bi][:, 48:64], in1=bc["dtb"][0][:], op=ALU.add),
             reads=[rpb[bi], bc["dtb"][1]], writes=[r_graw])
    S.op("act", lambda e: e.activation(out=gates[:], in_=graw[:, :, 0:48], func=AF.Sigmoid), reads=[r_graw], writes=[r_gates])
    S.op("act", lambda e: e.activation(out=dtv[:], in_=graw[:, :, 48:64], func=AF.Exp), reads=[r_graw], writes=[r_dtv])
    S.op("act", lambda e: e.activation(out=dtv[:], in_=dtv[:], func=AF.Ln, bias=1.0, scale=1.0), reads=[r_dtv], writes=[r_dtv])
    S.op("dve", lambda e: e.tensor_tensor(out=a_tm[:], in0=dtv[:], in1=aneg[:].unsqueeze(1).to_broadcast([128, 16, 16]), op=ALU.mult),
         reads=[r_dtv, r_aneg], writes=[r_atm])
    ares, r_ares = G("ares", [128, 16, 16], F32)
    S.op("dve", lambda e: e.tensor_copy(out=a3[0][0][:], in_=a_tm[:]), reads=[r_atm], writes=[a3[0][1]])
    S.op("dve", lambda e: e.tensor_tensor(out=ares[:], in0=a_tm[:], in1=a3[0][0][:], op=ALU.subtract), reads=[r_atm, a3[0][1]], writes=[r_ares])
    S.op("dve", lambda e: e.tensor_copy(out=a3[1][0][:], in_=ares[:]), reads=[r_ares], writes=[a3[1][1]])
    S.op("dve", lambda e: e.tensor_tensor(out=ares[:], in0=ares[:], in1=a3[1][0][:], op=ALU.subtract), reads=[r_ares, a3[1][1]], writes=[r_ares])
    S.op("dve", lambda e: e.tensor_copy(out=a3[2][0][:], in_=ares[:]), reads=[r_ares], writes=[a3[2][1]])
    for c in range(8):
        bi = next_bank()
        S.op("pe", lambda e: e.matmul(pb[bi][:, 0:16], lhsT=tri2[:, 0:128], rhs=a_tm[:, 2 * c, :], start=True, stop=True),
             reads=[r_tri, r_atm], writes=[rpb[bi]], rg=(0, 4))
        S.op("pe", lambda e: e.matmul(pb[bi][:, 16:32], lhsT=tri2[:, 128:256], rhs=a_tm[:, 2 * c, :], start=True, stop=False),
             reads=[r_tri, r_atm], writes=[rpb[bi]], rg=(0, 4))
        S.op("pe", lambda e: e.matmul(pb[bi][:, 16:32], lhsT=tri2[:, 0:128], rhs=a_tm[:, 2 * c + 1, :], start=False, stop=True),
             reads=[r_tri, r_atm], writes=[rpb[bi]], rg=(0, 4))
        S.op("dve", lambda e: e.tensor_copy(out=acs_tm[:, 2 * c:2 * c + 2, :], in_=pb[bi][:, 0:32].rearrange("p (a b) -> p a b", a=2)),
             reads=[rpb[bi]], writes=[r_acs])
    dbg("gates", gates[:].rearrange("p a b -> p (a b)"), r_gates, [128, 768], F32)
    dbg("dtv", dtv[:].rearrange("p a b -> p (a b)"), r_dtv, [128, 256], F32)
    dbg("acs", acs_tm[:].rearrange("p a b -> p (a b)"), r_acs, [128, 256], F32)
    chk("pg")

    kvT, r_kvT = G("kvT", [128, 4, S_LEN], BF16)
    w1b, r_w1b = G("w1b", [128, 32, 256], BF16)
    hid = [G("hid%d" % i, [128, 4, 128], BF16) for i in range(4)]
    for pc in range(4):
        for kv in range(2):
            S.dma(lambda e, kv=kv: e.dma_start(out=w1b[kv * 64:(kv + 1) * 64, pc * 8:(pc + 1) * 8, :],
                                               in_=w1_d[kv, pc * 512:(pc + 1) * 512, :].rearrange("(l d) h -> d l h", d=64)), writes=[r_w1b], q="pool")
    for g in range(4):
        wb, rwb = load_wtile()
        for tc in range(4):
            bi = next_bank()
            fm_matmuls(wb, rwb, tc, bi)
            S.op("act", lambda e: e.copy(out=kvT[:, g, tc * 512:(tc + 1) * 512], in_=pb[bi][:, :]), reads=[rpb[bi]], writes=[r_kvT])
    for kv in range(2):
        rows = slice(kv * 64, kv * 64 + 64)
        for hc in range(2):
            bi = 2 + kv
            for l in range(32):
                S.op("pe", lambda e, l=l: e.matmul(pb[bi][:, 0:1], lhsT=w1b[rows, l, hc * 128:(hc + 1) * 128], rhs=posTb[rows, l:l + 1],
                                                    start=(l == 0), stop=(l == 31)), reads=[r_w1b, r_posTb], writes=[rpb[bi]], rg=(2 * kv, 2 * kv + 2))
            col = kv * 2 + hc
            S.op("dve", lambda e: e.tensor_tensor(out=bias1[:, col:col + 1], in0=pb[bi][:, 0:1], in1=sm["b1T"][0][:, col:col + 1], op=ALU.add),
                 reads=[rpb[bi], sm["b1T"][1]], writes=[r_bias1])
    for hc in range(2):
        for l in range(32):
            for kv in range(2):
                rows = slice(kv * 64, kv * 64 + 64)
                bi = 2 + 2 * (hc % 2) + kv
                S.op("pe", lambda e, l=l: e.matmul(pb[bi][:, 0:508].rearrange("p (g c) -> p g c", g=4), lhsT=w1b[rows, l, hc * 128:(hc + 1) * 128],
                                                    rhs=kvT[rows, :, l:l + 2017:16], start=(l == 0), stop=(l == 31)),
                     reads=[r_w1b, r_kvT], writes=[rpb[bi]], rg=(2 * kv, 2 * kv + 2))
        for kv in range(2):
            bi = 2 + 2 * (hc % 2) + kv
            col = kv * 2 + hc
            hd, rhd = hid[col]
            S.op("act", lambda e: e.activation(out=hd[:, :, 0:127], in_=pb[bi][:, 0:508].rearrange("p (g c) -> p g c", g=4), func=AF.Silu, bias=bias1[:, col:col + 1], scale=1.0),
                 reads=[rpb[bi], r_bias1], writes=[rhd])
    for g in range(4):
        bi = 6
        for hc in range(2):
            S.op("pe", lambda e, hc=hc: e.matmul(pb[bi][:, 0:127], lhsT=w2kb[:, hc, :], rhs=hid[hc][0][:, g, 0:127], start=(hc == 0), stop=(hc == 1)),
                 reads=[r_w2kb, hid[hc][1]], writes=[rpb[bi]], rg=(0, 4))
        S.op("act", lambda e: e.activation(out=kcT2[:, g, 0:127], in_=pb[bi][:, 0:127], func=AF.Identity, bias=sm["b2k"][0][:, 0:1], scale=1.0),
             reads=[rpb[bi], sm["b2k"][1]], writes=[r_kc])
        bi = 0 + (g % 2)
        for hc in range(2):
            S.op("pe", lambda e, hc=hc: e.matmul(pb[bi][0:127, 0:64], lhsT=hid[2 + hc][0][:, g, 0:127], rhs=w2vb[:, hc, :], start=(hc == 0), stop=(hc == 1)),
                 reads=[r_w2vb, hid[2 + hc][1]], writes=[rpb[bi]], rg=(0, 4))
        S.op("dve", lambda e: e.tensor_tensor(out=vcaug[0:127, g, 0:64], in0=pb[bi][0:127, 0:64], in1=bc["b2v"][0][0:127, :], op=ALU.add),
             reads=[rpb[bi], bc["b2v"][1]], writes=[r_vc])
    dbg("kcT", kcT2[:, 0, :], r_kc, [128, 128])
    dbg("vc", vcaug[:, 0, :], r_vc, [128, 97])
    chk("cmp")
    S.barrier()
    AR.reset(PHASE)

    def rope_pair(dst_ap_fn, rdst, raw_dst_fn=None, rraw=None, split=None):
        wa, rwa = load_wtile()
        pend = []

        def tail():
            (tc, ba, raw_ap, rrw) = pend.pop(0)
            bp = 2 + (tc % 2)
            t1, rt1 = ropet[0][0]
            t2, rt2 = ropet[0][1]
            S.op("pe", lambda e: e.matmul(pb[bp][:, :], lhsT=permm[:], rhs=raw_ap, start=True, stop=True), reads=[r_pm, rrw], writes=[rpb[bp]], rg=(0, 4))
            S.op("dve", lambda e: e.tensor_tensor(out=t1[:], in0=pb[ba][:, :], in1=ropeC[:, tc * 512:(tc + 1) * 512], op=ALU.mult),
                 reads=[rpb[ba], r_ropeC], writes=[rt1])
            S.op("dve", lambda e: e.tensor_tensor(out=t2[:], in0=pb[bp][:, :], in1=ropeS[:, tc * 512:(tc + 1) * 512], op=ALU.mult),
                 reads=[rpb[bp], r_ropeS], writes=[rt2])
            if split is None:
                S.op("pool", lambda e: e.tensor_tensor(out=dst_ap_fn(tc), in0=t1[:], in1=t2[:], op=ALU.add), reads=[rt1, rt2], writes=[rdst])
            else:
                (d0, rd0), (d1_, rd1_) = split
                S.op("pool", lambda e: e.tensor_tensor(out=d0[0:64, tc * 512:(tc + 1) * 512], in0=t1[0:64, :], in1=t2[0:64, :], op=ALU.add), reads=[rt1, rt2], writes=[rd0])
                S.op("pool", lambda e: e.tensor_tensor(out=d1_[64:128, tc * 512:(tc + 1) * 512], in0=t1[64:128, :], in1=t2[64:128, :], op=ALU.add), reads=[rt1, rt2], writes=[rd1_])

        for tc in range(4):
            ba = next_bank(0, 2)
            fm_matmuls(wa, rwa, tc, ba)
            if raw_dst_fn is not None:
                raw_ap, rrw = raw_dst_fn(tc), rraw
            else:
                kr, rkr = kraws[tc % 2]
                raw_ap, rrw = kr[:], rkr
            S.op("act", lambda e: e.copy(out=raw_ap, in_=pb[ba][:, :]), reads=[rpb[ba]], writes=[rrw])
            pend.append((tc, ba, raw_ap, rrw))
            if len(pend) >= 2:
                tail()
        while pend:
            tail()

    for g in range(4):
        AR.reset(PHASE)
        qraw, r_qraw = G("qraw", [128, 2, S_LEN], BF16)
        qrot, r_qrot = G("qrot", [128, 2, S_LEN], BF16)
        kTs, r_kTs = G("kTs", [128, S_LEN], BF16)
        kTw, r_kTw = G("kTw", [128, S_LEN], BF16)
        kTx = [(kTs, r_kTs), (kTw, r_kTw)]
        zsT, r_zsT = G("zsT", [128, 2, S_LEN], BF16)
        vaug, r_vaug = G("vaug", [128, 16, 2, 65], BF16)
        ropet = [[G("ropet%d%d" % (i, k), [128, 512], F32) for k in range(2)] for i in range(1)]
        kraws = [G("kraw%d" % i, [128, 512], BF16) for i in range(2)]
        PT = [G("PT%d" % i, [128, 512], BF16) for i in range(3)]
        oacc, r_oacc = G("oacc", [128, 4, 256], F32)
        oaccb, r_oaccb = G("oaccb", [128, 4, 256], BF16)
        impacc, r_imp = G("impacc", [128, 4, 32], F32)
        impt, r_impt = G("impt", [128, 4, 32], F32)
        den, r_den = G("den", [128, 4], F32)
        fac, r_fac = G("fac", [128, 4], F32)
        m8, r_m8 = G("m8", [128, 8], F32)
        wk, r_wk = G("wk", [128, 32], F32)
        selq, r_selq = G("selq", [128, 96], BF16)
        KE1, r_KE1 = G("KE1", [128, S_LEN], BF16)
        QS = [[G("QS%d%d" % (j_, p_), [128, 512], BF16) for p_ in range(2)] for j_ in range(2)]
        S.op("pool", lambda e: e.memset(kTs[64:128, :], 0.0), writes=[r_kTs])
        S.op("pool", lambda e: e.memset(KE1[0:64, :], 0.0), writes=[r_KE1])
        S.dma(lambda e: e.dma_start(out=kTs[64:96, :], in_=cst_d["Eall"][0:32, :]), writes=[r_kTs], q="pool")
        S.dma(lambda e: e.dma_start(out=KE1[0:32, :], in_=cst_d["Eall"][0:32, :]), writes=[r_KE1], q="pool")
        for j_ in range(2):
            S.op("pool", lambda e, j_=j_: e.memset(QS[j_][0][0][64:128, :], 0.0), writes=[QS[j_][0][1]])
            S.op("pool", lambda e, j_=j_: e.memset(QS[j_][1][0][0:64, :], 0.0), writes=[QS[j_][1][1]])

        for j in range(2):
            rope_pair(lambda tc, j=j: qrot[:, j, tc * 512:(tc + 1) * 512], r_qrot,
                      lambda tc, j=j: qraw[:, j, tc * 512:(tc + 1) * 512], r_qraw)
        rope_pair(None, None, split=((kTs, r_kTs), (KE1, r_KE1)))
        rope_pair(lambda tc: kTw[:, tc * 512:(tc + 1) * 512], r_kTw)
        for j in range(2):
            wb, rwb = load_wtile()
            for tc in range(4):
                bi = next_bank()
                fm_matmuls(wb, rwb, tc, bi)
                S.op("act", lambda e: e.activation(out=zsT[:, j, tc * 512:(tc + 1) * 512], in_=pb[bi][:, :], func=AF.Silu),
                     reads=[rpb[bi]], writes=[r_zsT])
        wb, rwb = load_wtile()
        S.op("pool", lambda e: e.memset(vaug[:, :, :, 64:65], 1.0), writes=[r_vaug])
        for tt in range(NT):
            bi = next_bank()
            for kc in range(8):
                S.op("pe", lambda e, kc=kc: e.matmul(pb[bi][:, 0:128], lhsT=hT[:, kc, tt * 128:(tt + 1) * 128], rhs=wb[:, kc, :],
                                                      start=(kc == 0), stop=(kc == 7)), reads=[rwb, r_hTc[tt // 4]], writes=[rpb[bi]], rg=(0, 4))
            S.op("act", lambda e: e.copy(out=vaug[:, tt, :, 0:64], in_=pb[bi][:, 0:128].rearrange("p (a b) -> p a b", a=2)),
                 reads=[rpb[bi]], writes=[r_vaug])
        if g == 0:
            dbg("qrot", qrot[:, 0, :], r_qrot, [128, 2048])
            dbg("kTs", kTs[:, :], r_kTs, [128, 2048])
            dbg("vaug", vaug[:].rearrange("p a b c -> p (a b c)"), r_vaug, [128, 2080])
            chk("aproj")

        SB = [0, 1, 2, 3, 6]
        ACCB = [4, 5]
        rr = {"s": 0, "pt": 0, "acc": 0, "cm": 0, "df": 0, "ac": 0}
        oaccs = [(oacc, r_oacc), G("oacc1", [128, 4, 256], F32)]
        dens = [(den, r_den), G("den1", [128, 4], F32)]
        facs = [(fac, r_fac), G("fac1", [128, 4], F32)]
        otmps = [G("otmp%d" % i, [128, 4, 64], F32) for i in range(2)]
        for i_ in range(3, 6):
            PT.append(G("PT%d" % i_, [128, 512], BF16))
        for (kr_, rkr_) in kraws:
            PT.append((kr_, rkr_))

        def gidx(r, br):
            return (4 * g + r) * 3 + br

        def finalize(pv, rbank, r, br, qc, oa, roa, first):
            dn, rdn = dens[rr["df"] % 2]
            fc, rfc = facs[rr["df"] % 2]
            ot, rot = otmps[rr["df"] % 2]
            rr["df"] += 1
            gi = gidx(r, br)
            S.op("dve", lambda e: e.tensor_scalar(out=dn[:], in0=pv[:, :, 64], scalar1=1e-30, scalar2=None, op0=ALU.max), reads=[rbank], writes=[rdn])
            S.op("dve", lambda e: e.reciprocal(out=dn[:], in_=dn[:]), reads=[rdn], writes=[rdn])
            S.op("dve", lambda e: e.tensor_tensor(out=fc[:], in0=dn[:], in1=gates[:, 4 * qc:4 * qc + 4, gi], op=ALU.mult), reads=[rdn, r_gates], writes=[rfc])
            if first:
                S.op("dve", lambda e: e.tensor_tensor(out=oa[:, :, r * 64:(r + 1) * 64], in0=pv[:, :, 0:64], in1=fc[:].unsqueeze(2).to_broadcast([128, 4, 64]), op=ALU.mult),
                     reads=[rbank, rfc], writes=[roa])
            else:
                S.op("dve", lambda e: e.tensor_tensor(out=ot[:], in0=pv[:, :, 0:64], in1=fc[:].unsqueeze(2).to_broadcast([128, 4, 64]), op=ALU.mult),
                     reads=[rbank, rfc], writes=[rot])
                if br == 1:
                    S.op("pool", lambda e: e.tensor_tensor(out=oaccb[:, :, r * 64:(r + 1) * 64], in0=oa[:, :, r * 64:(r + 1) * 64], in1=ot[:], op=ALU.add),
                         reads=[roa, rot], writes=[r_oaccb])
                else:
                    S.op("pool", lambda e: e.tensor_tensor(out=oa[:, :, r * 64:(r + 1) * 64], in0=oa[:, :, r * 64:(r + 1) * 64], in1=ot[:], op=ALU.add),
                         reads=[roa, rot], writes=[roa])
            return dn, rdn

        accs = [G("accs%d" % i, [128, 388], F32) for i in range(2)]
        selqs = [(selq, r_selq)] + [G("selq%d" % i, [128, 96], BF16) for i in range(1, 4)]

        def topk(qc, sT, rsT):
            S.op("dve", lambda e: e.tensor_tensor(out=impacc[:], in0=impacc[:], in1=topkA[:, 4 * qc:4 * qc + 4, :], op=ALU.max), reads=[r_imp, r_tA], writes=[r_imp])
            S.op("dve", lambda e: e.tensor_tensor(out=impacc[:], in0=impacc[:], in1=topkB[:, 4 * qc:4 * qc + 4, :], op=ALU.min), reads=[r_imp, r_tB], writes=[r_imp])
            for qi in range(4):
                sq_, rsq_ = selqs[qi]
                S.op("dve", lambda e: e.max(out=m8[:], in_=impacc[:, qi, :]), reads=[r_imp], writes=[r_m8])
                S.op("dve", lambda e: e.match_replace(out=wk[:], in_to_replace=m8[:], in_values=impacc[:, qi, :], imm_value=-3.0e38), reads=[r_imp, r_m8], writes=[r_wk])
                S.op("dve", lambda e: e.max(out=m8[:], in_=wk[:]), reads=[r_wk], writes=[r_m8])
                S.op("dve", lambda e: e.tensor_scalar(out=sq_[:].rearrange("p (a b) -> p a b", a=3), in0=impacc[:, qi, :].unsqueeze(1).to_broadcast([128, 3, 32]),
                                                      scalar1=m8[:, 7:8], scalar2=NEGB, op0=ALU.is_lt, op1=ALU.mult), reads=[r_imp, r_m8], writes=[rsq_])

        def topk_pe(qc, qi, sT, rsT):
            sq_, rsq_ = selqs[qi]
            S.op("pe", lambda e: e.transpose(out=pbt[0:96, 512 + qi * 128:640 + qi * 128], in_=sq_[:, 0:96], identity=ident[:]), reads=[rsq_, r_id], writes=[rpbt], rg=(0, 4))
            if qi == 3:
                for j_ in range(2):
                    S.op("act", lambda e, j_=j_: e.copy(out=QS[j_][0][0][64:96, :], in_=pbt[64:96, 512:1024]), reads=[rpbt], writes=[QS[j_][0][1]])
                    S.op("act", lambda e, j_=j_: e.copy(out=QS[j_][1][0][0:32, :], in_=pbt[0:32, 512:1024]), reads=[rpbt], writes=[QS[j_][1][1]])
                if g == 0 and qc == 3:
                    dbg("selT", QS[0][1][0][:, :], QS[0][1][1], [128, 512])

        def cmp_step(qc, j, oa, roa, sT, rsT):
            st = {}

            def qk():
                st["pt"] = []
                bss = []
                for par in range(2):
                    bss.append(SB[rr["s"] % 5])
                    rr["s"] += 1
                    st["pt"].append(PT[rr["pt"] % 8])
                    rr["pt"] += 1
                for par in range(2):
                    rows = slice(par * 64, par * 64 + 64)
                    bs = bss[par]
                    S.op("pe", lambda e: e.matmul(pb[bs][0:127, :], lhsT=kcT2[rows, g, 0:127], rhs=qraw[rows, j, qc * 512:(qc + 1) * 512], start=True, stop=False),
                         reads=[r_kc, r_qraw], writes=[rpb[bs]], rg=(2 * par, 2 * par + 2))
                for par in range(2):
                    bs = bss[par]
                    S.op("pe", lambda e: e.matmul(pb[bs][0:127, :], lhsT=ident[0:127, 0:127], rhs=cmpbias[0:127, qc * 512:(qc + 1) * 512], start=False, stop=True),
                         reads=[r_id, r_cmpb], writes=[rpb[bs]], rg=(0, 4))
                for par in range(2):
                    bs = bss[par]
                    pt, rpt = st["pt"][par]
                    S.op("act", lambda e: e.activation(out=pt[0:127, :], in_=pb[bs][0:127, :], func=AF.Exp, scale=0.125), reads=[rpb[bs]], writes=[rpt])

            def pv():
                for par in range(2):
                    r = 2 * j + par
                    pt, rpt = st["pt"][par]
                    bc_ = ACCB[par]
                    for qi in range(4):
                        S.op("pe", lambda e, qi=qi: e.matmul(pb[bc_][:, qi * 97:(qi + 1) * 97], lhsT=pt[0:127, qi * 128:(qi + 1) * 128], rhs=vcaug[0:127, g, :],
                                                              start=True, stop=True), reads=[rpt, r_vc], writes=[rpb[bc_]], rg=(0, 4))
                for par in range(2):
                    r = 2 * j + par
                    bc_ = ACCB[par]
                    ac_, rac_ = accs[rr["ac"] % 2]
                    rr["ac"] += 1
                    S.op("act", lambda e: e.copy(out=ac_[:, 0:388], in_=pb[bc_][:, 0:388]), reads=[rpb[bc_]], writes=[rac_])
                    pvv = ac_[:, 0:388].rearrange("p (a b) -> p a b", a=4)
                    dn, rdn = finalize(pvv, rac_, r, 0, qc, oa, roa, True)
                    if r == 0:
                        S.op("dve", lambda e: e.tensor_tensor(out=impacc[:], in0=pvv[:, :, 65:97], in1=dn[:].unsqueeze(2).to_broadcast([128, 4, 32]), op=ALU.mult),
                             reads=[rac_, rdn], writes=[r_imp])
                    else:
                        S.op("dve", lambda e: e.tensor_tensor(out=impt[:], in0=pvv[:, :, 65:97], in1=dn[:].unsqueeze(2).to_broadcast([128, 4, 32]), op=ALU.mult),
                             reads=[rac_, rdn], writes=[r_impt])
                        S.op("pool", lambda e: e.tensor_tensor(out=impacc[:], in0=impacc[:], in1=impt[:], op=ALU.add), reads=[r_imp, r_impt], writes=[r_imp])
                if j == 1 and g == 0 and qc == 3:
                    dbg("imp", impacc[:].rearrange("p a b -> p (a b)"), r_imp, [128, 128], F32)
            return qk, pv

        def att_step(qc, br, j, kt, kt_lo, kt_hi, oa, roa, sT, rsT):
            st = {}
            kT_, rkT_ = kTx[br - 1]
            qlo = max(0, kt - 4 * qc)
            qhi = 4 if br == 1 else min(4, kt + 5 - 4 * qc)
            c0, c1 = qlo * 128, qhi * 128

            def qk():
                st["pt"] = []
                bss = []
                for par in range(2):
                    bss.append(SB[rr["s"] % 5])
                    rr["s"] += 1
                    st["pt"].append(PT[rr["pt"] % 8])
                    rr["pt"] += 1
                masks = []
                if kt >= 4 * qc:
                    masks.append((kt - 4 * qc, trimask, r_tm))
                if br == 2 and 0 <= kt + 4 - 4 * qc < 4:
                    masks.append((kt + 4 - 4 * qc, antimask, r_am))
                if br == 1:
                    for par in range(2):
                        bs = bss[par]
                        ke_, rke_ = (kTs, r_kTs) if par == 0 else (KE1, r_KE1)
                        qs_, rqs_ = QS[j][par]
                        S.op("pe", lambda e: e.matmul(pb[bs][:, c0:c1], lhsT=ke_[:, kt * 128:(kt + 1) * 128], rhs=qs_[:, c0:c1], start=True, stop=True),
                             reads=[rke_, rqs_], writes=[rpb[bs]], rg=(0, 4))
                else:
                    for par in range(2):
                        bs = bss[par]
                        rows = slice(par * 64, par * 64 + 64)
                        S.op("pe", lambda e: e.matmul(pb[bs][:, c0:c1], lhsT=kT_[rows, kt * 128:(kt + 1) * 128], rhs=qrot[rows, j, qc * 512 + c0:qc * 512 + c1],
                                                      start=True, stop=True),
                             reads=[rkT_, r_qrot], writes=[rpb[bs]], rg=(2 * par, 2 * par + 2))
                for par in range(2):
                    bs = bss[par]
                    pt, rpt = st["pt"][par]
                    S.op("act", lambda e: e.activation(out=pt[:, c0:c1], in_=pb[bs][:, c0:c1], func=AF.Exp, scale=0.125), reads=[rpb[bs]], writes=[rpt])
                for par in range(2):
                    pt, rpt = st["pt"][par]
                    for (qb, mt, rmt) in masks:
                        S.op("pool", lambda e, qb=qb, mt=mt: e.tensor_tensor(out=pt[:, qb * 128:(qb + 1) * 128], in0=pt[:, qb * 128:(qb + 1) * 128], in1=mt[:], op=ALU.mult),
                             reads=[rpt, rmt], writes=[rpt])

            def pv():
                for par in range(2):
                    pt, rpt = st["pt"][par]
                    accb = ACCB[par]
                    for qi in range(qlo, qhi):
                        st_ = (kt == kt_lo and qi == qlo)
                        S.op("pe", lambda e, qi=qi, st_=st_: e.matmul(pb[accb][:, qi * 65:(qi + 1) * 65], lhsT=pt[:, qi * 128:(qi + 1) * 128], rhs=vaug[:, kt, br - 1, :],
                                                                       start=st_, stop=False, skip_group_check=True),
                             reads=[rpt, r_vaug], writes=[rpb[accb]], rg=(0, 4))
                if kt == kt_hi:
                    for par in range(2):
                        accb = ACCB[par]
                        ac_, rac_ = accs[rr["ac"] % 2]
                        rr["ac"] += 1
                        S.op("act", lambda e: e.copy(out=ac_[:, 0:260], in_=pb[accb][:, 0:260]), reads=[rpb[accb]], writes=[rac_])
                        pvv = ac_[:, 0:260].rearrange("p (a b) -> p a b", a=4)
                        finalize(pvv, rac_, 2 * j + par, br, qc, oa, roa, False)
            return qk, pv

        def finish_chunk(qc, oa, roa):
            if g == 0 and qc == 3:
                dbg("oatt", oaccb[:].rearrange("p a b -> p (a b)"), r_oaccb, [128, 1024])
            for j in range(2):
                for qi in range(4):
                    S.op("pe", lambda e: e.transpose(out=pbt[:, qi * 128:(qi + 1) * 128], in_=oaccb[:, qi, j * 128:(j + 1) * 128], identity=ident[:]),
                         reads=[r_oaccb, r_id], writes=[rpbt], rg=(0, 4))
                S.op("dve", lambda e: e.tensor_tensor(out=mixT[:, 2 * g + j, qc * 512:(qc + 1) * 512], in0=pbt[:, 0:512], in1=zsT[:, j, qc * 512:(qc + 1) * 512], op=ALU.mult),
                     reads=[rpbt, r_zsT], writes=[r_mixt[2 * g + j]])

        def qcopies(qc):
            for j_ in range(2):
                S.op("pool", lambda e, j_=j_: e.tensor_copy(out=QS[j_][0][0][0:64, :], in_=qrot[0:64, j_, qc * 512:(qc + 1) * 512]), reads=[r_qrot], writes=[QS[j_][0][1]])
                S.op("pool", lambda e, j_=j_: e.tensor_copy(out=QS[j_][1][0][64:128, :], in_=qrot[64:128, j_, qc * 512:(qc + 1) * 512]), reads=[r_qrot], writes=[QS[j_][1][1]])

        steps = []
        posts = {}
        for qc in range(4):
            oa, roa = oaccs[qc % 2]
            sT, rsT = None, None
            base = len(steps)
            if qc > 0:
                poa, proa = oaccs[(qc - 1) % 2]
                posts.setdefault(base + 17, []).append(lambda qc=qc, poa=poa, proa=proa: finish_chunk(qc - 1, poa, proa))
            for j in range(2):
                steps.append(cmp_step(qc, j, oa, roa, sT, rsT))
            posts.setdefault(base, []).append(lambda qc=qc: qcopies(qc))
            posts.setdefault(base + 2, []).append(lambda qc=qc, sT=sT, rsT=rsT: topk(qc, sT, rsT))
            for qi in range(4):
                posts.setdefault(base + 4 + qi, []).append(lambda qc=qc, qi=qi, sT=sT, rsT=rsT: topk_pe(qc, qi, sT, rsT))
            for br in (2, 1):
                for j in range(2):
                    kt_lo = 0 if br == 1 else max(0, 4 * qc - 4)
                    kt_hi = 4 * qc + 3
                    for kt in range(kt_lo, kt_hi + 1):
                        steps.append(att_step(qc, br, j, kt, kt_lo, kt_hi, oa, roa, sT, rsT))
        ADEPTH = 3
        nq = 0
        for si_, (qk, pv) in enumerate(steps):
            while nq <= min(si_ + ADEPTH, len(steps) - 1):
                steps[nq][0]()
                nq += 1
            pv()
            for f_ in posts.get(si_, []):
                f_()
        finish_chunk(3, *oaccs[3 % 2])
        S.barrier()
    dbg("mixA", mixT[:, 0, :], r_mixt[0], [128, 2048])
    chk("att")

    for g in range(4):
        AR.reset(PHASE)
        zsS, r_zsS = G("zsS", [128, 2, S_LEN], BF16)
        xcT, r_xcT = G("xcT", [128, 2, S_LEN], BF16)
        BT, r_BT = G("BT", [128, S_LEN], BF16)
        CT, r_CT = G("CT", [128, S_LEN], BF16)
        xdt, r_xdt = G("xdt", [128, 16, 4, 64], BF16)
        Btm, r_Btm = G("Btm", [128, 16, 128], BF16)
        SSD_TMP = AR.mark()
        ub = [G("ub%d" % i, [128, 515], BF16) for i in range(3)]
        dg, r_dg = G("dg", [128, 4, 4, 128], BF16)
        for j in range(2):
            wb, rwb = load_wtile()
            for tc in range(4):
                bi = next_bank()
                fm_matmuls(wb, rwb, tc, bi)
                S.op("act", lambda e: e.activation(out=zsS[:, j, tc * 512:(tc + 1) * 512], in_=pb[bi][:, :], func=AF.Silu), reads=[rpb[bi]], writes=[r_zsS])
        conv_targets = [(lambda tc: xcT[:, 0, tc * 512:(tc + 1) * 512], r_xcT, 2 * g),
                        (lambda tc: xcT[:, 1, tc * 512:(tc + 1) * 512], r_xcT, 2 * g + 1),
                        (lambda tc: BT[:, tc * 512:(tc + 1) * 512], r_BT, 8 + g),
                        (lambda tc: CT[:, tc * 512:(tc + 1) * 512], r_CT, 12 + g)]
        cw, r_cw = sm["convw"]
        cbv, r_cbv = sm["convb"]
        for ti, (dst_fn, rdst, ct) in enumerate(conv_targets):
            for k in range(4):
                S.op("dve", lambda e, ti=ti, k=k, ct=ct: e.tensor_scalar(out=dg[:, ti, k, :], in0=ident[:], scalar1=cw[:, ct * 4 + k:ct * 4 + k + 1], scalar2=None, op0=ALU.mult),
                     reads=[r_id, r_cw], writes=[r_dg])
        pend_conv = []

        def conv_tail():
            (ti, dst_fn, rdst, ct, tc, u, ru) = pend_conv.pop(0)
            pc = 2 + (tc % 2)
            for k in range(4):
                S.op("pe", lambda e, k=k: e.matmul(pb[pc][:, :], lhsT=dg[:, ti, k, :], rhs=u[:, k:k + 512], start=(k == 0), stop=(k == 3)),
                     reads=[r_dg, ru], writes=[rpb[pc]], rg=(0, 4))
            S.op("act", lambda e: e.activation(out=dst_fn(tc), in_=pb[pc][:, :], func=AF.Silu, bias=cbv[:, ct:ct + 1], scale=1.0), reads=[rpb[pc], r_cbv], writes=[rdst])

        for ti, (dst_fn, rdst, ct) in enumerate(conv_targets):
            wb, rwb = load_wtile()
            for tc in range(4):
                bi = next_bank()
                fm_matmuls(wb, rwb, tc, bi)
                u, ru = ub[(ti * 4 + tc) % 3]
                if tc == 0:
                    S.op("pool", lambda e: e.memset(u[:, 0:3], 0.0), writes=[ru])
                S.op("act", lambda e: e.copy(out=u[:, 3:515], in_=pb[bi][:, :]), reads=[rpb[bi]], writes=[ru])
                if tc < 3:
                    un, run = ub[(ti * 4 + tc + 1) % 3]
                    S.op("pool", lambda e: e.tensor_copy(out=un[:, 0:3], in_=u[:, 512:515]), reads=[ru], writes=[run])
                pend_conv.append((ti, dst_fn, rdst, ct, tc, u, ru))
                if len(pend_conv) >= 2:
                    conv_tail()
        while pend_conv:
            conv_tail()
        for tt in range(NT):
            for j in range(2):
                S.op("pe", lambda e, j=j: e.transpose(out=pbt[:, j * 128:(j + 1) * 128], in_=xcT[:, j, tt * 128:(tt + 1) * 128], identity=ident[:]),
                     reads=[r_xcT, r_id], writes=[rpbt], rg=(0, 4))
            S.op("pe", lambda e: e.transpose(out=pbt[:, 256:384], in_=BT[:, tt * 128:(tt + 1) * 128], identity=ident[:]), reads=[r_BT, r_id], writes=[rpbt], rg=(0, 4))
            S.op("dve", lambda e: e.tensor_tensor(out=xdt[:, tt, :, :], in0=pbt[:, 0:256].rearrange("p (a b) -> p a b", a=4),
                                                  in1=dtv[:, tt, 4 * g:4 * g + 4].unsqueeze(2).to_broadcast([128, 4, 64]), op=ALU.mult),
                 reads=[rpbt, r_dtv], writes=[r_xdt])
            S.op("act", lambda e: e.copy(out=Btm[:, tt, :], in_=pbt[:, 256:384]), reads=[rpbt], writes=[r_Btm])
        if g == 0:
            dbg("xcT", xcT[:, 0, :], r_xcT, [128, 2048])
            dbg("BT", BT[:, :], r_BT, [128, 2048])
        S.barrier()
        AR.reset(SSD_TMP)
        NB3 = 4
        CBm = [G("CBm%d" % i, [128, 2, 256], BF16) for i in range(2)]
        EA = [G("EA%d" % i, [128, 256], F32) for i in range(NB3)]
        Cdec = [G("Cdec%d" % i, [128, 256], BF16) for i in range(NB3)]
        D1 = [G("D1%d" % i, [128, 384], F32) for i in range(NB3)]
        MT = [G("MT%d" % i, [128, 384], BF16) for i in range(NB3)]
        xws = [G("xw%d" % i, [128, 2, 4, 64], BF16) for i in range(2)]
        cds = [G("cdall%d" % i, [128, 4], F32) for i in range(2)]
        dtes = [G("dte%d" % i, [128, 2, 4], F32) for i in range(2)]
        htmp, r_htmp = G("htmp", [128, 4, 64], F32)
        h32, r_h32 = G("h32", [128, 4, 64], F32)
        hbf, r_hbf = G("hbf", [128, 4, 64], BF16)
        ytmp = [G("ytmp%d" % i, [128, 256], F32) for i in range(4)]
        yg = [G("yg%d" % i, [128, 256], F32) for i in range(2)]
        S.op("pool", lambda e: e.memset(h32[:], 0.0), writes=[r_h32])
        S.op("pool", lambda e: e.memset(hbf[:], 0.0), writes=[r_hbf])
        dsk, r_dsk = sm["dskip"]
        snw, r_snw = sm["snw"]
        BY = [0, 1, 2]
        sst = {"by": 0, "k": 0}
        info = {}

        def Cc(c):
            l0 = c * 256
            bx = 6
            cb_, rcb_ = CBm[c % 2]
            S.op("pe", lambda e: e.matmul(pb[bx][:, 0:256], lhsT=BT[:, l0:l0 + 128], rhs=CT[:, l0:l0 + 256], start=True, stop=True),
                 reads=[r_BT, r_CT], writes=[rpb[bx]], rg=(0, 4))
            S.op("pe", lambda e: e.matmul(pb[bx][:, 256:384], lhsT=BT[:, l0 + 128:l0 + 256], rhs=CT[:, l0 + 128:l0 + 256], start=True, stop=True),
                 reads=[r_BT, r_CT], writes=[rpb[bx]], rg=(0, 4))
            S.op("dve", lambda e: e.tensor_tensor(out=cb_[:, 0, 0:128], in0=pb[bx][:, 0:128], in1=trimask[:], op=ALU.mult), reads=[rpb[bx], r_tm], writes=[rcb_])
            S.op("act", lambda e: e.copy(out=cb_[:, 0, 128:256], in_=pb[bx][:, 128:256]), reads=[rpb[bx]], writes=[rcb_])
            S.op("dve", lambda e: e.tensor_tensor(out=cb_[:, 1, 0:128], in0=pb[bx][:, 256:384], in1=trimask[:], op=ALU.mult), reads=[rpb[bx], r_tm], writes=[rcb_])

        def A(c, r):
            l0 = c * 256
            h = 4 * g + r
            k = sst["k"] % NB3
            sst["k"] += 1
            info[(c, r)] = k
            by = BY[sst["by"] % 3]
            sst["by"] += 1
            cb_, rcb_ = CBm[c % 2]
            xw, r_xw = xws[c % 2]
            cdall, r_cd = cds[c % 2]
            dte, r_dte = dtes[c % 2]
            for pi in range(3):
                ap_, rap_ = a3[pi]
                S.op("pe", lambda e, pi=pi, ap_=ap_: e.matmul(pb[by][:, 0:256], lhsT=ap_[:, 2 * c, h:h + 1].to_broadcast([128, 128]), rhs=tri2b[:, :], start=(pi == 0), stop=False),
                     reads=[rap_, r_trib], writes=[rpb[by]], rg=(0, 4))
            for pi in range(3):
                ap_, rap_ = a3[pi]
                S.op("pe", lambda e, pi=pi, ap_=ap_: e.matmul(pb[by][:, 128:256], lhsT=ap_[:, 2 * c + 1, h:h + 1].to_broadcast([128, 128]), rhs=tri2b[:, 0:128], start=False, stop=(pi == 2)),
                     reads=[rap_, r_trib], writes=[rpb[by]], rg=(0, 4))
            ea, rea = EA[k]
            cd_, rcd_ = Cdec[k]
            d1, rd1 = D1[k]
            mt, rmt = MT[k]
            S.op("dve", lambda e: e.tensor_scalar(out=d1[:, 0:256], in0=pb[by][:, 0:256], scalar1=acs_tm[:, 2 * c, h:h + 1], scalar2=0.0, op0=ALU.subtract, op1=ALU.min),
                 reads=[rpb[by], r_acs], writes=[rd1])
            S.op("dve", lambda e: e.tensor_scalar(out=d1[:, 256:384], in0=pb[by][:, 128:256], scalar1=acs_tm[:, 2 * c + 1, h:h + 1], scalar2=0.0, op0=ALU.subtract, op1=ALU.min),
                 reads=[rpb[by], r_acs], writes=[rd1])
            S.op("act", lambda e: e.activation(out=ea[:], in_=pb[by][:, 0:256], func=AF.Exp), reads=[rpb[by]], writes=[rea])
            S.op("act", lambda e: e.activation(out=d1[:], in_=d1[:], func=AF.Exp), reads=[rd1], writes=[rd1])
            S.op("pool", lambda e: e.tensor_tensor(out=mt[:, 0:256], in0=d1[:, 0:256], in1=cb_[:, 0, :], op=ALU.mult), reads=[rd1, rcb_], writes=[rmt])
            S.op("pool", lambda e: e.tensor_tensor(out=mt[:, 256:384], in0=d1[:, 256:384], in1=cb_[:, 1, 0:128], op=ALU.mult), reads=[rd1, rcb_], writes=[rmt])
            if c > 0:
                S.op("dve", lambda e: e.tensor_tensor(out=cd_[:], in0=CT[:, l0:l0 + 256], in1=ea[:], op=ALU.mult), reads=[r_CT, rea], writes=[rcd_])
            if c < 7:
                S.op("act", lambda e: e.copy(out=cdall[:, r:r + 1], in_=ea[:, 255:256]), reads=[rea], writes=[r_cd])
                S.op("act", lambda e: e.copy(out=dte[:, :, r], in_=d1[:, 255:384:128]), reads=[rd1], writes=[r_dte])

        def B(c, r):
            k = info[(c, r)]
            jp, par = r // 2, r % 2
            cd_, rcd_ = Cdec[k]
            mt, rmt = MT[k]
            bo = 3 + jp
            yo = pb[bo][par * 64:(par + 1) * 64, :]
            S.op("pe", lambda e: e.matmul(yo[:, 0:256], lhsT=xdt[:, 2 * c, r, :], rhs=mt[:, 0:256], start=True, stop=False),
                 reads=[r_xdt, rmt], writes=[rpb[bo]], rg=(0, 4))
            S.op("pe", lambda e: e.matmul(yo[:, 128:256], lhsT=xdt[:, 2 * c + 1, r, :], rhs=mt[:, 256:384], start=False, stop=(c == 0)),
                 reads=[r_xdt, rmt], writes=[rpb[bo]], rg=(0, 4))
            if c > 0:
                S.op("pe", lambda e: e.matmul(yo[:, 0:256], lhsT=hbf[:, r, :], rhs=cd_[:], start=False, stop=True),
                     reads=[r_hbf, rcd_], writes=[rpb[bo]], rg=(0, 4))

        def St_pre(c):
            xw, r_xw = xws[c % 2]
            dte, r_dte = dtes[c % 2]
            for si_ in range(2):
                S.op("dve", lambda e, si_=si_: e.tensor_tensor(out=xw[:, si_, :, :], in0=xdt[:, 2 * c + si_, :, :], in1=dte[:, si_, :].unsqueeze(2).to_broadcast([128, 4, 64]), op=ALU.mult),
                     reads=[r_xdt, r_dte], writes=[r_xw])

        def St(c):
            bst = 5
            xw, r_xw = xws[c % 2]
            cdall, r_cd = cds[c % 2]
            dte, r_dte = dtes[c % 2]
            S.op("pe", lambda e: e.matmul(pb[bst][:, 0:256], lhsT=Btm[:, 2 * c, :], rhs=xw[:, 0, :, :].rearrange("p a b -> p (a b)"), start=True, stop=False),
                 reads=[r_Btm, r_xw], writes=[rpb[bst]], rg=(0, 4))
            S.op("pe", lambda e: e.matmul(pb[bst][:, 0:256], lhsT=Btm[:, 2 * c + 1, :], rhs=xw[:, 1, :, :].rearrange("p a b -> p (a b)"), start=False, stop=True),
                 reads=[r_Btm, r_xw], writes=[rpb[bst]], rg=(0, 4))
            S.op("dve", lambda e: e.tensor_tensor(out=htmp[:], in0=h32[:], in1=cdall[:, 0:4].unsqueeze(2).to_broadcast([128, 4, 64]), op=ALU.mult),
                 reads=[r_h32, r_cd], writes=[r_htmp])
            S.op("dve", lambda e: e.tensor_tensor(out=h32[:], in0=htmp[:], in1=pb[bst][:, 0:256].rearrange("p (a b) -> p a b", a=4), op=ALU.add),
                 reads=[r_htmp, rpb[bst]], writes=[r_h32])
            S.op("pool", lambda e: e.tensor_copy(out=hbf[:], in_=h32[:]), reads=[r_h32], writes=[r_hbf])

        def Yev(c, jp):
            l0 = c * 256
            bo = 3 + jp
            ft = 2 * g + jp
            yk = sst["y"] % 4
            sst["y"] += 1
            yt, ryt = ytmp[yk]
            ygg, rygg = yg[jp]
            S.op("dve", lambda e: e.scalar_tensor_tensor(out=yt[:], in0=xcT[:, jp, l0:l0 + 256], scalar=dsk[:, ft:ft + 1], in1=pb[bo][:, 0:256], op0=ALU.mult, op1=ALU.add),
                 reads=[r_xcT, r_dsk, rpb[bo]], writes=[ryt])
            S.op("dve", lambda e: e.tensor_tensor(out=ygg[:], in0=yt[:], in1=zsS[:, jp, l0:l0 + 256], op=ALU.mult), reads=[ryt, r_zsS], writes=[rygg])
            S.op("act", lambda e: e.mul(out=mixT[:, 8 + ft, l0:l0 + 256], in_=ygg[:], mul=snw[:, ft:ft + 1]), reads=[rygg, r_snw], writes=[r_mixt[8 + ft]])
            S.op("act", lambda e: e.activation(out=yt[:], in_=ygg[:], func=AF.Square), reads=[rygg], writes=[ryt])
            pendpe.append((c, yk))
            if g == 0 and jp == 0 and c == 1:
                dbg("yg", ygg[:], rygg, [128, 256], F32)

        def YevPE():
            c, yk = pendpe.pop(0)
            yt, ryt = ytmp[yk]
            bq = 5
            kq = sst["q"] % 4
            sst["q"] += 1
            for hh in range(2):
                S.op("pe", lambda e, hh=hh: e.matmul(pb[bq][:, 400 + 2 * kq + hh:401 + 2 * kq + hh], lhsT=yt[:, hh * 128:(hh + 1) * 128], rhs=onesf[:, 0:1], start=True, stop=True),
                     reads=[ryt, r_ones], writes=[rpb[bq]], rg=(0, 4))
            pend.append((c, kq))

        def Yev2():
            c, kq = pend.pop(0)
            bq = 5
            S.op("dve", lambda e: e.tensor_tensor(out=ssq[:, 2 * c:2 * c + 2], in0=ssq[:, 2 * c:2 * c + 2], in1=pb[bq][:, 400 + 2 * kq:402 + 2 * kq], op=ALU.add), reads=[r_ssq, rpb[bq]], writes=[r_ssq])

        pend = []
        pendpe = []
        sst["q"] = 0
        sst["y"] = 0
        DEPTH = 3
        sl = [(c, r) for c in range(8) for r in range(4)]
        issued = 0

        def issue_A(upto):
            nonlocal_issued = issued_box[0]
            while nonlocal_issued <= upto and nonlocal_issued < len(sl):
                c_, r_ = sl[nonlocal_issued]
                if r_ == 0:
                    Cc(c_)
                A(c_, r_)
                nonlocal_issued += 1
            issued_box[0] = nonlocal_issued

        issued_box = [0]
        for si, (c, r) in enumerate(sl):
            issue_A(si + DEPTH)
            if r == 3 and c < 7:
                St_pre(c)
            B(c, r)
            if r % 2 == 1:
                if len(pendpe) >= 2:
                    YevPE()
                if len(pend) >= 3:
                    Yev2()
                Yev(c, r // 2)
            if r == 3 and c < 7:
                St(c)
            if g == 3 and r == 1:
                for kc in (2 * c, 2 * c + 1):
                    S.dma(lambda e, kc=kc: e.dma_start(out=woutb[:, kc, :], in_=wout_d[kc * 128:(kc + 1) * 128, :]), writes=[r_woutb] + r_hTc, q="pool")
        while pendpe:
            YevPE()
        while pend:
            Yev2()
        S.barrier()
    dbg("ssq", ssq[:], r_ssq, [128, 16], F32)
    dbg("mixS", mixT[:, 8, :], r_mixt[8], [128, 2048])
    chk("ssd")

    AR.reset(PHASE)
    pw2, r_pw2 = G("pw2", [128, 1024], F32)
    xt2 = [G("xt2%d" % i, [128, 1024], F32) for i in range(2)]
    ob = [G("ob%d" % i, [128, 1024], F32) for i in range(2)]
    tmpo, r_tmpo = G("tmpo", [128, 1024], F32)
    junk2, r_junk2 = G("junk2", [128, 1024], F32)
    ss2, r_ss2 = G("ss2", [128, 16], F32)
    S.dma(lambda e: e.dma_start(out=pw2[:], in_=sm_d["postw"].partition_broadcast(128)), writes=[r_pw2])
    S.op("dve", lambda e: e.tensor_scalar(out=ssq[:], in0=ssq[:], scalar1=1.0 / D, scalar2=EPS, op0=ALU.mult, op1=ALU.add), reads=[r_ssq], writes=[r_ssq])
    S.op("act", lambda e: e.activation(out=ssq[:], in_=ssq[:], func=AF.Sqrt), reads=[r_ssq], writes=[r_ssq])
    S.op("dve", lambda e: e.reciprocal(out=ssq[:], in_=ssq[:]), reads=[r_ssq], writes=[r_ssq])
    for tt in range(NT):
        xb, rxb = xt2[tt % 2]
        o_, ro_ = ob[tt % 2]
        S.dma(lambda e: e.dma_start(out=xb[:], in_=x_d[tt * 128:(tt + 1) * 128, :]), writes=[rxb])
        for half in range(2):
            for hs in range(2):
                bi = half * 2 + hs
                for kk in range(8):
                    kc = half * 8 + kk
                    S.op("pe", lambda e, kc=kc, kk=kk: e.matmul(pb[bi][:, :], lhsT=mixT[:, kc, tt * 128:(tt + 1) * 128], rhs=woutb[:, kc, hs * 512:(hs + 1) * 512],
                                                                 start=(kk == 0), stop=(kk == 7)), reads=[r_mixt[kc], r_woutb], writes=[rpb[bi]], rg=(0, 4))
        for hs in range(2):
            S.op("act", lambda e: e.mul(out=tmpo[:, hs * 512:(hs + 1) * 512], in_=pb[2 + hs][:, :], mul=ssq[:, tt:tt + 1]),
                 reads=[rpb[2 + hs], r_ssq], writes=[r_tmpo])
            S.op("dve", lambda e: e.tensor_tensor(out=o_[:, hs * 512:(hs + 1) * 512], in0=tmpo[:, hs * 512:(hs + 1) * 512], in1=pb[hs][:, :], op=ALU.add),
                 reads=[r_tmpo, rpb[hs]], writes=[ro_])
        S.op("act", lambda e: e.activation(out=junk2[:], in_=o_[:], func=AF.Square, accum_out=ss2[:, tt:tt + 1]), reads=[ro_], writes=[r_junk2, r_ss2])
        S.op("dve", lambda e: e.tensor_scalar(out=ss2[:, tt:tt + 1], in0=ss2[:, tt:tt + 1], scalar1=1.0 / D, scalar2=EPS, op0=ALU.mult, op1=ALU.add), reads=[r_ss2], writes=[r_ss2])
        S.op("act", lambda e: e.activation(out=ss2[:, tt:tt + 1], in_=ss2[:, tt:tt + 1], func=AF.Sqrt), reads=[r_ss2], writes=[r_ss2])
        S.op("dve", lambda e: e.reciprocal(out=ss2[:, tt:tt + 1], in_=ss2[:, tt:tt + 1]), reads=[r_ss2], writes=[r_ss2])
        S.op("dve", lambda e: e.scalar_tensor_tensor(out=o_[:], in0=o_[:], scalar=ss2[:, tt:tt + 1], in1=pw2[:], op0=ALU.mult, op1=ALU.mult),
             reads=[ro_, r_ss2, r_pw2], writes=[ro_])
        S.op("pool", lambda e: e.tensor_tensor(out=o_[:], in0=o_[:], in1=xb[:], op=ALU.add), reads=[ro_, rxb], writes=[ro_])
        S.dma(lambda e: e.dma_start(out=out_d[tt * 128:(tt + 1) * 128, :], in_=o_[:]), reads=[ro_])


_CACHE = {}


def prepare_inputs(inp):
    inp = {k: np.asarray(v) for k, v in inp.items()}
    shared = {}
    shared["wt"] = host_wtiles(np.ascontiguousarray(inp["w_in"][0], dtype=np.float32))
    shared["w1"] = np.ascontiguousarray(inp["cmp_w1"][0], dtype=np.float32)
    shared["wout"] = np.ascontiguousarray(inp["w_out"][0], dtype=np.float32)
    shared.update(host_consts())
    shared.update(host_small(inp))
    return inp, shared


def kernel(**inputs):
    inp, shared = prepare_inputs(inputs)
    if "nc" not in _CACHE:
        _CACHE["nc"] = build()[0]
    nc = _CACHE["nc"]
    x = np.ascontiguousarray(inp["x"], dtype=np.float32)
    in_maps = []
    for b in range(8):
        m = dict(shared)
        m["x"] = x[b]
        in_maps.append(m)
    res = run_bass_kernel_spmd(nc, in_maps, core_ids=list(range(8)))
    return np.stack([res.results[b]["out"] for b in range(8)], 0).astype(np.float32)
```

```python
import numpy as np
import concourse.bass as bass
import concourse.mybir as mybir
from concourse.bass_utils import run_bass_kernel_spmd

F32 = mybir.dt.float32
BF16 = mybir.dt.bfloat16
AF = mybir.ActivationFunctionType
ALU = mybir.AluOpType

S_LEN = 2048
D = 1024
NT = 16
NEGB = -1.0e9
EPS = 1e-6
SAME_ENGINE_SYNC = True

O_Q, O_KCM, O_VCM, O_KSL, O_VSL, O_KWN, O_VWN, O_GLOG, O_ZATT, O_ZSSM, O_XSSM, O_B, O_C, O_DT = (
    0, 1024, 1280, 1536, 1792, 2048, 2304, 2560, 2608, 3632, 4656, 5680, 6192, 6704)


class Res:
    __slots__ = ("name", "last_write", "readers", "excl", "rg")

    def __init__(self, name, excl=False):
        self.name = name
        self.last_write = None
        self.readers = {}
        self.excl = excl
        self.rg = None


class Sched:
    ENGS = ("pe", "act", "dve", "pool", "sp")

    def __init__(self, nc, n_dma_sems=8):
        self.nc = nc
        self.eng = {"pe": nc.tensor, "act": nc.scalar, "dve": nc.vector, "pool": nc.gpsimd, "sp": nc.sync}
        self.sem = {e: nc.alloc_semaphore("s_" + e) for e in ("pe", "act", "dve", "pool")}
        self.cnt = {e: 0 for e in ("pe", "act", "dve", "pool")}
        self.dsem = [nc.alloc_semaphore("s_dma%d" % i) for i in range(n_dma_sems)]
        self.dcnt = [0] * n_dma_sems
        half = n_dma_sems // 2
        self.dpool = {"sp": list(range(0, half)), "pool": list(range(half, n_dma_sems))}
        self.dnext = {"sp": 0, "pool": 0}
        self.seen = {e: {} for e in self.ENGS}
        self.nops = 0

    def _semof(self, f):
        return self.dsem[f] if isinstance(f, int) else self.sem[f]

    def _collect(self, e, reads, writes, rg=None):
        waits = {}

        def need(ev, force=False):
            f, c = ev
            if f == e and (e == "pe" or not SAME_ENGINE_SYNC) and not force:
                return
            if self.seen[e].get(f, 0) >= c:
                return
            if waits.get(f, 0) < c:
                waits[f] = c

        for r in reads:
            if r.excl:
                if r.last_write:
                    need(r.last_write)
                for ev in r.readers.items():
                    need(ev)
            elif r.last_write:
                need(r.last_write)
        for w in writes:
            if w.last_write:
                force = False
                if e == "pe" and w.excl and rg is not None and w.rg is not None:
                    if rg[1] <= w.rg[0] or w.rg[1] <= rg[0]:
                        force = True
                need(w.last_write, force)
            for ev in w.readers.items():
                need(ev)
        return waits

    def _emit_waits(self, e, waits):
        for f, c in waits.items():
            self.seen[e][f] = c
            self.eng[e].wait_ge(self._semof(f), c)

    def op(self, e, fn, reads=(), writes=(), rg=None):
        waits = self._collect(e, reads, writes, rg)
        self._emit_waits(e, waits)
        self.cnt[e] += 1
        c = self.cnt[e]
        fn(self.eng[e]).then_inc(self.sem[e], 1)
        self.nops += 1
        for r in reads:
            if r.excl:
                r.last_write = (e, c)
                r.readers = {}
            elif r.readers.get(e, 0) < c:
                r.readers[e] = c
        for w in writes:
            w.last_write = (e, c)
            w.readers = {}
            if e == "pe" and w.excl:
                w.rg = rg
        return (e, c)

    def dma(self, fn, reads=(), writes=(), q="sp"):
        waits = self._collect(q, reads, writes)
        pool_ = self.dpool[q]
        s = pool_[self.dnext[q] % len(pool_)]
        self.dnext[q] += 1
        if self.dcnt[s] > 0 and self.seen[q].get(s, 0) < self.dcnt[s]:
            if waits.get(s, 0) < self.dcnt[s]:
                waits[s] = self.dcnt[s]
        self._emit_waits(q, waits)
        self.dcnt[s] += 16
        c = self.dcnt[s]
        fn(self.eng[q]).then_inc(self.dsem[s], 16)
        self.nops += 1
        for r in reads:
            if r.readers.get(s, 0) < c:
                r.readers[s] = c
        for w in writes:
            w.last_write = (s, c)
            w.readers = {}
        return (s, c)

    def barrier(self):
        for e in self.ENGS:
            waits = {}
            for f in ("pe", "act", "dve", "pool"):
                if f != e and self.cnt[f] > self.seen[e].get(f, 0):
                    waits[f] = self.cnt[f]
            for s in range(len(self.dsem)):
                if self.dcnt[s] > self.seen[e].get(s, 0):
                    waits[s] = self.dcnt[s]
            self._emit_waits(e, waits)

    def finish(self):
        waits = {}
        for s in range(len(self.dsem)):
            if self.dcnt[s] > self.seen["sp"].get(s, 0):
                waits[s] = self.dcnt[s]
        self._emit_waits("sp", waits)


class Arena:
    def __init__(self, nc):
        self.nc = nc
        self.base = (nc.sbuf_base + 63) // 64 * 64
        self.top = nc.sbuf_top
        self.cur = self.base
        self.n = 0

    def alloc(self, name, shape, dt):
        esz = 2 if dt == BF16 else 4
        per = esz
        for s in shape[1:]:
            per *= s
        off = self.cur
        self.cur = (off + per + 63) // 64 * 64
        assert self.cur <= self.top, "SBUF overflow at %s: %d > %d" % (name, self.cur, self.top)
        self.n += 1
        self.last_off = off
        return self.nc.alloc_sbuf_tensor_at("%s_%d" % (name, self.n), list(shape), dt, offset=off)

    def mark(self):
        return self.cur

    def reset(self, m):
        self.cur = m


def _tile_cols():
    def partner(d):
        return d + 8 if d < 8 else (d - 8 if d < 16 else -1)

    tiles = []
    tmg = [O_GLOG + i for i in range(48)] + [O_DT + i for i in range(16)] + [-1] * 64
    tiles.append(("tmg", tmg))
    for g in range(4):
        tiles.append(("kvcm%d" % g, [O_KCM + g * 64 + d for d in range(64)] + [O_VCM + g * 64 + d for d in range(64)]))
    for g in range(4):
        for j in range(2):
            hs = (4 * g + 2 * j, 4 * g + 2 * j + 1)
            tiles.append(("q", [O_Q + h * 64 + d for h in hs for d in range(64)]))
        for off in (O_KSL, O_KWN):
            tiles.append(("k", [off + g * 64 + d for d in range(64)] * 2))
        for j in range(2):
            hs = (4 * g + 2 * j, 4 * g + 2 * j + 1)
            tiles.append(("z", [O_ZATT + h * 64 + d for h in hs for d in range(64)]))
        tiles.append(("tmv", [O_VSL + g * 64 + d for d in range(64)] + [O_VWN + g * 64 + d for d in range(64)]))
    for g in range(4):
        for off in (O_ZSSM, O_XSSM):
            for j in range(2):
                h0 = 4 * g + 2 * j
                tiles.append(("s", [off + h0 * 64 + i for i in range(128)]))
        tiles.append(("b", [O_B + g * 128 + i for i in range(128)]))
        tiles.append(("c", [O_C + g * 128 + i for i in range(128)]))
    return tiles


N_WT = 1 + 4 + 4 * 7 + 4 * 6


def host_consts():
    c = {}
    half = 8
    inv_freq = (500000.0 ** (-(np.arange(half, dtype=np.float32) * 2.0 / 16))).astype(np.float32)
    pos = np.arange(S_LEN, dtype=np.float32)
    ang = pos[:, None] * inv_freq[None, :]
    cos = np.cos(ang).astype(np.float32).T
    sin = np.sin(ang).astype(np.float32).T
    C = np.ones((128, S_LEN), np.float32)
    Sg = np.zeros((128, S_LEN), np.float32)
    for hp in range(2):
        for d in range(16):
            p = hp * 64 + d
            C[p] = cos[d % 8]
            Sg[p] = -sin[d % 8] if d < 8 else sin[d % 8]
    c["ropeC"] = C
    c["ropeS"] = Sg
    k = np.arange(128)[:, None]
    q = np.arange(128)[None, :]
    c["causalb"] = np.where(k <= q, 0.0, NEGB).astype(np.float32)
    c["antib"] = np.where(k > q, 0.0, NEGB).astype(np.float32)
    c["trimask"] = (k <= q).astype(np.float32)
    pm = np.zeros((128, 128), np.float32)
    for dst in range(128):
        dl = dst % 64
        if dl < 16:
            pm[(dst // 64) * 64 + (dl + 8 if dl < 8 else dl - 8), dst] = 1.0
    c["permm"] = pm
    c["antimask"] = (k > q).astype(np.float32)
    cc = np.arange(128)[:, None]
    qq = np.arange(S_LEN)[None, :]
    c["cmpbias"] = np.where((16 * cc + 31 <= qq) & (cc < 127), 1.0, 0.0).astype(np.float32)
    E = np.zeros((128, 16, 128), np.float32)
    for kt in range(16):
        for kk in range(128):
            j = 2 * kt + kk // 64
            E[j, kt, kk] = 1.0
            E[64 + j, kt, kk] = 1.0
    c["Eall"] = E.reshape(128, 16 * 128)
    c["ident"] = np.eye(128, dtype=np.float32)
    tok = np.arange(S_LEN)
    cur = tok // 64
    j = np.arange(32)[None, :]
    A = np.full((S_LEN, 32), -3.0e38, np.float32)
    A[np.arange(S_LEN), np.maximum(cur - 1, 0)] = 1.0e30
    A[np.arange(S_LEN), cur] = 2.0e30
    A[:, 0] = 3.0e30
    Bm = np.where(j > cur[:, None], -1.0e30, 3.0e38).astype(np.float32)
    c["topkA"] = A.reshape(16, 128, 32).transpose(1, 0, 2).reshape(128, 512).copy()
    c["topkB"] = Bm.reshape(16, 128, 32).transpose(1, 0, 2).reshape(128, 512).copy()
    ci = np.arange(128)[:, None] * 16
    bj = np.arange(32)[None, :] * 64
    ov = ((ci <= bj + 63) & (ci + 31 >= bj)).astype(np.float32)
    ov[127] = 0.0
    vx = np.zeros((128, 33), np.float32)
    vx[:, 0] = 1.0
    vx[:, 1:] = ov
    c["vcext"] = vx
    tri = np.zeros((128, 256), np.float32)
    tri[:, 0:128] = (k <= q)
    tri[:, 128:256] = 1.0
    c["tri2"] = tri
    return c


CONST_SHAPES = {"ropeC": [128, 2048], "ropeS": [128, 2048], "causalb": [128, 128], "antib": [128, 128],
                "trimask": [128, 128], "permm": [128, 128], "antimask": [128, 128], "cmpbias": [128, 2048], "Eall": [128, 2048], "ident": [128, 128],
                "topkA": [128, 512], "topkB": [128, 512], "vcext": [128, 33], "tri2": [128, 256]}

SMALL_SHAPES = {"prew": [1, 1024], "postw": [1, 1024], "posT": [128, 32], "b1T": [128, 4], "w2k": [128, 256],
                "w2v": [128, 128], "b2k": [128, 1], "b2v": [1, 64], "gateb": [1, 48], "convw": [128, 64],
                "convb": [128, 16], "dtb": [1, 16], "alog": [1, 16], "dskip": [128, 8], "snw": [128, 8]}


def host_small(inp):
    s = {}
    s["prew"] = inp["pre_norm_w"][0][None, :]
    s["postw"] = inp["post_norm_w"][0][None, :]
    pos = inp["cmp_pos"][0]
    s["posT"] = np.concatenate([pos[0].T, pos[1].T], 0)
    b1 = inp["cmp_b1"][0]
    s["b1T"] = np.stack([b1[kv, hc * 128:(hc + 1) * 128] for kv in range(2) for hc in range(2)], 1)
    w2 = inp["cmp_w2"][0]
    w2k = w2[0].reshape(2, 128, 64).transpose(1, 0, 2)
    s["w2k"] = np.concatenate([w2k, w2k], 2).reshape(128, 256)
    s["w2v"] = w2[1].reshape(2, 128, 64).transpose(1, 0, 2).reshape(128, 128)
    b2 = inp["cmp_b2"][0]
    s["b2k"] = np.concatenate([b2[0], b2[0]])[:, None]
    s["b2v"] = b2[1][None, :]
    s["gateb"] = inp["gate_b"][0][None, :]
    cw = inp["conv_w"][0]
    s["convw"] = cw.T.reshape(16, 128, 4).transpose(1, 0, 2).reshape(128, 64)
    s["convb"] = inp["conv_b"][0].reshape(16, 128).T
    s["dtb"] = inp["dt_bias"][0][None, :]
    s["alog"] = inp["a_log"][0][None, :]
    s["dskip"] = np.repeat(inp["d_skip"][0], 64).reshape(8, 128).T
    s["snw"] = inp["ssm_norm_w"][0].reshape(8, 128).T
    return {k: np.ascontiguousarray(v, dtype=np.float32) for k, v in s.items()}


def host_wtiles(w_in):
    tiles = _tile_cols()
    assert len(tiles) == N_WT
    out = np.zeros((N_WT, 128, 8, 128), np.float32)
    for t, (_, cols) in enumerate(tiles):
        cols = np.asarray(cols)
        wc = np.zeros((1024, 128), np.float32)
        m = cols >= 0
        wc[:, m] = w_in[:, cols[m]]
        out[t] = wc.reshape(8, 128, 128).transpose(1, 0, 2)
    return out.reshape(N_WT, 128, 1024)


class _Stop(Exception):
    pass


def build(debug=(), stop=None):
    nc = bass.Bass("TRN2", target_bir_lowering=False)
    S = Sched(nc, n_dma_sems=16)
    dbg_outs = {}
    try:
        _build_body(nc, S, dbg_outs, debug, stop)
    except _Stop:
        pass
    S.finish()
    return nc, dbg_outs


def _build_body(nc, S, dbg_outs, debug, stop):
    AR = Arena(nc)

    def chk(name):
        if stop == name:
            raise _Stop()

    def din(name, shape):
        return nc.dram_tensor(name, list(shape), F32, kind="ExternalInput").ap()

    x_d = din("x", [S_LEN, D])
    wt_d = din("wt", [N_WT, 128, 1024])
    w1_d = din("w1", [2, 2048, 256])
    wout_d = din("wout", [2048, 1024])
    cst_d = {k: din(k, v) for k, v in CONST_SHAPES.items()}
    sm_d = {k: din(k, v) for k, v in SMALL_SHAPES.items()}
    out_d = nc.dram_tensor("out", [S_LEN, D], F32, kind="ExternalOutput").ap()

    def dbg(name, ap, res, shape, dt=BF16):
        if name not in debug:
            return
        d = nc.dram_tensor("dbg_" + name, list(shape), dt, kind="ExternalOutput").ap()
        dbg_outs[name] = shape
        S.dma(lambda e: e.dma_start(out=d, in_=ap), reads=[res])

    pb = [nc.alloc_psum_tensor("pb%d" % i, [128, 512], F32) for i in range(7)]
    rpb = [Res("pb%d" % i, excl=True) for i in range(7)]
    pbt = nc.alloc_psum_tensor("pbt", [128, 1024], BF16)
    rpbt = Res("pbt", excl=True)

    def G(name, shape, dt):
        return AR.alloc(name, shape, dt), Res(name)

    hT, r_hT = G("hT", [128, 8, S_LEN], BF16)
    r_hTc = [Res("hTc%d" % i) for i in range(4)]
    woutb = nc.alloc_sbuf_tensor_at("woutb_alias", [128, 16, 1024], BF16, offset=AR.last_off)
    r_woutb = Res("woutb")
    mixT, r_mix = G("mixT", [128, 16, S_LEN], BF16)
    r_mixt = [Res("mix%d" % i) for i in range(16)]
    ropeC, r_ropeC = G("ropeC", [128, S_LEN], BF16)
    ropeS, r_ropeS = G("ropeS", [128, S_LEN], BF16)
    cmpbias, r_cmpb = G("cmpbias", [128, S_LEN], BF16)
    ident, r_id = G("ident", [128, 128], BF16)
    causalb, r_cb = G("causalb", [128, 128], BF16)
    antib, r_ab = G("antib", [128, 128], BF16)
    trimask, r_tm = G("trimask", [128, 128], BF16)
    permm, r_pm = G("permm", [128, 128], BF16)
    antimask, r_am = G("antimask", [128, 128], BF16)
    topkA, r_tA = G("topkA", [128, 16, 32], BF16)
    topkB, r_tB = G("topkB", [128, 16, 32], BF16)
    onesf, r_ones = G("onesf", [128, 1], F32)
    wbf = [G("wbf%d" % i, [128, 8, 128], BF16) for i in range(4)]
    gates, r_gates = G("gates", [128, 16, 48], F32)
    dtv, r_dtv = G("dtv", [128, 16, 16], F32)
    a_tm, r_atm = G("a_tm", [128, 16, 16], F32)
    a3 = [G("a3_%d" % i, [128, 16, 16], BF16) for i in range(3)]
    tri2b, r_trib = G("tri2b", [128, 256], BF16)
    acs_tm, r_acs = G("acs_tm", [128, 16, 16], F32)
    ssq, r_ssq = G("ssq", [128, 16], F32)
    kcT2, r_kc = G("kcT2", [128, 4, 128], BF16)
    vcaug, r_vc = G("vcaug", [128, 4, 97], BF16)
    sm = {}
    for k, shp in SMALL_SHAPES.items():
        if shp[0] == 128:
            sm[k] = G("sm_" + k, shp, F32)
    bc = {}
    for k in ("b2v", "gateb", "dtb", "alog"):
        bc[k] = G("bc_" + k, [128, SMALL_SHAPES[k][1]], F32)
    aneg, r_aneg = G("aneg", [128, 16], F32)
    bias1, r_bias1 = G("bias1", [128, 4], F32)
    w2kb, r_w2kb = G("w2kb", [128, 2, 128], BF16)
    w2vb, r_w2vb = G("w2vb", [128, 2, 64], BF16)
    posTb, r_posTb = G("posTb", [128, 32], BF16)
    PHASE = AR.mark()
    tri2, r_tri = G("tri2", [128, 256], F32)
    cstg, r_cstg = G("cstg", [128, 64], F32)

    def load_const(name, dst, rdst, ncols, view=None):
        o = dst[:] if view is None else view
        S.dma(lambda e: e.dma_start(out=o, in_=cst_d[name]), writes=[rdst], q="pool")

    load_const("ropeC", ropeC, r_ropeC, 2048)
    load_const("ropeS", ropeS, r_ropeS, 2048)
    load_const("cmpbias", cmpbias, r_cmpb, 2048)
    load_const("ident", ident, r_id, 128)
    load_const("causalb", causalb, r_cb, 128)
    load_const("antib", antib, r_ab, 128)
    load_const("trimask", trimask, r_tm, 128)
    load_const("permm", permm, r_pm, 128)
    load_const("antimask", antimask, r_am, 128)
    S.dma(lambda e: e.dma_start(out=topkA[:].rearrange("p a b -> p (a b)"), in_=cst_d["topkA"]), writes=[r_tA], q="pool")
    S.dma(lambda e: e.dma_start(out=topkB[:].rearrange("p a b -> p (a b)"), in_=cst_d["topkB"]), writes=[r_tB], q="pool")
    S.dma(lambda e: e.dma_start(out=tri2[:], in_=cst_d["tri2"]), writes=[r_tri])
    S.dma(lambda e: e.dma_start(out=tri2b[:], in_=cst_d["tri2"]), writes=[r_trib], q="pool")
    S.op("pool", lambda e: e.memset(onesf[:], 1.0), writes=[r_ones])
    for k in sm:
        t, r = sm[k]
        S.dma(lambda e, t=t, k=k: e.dma_start(out=t[:], in_=sm_d[k]), writes=[r])
    for k in bc:
        t, r = bc[k]
        S.dma(lambda e, t=t, k=k: e.dma_start(out=t[:], in_=sm_d[k].partition_broadcast(128)), writes=[r])
    S.dma(lambda e: e.dma_start(out=cstg[:, 0:33], in_=cst_d["vcext"]), writes=[r_cstg])
    for g in range(4):
        S.op("pool", lambda e, g=g: e.tensor_copy(out=vcaug[:, g, 64:97], in_=cstg[:, 0:33]), reads=[r_cstg], writes=[r_vc])
    S.op("act", lambda e: e.activation(out=aneg[:], in_=bc["alog"][0][:], func=AF.Exp), reads=[bc["alog"][1]], writes=[r_aneg])
    S.op("dve", lambda e: e.tensor_scalar(out=aneg[:], in0=aneg[:], scalar1=-1.0, scalar2=None, op0=ALU.mult), reads=[r_aneg], writes=[r_aneg])
    S.op("pool", lambda e: e.tensor_copy(out=w2kb[:].rearrange("p a b -> p (a b)"), in_=sm["w2k"][0][:]), reads=[sm["w2k"][1]], writes=[r_w2kb])
    S.op("pool", lambda e: e.tensor_copy(out=w2vb[:].rearrange("p a b -> p (a b)"), in_=sm["w2v"][0][:]), reads=[sm["w2v"][1]], writes=[r_w2vb])
    S.op("pool", lambda e: e.tensor_copy(out=posTb[:], in_=sm["posT"][0][:]), reads=[sm["posT"][1]], writes=[r_posTb])
    S.op("pool", lambda e: e.memset(ssq[:], 0.0), writes=[r_ssq])

    wstate = {"next": 0, "issued": 0}
    WPF = 2

    def load_wtile():
        t = wstate["next"]
        wstate["next"] += 1
        while wstate["issued"] <= min(t + WPF, N_WT - 1):
            ti = wstate["issued"]
            wstate["issued"] += 1
            wb_, rwb_ = wbf[ti % 4]
            S.dma(lambda e, ti=ti, wb_=wb_: e.dma_start(out=wb_[:].rearrange("p a b -> p (a b)"), in_=wt_d[ti]), writes=[rwb_], q="pool")
        return wbf[t % 4]

    bank_rr = {"i": 0}

    def next_bank(lo=0, hi=2):
        i = lo + bank_rr["i"] % (hi - lo)
        bank_rr["i"] += 1
        return i

    def fm_matmuls(wb, rwb, tc, bi):
        for kc in range(8):
            S.op("pe", lambda e, kc=kc: e.matmul(pb[bi][:, :], lhsT=wb[:, kc, :], rhs=hT[:, kc, tc * 512:(tc + 1) * 512],
                                                  start=(kc == 0), stop=(kc == 7)),
                 reads=[rwb, r_hTc[tc]], writes=[rpb[bi]], rg=(0, 4))

    xt = [G("xt%d" % i, [128, 1024], F32) for i in range(2)]
    pw, r_pw = G("pw", [128, 1024], F32)
    junk, r_junk = G("junk", [128, 1024], F32)
    hb = [G("hb%d" % i, [128, 1024], BF16) for i in range(2)]
    ss0, r_ss0 = G("ss0", [128, 16], F32)
    S.dma(lambda e: e.dma_start(out=pw[:], in_=sm_d["prew"].partition_broadcast(128)), writes=[r_pw])
    hb3 = hb + [G("hb2", [128, 1024], BF16)]
    xt3 = xt + [G("xt2_", [128, 1024], F32)]

    def p0_a(tt):
        xb, rxb = xt3[tt % 3]
        hbb, rhbb = hb3[tt % 3]
        S.dma(lambda e: e.dma_start(out=xb[:], in_=x_d[tt * 128:(tt + 1) * 128, :]), writes=[rxb])
        S.op("act", lambda e: e.activation(out=junk[:], in_=xb[:], func=AF.Square, accum_out=ss0[:, tt:tt + 1]), reads=[rxb], writes=[r_junk, r_ss0])
        S.op("dve", lambda e: e.tensor_scalar(out=ss0[:, tt:tt + 1], in0=ss0[:, tt:tt + 1], scalar1=1.0 / D, scalar2=EPS, op0=ALU.mult, op1=ALU.add), reads=[r_ss0], writes=[r_ss0])
        S.op("act", lambda e: e.activation(out=ss0[:, tt:tt + 1], in_=ss0[:, tt:tt + 1], func=AF.Sqrt), reads=[r_ss0], writes=[r_ss0])
        S.op("dve", lambda e: e.reciprocal(out=ss0[:, tt:tt + 1], in_=ss0[:, tt:tt + 1]), reads=[r_ss0], writes=[r_ss0])
        S.op("dve", lambda e: e.scalar_tensor_tensor(out=hbb[:], in0=xb[:], scalar=ss0[:, tt:tt + 1], in1=pw[:], op0=ALU.mult, op1=ALU.mult),
             reads=[rxb, r_ss0, r_pw], writes=[rhbb])

    def p0_b(tt):
        hbb, rhbb = hb3[tt % 3]
        for kc in range(8):
            S.op("pe", lambda e, kc=kc: e.transpose(out=pbt[:, kc * 128:(kc + 1) * 128], in_=hbb[:, kc * 128:(kc + 1) * 128], identity=ident[:]),
                 reads=[rhbb, r_id], writes=[rpbt], rg=(0, 4))
        S.op("act", lambda e: e.copy(out=hT[:, 0:4, tt * 128:(tt + 1) * 128], in_=pbt[:, 0:512].rearrange("p (k t) -> p k t", k=4)),
             reads=[rpbt], writes=[r_hTc[tt // 4]])
        S.op("dve", lambda e: e.tensor_copy(out=hT[:, 4:8, tt * 128:(tt + 1) * 128], in_=pbt[:, 512:1024].rearrange("p (k t) -> p k t", k=4)),
             reads=[rpbt], writes=[r_hTc[tt // 4]])

    p0_a(0)
    for tt in range(NT):
        if tt + 1 < NT:
            p0_a(tt + 1)
        p0_b(tt)
    dbg("hT", hT[:, 0, :], r_hTc[3], [128, 2048])
    chk("p0")

    graw, r_graw = G("graw", [128, 16, 64], F32)
    wb, rwb = load_wtile()
    for tt in range(NT):
        bi = next_bank()
        for kc in range(8):
            S.op("pe", lambda e, kc=kc: e.matmul(pb[bi][:, 0:128], lhsT=hT[:, kc, tt * 128:(tt + 1) * 128], rhs=wb[:, kc, :],
                                                  start=(kc == 0), stop=(kc == 7)), reads=[rwb, r_hTc[tt // 4]], writes=[rpb[bi]], rg=(0, 4))
        S.op("dve", lambda e: e.tensor_tensor(out=graw[:, tt, 0:48], in0=pb[bi][:, 0:48], in1=bc["gateb"][0][:], op=ALU.add),
             reads=[rpb[bi], bc["gateb"][1]], writes=[r_graw])
        S.op("dve", lambda e: e.tensor_tensor(out=graw[:, tt, 48:64], in0=pb[bi][:, 48:64], in1=bc["dtb"][0][:], op=ALU.add),
             reads=[rpb[bi], bc["dtb"][1]], writes=[r_graw])
    S.op("act", lambda e: e.activation(out=gates[:], in_=graw[:, :, 0:48], func=AF.Sigmoid), reads=[r_graw], writes=[r_gates])
    S.op("act", lambda e: e.activation(out=dtv[:], in_=graw[:, :, 48:64], func=AF.Exp), reads=[r_graw], writes=[r_dtv])
    S.op("act", lambda e: e.activation(out=dtv[:], in_=dtv[:], func=AF.Ln, bias=1.0, scale=1.0), reads=[r_dtv], writes=[r_dtv])
    S.op("dve", lambda e: e.tensor_tensor(out=a_tm[:], in0=dtv[:], in1=aneg[:].unsqueeze(1).to_broadcast([128, 16, 16]), op=ALU.mult),
         reads=[r_dtv, r_aneg], writes=[r_atm])
    ares, r_ares = G("ares", [128, 16, 16], F32)
    S.op("dve", lambda e: e.tensor_copy(out=a3[0][0][:], in_=a_tm[:]), reads=[r_atm], writes=[a3[0][1]])
    S.op("dve", lambda e: e.tensor_tensor(out=ares[:], in0=a_tm[:], in1=a3[0][0][:], op=ALU.subtract), reads=[r_atm, a3[0][1]], writes=[r_ares])
    S.op("dve", lambda e: e.tensor_copy(out=a3[1][0][:], in_=ares[:]), reads=[r_ares], writes=[a3[1][1]])
    S.op("dve", lambda e: e.tensor_tensor(out=ares[:], in0=ares[:], in1=a3[1][0][:], op=ALU.subtract), reads=[r_ares, a3[1][1]], writes=[r_ares])
    S.op("dve", lambda e: e.tensor_copy(out=a3[2][0][:], in_=ares[:]), reads=[r_ares], writes=[a3[2][1]])
    for c in range(8):
        bi = next_bank()
        S.op("pe", lambda e: e.matmul(pb[bi][:, 0:16], lhsT=tri2[:, 0:128], rhs=a_tm[:, 2 * c, :], start=True, stop=True),
             reads=[r_tri, r_atm], writes=[rpb[bi]], rg=(0, 4))
        S.op("pe", lambda e: e.matmul(pb[bi][:, 16:32], lhsT=tri2[:, 128:256], rhs=a_tm[:, 2 * c, :], start=True, stop=False),
             reads=[r_tri, r_atm], writes=[rpb[bi]], rg=(0, 4))
        S.op("pe", lambda e: e.matmul(pb[bi][:, 16:32], lhsT=tri2[:, 0:128], rhs=a_tm[:, 2 * c + 1, :], start=False, stop=True),
             reads=[r_tri, r_atm], writes=[rpb[bi]], rg=(0, 4))
        S.op("dve", lambda e: e.tensor_copy(out=acs_tm[:, 2 * c:2 * c + 2, :], in_=pb[bi][:, 0:32].rearrange("p (a b) -> p a b", a=2)),
             reads=[rpb[bi]], writes=[r_acs])
    dbg("gates", gates[:].rearrange("p a b -> p (a b)"), r_gates, [128, 768], F32)
    dbg("dtv", dtv[:].rearrange("p a b -> p (a b)"), r_dtv, [128, 256], F32)
    dbg("acs", acs_tm[:].rearrange("p a b -> p (a b)"), r_acs, [128, 256], F32)
    chk("pg")

    kvT, r_kvT = G("kvT", [128, 4, S_LEN], BF16)
    w1b, r_w1b = G("w1b", [128, 32, 256], BF16)
    hid = [G("hid%d" % i, [128, 4, 128], BF16) for i in range(4)]
    for pc in range(4):
        for kv in range(2):
            S.dma(lambda e, kv=kv: e.dma_start(out=w1b[kv * 64:(kv + 1) * 64, pc * 8:(pc + 1) * 8, :],
                                               in_=w1_d[kv, pc * 512:(pc + 1) * 512, :].rearrange("(l d) h -> d l h", d=64)), writes=[r_w1b], q="pool")
    for g in range(4):
        wb, rwb = load_wtile()
        for tc in range(4):
            bi = next_bank()
            fm_matmuls(wb, rwb, tc, bi)
            S.op("act", lambda e: e.copy(out=kvT[:, g, tc * 512:(tc + 1) * 512], in_=pb[bi][:, :]), reads=[rpb[bi]], writes=[r_kvT])
    for kv in range(2):
        rows = slice(kv * 64, kv * 64 + 64)
        for hc in range(2):
            bi = 2 + kv
            for l in range(32):
                S.op("pe", lambda e, l=l: e.matmul(pb[bi][:, 0:1], lhsT=w1b[rows, l, hc * 128:(hc + 1) * 128], rhs=posTb[rows, l:l + 1],
                                                    start=(l == 0), stop=(l == 31)), reads=[r_w1b, r_posTb], writes=[rpb[bi]], rg=(2 * kv, 2 * kv + 2))
            col = kv * 2 + hc
            S.op("dve", lambda e: e.tensor_tensor(out=bias1[:, col:col + 1], in0=pb[bi][:, 0:1], in1=sm["b1T"][0][:, col:col + 1], op=ALU.add),
                 reads=[rpb[bi], sm["b1T"][1]], writes=[r_bias1])
    for hc in range(2):
        for l in range(32):
            for kv in range(2):
                rows = slice(kv * 64, kv * 64 + 64)
                bi = 2 + 2 * (hc % 2) + kv
                S.op("pe", lambda e, l=l: e.matmul(pb[bi][:, 0:508].rearrange("p (g c) -> p g c", g=4), lhsT=w1b[rows, l, hc * 128:(hc + 1) * 128],
                                                    rhs=kvT[rows, :, l:l + 2017:16], start=(l == 0), stop=(l == 31)),
                     reads=[r_w1b, r_kvT], writes=[rpb[bi]], rg=(2 * kv, 2 * kv + 2))
        for kv in range(2):
            bi = 2 + 2 * (hc % 2) + kv
            col = kv * 2 + hc
            hd, rhd = hid[col]
            S.op("act", lambda e: e.activation(out=hd[:, :, 0:127], in_=pb[bi][:, 0:508].rearrange("p (g c) -> p g c", g=4), func=AF.Silu, bias=bias1[:, col:col + 1], scale=1.0),
                 reads=[rpb[bi], r_bias1], writes=[rhd])
    for g in range(4):
        bi = 6
        for hc in range(2):
            S.op("pe", lambda e, hc=hc: e.matmul(pb[bi][:, 0:127], lhsT=w2kb[:, hc, :], rhs=hid[hc][0][:, g, 0:127], start=(hc == 0), stop=(hc == 1)),
                 reads=[r_w2kb, hid[hc][1]], writes=[rpb[bi]], rg=(0, 4))
        S.op("act", lambda e: e.activation(out=kcT2[:, g, 0:127], in_=pb[bi][:, 0:127], func=AF.Identity, bias=sm["b2k"][0][:, 0:1], scale=1.0),
             reads=[rpb[bi], sm["b2k"][1]], writes=[r_kc])
        bi = 0 + (g % 2)
        for hc in range(2):
            S.op("pe", lambda e, hc=hc: e.matmul(pb[bi][0:127, 0:64], lhsT=hid[2 + hc][0][:, g, 0:127], rhs=w2vb[:, hc, :], start=(hc == 0), stop=(hc == 1)),
                 reads=[r_w2vb, hid[2 + hc][1]], writes=[rpb[bi]], rg=(0, 4))
        S.op("dve", lambda e: e.tensor_tensor(out=vcaug[0:127, g, 0:64], in0=pb[bi][0:127, 0:64], in1=bc["b2v"][0][0:127, :], op=ALU.add),
             reads=[rpb[bi], bc["b2v"][1]], writes=[r_vc])
    dbg("kcT", kcT2[:, 0, :], r_kc, [128, 128])
    dbg("vc", vcaug[:, 0, :], r_vc, [128, 97])
    chk("cmp")
    S.barrier()
    AR.reset(PHASE)

    def rope_pair(dst_ap_fn, rdst, raw_dst_fn=None, rraw=None, split=None):
        wa, rwa = load_wtile()
        pend = []

        def tail():
            (tc, ba, raw_ap, rrw) = pend.pop(0)
            bp = 2 + (tc % 2)
            t1, rt1 = ropet[0][0]
            t2, rt2 = ropet[0][1]
            S.op("pe", lambda e: e.matmul(pb[bp][:, :], lhsT=permm[:], rhs=raw_ap, start=True, stop=True), reads=[r_pm, rrw], writes=[rpb[bp]], rg=(0, 4))
            S.op("dve", lambda e: e.tensor_tensor(out=t1[:], in0=pb[ba][:, :], in1=ropeC[:, tc * 512:(tc + 1) * 512], op=ALU.mult),
                 reads=[rpb[ba], r_ropeC], writes=[rt1])
            S.op("dve", lambda e: e.tensor_tensor(out=t2[:], in0=pb[bp][:, :], in1=ropeS[:, tc * 512:(tc + 1) * 512], op=ALU.mult),
                 reads=[rpb[bp], r_ropeS], writes=[rt2])
            if split is None:
                S.op("pool", lambda e: e.tensor_tensor(out=dst_ap_fn(tc), in0=t1[:], in1=t2[:], op=ALU.add), reads=[rt1, rt2], writes=[rdst])
            else:
                (d0, rd0), (d1_, rd1_) = split
                S.op("pool", lambda e: e.tensor_tensor(out=d0[0:64, tc * 512:(tc + 1) * 512], in0=t1[0:64, :], in1=t2[0:64, :], op=ALU.add), reads=[rt1, rt2], writes=[rd0])
                S.op("pool", lambda e: e.tensor_tensor(out=d1_[64:128, tc * 512:(tc + 1) * 512], in0=t1[64:128, :], in1=t2[64:128, :], op=ALU.add), reads=[rt1, rt2], writes=[rd1_])

        for tc in range(4):
            ba = next_bank(0, 2)
            fm_matmuls(wa, rwa, tc, ba)
            if raw_dst_fn is not None:
                raw_ap, rrw = raw_dst_fn(tc), rraw
            else:
                kr, rkr = kraws[tc % 2]
                raw_ap, rrw = kr[:], rkr
            S.op("act", lambda e: e.copy(out=raw_ap, in_=pb[ba][:, :]), reads=[rpb[ba]], writes=[rrw])
            pend.append((tc, ba, raw_ap, rrw))
            if len(pend) >= 2:
                tail()
        while pend:
            tail()

    for g in range(4):
        AR.reset(PHASE)
        qraw, r_qraw = G("qraw", [128, 2, S_LEN], BF16)
        qrot, r_qrot = G("qrot", [128, 2, S_LEN], BF16)
        kTs, r_kTs = G("kTs", [128, S_LEN], BF16)
        kTw, r_kTw = G("kTw", [128, S_LEN], BF16)
        kTx = [(kTs, r_kTs), (kTw, r_kTw)]
        zsT, r_zsT = G("zsT", [128, 2, S_LEN], BF16)
        vaug, r_vaug = G("vaug", [128, 16, 2, 65], BF16)
        ropet = [[G("ropet%d%d" % (i, k), [128, 512], F32) for k in range(2)] for i in range(1)]
        kraws = [G("kraw%d" % i, [128, 512], BF16) for i in range(2)]
        PT = [G("PT%d" % i, [128, 512], BF16) for i in range(3)]
        oacc, r_oacc = G("oacc", [128, 4, 256], F32)
        oaccb, r_oaccb = G("oaccb", [128, 4, 256], BF16)
        impacc, r_imp = G("impacc", [128, 4, 32], F32)
        impt, r_impt = G("impt", [128, 4, 32], F32)
        den, r_den = G("den", [128, 4], F32)
        fac, r_fac = G("fac", [128, 4], F32)
        m8, r_m8 = G("m8", [128, 8], F32)
        wk, r_wk = G("wk", [128, 32], F32)
        selq, r_selq = G("selq", [128, 96], BF16)
        KE1, r_KE1 = G("KE1", [128, S_LEN], BF16)
        QS = [[G("QS%d%d" % (j_, p_), [128, 512], BF16) for p_ in range(2)] for j_ in range(2)]
        S.op("pool", lambda e: e.memset(kTs[64:128, :], 0.0), writes=[r_kTs])
        S.op("pool", lambda e: e.memset(KE1[0:64, :], 0.0), writes=[r_KE1])
        S.dma(lambda e: e.dma_start(out=kTs[64:96, :], in_=cst_d["Eall"][0:32, :]), writes=[r_kTs], q="pool")
        S.dma(lambda e: e.dma_start(out=KE1[0:32, :], in_=cst_d["Eall"][0:32, :]), writes=[r_KE1], q="pool")
        for j_ in range(2):
            S.op("pool", lambda e, j_=j_: e.memset(QS[j_][0][0][64:128, :], 0.0), writes=[QS[j_][0][1]])
            S.op("pool", lambda e, j_=j_: e.memset(QS[j_][1][0][0:64, :], 0.0), writes=[QS[j_][1][1]])

        for j in range(2):
            rope_pair(lambda tc, j=j: qrot[:, j, tc * 512:(tc + 1) * 512], r_qrot,
                      lambda tc, j=j: qraw[:, j, tc * 512:(tc + 1) * 512], r_qraw)
        rope_pair(None, None, split=((kTs, r_kTs), (KE1, r_KE1)))
        rope_pair(lambda tc: kTw[:, tc * 512:(tc + 1) * 512], r_kTw)
        for j in range(2):
            wb, rwb = load_wtile()
            for tc in range(4):
                bi = next_bank()
                fm_matmuls(wb, rwb, tc, bi)
                S.op("act", lambda e: e.activation(out=zsT[:, j, tc * 512:(tc + 1) * 512], in_=pb[bi][:, :], func=AF.Silu),
                     reads=[rpb[bi]], writes=[r_zsT])
        wb, rwb = load_wtile()
        S.op("pool", lambda e: e.memset(vaug[:, :, :, 64:65], 1.0), writes=[r_vaug])
        for tt in range(NT):
            bi = next_bank()
            for kc in range(8):
                S.op("pe", lambda e, kc=kc: e.matmul(pb[bi][:, 0:128], lhsT=hT[:, kc, tt * 128:(tt + 1) * 128], rhs=wb[:, kc, :],
                                                      start=(kc == 0), stop=(kc == 7)), reads=[rwb, r_hTc[tt // 4]], writes=[rpb[bi]], rg=(0, 4))
            S.op("act", lambda e: e.copy(out=vaug[:, tt, :, 0:64], in_=pb[bi][:, 0:128].rearrange("p (a b) -> p a b", a=2)),
                 reads=[rpb[bi]], writes=[r_vaug])
        if g == 0:
            dbg("qrot", qrot[:, 0, :], r_qrot, [128, 2048])
            dbg("kTs", kTs[:, :], r_kTs, [128, 2048])
            dbg("vaug", vaug[:].rearrange("p a b c -> p (a b c)"), r_vaug, [128, 2080])
            chk("aproj")

        SB = [0, 1, 2, 3, 6]
        ACCB = [4, 5]
        rr = {"s": 0, "pt": 0, "acc": 0, "cm": 0, "df": 0, "ac": 0}
        oaccs = [(oacc, r_oacc), G("oacc1", [128, 4, 256], F32)]
        dens = [(den, r_den), G("den1", [128, 4], F32)]
        facs = [(fac, r_fac), G("fac1", [128, 4], F32)]
        otmps = [G("otmp%d" % i, [128, 4, 64], F32) for i in range(2)]
        for i_ in range(3, 6):
            PT.append(G("PT%d" % i_, [128, 512], BF16))
        for (kr_, rkr_) in kraws:
            PT.append((kr_, rkr_))

        def gidx(r, br):
            return (4 * g + r) * 3 + br

        def finalize(pv, rbank, r, br, qc, oa, roa, first):
            dn, rdn = dens[rr["df"] % 2]
            fc, rfc = facs[rr["df"] % 2]
            ot, rot = otmps[rr["df"] % 2]
            rr["df"] += 1
            gi = gidx(r, br)
            S.op("dve", lambda e: e.tensor_scalar(out=dn[:], in0=pv[:, :, 64], scalar1=1e-30, scalar2=None, op0=ALU.max), reads=[rbank], writes=[rdn])
            S.op("dve", lambda e: e.reciprocal(out=dn[:], in_=dn[:]), reads=[rdn], writes=[rdn])
            S.op("dve", lambda e: e.tensor_tensor(out=fc[:], in0=dn[:], in1=gates[:, 4 * qc:4 * qc + 4, gi], op=ALU.mult), reads=[rdn, r_gates], writes=[rfc])
            if first:
                S.op("dve", lambda e: e.tensor_tensor(out=oa[:, :, r * 64:(r + 1) * 64], in0=pv[:, :, 0:64], in1=fc[:].unsqueeze(2).to_broadcast([128, 4, 64]), op=ALU.mult),
                     reads=[rbank, rfc], writes=[roa])
            else:
                S.op("dve", lambda e: e.tensor_tensor(out=ot[:], in0=pv[:, :, 0:64], in1=fc[:].unsqueeze(2).to_broadcast([128, 4, 64]), op=ALU.mult),
                     reads=[rbank, rfc], writes=[rot])
                if br == 1:
                    S.op("pool", lambda e: e.tensor_tensor(out=oaccb[:, :, r * 64:(r + 1) * 64], in0=oa[:, :, r * 64:(r + 1) * 64], in1=ot[:], op=ALU.add),
                         reads=[roa, rot], writes=[r_oaccb])
                else:
                    S.op("pool", lambda e: e.tensor_tensor(out=oa[:, :, r * 64:(r + 1) * 64], in0=oa[:, :, r * 64:(r + 1) * 64], in1=ot[:], op=ALU.add),
                         reads=[roa, rot], writes=[roa])
            return dn, rdn

        accs = [G("accs%d" % i, [128, 388], F32) for i in range(2)]
        selqs = [(selq, r_selq)] + [G("selq%d" % i, [128, 96], BF16) for i in range(1, 4)]

        def topk(qc, sT, rsT):
            S.op("dve", lambda e: e.tensor_tensor(out=impacc[:], in0=impacc[:], in1=topkA[:, 4 * qc:4 * qc + 4, :], op=ALU.max), reads=[r_imp, r_tA], writes=[r_imp])
            S.op("dve", lambda e: e.tensor_tensor(out=impacc[:], in0=impacc[:], in1=topkB[:, 4 * qc:4 * qc + 4, :], op=ALU.min), reads=[r_imp, r_tB], writes=[r_imp])
            for qi in range(4):
                sq_, rsq_ = selqs[qi]
                S.op("dve", lambda e: e.max(out=m8[:], in_=impacc[:, qi, :]), reads=[r_imp], writes=[r_m8])
                S.op("dve", lambda e: e.match_replace(out=wk[:], in_to_replace=m8[:], in_values=impacc[:, qi, :], imm_value=-3.0e38), reads=[r_imp, r_m8], writes=[r_wk])
                S.op("dve", lambda e: e.max(out=m8[:], in_=wk[:]), reads=[r_wk], writes=[r_m8])
                S.op("dve", lambda e: e.tensor_scalar(out=sq_[:].rearrange("p (a b) -> p a b", a=3), in0=impacc[:, qi, :].unsqueeze(1).to_broadcast([128, 3, 32]),
                                                      scalar1=m8[:, 7:8], scalar2=NEGB, op0=ALU.is_lt, op1=ALU.mult), reads=[r_imp, r_m8], writes=[rsq_])

        def topk_pe(qc, qi, sT, rsT):
            sq_, rsq_ = selqs[qi]
            S.op("pe", lambda e: e.transpose(out=pbt[0:96, 512 + qi * 128:640 + qi * 128], in_=sq_[:, 0:96], identity=ident[:]), reads=[rsq_, r_id], writes=[rpbt], rg=(0, 4))
            if qi == 3:
                for j_ in range(2):
                    S.op("act", lambda e, j_=j_: e.copy(out=QS[j_][0][0][64:96, :], in_=pbt[64:96, 512:1024]), reads=[rpbt], writes=[QS[j_][0][1]])
                    S.op("act", lambda e, j_=j_: e.copy(out=QS[j_][1][0][0:32, :], in_=pbt[0:32, 512:1024]), reads=[rpbt], writes=[QS[j_][1][1]])
                if g == 0 and qc == 3:
                    dbg("selT", QS[0][1][0][:, :], QS[0][1][1], [128, 512])

        def cmp_step(qc, j, oa, roa, sT, rsT):
            st = {}

            def qk():
                st["pt"] = []
                bss = []
                for par in range(2):
                    bss.append(SB[rr["s"] % 5])
                    rr["s"] += 1
                    st["pt"].append(PT[rr["pt"] % 8])
                    rr["pt"] += 1
                for par in range(2):
                    rows = slice(par * 64, par * 64 + 64)
                    bs = bss[par]
                    S.op("pe", lambda e: e.matmul(pb[bs][0:127, :], lhsT=kcT2[rows, g, 0:127], rhs=qraw[rows, j, qc * 512:(qc + 1) * 512], start=True, stop=True),
                         reads=[r_kc, r_qraw], writes=[rpb[bs]], rg=(2 * par, 2 * par + 2))
                for par in range(2):
                    bs = bss[par]
                    pt, rpt = st["pt"][par]
                    S.op("act", lambda e: e.activation(out=pt[0:127, :], in_=pb[bs][0:127, :], func=AF.Exp, scale=0.125), reads=[rpb[bs]], writes=[rpt])
                for par in range(2):
                    pt, rpt = st["pt"][par]
                    S.op("pool", lambda e: e.tensor_tensor(out=pt[0:127, :], in0=pt[0:127, :], in1=cmpbias[0:127, qc * 512:(qc + 1) * 512], op=ALU.mult),
                         reads=[rpt, r_cmpb], writes=[rpt])

            def pv():
                for par in range(2):
                    r = 2 * j + par
                    pt, rpt = st["pt"][par]
                    bc_ = ACCB[par]
                    for qi in range(4):
                        S.op("pe", lambda e, qi=qi: e.matmul(pb[bc_][:, qi * 97:(qi + 1) * 97], lhsT=pt[0:127, qi * 128:(qi + 1) * 128], rhs=vcaug[0:127, g, :],
                                                              start=True, stop=True), reads=[rpt, r_vc], writes=[rpb[bc_]], rg=(0, 4))
                for par in range(2):
                    r = 2 * j + par
                    bc_ = ACCB[par]
                    ac_, rac_ = accs[rr["ac"] % 2]
                    rr["ac"] += 1
                    S.op("act", lambda e: e.copy(out=ac_[:, 0:388], in_=pb[bc_][:, 0:388]), reads=[rpb[bc_]], writes=[rac_])
                    pvv = ac_[:, 0:388].rearrange("p (a b) -> p a b", a=4)
                    dn, rdn = finalize(pvv, rac_, r, 0, qc, oa, roa, True)
                    if r == 0:
                        S.op("dve", lambda e: e.tensor_tensor(out=impacc[:], in0=pvv[:, :, 65:97], in1=dn[:].unsqueeze(2).to_broadcast([128, 4, 32]), op=ALU.mult),
                             reads=[rac_, rdn], writes=[r_imp])
                    else:
                        S.op("dve", lambda e: e.tensor_tensor(out=impt[:], in0=pvv[:, :, 65:97], in1=dn[:].unsqueeze(2).to_broadcast([128, 4, 32]), op=ALU.mult),
                             reads=[rac_, rdn], writes=[r_impt])
                        S.op("pool", lambda e: e.tensor_tensor(out=impacc[:], in0=impacc[:], in1=impt[:], op=ALU.add), reads=[r_imp, r_impt], writes=[r_imp])
                if j == 1 and g == 0 and qc == 3:
                    dbg("imp", impacc[:].rearrange("p a b -> p (a b)"), r_imp, [128, 128], F32)
            return qk, pv

        def att_step(qc, br, j, kt, kt_lo, kt_hi, oa, roa, sT, rsT):
            st = {}
            kT_, rkT_ = kTx[br - 1]
            qlo = max(0, kt - 4 * qc)
            qhi = 4 if br == 1 else min(4, kt + 5 - 4 * qc)
            c0, c1 = qlo * 128, qhi * 128

            def qk():
                st["pt"] = []
                bss = []
                for par in range(2):
                    bss.append(SB[rr["s"] % 5])
                    rr["s"] += 1
                    st["pt"].append(PT[rr["pt"] % 8])
                    rr["pt"] += 1
                masks = []
                if kt >= 4 * qc:
                    masks.append((kt - 4 * qc, trimask, r_tm))
                if br == 2 and 0 <= kt + 4 - 4 * qc < 4:
                    masks.append((kt + 4 - 4 * qc, antimask, r_am))
                if br == 1:
                    for par in range(2):
                        bs = bss[par]
                        ke_, rke_ = (kTs, r_kTs) if par == 0 else (KE1, r_KE1)
                        qs_, rqs_ = QS[j][par]
                        S.op("pe", lambda e: e.matmul(pb[bs][:, c0:c1], lhsT=ke_[:, kt * 128:(kt + 1) * 128], rhs=qs_[:, c0:c1], start=True, stop=True),
                             reads=[rke_, rqs_], writes=[rpb[bs]], rg=(0, 4))
                else:
                    for par in range(2):
                        bs = bss[par]
                        rows = slice(par * 64, par * 64 + 64)
                        S.op("pe", lambda e: e.matmul(pb[bs][:, c0:c1], lhsT=kT_[rows, kt * 128:(kt + 1) * 128], rhs=qrot[rows, j, qc * 512 + c0:qc * 512 + c1],
                                                      start=True, stop=True),
                             reads=[rkT_, r_qrot], writes=[rpb[bs]], rg=(2 * par, 2 * par + 2))
                for par in range(2):
                    bs = bss[par]
                    pt, rpt = st["pt"][par]
                    S.op("act", lambda e: e.activation(out=pt[:, c0:c1], in_=pb[bs][:, c0:c1], func=AF.Exp, scale=0.125), reads=[rpb[bs]], writes=[rpt])
                for par in range(2):
                    pt, rpt = st["pt"][par]
                    for (qb, mt, rmt) in masks:
                        S.op("pool", lambda e, qb=qb, mt=mt: e.tensor_tensor(out=pt[:, qb * 128:(qb + 1) * 128], in0=pt[:, qb * 128:(qb + 1) * 128], in1=mt[:], op=ALU.mult),
                             reads=[rpt, rmt], writes=[rpt])

            def pv():
                for par in range(2):
                    pt, rpt = st["pt"][par]
                    accb = ACCB[par]
                    for qi in range(qlo, qhi):
                        st_ = (kt == kt_lo and qi == qlo)
                        S.op("pe", lambda e, qi=qi, st_=st_: e.matmul(pb[accb][:, qi * 65:(qi + 1) * 65], lhsT=pt[:, qi * 128:(qi + 1) * 128], rhs=vaug[:, kt, br - 1, :],
                                                                       start=st_, stop=False, skip_group_check=True),
                             reads=[rpt, r_vaug], writes=[rpb[accb]], rg=(0, 4))
                if kt == kt_hi:
                    for par in range(2):
                        accb = ACCB[par]
                        ac_, rac_ = accs[rr["ac"] % 2]
                        rr["ac"] += 1
                        S.op("act", lambda e: e.copy(out=ac_[:, 0:260], in_=pb[accb][:, 0:260]), reads=[rpb[accb]], writes=[rac_])
                        pvv = ac_[:, 0:260].rearrange("p (a b) -> p a b", a=4)
                        finalize(pvv, rac_, 2 * j + par, br, qc, oa, roa, False)
            return qk, pv

        def finish_chunk(qc, oa, roa):
            if g == 0 and qc == 3:
                dbg("oatt", oaccb[:].rearrange("p a b -> p (a b)"), r_oaccb, [128, 1024])
            for j in range(2):
                for qi in range(4):
                    S.op("pe", lambda e: e.transpose(out=pbt[:, qi * 128:(qi + 1) * 128], in_=oaccb[:, qi, j * 128:(j + 1) * 128], identity=ident[:]),
                         reads=[r_oaccb, r_id], writes=[rpbt], rg=(0, 4))
                S.op("dve", lambda e: e.tensor_tensor(out=mixT[:, 2 * g + j, qc * 512:(qc + 1) * 512], in0=pbt[:, 0:512], in1=zsT[:, j, qc * 512:(qc + 1) * 512], op=ALU.mult),
                     reads=[rpbt, r_zsT], writes=[r_mixt[2 * g + j]])

        def qcopies(qc):
            for j_ in range(2):
                S.op("pool", lambda e, j_=j_: e.tensor_copy(out=QS[j_][0][0][0:64, :], in_=qrot[0:64, j_, qc * 512:(qc + 1) * 512]), reads=[r_qrot], writes=[QS[j_][0][1]])
                S.op("pool", lambda e, j_=j_: e.tensor_copy(out=QS[j_][1][0][64:128, :], in_=qrot[64:128, j_, qc * 512:(qc + 1) * 512]), reads=[r_qrot], writes=[QS[j_][1][1]])

        steps = []
        posts = {}
        for qc in range(4):
            oa, roa = oaccs[qc % 2]
            sT, rsT = None, None
            base = len(steps)
            if qc > 0:
                poa, proa = oaccs[(qc - 1) % 2]
                posts.setdefault(base + 12, []).append(lambda qc=qc, poa=poa, proa=proa: finish_chunk(qc - 1, poa, proa))
            for j in range(2):
                steps.append(cmp_step(qc, j, oa, roa, sT, rsT))
            posts.setdefault(base, []).append(lambda qc=qc: qcopies(qc))
            posts.setdefault(base + 2, []).append(lambda qc=qc, sT=sT, rsT=rsT: topk(qc, sT, rsT))
            for qi in range(4):
                posts.setdefault(base + 4 + qi, []).append(lambda qc=qc, qi=qi, sT=sT, rsT=rsT: topk_pe(qc, qi, sT, rsT))
            for br in (2, 1):
                for j in range(2):
                    kt_lo = 0 if br == 1 else max(0, 4 * qc - 4)
                    kt_hi = 4 * qc + 3
                    for kt in range(kt_lo, kt_hi + 1):
                        steps.append(att_step(qc, br, j, kt, kt_lo, kt_hi, oa, roa, sT, rsT))
        ADEPTH = 3
        nq = 0
        for si_, (qk, pv) in enumerate(steps):
            while nq <= min(si_ + ADEPTH, len(steps) - 1):
                steps[nq][0]()
                nq += 1
            pv()
            for f_ in posts.get(si_, []):
                f_()
        finish_chunk(3, *oaccs[3 % 2])
        S.barrier()
    dbg("mixA", mixT[:, 0, :], r_mixt[0], [128, 2048])
    chk("att")

    for g in range(4):
        AR.reset(PHASE)
        zsS, r_zsS = G("zsS", [128, 2, S_LEN], BF16)
        xcT, r_xcT = G("xcT", [128, 2, S_LEN], BF16)
        BT, r_BT = G("BT", [128, S_LEN], BF16)
        CT, r_CT = G("CT", [128, S_LEN], BF16)
        xdt, r_xdt = G("xdt", [128, 16, 4, 64], BF16)
        Btm, r_Btm = G("Btm", [128, 16, 128], BF16)
        SSD_TMP = AR.mark()
        ub = [G("ub%d" % i, [128, 515], BF16) for i in range(3)]
        dg, r_dg = G("dg", [128, 4, 4, 128], BF16)
        for j in range(2):
            wb, rwb = load_wtile()
            for tc in range(4):
                bi = next_bank()
                fm_matmuls(wb, rwb, tc, bi)
                S.op("act", lambda e: e.activation(out=zsS[:, j, tc * 512:(tc + 1) * 512], in_=pb[bi][:, :], func=AF.Silu), reads=[rpb[bi]], writes=[r_zsS])
        conv_targets = [(lambda tc: xcT[:, 0, tc * 512:(tc + 1) * 512], r_xcT, 2 * g),
                        (lambda tc: xcT[:, 1, tc * 512:(tc + 1) * 512], r_xcT, 2 * g + 1),
                        (lambda tc: BT[:, tc * 512:(tc + 1) * 512], r_BT, 8 + g),
                        (lambda tc: CT[:, tc * 512:(tc + 1) * 512], r_CT, 12 + g)]
        cw, r_cw = sm["convw"]
        cbv, r_cbv = sm["convb"]
        for ti, (dst_fn, rdst, ct) in enumerate(conv_targets):
            for k in range(4):
                S.op("dve", lambda e, ti=ti, k=k, ct=ct: e.tensor_scalar(out=dg[:, ti, k, :], in0=ident[:], scalar1=cw[:, ct * 4 + k:ct * 4 + k + 1], scalar2=None, op0=ALU.mult),
                     reads=[r_id, r_cw], writes=[r_dg])
        pend_conv = []

        def conv_tail():
            (ti, dst_fn, rdst, ct, tc, u, ru) = pend_conv.pop(0)
            pc = 2 + (tc % 2)
            for k in range(4):
                S.op("pe", lambda e, k=k: e.matmul(pb[pc][:, :], lhsT=dg[:, ti, k, :], rhs=u[:, k:k + 512], start=(k == 0), stop=(k == 3)),
                     reads=[r_dg, ru], writes=[rpb[pc]], rg=(0, 4))
            S.op("act", lambda e: e.activation(out=dst_fn(tc), in_=pb[pc][:, :], func=AF.Silu, bias=cbv[:, ct:ct + 1], scale=1.0), reads=[rpb[pc], r_cbv], writes=[rdst])

        for ti, (dst_fn, rdst, ct) in enumerate(conv_targets):
            wb, rwb = load_wtile()
            for tc in range(4):
                bi = next_bank()
                fm_matmuls(wb, rwb, tc, bi)
                u, ru = ub[(ti * 4 + tc) % 3]
                if tc == 0:
                    S.op("pool", lambda e: e.memset(u[:, 0:3], 0.0), writes=[ru])
                S.op("act", lambda e: e.copy(out=u[:, 3:515], in_=pb[bi][:, :]), reads=[rpb[bi]], writes=[ru])
                if tc < 3:
                    un, run = ub[(ti * 4 + tc + 1) % 3]
                    S.op("pool", lambda e: e.tensor_copy(out=un[:, 0:3], in_=u[:, 512:515]), reads=[ru], writes=[run])
                pend_conv.append((ti, dst_fn, rdst, ct, tc, u, ru))
                if len(pend_conv) >= 2:
                    conv_tail()
        while pend_conv:
            conv_tail()
        for tt in range(NT):
            for j in range(2):
                S.op("pe", lambda e, j=j: e.transpose(out=pbt[:, j * 128:(j + 1) * 128], in_=xcT[:, j, tt * 128:(tt + 1) * 128], identity=ident[:]),
                     reads=[r_xcT, r_id], writes=[rpbt], rg=(0, 4))
            S.op("pe", lambda e: e.transpose(out=pbt[:, 256:384], in_=BT[:, tt * 128:(tt + 1) * 128], identity=ident[:]), reads=[r_BT, r_id], writes=[rpbt], rg=(0, 4))
            S.op("dve", lambda e: e.tensor_tensor(out=xdt[:, tt, :, :], in0=pbt[:, 0:256].rearrange("p (a b) -> p a b", a=4),
                                                  in1=dtv[:, tt, 4 * g:4 * g + 4].unsqueeze(2).to_broadcast([128, 4, 64]), op=ALU.mult),
                 reads=[rpbt, r_dtv], writes=[r_xdt])
            S.op("act", lambda e: e.copy(out=Btm[:, tt, :], in_=pbt[:, 256:384]), reads=[rpbt], writes=[r_Btm])
        if g == 0:
            dbg("xcT", xcT[:, 0, :], r_xcT, [128, 2048])
            dbg("BT", BT[:, :], r_BT, [128, 2048])
        S.barrier()
        AR.reset(SSD_TMP)
        NB3 = 4
        CBm = [G("CBm%d" % i, [128, 2, 256], BF16) for i in range(2)]
        EA = [G("EA%d" % i, [128, 256], F32) for i in range(NB3)]
        Cdec = [G("Cdec%d" % i, [128, 256], BF16) for i in range(NB3)]
        D1 = [G("D1%d" % i, [128, 384], F32) for i in range(NB3)]
        MT = [G("MT%d" % i, [128, 384], BF16) for i in range(NB3)]
        xws = [G("xw%d" % i, [128, 2, 4, 64], BF16) for i in range(2)]
        cds = [G("cdall%d" % i, [128, 4], F32) for i in range(2)]
        dtes = [G("dte%d" % i, [128, 2, 4], F32) for i in range(2)]
        htmp, r_htmp = G("htmp", [128, 4, 64], F32)
        h32, r_h32 = G("h32", [128, 4, 64], F32)
        hbf, r_hbf = G("hbf", [128, 4, 64], BF16)
        ytmp = [G("ytmp%d" % i, [128, 256], F32) for i in range(4)]
        yg = [G("yg%d" % i, [128, 256], F32) for i in range(2)]
        S.op("pool", lambda e: e.memset(h32[:], 0.0), writes=[r_h32])
        S.op("pool", lambda e: e.memset(hbf[:], 0.0), writes=[r_hbf])
        dsk, r_dsk = sm["dskip"]
        snw, r_snw = sm["snw"]
        BY = [0, 1, 2]
        sst = {"by": 0, "k": 0}
        info = {}

        def Cc(c):
            l0 = c * 256
            bx = 6
            cb_, rcb_ = CBm[c % 2]
            S.op("pe", lambda e: e.matmul(pb[bx][:, 0:256], lhsT=BT[:, l0:l0 + 128], rhs=CT[:, l0:l0 + 256], start=True, stop=True),
                 reads=[r_BT, r_CT], writes=[rpb[bx]], rg=(0, 4))
            S.op("pe", lambda e: e.matmul(pb[bx][:, 256:384], lhsT=BT[:, l0 + 128:l0 + 256], rhs=CT[:, l0 + 128:l0 + 256], start=True, stop=True),
                 reads=[r_BT, r_CT], writes=[rpb[bx]], rg=(0, 4))
            S.op("dve", lambda e: e.tensor_tensor(out=cb_[:, 0, 0:128], in0=pb[bx][:, 0:128], in1=trimask[:], op=ALU.mult), reads=[rpb[bx], r_tm], writes=[rcb_])
            S.op("act", lambda e: e.copy(out=cb_[:, 0, 128:256], in_=pb[bx][:, 128:256]), reads=[rpb[bx]], writes=[rcb_])
            S.op("dve", lambda e: e.tensor_tensor(out=cb_[:, 1, 0:128], in0=pb[bx][:, 256:384], in1=trimask[:], op=ALU.mult), reads=[rpb[bx], r_tm], writes=[rcb_])

        def A(c, r):
            l0 = c * 256
            h = 4 * g + r
            k = sst["k"] % NB3
            sst["k"] += 1
            info[(c, r)] = k
            by = BY[sst["by"] % 3]
            sst["by"] += 1
            cb_, rcb_ = CBm[c % 2]
            xw, r_xw = xws[c % 2]
            cdall, r_cd = cds[c % 2]
            dte, r_dte = dtes[c % 2]
            for pi in range(3):
                ap_, rap_ = a3[pi]
                S.op("pe", lambda e, pi=pi, ap_=ap_: e.matmul(pb[by][:, 0:256], lhsT=ap_[:, 2 * c, h:h + 1].to_broadcast([128, 128]), rhs=tri2b[:, :], start=(pi == 0), stop=False),
                     reads=[rap_, r_trib], writes=[rpb[by]], rg=(0, 4))
            for pi in range(3):
                ap_, rap_ = a3[pi]
                S.op("pe", lambda e, pi=pi, ap_=ap_: e.matmul(pb[by][:, 128:256], lhsT=ap_[:, 2 * c + 1, h:h + 1].to_broadcast([128, 128]), rhs=tri2b[:, 0:128], start=False, stop=(pi == 2)),
                     reads=[rap_, r_trib], writes=[rpb[by]], rg=(0, 4))
            ea, rea = EA[k]
            cd_, rcd_ = Cdec[k]
            d1, rd1 = D1[k]
            mt, rmt = MT[k]
            S.op("dve", lambda e: e.tensor_scalar(out=d1[:, 0:256], in0=pb[by][:, 0:256], scalar1=acs_tm[:, 2 * c, h:h + 1], scalar2=0.0, op0=ALU.subtract, op1=ALU.min),
                 reads=[rpb[by], r_acs], writes=[rd1])
            S.op("dve", lambda e: e.tensor_scalar(out=d1[:, 256:384], in0=pb[by][:, 128:256], scalar1=acs_tm[:, 2 * c + 1, h:h + 1], scalar2=0.0, op0=ALU.subtract, op1=ALU.min),
                 reads=[rpb[by], r_acs], writes=[rd1])
            S.op("act", lambda e: e.activation(out=ea[:], in_=pb[by][:, 0:256], func=AF.Exp), reads=[rpb[by]], writes=[rea])
            S.op("act", lambda e: e.activation(out=d1[:], in_=d1[:], func=AF.Exp), reads=[rd1], writes=[rd1])
            S.op("pool", lambda e: e.tensor_tensor(out=mt[:, 0:256], in0=d1[:, 0:256], in1=cb_[:, 0, :], op=ALU.mult), reads=[rd1, rcb_], writes=[rmt])
            S.op("pool", lambda e: e.tensor_tensor(out=mt[:, 256:384], in0=d1[:, 256:384], in1=cb_[:, 1, 0:128], op=ALU.mult), reads=[rd1, rcb_], writes=[rmt])
            if c > 0:
                S.op("dve", lambda e: e.tensor_tensor(out=cd_[:], in0=CT[:, l0:l0 + 256], in1=ea[:], op=ALU.mult), reads=[r_CT, rea], writes=[rcd_])
            if c < 7:
                S.op("act", lambda e: e.copy(out=cdall[:, r:r + 1], in_=ea[:, 255:256]), reads=[rea], writes=[r_cd])
                S.op("act", lambda e: e.copy(out=dte[:, :, r], in_=d1[:, 255:384:128]), reads=[rd1], writes=[r_dte])

        def B(c, r):
            k = info[(c, r)]
            jp, par = r // 2, r % 2
            cd_, rcd_ = Cdec[k]
            mt, rmt = MT[k]
            bo = 3 + jp
            yo = pb[bo][par * 64:(par + 1) * 64, :]
            S.op("pe", lambda e: e.matmul(yo[:, 0:256], lhsT=xdt[:, 2 * c, r, :], rhs=mt[:, 0:256], start=True, stop=False),
                 reads=[r_xdt, rmt], writes=[rpb[bo]], rg=(0, 4))
            S.op("pe", lambda e: e.matmul(yo[:, 128:256], lhsT=xdt[:, 2 * c + 1, r, :], rhs=mt[:, 256:384], start=False, stop=(c == 0)),
                 reads=[r_xdt, rmt], writes=[rpb[bo]], rg=(0, 4))
            if c > 0:
                S.op("pe", lambda e: e.matmul(yo[:, 0:256], lhsT=hbf[:, r, :], rhs=cd_[:], start=False, stop=True),
                     reads=[r_hbf, rcd_], writes=[rpb[bo]], rg=(0, 4))

        def St_pre(c):
            xw, r_xw = xws[c % 2]
            dte, r_dte = dtes[c % 2]
            for si_ in range(2):
                S.op("dve", lambda e, si_=si_: e.tensor_tensor(out=xw[:, si_, :, :], in0=xdt[:, 2 * c + si_, :, :], in1=dte[:, si_, :].unsqueeze(2).to_broadcast([128, 4, 64]), op=ALU.mult),
                     reads=[r_xdt, r_dte], writes=[r_xw])

        def St(c):
            bst = 5
            xw, r_xw = xws[c % 2]
            cdall, r_cd = cds[c % 2]
            dte, r_dte = dtes[c % 2]
            S.op("pe", lambda e: e.matmul(pb[bst][:, 0:256], lhsT=Btm[:, 2 * c, :], rhs=xw[:, 0, :, :].rearrange("p a b -> p (a b)"), start=True, stop=False),
                 reads=[r_Btm, r_xw], writes=[rpb[bst]], rg=(0, 4))
            S.op("pe", lambda e: e.matmul(pb[bst][:, 0:256], lhsT=Btm[:, 2 * c + 1, :], rhs=xw[:, 1, :, :].rearrange("p a b -> p (a b)"), start=False, stop=True),
                 reads=[r_Btm, r_xw], writes=[rpb[bst]], rg=(0, 4))
            S.op("dve", lambda e: e.tensor_tensor(out=htmp[:], in0=h32[:], in1=cdall[:, 0:4].unsqueeze(2).to_broadcast([128, 4, 64]), op=ALU.mult),
                 reads=[r_h32, r_cd], writes=[r_htmp])
            S.op("dve", lambda e: e.tensor_tensor(out=h32[:], in0=htmp[:], in1=pb[bst][:, 0:256].rearrange("p (a b) -> p a b", a=4), op=ALU.add),
                 reads=[r_htmp, rpb[bst]], writes=[r_h32])
            S.op("pool", lambda e: e.tensor_copy(out=hbf[:], in_=h32[:]), reads=[r_h32], writes=[r_hbf])

        def Yev(c, jp):
            l0 = c * 256
            bo = 3 + jp
            ft = 2 * g + jp
            yk = sst["y"] % 4
            sst["y"] += 1
            yt, ryt = ytmp[yk]
            ygg, rygg = yg[jp]
            S.op("dve", lambda e: e.scalar_tensor_tensor(out=yt[:], in0=xcT[:, jp, l0:l0 + 256], scalar=dsk[:, ft:ft + 1], in1=pb[bo][:, 0:256], op0=ALU.mult, op1=ALU.add),
                 reads=[r_xcT, r_dsk, rpb[bo]], writes=[ryt])
            S.op("dve", lambda e: e.tensor_tensor(out=ygg[:], in0=yt[:], in1=zsS[:, jp, l0:l0 + 256], op=ALU.mult), reads=[ryt, r_zsS], writes=[rygg])
            S.op("act", lambda e: e.mul(out=mixT[:, 8 + ft, l0:l0 + 256], in_=ygg[:], mul=snw[:, ft:ft + 1]), reads=[rygg, r_snw], writes=[r_mixt[8 + ft]])
            S.op("act", lambda e: e.activation(out=yt[:], in_=ygg[:], func=AF.Square), reads=[rygg], writes=[ryt])
            pendpe.append((c, yk))
            if g == 0 and jp == 0 and c == 1:
                dbg("yg", ygg[:], rygg, [128, 256], F32)

        def YevPE():
            c, yk = pendpe.pop(0)
            yt, ryt = ytmp[yk]
            bq = 5
            kq = sst["q"] % 4
            sst["q"] += 1
            for hh in range(2):
                S.op("pe", lambda e, hh=hh: e.matmul(pb[bq][:, 400 + 2 * kq + hh:401 + 2 * kq + hh], lhsT=yt[:, hh * 128:(hh + 1) * 128], rhs=onesf[:, 0:1], start=True, stop=True),
                     reads=[ryt, r_ones], writes=[rpb[bq]], rg=(0, 4))
            pend.append((c, kq))

        def Yev2():
            c, kq = pend.pop(0)
            bq = 5
            S.op("dve", lambda e: e.tensor_tensor(out=ssq[:, 2 * c:2 * c + 2], in0=ssq[:, 2 * c:2 * c + 2], in1=pb[bq][:, 400 + 2 * kq:402 + 2 * kq], op=ALU.add), reads=[r_ssq, rpb[bq]], writes=[r_ssq])

        pend = []
        pendpe = []
        sst["q"] = 0
        sst["y"] = 0
        DEPTH = 3
        sl = [(c, r) for c in range(8) for r in range(4)]
        issued = 0

        def issue_A(upto):
            nonlocal_issued = issued_box[0]
            while nonlocal_issued <= upto and nonlocal_issued < len(sl):
                c_, r_ = sl[nonlocal_issued]
                if r_ == 0:
                    Cc(c_)
                A(c_, r_)
                nonlocal_issued += 1
            issued_box[0] = nonlocal_issued

        issued_box = [0]
        for si, (c, r) in enumerate(sl):
            issue_A(si + DEPTH)
            if r == 3 and c < 7:
                St_pre(c)
            B(c, r)
            if r % 2 == 1:
                if len(pendpe) >= 2:
                    YevPE()
                if len(pend) >= 3:
                    Yev2()
                Yev(c, r // 2)
            if r == 3 and c < 7:
                St(c)
            if g == 3 and r == 1:
                for kc in (2 * c, 2 * c + 1):
                    S.dma(lambda e, kc=kc: e.dma_start(out=woutb[:, kc, :], in_=wout_d[kc * 128:(kc + 1) * 128, :]), writes=[r_woutb] + r_hTc, q="pool")
        while pendpe:
            YevPE()
        while pend:
            Yev2()
        S.barrier()
    dbg("ssq", ssq[:], r_ssq, [128, 16], F32)
    dbg("mixS", mixT[:, 8, :], r_mixt[8], [128, 2048])
    chk("ssd")

    AR.reset(PHASE)
    pw2, r_pw2 = G("pw2", [128, 1024], F32)
    xt2 = [G("xt2%d" % i, [128, 1024], F32) for i in range(2)]
    ob = [G("ob%d" % i, [128, 1024], F32) for i in range(2)]
    tmpo, r_tmpo = G("tmpo", [128, 1024], F32)
    junk2, r_junk2 = G("junk2", [128, 1024], F32)
    ss2, r_ss2 = G("ss2", [128, 16], F32)
    S.dma(lambda e: e.dma_start(out=pw2[:], in_=sm_d["postw"].partition_broadcast(128)), writes=[r_pw2])
    S.op("dve", lambda e: e.tensor_scalar(out=ssq[:], in0=ssq[:], scalar1=1.0 / D, scalar2=EPS, op0=ALU.mult, op1=ALU.add), reads=[r_ssq], writes=[r_ssq])
    S.op("act", lambda e: e.activation(out=ssq[:], in_=ssq[:], func=AF.Sqrt), reads=[r_ssq], writes=[r_ssq])
    S.op("dve", lambda e: e.reciprocal(out=ssq[:], in_=ssq[:]), reads=[r_ssq], writes=[r_ssq])
    for tt in range(NT):
        xb, rxb = xt2[tt % 2]
        o_, ro_ = ob[tt % 2]
        S.dma(lambda e: e.dma_start(out=xb[:], in_=x_d[tt * 128:(tt + 1) * 128, :]), writes=[rxb])
        for half in range(2):
            for hs in range(2):
                bi = half * 2 + hs
                for kk in range(8):
                    kc = half * 8 + kk
                    S.op("pe", lambda e, kc=kc, kk=kk: e.matmul(pb[bi][:, :], lhsT=mixT[:, kc, tt * 128:(tt + 1) * 128], rhs=woutb[:, kc, hs * 512:(hs + 1) * 512],
                                                                 start=(kk == 0), stop=(kk == 7)), reads=[r_mixt[kc], r_woutb], writes=[rpb[bi]], rg=(0, 4))
        for hs in range(2):
            S.op("act", lambda e: e.mul(out=tmpo[:, hs * 512:(hs + 1) * 512], in_=pb[2 + hs][:, :], mul=ssq[:, tt:tt + 1]),
                 reads=[rpb[2 + hs], r_ssq], writes=[r_tmpo])
            S.op("dve", lambda e: e.tensor_tensor(out=o_[:, hs * 512:(hs + 1) * 512], in0=tmpo[:, hs * 512:(hs + 1) * 512], in1=pb[hs][:, :], op=ALU.add),
                 reads=[r_tmpo, rpb[hs]], writes=[ro_])
        S.op("act", lambda e: e.activation(out=junk2[:], in_=o_[:], func=AF.Square, accum_out=ss2[:, tt:tt + 1]), reads=[ro_], writes=[r_junk2, r_ss2])
        S.op("dve", lambda e: e.tensor_scalar(out=ss2[:, tt:tt + 1], in0=ss2[:, tt:tt + 1], scalar1=1.0 / D, scalar2=EPS, op0=ALU.mult, op1=ALU.add), reads=[r_ss2], writes=[r_ss2])
        S.op("act", lambda e: e.activation(out=ss2[:, tt:tt + 1], in_=ss2[:, tt:tt + 1], func=AF.Sqrt), reads=[r_ss2], writes=[r_ss2])
        S.op("dve", lambda e: e.reciprocal(out=ss2[:, tt:tt + 1], in_=ss2[:, tt:tt + 1]), reads=[r_ss2], writes=[r_ss2])
        S.op("dve", lambda e: e.scalar_tensor_tensor(out=o_[:], in0=o_[:], scalar=ss2[:, tt:tt + 1], in1=pw2[:], op0=ALU.mult, op1=ALU.mult),
             reads=[ro_, r_ss2, r_pw2], writes=[ro_])
        S.op("pool", lambda e: e.tensor_tensor(out=o_[:], in0=o_[:], in1=xb[:], op=ALU.add), reads=[ro_, rxb], writes=[ro_])
        S.dma(lambda e: e.dma_start(out=out_d[tt * 128:(tt + 1) * 128, :], in_=o_[:]), reads=[ro_])


_CACHE = {}


def prepare_inputs(inp):
    inp = {k: np.asarray(v) for k, v in inp.items()}
    shared = {}
    shared["wt"] = host_wtiles(np.ascontiguousarray(inp["w_in"][0], dtype=np.float32))
    shared["w1"] = np.ascontiguousarray(inp["cmp_w1"][0], dtype=np.float32)
    shared["wout"] = np.ascontiguousarray(inp["w_out"][0], dtype=np.float32)
    shared.update(host_consts())
    shared.update(host_small(inp))
    return inp, shared


def kernel(**inputs):
    inp, shared = prepare_inputs(inputs)
    if "nc" not in _CACHE:
        _CACHE["nc"] = build()[0]
    nc = _CACHE["nc"]
    x = np.ascontiguousarray(inp["x"], dtype=np.float32)
    in_maps = []
    for b in range(8):
        m = dict(shared)
        m["x"] = x[b]
        in_maps.append(m)
    res = run_bass_kernel_spmd(nc, in_maps, core_ids=list(range(8)))
    return np.stack([res.results[b]["out"] for b in range(8)], 0).astype(np.float32)
```

```python
import numpy as np
import concourse.bass as bass
import concourse.mybir as mybir
from concourse.bass_utils import run_bass_kernel_spmd

F32 = mybir.dt.float32
BF16 = mybir.dt.bfloat16
AF = mybir.ActivationFunctionType
ALU = mybir.AluOpType

S_LEN = 2048
D = 1024
NT = 16
NEGB = -1.0e9
EPS = 1e-6
SAME_ENGINE_SYNC = True

O_Q, O_KCM, O_VCM, O_KSL, O_VSL, O_KWN, O_VWN, O_GLOG, O_ZATT, O_ZSSM, O_XSSM, O_B, O_C, O_DT = (
    0, 1024, 1280, 1536, 1792, 2048, 2304, 2560, 2608, 3632, 4656, 5680, 6192, 6704)


class Res:
    __slots__ = ("name", "last_write", "readers", "excl", "rg")

    def __init__(self, name, excl=False):
        self.name = name
        self.last_write = None
        self.readers = {}
        self.excl = excl
        self.rg = None


class Sched:
    ENGS = ("pe", "act", "dve", "pool", "sp")

    def __init__(self, nc, n_dma_sems=8):
        self.nc = nc
        self.eng = {"pe": nc.tensor, "act": nc.scalar, "dve": nc.vector, "pool": nc.gpsimd, "sp": nc.sync}
        self.sem = {e: nc.alloc_semaphore("s_" + e) for e in ("pe", "act", "dve", "pool")}
        self.cnt = {e: 0 for e in ("pe", "act", "dve", "pool")}
        self.dsem = [nc.alloc_semaphore("s_dma%d" % i) for i in range(n_dma_sems)]
        self.dcnt = [0] * n_dma_sems
        half = n_dma_sems // 2
        self.dpool = {"sp": list(range(0, half)), "pool": list(range(half, n_dma_sems))}
        self.dnext = {"sp": 0, "pool": 0}
        self.seen = {e: {} for e in self.ENGS}
        self.nops = 0

    def _semof(self, f):
        return self.dsem[f] if isinstance(f, int) else self.sem[f]

    def _collect(self, e, reads, writes, rg=None):
        waits = {}

        def need(ev, force=False):
            f, c = ev
            if f == e and (e == "pe" or not SAME_ENGINE_SYNC) and not force:
                return
            if self.seen[e].get(f, 0) >= c:
                return
            if waits.get(f, 0) < c:
                waits[f] = c

        for r in reads:
            if r.excl:
                if r.last_write:
                    need(r.last_write)
                for ev in r.readers.items():
                    need(ev)
            elif r.last_write:
                need(r.last_write)
        for w in writes:
            if w.last_write:
                force = False
                if e == "pe" and w.excl and rg is not None and w.rg is not None:
                    if rg[1] <= w.rg[0] or w.rg[1] <= rg[0]:
                        force = True
                need(w.last_write, force)
            for ev in w.readers.items():
                need(ev)
        return waits

    def _emit_waits(self, e, waits):
        for f, c in waits.items():
            self.seen[e][f] = c
            self.eng[e].wait_ge(self._semof(f), c)

    def op(self, e, fn, reads=(), writes=(), rg=None):
        waits = self._collect(e, reads, writes, rg)
        self._emit_waits(e, waits)
        self.cnt[e] += 1
        c = self.cnt[e]
        fn(self.eng[e]).then_inc(self.sem[e], 1)
        self.nops += 1
        for r in reads:
            if r.excl:
                r.last_write = (e, c)
                r.readers = {}
            elif r.readers.get(e, 0) < c:
                r.readers[e] = c
        for w in writes:
            w.last_write = (e, c)
            w.readers = {}
            if e == "pe" and w.excl:
                w.rg = rg
        return (e, c)

    def dma(self, fn, reads=(), writes=(), q="sp"):
        waits = self._collect(q, reads, writes)
        pool_ = self.dpool[q]
        s = pool_[self.dnext[q] % len(pool_)]
        self.dnext[q] += 1
        if self.dcnt[s] > 0 and self.seen[q].get(s, 0) < self.dcnt[s]:
            if waits.get(s, 0) < self.dcnt[s]:
                waits[s] = self.dcnt[s]
        self._emit_waits(q, waits)
        self.dcnt[s] += 16
        c = self.dcnt[s]
        fn(self.eng[q]).then_inc(self.dsem[s], 16)
        self.nops += 1
        for r in reads:
            if r.readers.get(s, 0) < c:
                r.readers[s] = c
        for w in writes:
            w.last_write = (s, c)
            w.readers = {}
        return (s, c)

    def barrier(self):
        for e in self.ENGS:
            waits = {}
            for f in ("pe", "act", "dve", "pool"):
                if f != e and self.cnt[f] > self.seen[e].get(f, 0):
                    waits[f] = self.cnt[f]
            for s in range(len(self.dsem)):
                if self.dcnt[s] > self.seen[e].get(s, 0):
                    waits[s] = self.dcnt[s]
            self._emit_waits(e, waits)

    def finish(self):
        waits = {}
        for s in range(len(self.dsem)):
            if self.dcnt[s] > self.seen["sp"].get(s, 0):
                waits[s] = self.dcnt[s]
        self._emit_waits("sp", waits)


class Arena:
    def __init__(self, nc):
        self.nc = nc
        self.base = (nc.sbuf_base + 63) // 64 * 64
        self.top = nc.sbuf_top
        self.cur = self.base
        self.n = 0

    def alloc(self, name, shape, dt):
        esz = 2 if dt == BF16 else 4
        per = esz
        for s in shape[1:]:
            per *= s
        off = self.cur
        self.cur = (off + per + 63) // 64 * 64
        assert self.cur <= self.top, "SBUF overflow at %s: %d > %d" % (name, self.cur, self.top)
        self.n += 1
        self.last_off = off
        return self.nc.alloc_sbuf_tensor_at("%s_%d" % (name, self.n), list(shape), dt, offset=off)

    def mark(self):
        return self.cur

    def reset(self, m):
        self.cur = m


def _tile_cols():
    def partner(d):
        return d + 8 if d < 8 else (d - 8 if d < 16 else -1)

    tiles = []
    tmg = [O_GLOG + i for i in range(48)] + [O_DT + i for i in range(16)] + [-1] * 64
    tiles.append(("tmg", tmg))
    for g in range(4):
        tiles.append(("kvcm%d" % g, [O_KCM + g * 64 + d for d in range(64)] + [O_VCM + g * 64 + d for d in range(64)]))
    for g in range(4):
        for j in range(2):
            hs = (4 * g + 2 * j, 4 * g + 2 * j + 1)
            tiles.append(("q", [O_Q + h * 64 + d for h in hs for d in range(64)]))
        for off in (O_KSL, O_KWN):
            tiles.append(("k", [off + g * 64 + d for d in range(64)] * 2))
        for j in range(2):
            hs = (4 * g + 2 * j, 4 * g + 2 * j + 1)
            tiles.append(("z", [O_ZATT + h * 64 + d for h in hs for d in range(64)]))
        tiles.append(("tmv", [O_VSL + g * 64 + d for d in range(64)] + [O_VWN + g * 64 + d for d in range(64)]))
    for g in range(4):
        for off in (O_ZSSM, O_XSSM):
            for j in range(2):
                h0 = 4 * g + 2 * j
                tiles.append(("s", [off + h0 * 64 + i for i in range(128)]))
        tiles.append(("b", [O_B + g * 128 + i for i in range(128)]))
        tiles.append(("c", [O_C + g * 128 + i for i in range(128)]))
    return tiles


N_WT = 1 + 4 + 4 * 7 + 4 * 6


def host_consts():
    c = {}
    half = 8
    inv_freq = (500000.0 ** (-(np.arange(half, dtype=np.float32) * 2.0 / 16))).astype(np.float32)
    pos = np.arange(S_LEN, dtype=np.float32)
    ang = pos[:, None] * inv_freq[None, :]
    cos = np.cos(ang).astype(np.float32).T
    sin = np.sin(ang).astype(np.float32).T
    C = np.ones((128, S_LEN), np.float32)
    Sg = np.zeros((128, S_LEN), np.float32)
    for hp in range(2):
        for d in range(16):
            p = hp * 64 + d
            C[p] = cos[d % 8]
            Sg[p] = -sin[d % 8] if d < 8 else sin[d % 8]
    c["ropeC"] = C
    c["ropeS"] = Sg
    k = np.arange(128)[:, None]
    q = np.arange(128)[None, :]
    c["causalb"] = np.where(k <= q, 0.0, NEGB).astype(np.float32)
    c["antib"] = np.where(k > q, 0.0, NEGB).astype(np.float32)
    c["trimask"] = (k <= q).astype(np.float32)
    pm = np.zeros((128, 128), np.float32)
    for dst in range(128):
        dl = dst % 64
        if dl < 16:
            pm[(dst // 64) * 64 + (dl + 8 if dl < 8 else dl - 8), dst] = 1.0
    c["permm"] = pm
    c["antimask"] = (k > q).astype(np.float32)
    cc = np.arange(128)[:, None]
    qq = np.arange(S_LEN)[None, :]
    c["cmpbias"] = np.where((16 * cc + 31 <= qq) & (cc < 127), 0.0, NEGB).astype(np.float32)
    E = np.zeros((128, 16, 128), np.float32)
    for kt in range(16):
        for kk in range(128):
            j = 2 * kt + kk // 64
            E[j, kt, kk] = 1.0
            E[64 + j, kt, kk] = 1.0
    c["Eall"] = E.reshape(128, 16 * 128)
    c["ident"] = np.eye(128, dtype=np.float32)
    tok = np.arange(S_LEN)
    cur = tok // 64
    j = np.arange(32)[None, :]
    A = np.full((S_LEN, 32), -3.0e38, np.float32)
    A[np.arange(S_LEN), np.maximum(cur - 1, 0)] = 1.0e30
    A[np.arange(S_LEN), cur] = 2.0e30
    A[:, 0] = 3.0e30
    Bm = np.where(j > cur[:, None], -1.0e30, 3.0e38).astype(np.float32)
    c["topkA"] = A.reshape(16, 128, 32).transpose(1, 0, 2).reshape(128, 512).copy()
    c["topkB"] = Bm.reshape(16, 128, 32).transpose(1, 0, 2).reshape(128, 512).copy()
    ci = np.arange(128)[:, None] * 16
    bj = np.arange(32)[None, :] * 64
    ov = ((ci <= bj + 63) & (ci + 31 >= bj)).astype(np.float32)
    ov[127] = 0.0
    vx = np.zeros((128, 33), np.float32)
    vx[:, 0] = 1.0
    vx[:, 1:] = ov
    c["vcext"] = vx
    tri = np.zeros((128, 256), np.float32)
    tri[:, 0:128] = (k <= q)
    tri[:, 128:256] = 1.0
    c["tri2"] = tri
    return c


CONST_SHAPES = {"ropeC": [128, 2048], "ropeS": [128, 2048], "causalb": [128, 128], "antib": [128, 128],
                "trimask": [128, 128], "permm": [128, 128], "antimask": [128, 128], "cmpbias": [128, 2048], "Eall": [128, 2048], "ident": [128, 128],
                "topkA": [128, 512], "topkB": [128, 512], "vcext": [128, 33], "tri2": [128, 256]}

SMALL_SHAPES = {"prew": [1, 1024], "postw": [1, 1024], "posT": [128, 32], "b1T": [128, 4], "w2k": [128, 256],
                "w2v": [128, 128], "b2k": [128, 1], "b2v": [1, 64], "gateb": [1, 48], "convw": [128, 64],
                "convb": [128, 16], "dtb": [1, 16], "alog": [1, 16], "dskip": [128, 8], "snw": [128, 8]}


def host_small(inp):
    s = {}
    s["prew"] = inp["pre_norm_w"][0][None, :]
    s["postw"] = inp["post_norm_w"][0][None, :]
    pos = inp["cmp_pos"][0]
    s["posT"] = np.concatenate([pos[0].T, pos[1].T], 0)
    b1 = inp["cmp_b1"][0]
    s["b1T"] = np.stack([b1[kv, hc * 128:(hc + 1) * 128] for kv in range(2) for hc in range(2)], 1)
    w2 = inp["cmp_w2"][0]
    w2k = w2[0].reshape(2, 128, 64).transpose(1, 0, 2)
    s["w2k"] = np.concatenate([w2k, w2k], 2).reshape(128, 256)
    s["w2v"] = w2[1].reshape(2, 128, 64).transpose(1, 0, 2).reshape(128, 128)
    b2 = inp["cmp_b2"][0]
    s["b2k"] = np.concatenate([b2[0], b2[0]])[:, None]
    s["b2v"] = b2[1][None, :]
    s["gateb"] = inp["gate_b"][0][None, :]
    cw = inp["conv_w"][0]
    s["convw"] = cw.T.reshape(16, 128, 4).transpose(1, 0, 2).reshape(128, 64)
    s["convb"] = inp["conv_b"][0].reshape(16, 128).T
    s["dtb"] = inp["dt_bias"][0][None, :]
    s["alog"] = inp["a_log"][0][None, :]
    s["dskip"] = np.repeat(inp["d_skip"][0], 64).reshape(8, 128).T
    s["snw"] = inp["ssm_norm_w"][0].reshape(8, 128).T
    return {k: np.ascontiguousarray(v, dtype=np.float32) for k, v in s.items()}


def host_wtiles(w_in):
    tiles = _tile_cols()
    assert len(tiles) == N_WT
    out = np.zeros((N_WT, 128, 8, 128), np.float32)
    for t, (_, cols) in enumerate(tiles):
        cols = np.asarray(cols)
        wc = np.zeros((1024, 128), np.float32)
        m = cols >= 0
        wc[:, m] = w_in[:, cols[m]]
        out[t] = wc.reshape(8, 128, 128).transpose(1, 0, 2)
    return out.reshape(N_WT, 128, 1024)


class _Stop(Exception):
    pass


def build(debug=(), stop=None):
    nc = bass.Bass("TRN2", target_bir_lowering=False)
    S = Sched(nc, n_dma_sems=16)
    dbg_outs = {}
    try:
        _build_body(nc, S, dbg_outs, debug, stop)
    except _Stop:
        pass
    S.finish()
    return nc, dbg_outs


def _build_body(nc, S, dbg_outs, debug, stop):
    AR = Arena(nc)

    def chk(name):
        if stop == name:
            raise _Stop()

    def din(name, shape):
        return nc.dram_tensor(name, list(shape), F32, kind="ExternalInput").ap()

    x_d = din("x", [S_LEN, D])
    wt_d = din("wt", [N_WT, 128, 1024])
    w1_d = din("w1", [2, 2048, 256])
    wout_d = din("wout", [2048, 1024])
    cst_d = {k: din(k, v) for k, v in CONST_SHAPES.items()}
    sm_d = {k: din(k, v) for k, v in SMALL_SHAPES.items()}
    out_d = nc.dram_tensor("out", [S_LEN, D], F32, kind="ExternalOutput").ap()

    def dbg(name, ap, res, shape, dt=BF16):
        if name not in debug:
            return
        d = nc.dram_tensor("dbg_" + name, list(shape), dt, kind="ExternalOutput").ap()
        dbg_outs[name] = shape
        S.dma(lambda e: e.dma_start(out=d, in_=ap), reads=[res])

    pb = [nc.alloc_psum_tensor("pb%d" % i, [128, 512], F32) for i in range(7)]
    rpb = [Res("pb%d" % i, excl=True) for i in range(7)]
    pbt = nc.alloc_psum_tensor("pbt", [128, 1024], BF16)
    rpbt = Res("pbt", excl=True)

    def G(name, shape, dt):
        return AR.alloc(name, shape, dt), Res(name)

    hT, r_hT = G("hT", [128, 8, S_LEN], BF16)
    r_hTc = [Res("hTc%d" % i) for i in range(4)]
    woutb = nc.alloc_sbuf_tensor_at("woutb_alias", [128, 16, 1024], BF16, offset=AR.last_off)
    r_woutb = Res("woutb")
    mixT, r_mix = G("mixT", [128, 16, S_LEN], BF16)
    r_mixt = [Res("mix%d" % i) for i in range(16)]
    ropeC, r_ropeC = G("ropeC", [128, S_LEN], BF16)
    ropeS, r_ropeS = G("ropeS", [128, S_LEN], BF16)
    cmpbias, r_cmpb = G("cmpbias", [128, S_LEN], BF16)
    ident, r_id = G("ident", [128, 128], BF16)
    causalb, r_cb = G("causalb", [128, 128], BF16)
    antib, r_ab = G("antib", [128, 128], BF16)
    trimask, r_tm = G("trimask", [128, 128], BF16)
    permm, r_pm = G("permm", [128, 128], BF16)
    antimask, r_am = G("antimask", [128, 128], BF16)
    topkA, r_tA = G("topkA", [128, 16, 32], BF16)
    topkB, r_tB = G("topkB", [128, 16, 32], BF16)
    onesf, r_ones = G("onesf", [128, 1], F32)
    wbf = [G("wbf%d" % i, [128, 8, 128], BF16) for i in range(4)]
    gates, r_gates = G("gates", [128, 16, 48], F32)
    dtv, r_dtv = G("dtv", [128, 16, 16], F32)
    a_tm, r_atm = G("a_tm", [128, 16, 16], F32)
    a3 = [G("a3_%d" % i, [128, 16, 16], BF16) for i in range(3)]
    tri2b, r_trib = G("tri2b", [128, 256], BF16)
    acs_tm, r_acs = G("acs_tm", [128, 16, 16], F32)
    ssq, r_ssq = G("ssq", [128, 16], F32)
    kcT2, r_kc = G("kcT2", [128, 4, 128], BF16)
    vcaug, r_vc = G("vcaug", [128, 4, 97], BF16)
    sm = {}
    for k, shp in SMALL_SHAPES.items():
        if shp[0] == 128:
            sm[k] = G("sm_" + k, shp, F32)
    bc = {}
    for k in ("b2v", "gateb", "dtb", "alog"):
        bc[k] = G("bc_" + k, [128, SMALL_SHAPES[k][1]], F32)
    aneg, r_aneg = G("aneg", [128, 16], F32)
    bias1, r_bias1 = G("bias1", [128, 4], F32)
    w2kb, r_w2kb = G("w2kb", [128, 2, 128], BF16)
    w2vb, r_w2vb = G("w2vb", [128, 2, 64], BF16)
    posTb, r_posTb = G("posTb", [128, 32], BF16)
    PHASE = AR.mark()
    tri2, r_tri = G("tri2", [128, 256], F32)
    cstg, r_cstg = G("cstg", [128, 64], F32)

    def load_const(name, dst, rdst, ncols, view=None):
        o = dst[:] if view is None else view
        S.dma(lambda e: e.dma_start(out=o, in_=cst_d[name]), writes=[rdst], q="pool")

    load_const("ropeC", ropeC, r_ropeC, 2048)
    load_const("ropeS", ropeS, r_ropeS, 2048)
    load_const("cmpbias", cmpbias, r_cmpb, 2048)
    load_const("ident", ident, r_id, 128)
    load_const("causalb", causalb, r_cb, 128)
    load_const("antib", antib, r_ab, 128)
    load_const("trimask", trimask, r_tm, 128)
    load_const("permm", permm, r_pm, 128)
    load_const("antimask", antimask, r_am, 128)
    S.dma(lambda e: e.dma_start(out=topkA[:].rearrange("p a b -> p (a b)"), in_=cst_d["topkA"]), writes=[r_tA], q="pool")
    S.dma(lambda e: e.dma_start(out=topkB[:].rearrange("p a b -> p (a b)"), in_=cst_d["topkB"]), writes=[r_tB], q="pool")
    S.dma(lambda e: e.dma_start(out=tri2[:], in_=cst_d["tri2"]), writes=[r_tri])
    S.dma(lambda e: e.dma_start(out=tri2b[:], in_=cst_d["tri2"]), writes=[r_trib], q="pool")
    S.op("pool", lambda e: e.memset(onesf[:], 1.0), writes=[r_ones])
    for k in sm:
        t, r = sm[k]
        S.dma(lambda e, t=t, k=k: e.dma_start(out=t[:], in_=sm_d[k]), writes=[r])
    for k in bc:
        t, r = bc[k]
        S.dma(lambda e, t=t, k=k: e.dma_start(out=t[:], in_=sm_d[k].partition_broadcast(128)), writes=[r])
    S.dma(lambda e: e.dma_start(out=cstg[:, 0:33], in_=cst_d["vcext"]), writes=[r_cstg])
    for g in range(4):
        S.op("pool", lambda e, g=g: e.tensor_copy(out=vcaug[:, g, 64:97], in_=cstg[:, 0:33]), reads=[r_cstg], writes=[r_vc])
    S.op("act", lambda e: e.activation(out=aneg[:], in_=bc["alog"][0][:], func=AF.Exp), reads=[bc["alog"][1]], writes=[r_aneg])
    S.op("dve", lambda e: e.tensor_scalar(out=aneg[:], in0=aneg[:], scalar1=-1.0, scalar2=None, op0=ALU.mult), reads=[r_aneg], writes=[r_aneg])
    S.op("pool", lambda e: e.tensor_copy(out=w2kb[:].rearrange("p a b -> p (a b)"), in_=sm["w2k"][0][:]), reads=[sm["w2k"][1]], writes=[r_w2kb])
    S.op("pool", lambda e: e.tensor_copy(out=w2vb[:].rearrange("p a b -> p (a b)"), in_=sm["w2v"][0][:]), reads=[sm["w2v"][1]], writes=[r_w2vb])
    S.op("pool", lambda e: e.tensor_copy(out=posTb[:], in_=sm["posT"][0][:]), reads=[sm["posT"][1]], writes=[r_posTb])
    S.op("pool", lambda e: e.memset(ssq[:], 0.0), writes=[r_ssq])

    wstate = {"next": 0, "issued": 0}
    WPF = 2

    def load_wtile():
        t = wstate["next"]
        wstate["next"] += 1
        while wstate["issued"] <= min(t + WPF, N_WT - 1):
            ti = wstate["issued"]
            wstate["issued"] += 1
            wb_, rwb_ = wbf[ti % 4]
            S.dma(lambda e, ti=ti, wb_=wb_: e.dma_start(out=wb_[:].rearrange("p a b -> p (a b)"), in_=wt_d[ti]), writes=[rwb_], q="pool")
        return wbf[t % 4]

    bank_rr = {"i": 0}

    def next_bank(lo=0, hi=2):
        i = lo + bank_rr["i"] % (hi - lo)
        bank_rr["i"] += 1
        return i

    def fm_matmuls(wb, rwb, tc, bi):
        for kc in range(8):
            S.op("pe", lambda e, kc=kc: e.matmul(pb[bi][:, :], lhsT=wb[:, kc, :], rhs=hT[:, kc, tc * 512:(tc + 1) * 512],
                                                  start=(kc == 0), stop=(kc == 7)),
                 reads=[rwb, r_hTc[tc]], writes=[rpb[bi]], rg=(0, 4))

    xt = [G("xt%d" % i, [128, 1024], F32) for i in range(2)]
    pw, r_pw = G("pw", [128, 1024], F32)
    junk, r_junk = G("junk", [128, 1024], F32)
    hb = [G("hb%d" % i, [128, 1024], BF16) for i in range(2)]
    ss0, r_ss0 = G("ss0", [128, 16], F32)
    S.dma(lambda e: e.dma_start(out=pw[:], in_=sm_d["prew"].partition_broadcast(128)), writes=[r_pw])
    hb3 = hb + [G("hb2", [128, 1024], BF16)]
    xt3 = xt + [G("xt2_", [128, 1024], F32)]

    def p0_a(tt):
        xb, rxb = xt3[tt % 3]
        hbb, rhbb = hb3[tt % 3]
        S.dma(lambda e: e.dma_start(out=xb[:], in_=x_d[tt * 128:(tt + 1) * 128, :]), writes=[rxb])
        S.op("act", lambda e: e.activation(out=junk[:], in_=xb[:], func=AF.Square, accum_out=ss0[:, tt:tt + 1]), reads=[rxb], writes=[r_junk, r_ss0])
        S.op("dve", lambda e: e.tensor_scalar(out=ss0[:, tt:tt + 1], in0=ss0[:, tt:tt + 1], scalar1=1.0 / D, scalar2=EPS, op0=ALU.mult, op1=ALU.add), reads=[r_ss0], writes=[r_ss0])
        S.op("act", lambda e: e.activation(out=ss0[:, tt:tt + 1], in_=ss0[:, tt:tt + 1], func=AF.Sqrt), reads=[r_ss0], writes=[r_ss0])
        S.op("dve", lambda e: e.reciprocal(out=ss0[:, tt:tt + 1], in_=ss0[:, tt:tt + 1]), reads=[r_ss0], writes=[r_ss0])
        S.op("dve", lambda e: e.scalar_tensor_tensor(out=hbb[:], in0=xb[:], scalar=ss0[:, tt:tt + 1], in1=pw[:], op0=ALU.mult, op1=ALU.mult),
             reads=[rxb, r_ss0, r_pw], writes=[rhbb])

    def p0_b(tt):
        hbb, rhbb = hb3[tt % 3]
        for kc in range(8):
            S.op("pe", lambda e, kc=kc: e.transpose(out=pbt[:, kc * 128:(kc + 1) * 128], in_=hbb[:, kc * 128:(kc + 1) * 128], identity=ident[:]),
                 reads=[rhbb, r_id], writes=[rpbt], rg=(0, 4))
        S.op("act", lambda e: e.copy(out=hT[:, 0:4, tt * 128:(tt + 1) * 128], in_=pbt[:, 0:512].rearrange("p (k t) -> p k t", k=4)),
             reads=[rpbt], writes=[r_hTc[tt // 4]])
        S.op("dve", lambda e: e.tensor_copy(out=hT[:, 4:8, tt * 128:(tt + 1) * 128], in_=pbt[:, 512:1024].rearrange("p (k t) -> p k t", k=4)),
             reads=[rpbt], writes=[r_hTc[tt // 4]])

    p0_a(0)
    for tt in range(NT):
        if tt + 1 < NT:
            p0_a(tt + 1)
        p0_b(tt)
    dbg("hT", hT[:, 0, :], r_hTc[3], [128, 2048])
    chk("p0")

    graw, r_graw = G("graw", [128, 16, 64], F32)
    wb, rwb = load_wtile()
    for tt in range(NT):
        bi = next_bank()
        for kc in range(8):
            S.op("pe", lambda e, kc=kc: e.matmul(pb[bi][:, 0:128], lhsT=hT[:, kc, tt * 128:(tt + 1) * 128], rhs=wb[:, kc, :],
                                                  start=(kc == 0), stop=(kc == 7)), reads=[rwb, r_hTc[tt // 4]], writes=[rpb[bi]], rg=(0, 4))
        S.op("dve", lambda e: e.tensor_tensor(out=graw[:, tt, 0:48], in0=pb[bi][:, 0:48], in1=bc["gateb"][0][:], op=ALU.add),
             reads=[rpb[bi], bc["gateb"][1]], writes=[r_graw])
        S.op("dve", lambda e: e.tensor_tensor(out=graw[:, tt, 48:64], in0=pb[bi][:, 48:64], in1=bc["dtb"][0][:], op=ALU.add),
             reads=[rpb[bi], bc["dtb"][1]], writes=[r_graw])
    S.op("act", lambda e: e.activation(out=gates[:], in_=graw[:, :, 0:48], func=AF.Sigmoid), reads=[r_graw], writes=[r_gates])
    S.op("act", lambda e: e.activation(out=dtv[:], in_=graw[:, :, 48:64], func=AF.Exp), reads=[r_graw], writes=[r_dtv])
    S.op("act", lambda e: e.activation(out=dtv[:], in_=dtv[:], func=AF.Ln, bias=1.0, scale=1.0), reads=[r_dtv], writes=[r_dtv])
    S.op("dve", lambda e: e.tensor_tensor(out=a_tm[:], in0=dtv[:], in1=aneg[:].unsqueeze(1).to_broadcast([128, 16, 16]), op=ALU.mult),
         reads=[r_dtv, r_aneg], writes=[r_atm])
    ares, r_ares = G("ares", [128, 16, 16], F32)
    S.op("dve", lambda e: e.tensor_copy(out=a3[0][0][:], in_=a_tm[:]), reads=[r_atm], writes=[a3[0][1]])
    S.op("dve", lambda e: e.tensor_tensor(out=ares[:], in0=a_tm[:], in1=a3[0][0][:], op=ALU.subtract), reads=[r_atm, a3[0][1]], writes=[r_ares])
    S.op("dve", lambda e: e.tensor_copy(out=a3[1][0][:], in_=ares[:]), reads=[r_ares], writes=[a3[1][1]])
    S.op("dve", lambda e: e.tensor_tensor(out=ares[:], in0=ares[:], in1=a3[1][0][:], op=ALU.subtract), reads=[r_ares, a3[1][1]], writes=[r_ares])
    S.op("dve", lambda e: e.tensor_copy(out=a3[2][0][:], in_=ares[:]), reads=[r_ares], writes=[a3[2][1]])
    for c in range(8):
        bi = next_bank()
        S.op("pe", lambda e: e.matmul(pb[bi][:, 0:16], lhsT=tri2[:, 0:128], rhs=a_tm[:, 2 * c, :], start=True, stop=True),
             reads=[r_tri, r_atm], writes=[rpb[bi]], rg=(0, 4))
        S.op("pe", lambda e: e.matmul(pb[bi][:, 16:32], lhsT=tri2[:, 128:256], rhs=a_tm[:, 2 * c, :], start=True, stop=False),
             reads=[r_tri, r_atm], writes=[rpb[bi]], rg=(0, 4))
        S.op("pe", lambda e: e.matmul(pb[bi][:, 16:32], lhsT=tri2[:, 0:128], rhs=a_tm[:, 2 * c + 1, :], start=False, stop=True),
             reads=[r_tri, r_atm], writes=[rpb[bi]], rg=(0, 4))
        S.op("dve", lambda e: e.tensor_copy(out=acs_tm[:, 2 * c:2 * c + 2, :], in_=pb[bi][:, 0:32].rearrange("p (a b) -> p a b", a=2)),
             reads=[rpb[bi]], writes=[r_acs])
    dbg("gates", gates[:].rearrange("p a b -> p (a b)"), r_gates, [128, 768], F32)
    dbg("dtv", dtv[:].rearrange("p a b -> p (a b)"), r_dtv, [128, 256], F32)
    dbg("acs", acs_tm[:].rearrange("p a b -> p (a b)"), r_acs, [128, 256], F32)
    chk("pg")

    kvT, r_kvT = G("kvT", [128, 4, S_LEN], BF16)
    w1b, r_w1b = G("w1b", [128, 32, 256], BF16)
    hid = [G("hid%d" % i, [128, 4, 128], BF16) for i in range(4)]
    for pc in range(4):
        for kv in range(2):
            S.dma(lambda e, kv=kv: e.dma_start(out=w1b[kv * 64:(kv + 1) * 64, pc * 8:(pc + 1) * 8, :],
                                               in_=w1_d[kv, pc * 512:(pc + 1) * 512, :].rearrange("(l d) h -> d l h", d=64)), writes=[r_w1b], q="pool")
    for g in range(4):
        wb, rwb = load_wtile()
        for tc in range(4):
            bi = next_bank()
            fm_matmuls(wb, rwb, tc, bi)
            S.op("act", lambda e: e.copy(out=kvT[:, g, tc * 512:(tc + 1) * 512], in_=pb[bi][:, :]), reads=[rpb[bi]], writes=[r_kvT])
    for kv in range(2):
        rows = slice(kv * 64, kv * 64 + 64)
        for hc in range(2):
            bi = 2 + kv
            for l in range(32):
                S.op("pe", lambda e, l=l: e.matmul(pb[bi][:, 0:1], lhsT=w1b[rows, l, hc * 128:(hc + 1) * 128], rhs=posTb[rows, l:l + 1],
                                                    start=(l == 0), stop=(l == 31)), reads=[r_w1b, r_posTb], writes=[rpb[bi]], rg=(2 * kv, 2 * kv + 2))
            col = kv * 2 + hc
            S.op("dve", lambda e: e.tensor_tensor(out=bias1[:, col:col + 1], in0=pb[bi][:, 0:1], in1=sm["b1T"][0][:, col:col + 1], op=ALU.add),
                 reads=[rpb[bi], sm["b1T"][1]], writes=[r_bias1])
    for hc in range(2):
        for l in range(32):
            for kv in range(2):
                rows = slice(kv * 64, kv * 64 + 64)
                bi = 2 + 2 * (hc % 2) + kv
                S.op("pe", lambda e, l=l: e.matmul(pb[bi][:, 0:508].rearrange("p (g c) -> p g c", g=4), lhsT=w1b[rows, l, hc * 128:(hc + 1) * 128],
                                                    rhs=kvT[rows, :, l:l + 2017:16], start=(l == 0), stop=(l == 31)),
                     reads=[r_w1b, r_kvT], writes=[rpb[bi]], rg=(2 * kv, 2 * kv + 2))
        for kv in range(2):
            bi = 2 + 2 * (hc % 2) + kv
            col = kv * 2 + hc
            hd, rhd = hid[col]
            S.op("act", lambda e: e.activation(out=hd[:, :, 0:127], in_=pb[bi][:, 0:508].rearrange("p (g c) -> p g c", g=4), func=AF.Silu, bias=bias1[:, col:col + 1], scale=1.0),
                 reads=[rpb[bi], r_bias1], writes=[rhd])
    for g in range(4):
        bi = 6
        for hc in range(2):
            S.op("pe", lambda e, hc=hc: e.matmul(pb[bi][:, 0:127], lhsT=w2kb[:, hc, :], rhs=hid[hc][0][:, g, 0:127], start=(hc == 0), stop=(hc == 1)),
                 reads=[r_w2kb, hid[hc][1]], writes=[rpb[bi]], rg=(0, 4))
        S.op("act", lambda e: e.activation(out=kcT2[:, g, 0:127], in_=pb[bi][:, 0:127], func=AF.Identity, bias=sm["b2k"][0][:, 0:1], scale=1.0),
             reads=[rpb[bi], sm["b2k"][1]], writes=[r_kc])
        bi = 0 + (g % 2)
        for hc in range(2):
            S.op("pe", lambda e, hc=hc: e.matmul(pb[bi][0:127, 0:64], lhsT=hid[2 + hc][0][:, g, 0:127], rhs=w2vb[:, hc, :], start=(hc == 0), stop=(hc == 1)),
                 reads=[r_w2vb, hid[2 + hc][1]], writes=[rpb[bi]], rg=(0, 4))
        S.op("dve", lambda e: e.tensor_tensor(out=vcaug[0:127, g, 0:64], in0=pb[bi][0:127, 0:64], in1=bc["b2v"][0][0:127, :], op=ALU.add),
             reads=[rpb[bi], bc["b2v"][1]], writes=[r_vc])
    dbg("kcT", kcT2[:, 0, :], r_kc, [128, 128])
    dbg("vc", vcaug[:, 0, :], r_vc, [128, 97])
    chk("cmp")
    S.barrier()
    AR.reset(PHASE)

    def rope_pair(dst_ap_fn, rdst, raw_dst_fn=None, rraw=None, split=None):
        wa, rwa = load_wtile()
        pend = []

        def tail():
            (tc, ba, raw_ap, rrw) = pend.pop(0)
            bp = 2 + (tc % 2)
            t1, rt1 = ropet[0][0]
            t2, rt2 = ropet[0][1]
            S.op("pe", lambda e: e.matmul(pb[bp][:, :], lhsT=permm[:], rhs=raw_ap, start=True, stop=True), reads=[r_pm, rrw], writes=[rpb[bp]], rg=(0, 4))
            S.op("dve", lambda e: e.tensor_tensor(out=t1[:], in0=pb[ba][:, :], in1=ropeC[:, tc * 512:(tc + 1) * 512], op=ALU.mult),
                 reads=[rpb[ba], r_ropeC], writes=[rt1])
            S.op("dve", lambda e: e.tensor_tensor(out=t2[:], in0=pb[bp][:, :], in1=ropeS[:, tc * 512:(tc + 1) * 512], op=ALU.mult),
                 reads=[rpb[bp], r_ropeS], writes=[rt2])
            if split is None:
                S.op("pool", lambda e: e.tensor_tensor(out=dst_ap_fn(tc), in0=t1[:], in1=t2[:], op=ALU.add), reads=[rt1, rt2], writes=[rdst])
            else:
                (d0, rd0), (d1_, rd1_) = split
                S.op("pool", lambda e: e.tensor_tensor(out=d0[0:64, tc * 512:(tc + 1) * 512], in0=t1[0:64, :], in1=t2[0:64, :], op=ALU.add), reads=[rt1, rt2], writes=[rd0])
                S.op("pool", lambda e: e.tensor_tensor(out=d1_[64:128, tc * 512:(tc + 1) * 512], in0=t1[64:128, :], in1=t2[64:128, :], op=ALU.add), reads=[rt1, rt2], writes=[rd1_])

        for tc in range(4):
            ba = next_bank(0, 2)
            fm_matmuls(wa, rwa, tc, ba)
            if raw_dst_fn is not None:
                raw_ap, rrw = raw_dst_fn(tc), rraw
            else:
                kr, rkr = kraws[tc % 2]
                raw_ap, rrw = kr[:], rkr
            S.op("act", lambda e: e.copy(out=raw_ap, in_=pb[ba][:, :]), reads=[rpb[ba]], writes=[rrw])
            pend.append((tc, ba, raw_ap, rrw))
            if len(pend) >= 2:
                tail()
        while pend:
            tail()

    for g in range(4):
        AR.reset(PHASE)
        qraw, r_qraw = G("qraw", [128, 2, S_LEN], BF16)
        qrot, r_qrot = G("qrot", [128, 2, S_LEN], BF16)
        kTs, r_kTs = G("kTs", [128, S_LEN], BF16)
        kTw, r_kTw = G("kTw", [128, S_LEN], BF16)
        kTx = [(kTs, r_kTs), (kTw, r_kTw)]
        zsT, r_zsT = G("zsT", [128, 2, S_LEN], BF16)
        vaug, r_vaug = G("vaug", [128, 16, 2, 65], BF16)
        ropet = [[G("ropet%d%d" % (i, k), [128, 512], F32) for k in range(2)] for i in range(1)]
        kraws = [G("kraw%d" % i, [128, 512], BF16) for i in range(2)]
        PT = [G("PT%d" % i, [128, 512], BF16) for i in range(3)]
        oacc, r_oacc = G("oacc", [128, 4, 256], F32)
        oaccb, r_oaccb = G("oaccb", [128, 4, 256], BF16)
        impacc, r_imp = G("impacc", [128, 4, 32], F32)
        impt, r_impt = G("impt", [128, 4, 32], F32)
        den, r_den = G("den", [128, 4], F32)
        fac, r_fac = G("fac", [128, 4], F32)
        m8, r_m8 = G("m8", [128, 8], F32)
        wk, r_wk = G("wk", [128, 32], F32)
        selq, r_selq = G("selq", [128, 96], BF16)
        KE1, r_KE1 = G("KE1", [128, S_LEN], BF16)
        QS = [[G("QS%d%d" % (j_, p_), [128, 512], BF16) for p_ in range(2)] for j_ in range(2)]
        S.op("pool", lambda e: e.memset(kTs[64:128, :], 0.0), writes=[r_kTs])
        S.op("pool", lambda e: e.memset(KE1[0:64, :], 0.0), writes=[r_KE1])
        S.dma(lambda e: e.dma_start(out=kTs[64:96, :], in_=cst_d["Eall"][0:32, :]), writes=[r_kTs], q="pool")
        S.dma(lambda e: e.dma_start(out=KE1[0:32, :], in_=cst_d["Eall"][0:32, :]), writes=[r_KE1], q="pool")
        for j_ in range(2):
            S.op("pool", lambda e, j_=j_: e.memset(QS[j_][0][0][64:128, :], 0.0), writes=[QS[j_][0][1]])
            S.op("pool", lambda e, j_=j_: e.memset(QS[j_][1][0][0:64, :], 0.0), writes=[QS[j_][1][1]])

        for j in range(2):
            rope_pair(lambda tc, j=j: qrot[:, j, tc * 512:(tc + 1) * 512], r_qrot,
                      lambda tc, j=j: qraw[:, j, tc * 512:(tc + 1) * 512], r_qraw)
        rope_pair(None, None, split=((kTs, r_kTs), (KE1, r_KE1)))
        rope_pair(lambda tc: kTw[:, tc * 512:(tc + 1) * 512], r_kTw)
        for j in range(2):
            wb, rwb = load_wtile()
            for tc in range(4):
                bi = next_bank()
                fm_matmuls(wb, rwb, tc, bi)
                S.op("act", lambda e: e.activation(out=zsT[:, j, tc * 512:(tc + 1) * 512], in_=pb[bi][:, :], func=AF.Silu),
                     reads=[rpb[bi]], writes=[r_zsT])
        wb, rwb = load_wtile()
        S.op("pool", lambda e: e.memset(vaug[:, :, :, 64:65], 1.0), writes=[r_vaug])
        for tt in range(NT):
            bi = next_bank()
            for kc in range(8):
                S.op("pe", lambda e, kc=kc: e.matmul(pb[bi][:, 0:128], lhsT=hT[:, kc, tt * 128:(tt + 1) * 128], rhs=wb[:, kc, :],
                                                      start=(kc == 0), stop=(kc == 7)), reads=[rwb, r_hTc[tt // 4]], writes=[rpb[bi]], rg=(0, 4))
            S.op("act", lambda e: e.copy(out=vaug[:, tt, :, 0:64], in_=pb[bi][:, 0:128].rearrange("p (a b) -> p a b", a=2)),
                 reads=[rpb[bi]], writes=[r_vaug])
        if g == 0:
            dbg("qrot", qrot[:, 0, :], r_qrot, [128, 2048])
            dbg("kTs", kTs[:, :], r_kTs, [128, 2048])
            dbg("vaug", vaug[:].rearrange("p a b c -> p (a b c)"), r_vaug, [128, 2080])
            chk("aproj")

        SB = [0, 1, 2, 3, 6]
        ACCB = [4, 5]
        rr = {"s": 0, "pt": 0, "acc": 0, "cm": 0, "df": 0, "ac": 0}
        oaccs = [(oacc, r_oacc), G("oacc1", [128, 4, 256], F32)]
        dens = [(den, r_den), G("den1", [128, 4], F32)]
        facs = [(fac, r_fac), G("fac1", [128, 4], F32)]
        otmps = [G("otmp%d" % i, [128, 4, 64], F32) for i in range(2)]
        for i_ in range(3, 6):
            PT.append(G("PT%d" % i_, [128, 512], BF16))
        for (kr_, rkr_) in kraws:
            PT.append((kr_, rkr_))

        def gidx(r, br):
            return (4 * g + r) * 3 + br

        def finalize(pv, rbank, r, br, qc, oa, roa, first):
            dn, rdn = dens[rr["df"] % 2]
            fc, rfc = facs[rr["df"] % 2]
            ot, rot = otmps[rr["df"] % 2]
            rr["df"] += 1
            gi = gidx(r, br)
            S.op("dve", lambda e: e.tensor_scalar(out=dn[:], in0=pv[:, :, 64], scalar1=1e-30, scalar2=None, op0=ALU.max), reads=[rbank], writes=[rdn])
            S.op("dve", lambda e: e.reciprocal(out=dn[:], in_=dn[:]), reads=[rdn], writes=[rdn])
            S.op("dve", lambda e: e.tensor_tensor(out=fc[:], in0=dn[:], in1=gates[:, 4 * qc:4 * qc + 4, gi], op=ALU.mult), reads=[rdn, r_gates], writes=[rfc])
            if first:
                S.op("dve", lambda e: e.tensor_tensor(out=oa[:, :, r * 64:(r + 1) * 64], in0=pv[:, :, 0:64], in1=fc[:].unsqueeze(2).to_broadcast([128, 4, 64]), op=ALU.mult),
                     reads=[rbank, rfc], writes=[roa])
            else:
                S.op("dve", lambda e: e.tensor_tensor(out=ot[:], in0=pv[:, :, 0:64], in1=fc[:].unsqueeze(2).to_broadcast([128, 4, 64]), op=ALU.mult),
                     reads=[rbank, rfc], writes=[rot])
                if br == 1:
                    S.op("pool", lambda e: e.tensor_tensor(out=oaccb[:, :, r * 64:(r + 1) * 64], in0=oa[:, :, r * 64:(r + 1) * 64], in1=ot[:], op=ALU.add),
                         reads=[roa, rot], writes=[r_oaccb])
                else:
                    S.op("pool", lambda e: e.tensor_tensor(out=oa[:, :, r * 64:(r + 1) * 64], in0=oa[:, :, r * 64:(r + 1) * 64], in1=ot[:], op=ALU.add),
                         reads=[roa, rot], writes=[roa])
            return dn, rdn

        accs = [G("accs%d" % i, [128, 388], F32) for i in range(2)]
        selqs = [(selq, r_selq)] + [G("selq%d" % i, [128, 96], BF16) for i in range(1, 4)]

        def topk(qc, sT, rsT):
            S.op("dve", lambda e: e.tensor_tensor(out=impacc[:], in0=impacc[:], in1=topkA[:, 4 * qc:4 * qc + 4, :], op=ALU.max), reads=[r_imp, r_tA], writes=[r_imp])
            S.op("dve", lambda e: e.tensor_tensor(out=impacc[:], in0=impacc[:], in1=topkB[:, 4 * qc:4 * qc + 4, :], op=ALU.min), reads=[r_imp, r_tB], writes=[r_imp])
            for qi in range(4):
                sq_, rsq_ = selqs[qi]
                S.op("dve", lambda e: e.max(out=m8[:], in_=impacc[:, qi, :]), reads=[r_imp], writes=[r_m8])
                S.op("dve", lambda e: e.match_replace(out=wk[:], in_to_replace=m8[:], in_values=impacc[:, qi, :], imm_value=-3.0e38), reads=[r_imp, r_m8], writes=[r_wk])
                S.op("dve", lambda e: e.max(out=m8[:], in_=wk[:]), reads=[r_wk], writes=[r_m8])
                S.op("dve", lambda e: e.tensor_scalar(out=sq_[:].rearrange("p (a b) -> p a b", a=3), in0=impacc[:, qi, :].unsqueeze(1).to_broadcast([128, 3, 32]),
                                                      scalar1=m8[:, 7:8], scalar2=NEGB, op0=ALU.is_lt, op1=ALU.mult), reads=[r_imp, r_m8], writes=[rsq_])

        def topk_pe(qc, qi, sT, rsT):
            sq_, rsq_ = selqs[qi]
            S.op("pe", lambda e: e.transpose(out=pbt[0:96, 512 + qi * 128:640 + qi * 128], in_=sq_[:, 0:96], identity=ident[:]), reads=[rsq_, r_id], writes=[rpbt], rg=(0, 4))
            if qi == 3:
                for j_ in range(2):
                    S.op("act", lambda e, j_=j_: e.copy(out=QS[j_][0][0][64:96, :], in_=pbt[64:96, 512:1024]), reads=[rpbt], writes=[QS[j_][0][1]])
                    S.op("act", lambda e, j_=j_: e.copy(out=QS[j_][1][0][0:32, :], in_=pbt[0:32, 512:1024]), reads=[rpbt], writes=[QS[j_][1][1]])
                if g == 0 and qc == 3:
                    dbg("selT", QS[0][1][0][:, :], QS[0][1][1], [128, 512])

        def cmp_step(qc, j, oa, roa, sT, rsT):
            st = {}

            def qk():
                st["pt"] = []
                bss = []
                for par in range(2):
                    bss.append(SB[rr["s"] % 5])
                    rr["s"] += 1
                    st["pt"].append(PT[rr["pt"] % 8])
                    rr["pt"] += 1
                for par in range(2):
                    rows = slice(par * 64, par * 64 + 64)
                    bs = bss[par]
                    S.op("pe", lambda e: e.matmul(pb[bs][0:127, :], lhsT=kcT2[rows, g, 0:127], rhs=qraw[rows, j, qc * 512:(qc + 1) * 512], start=True, stop=False),
                         reads=[r_kc, r_qraw], writes=[rpb[bs]], rg=(2 * par, 2 * par + 2))
                for par in range(2):
                    bs = bss[par]
                    S.op("pe", lambda e: e.matmul(pb[bs][0:127, :], lhsT=ident[0:127, 0:127], rhs=cmpbias[0:127, qc * 512:(qc + 1) * 512], start=False, stop=True),
                         reads=[r_id, r_cmpb], writes=[rpb[bs]], rg=(0, 4))
                for par in range(2):
                    bs = bss[par]
                    pt, rpt = st["pt"][par]
                    S.op("act", lambda e: e.activation(out=pt[0:127, :], in_=pb[bs][0:127, :], func=AF.Exp, scale=0.125), reads=[rpb[bs]], writes=[rpt])

            def pv():
                for par in range(2):
                    r = 2 * j + par
                    pt, rpt = st["pt"][par]
                    bc_ = ACCB[par]
                    for qi in range(4):
                        S.op("pe", lambda e, qi=qi: e.matmul(pb[bc_][:, qi * 97:(qi + 1) * 97], lhsT=pt[0:127, qi * 128:(qi + 1) * 128], rhs=vcaug[0:127, g, :],
                                                              start=True, stop=True), reads=[rpt, r_vc], writes=[rpb[bc_]], rg=(0, 4))
                for par in range(2):
                    r = 2 * j + par
                    bc_ = ACCB[par]
                    ac_, rac_ = accs[rr["ac"] % 2]
                    rr["ac"] += 1
                    S.op("act", lambda e: e.copy(out=ac_[:, 0:388], in_=pb[bc_][:, 0:388]), reads=[rpb[bc_]], writes=[rac_])
                    pvv = ac_[:, 0:388].rearrange("p (a b) -> p a b", a=4)
                    dn, rdn = finalize(pvv, rac_, r, 0, qc, oa, roa, True)
                    if r == 0:
                        S.op("dve", lambda e: e.tensor_tensor(out=impacc[:], in0=pvv[:, :, 65:97], in1=dn[:].unsqueeze(2).to_broadcast([128, 4, 32]), op=ALU.mult),
                             reads=[rac_, rdn], writes=[r_imp])
                    else:
                        S.op("dve", lambda e: e.tensor_tensor(out=impt[:], in0=pvv[:, :, 65:97], in1=dn[:].unsqueeze(2).to_broadcast([128, 4, 32]), op=ALU.mult),
                             reads=[rac_, rdn], writes=[r_impt])
                        S.op("pool", lambda e: e.tensor_tensor(out=impacc[:], in0=impacc[:], in1=impt[:], op=ALU.add), reads=[r_imp, r_impt], writes=[r_imp])
                if j == 1 and g == 0 and qc == 3:
                    dbg("imp", impacc[:].rearrange("p a b -> p (a b)"), r_imp, [128, 128], F32)
            return qk, pv

        def att_step(qc, br, j, kt, kt_lo, kt_hi, oa, roa, sT, rsT):
            st = {}
            kT_, rkT_ = kTx[br - 1]
            qlo = max(0, kt - 4 * qc)
            qhi = 4 if br == 1 else min(4, kt + 5 - 4 * qc)
            c0, c1 = qlo * 128, qhi * 128

            def qk():
                st["pt"] = []
                bss = []
                for par in range(2):
                    bss.append(SB[rr["s"] % 5])
                    rr["s"] += 1
                    st["pt"].append(PT[rr["pt"] % 8])
                    rr["pt"] += 1
                masks = []
                if kt >= 4 * qc:
                    masks.append((kt - 4 * qc, trimask, r_tm))
                if br == 2 and 0 <= kt + 4 - 4 * qc < 4:
                    masks.append((kt + 4 - 4 * qc, antimask, r_am))
                if br == 1:
                    for par in range(2):
                        bs = bss[par]
                        ke_, rke_ = (kTs, r_kTs) if par == 0 else (KE1, r_KE1)
                        qs_, rqs_ = QS[j][par]
                        S.op("pe", lambda e: e.matmul(pb[bs][:, c0:c1], lhsT=ke_[:, kt * 128:(kt + 1) * 128], rhs=qs_[:, c0:c1], start=True, stop=True),
                             reads=[rke_, rqs_], writes=[rpb[bs]], rg=(0, 4))
                else:
                    for par in range(2):
                        bs = bss[par]
                        rows = slice(par * 64, par * 64 + 64)
                        S.op("pe", lambda e: e.matmul(pb[bs][:, c0:c1], lhsT=kT_[rows, kt * 128:(kt + 1) * 128], rhs=qrot[rows, j, qc * 512 + c0:qc * 512 + c1],
                                                      start=True, stop=True),
                             reads=[rkT_, r_qrot], writes=[rpb[bs]], rg=(2 * par, 2 * par + 2))
                for par in range(2):
                    bs = bss[par]
                    pt, rpt = st["pt"][par]
                    S.op("act", lambda e: e.activation(out=pt[:, c0:c1], in_=pb[bs][:, c0:c1], func=AF.Exp, scale=0.125), reads=[rpb[bs]], writes=[rpt])
                for par in range(2):
                    pt, rpt = st["pt"][par]
                    for (qb, mt, rmt) in masks:
                        S.op("pool", lambda e, qb=qb, mt=mt: e.tensor_tensor(out=pt[:, qb * 128:(qb + 1) * 128], in0=pt[:, qb * 128:(qb + 1) * 128], in1=mt[:], op=ALU.mult),
                             reads=[rpt, rmt], writes=[rpt])

            def pv():
                for par in range(2):
                    pt, rpt = st["pt"][par]
                    accb = ACCB[par]
                    for qi in range(qlo, qhi):
                        st_ = (kt == kt_lo and qi == qlo)
                        S.op("pe", lambda e, qi=qi, st_=st_: e.matmul(pb[accb][:, qi * 65:(qi + 1) * 65], lhsT=pt[:, qi * 128:(qi + 1) * 128], rhs=vaug[:, kt, br - 1, :],
                                                                       start=st_, stop=False, skip_group_check=True),
                             reads=[rpt, r_vaug], writes=[rpb[accb]], rg=(0, 4))
                if kt == kt_hi:
                    for par in range(2):
                        accb = ACCB[par]
                        ac_, rac_ = accs[rr["ac"] % 2]
                        rr["ac"] += 1
                        S.op("act", lambda e: e.copy(out=ac_[:, 0:260], in_=pb[accb][:, 0:260]), reads=[rpb[accb]], writes=[rac_])
                        pvv = ac_[:, 0:260].rearrange("p (a b) -> p a b", a=4)
                        finalize(pvv, rac_, 2 * j + par, br, qc, oa, roa, False)
            return qk, pv

        def finish_chunk(qc, oa, roa):
            if g == 0 and qc == 3:
                dbg("oatt", oaccb[:].rearrange("p a b -> p (a b)"), r_oaccb, [128, 1024])
            for j in range(2):
                for qi in range(4):
                    S.op("pe", lambda e: e.transpose(out=pbt[:, qi * 128:(qi + 1) * 128], in_=oaccb[:, qi, j * 128:(j + 1) * 128], identity=ident[:]),
                         reads=[r_oaccb, r_id], writes=[rpbt], rg=(0, 4))
                S.op("dve", lambda e: e.tensor_tensor(out=mixT[:, 2 * g + j, qc * 512:(qc + 1) * 512], in0=pbt[:, 0:512], in1=zsT[:, j, qc * 512:(qc + 1) * 512], op=ALU.mult),
                     reads=[rpbt, r_zsT], writes=[r_mixt[2 * g + j]])

        def qcopies(qc):
            for j_ in range(2):
                S.op("pool", lambda e, j_=j_: e.tensor_copy(out=QS[j_][0][0][0:64, :], in_=qrot[0:64, j_, qc * 512:(qc + 1) * 512]), reads=[r_qrot], writes=[QS[j_][0][1]])
                S.op("pool", lambda e, j_=j_: e.tensor_copy(out=QS[j_][1][0][64:128, :], in_=qrot[64:128, j_, qc * 512:(qc + 1) * 512]), reads=[r_qrot], writes=[QS[j_][1][1]])

        steps = []
        posts = {}
        for qc in range(4):
            oa, roa = oaccs[qc % 2]
            sT, rsT = None, None
            base = len(steps)
            if qc > 0:
                poa, proa = oaccs[(qc - 1) % 2]
                posts.setdefault(base + 12, []).append(lambda qc=qc, poa=poa, proa=proa: finish_chunk(qc - 1, poa, proa))
            for j in range(2):
                steps.append(cmp_step(qc, j, oa, roa, sT, rsT))
            posts.setdefault(base, []).append(lambda qc=qc: qcopies(qc))
            posts.setdefault(base + 2, []).append(lambda qc=qc, sT=sT, rsT=rsT: topk(qc, sT, rsT))
            for qi in range(4):
                posts.setdefault(base + 4 + qi, []).append(lambda qc=qc, qi=qi, sT=sT, rsT=rsT: topk_pe(qc, qi, sT, rsT))
            for br in (2, 1):
                for j in range(2):
                    kt_lo = 0 if br == 1 else max(0, 4 * qc - 4)
                    kt_hi = 4 * qc + 3
                    for kt in range(kt_lo, kt_hi + 1):
                        steps.append(att_step(qc, br, j, kt, kt_lo, kt_hi, oa, roa, sT, rsT))
        ADEPTH = 3
        nq = 0
        for si_, (qk, pv) in enumerate(steps):
            while nq <= min(si_ + ADEPTH, len(steps) - 1):
                steps[nq][0]()
                nq += 1
            pv()
            for f_ in posts.get(si_, []):
                f_()
        finish_chunk(3, *oaccs[3 % 2])
        S.barrier()
    dbg("mixA", mixT[:, 0, :], r_mixt[0], [128, 2048])
    chk("att")

    for g in range(4):
        AR.reset(PHASE)
        zsS, r_zsS = G("zsS", [128, 2, S_LEN], BF16)
        xcT, r_xcT = G("xcT", [128, 2, S_LEN], BF16)
        BT, r_BT = G("BT", [128, S_LEN], BF16)
        CT, r_CT = G("CT", [128, S_LEN], BF16)
        xdt, r_xdt = G("xdt", [128, 16, 4, 64], BF16)
        Btm, r_Btm = G("Btm", [128, 16, 128], BF16)
        SSD_TMP = AR.mark()
        ub = [G("ub%d" % i, [128, 515], BF16) for i in range(3)]
        dg, r_dg = G("dg", [128, 4, 4, 128], BF16)
        for j in range(2):
            wb, rwb = load_wtile()
            for tc in range(4):
                bi = next_bank()
                fm_matmuls(wb, rwb, tc, bi)
                S.op("act", lambda e: e.activation(out=zsS[:, j, tc * 512:(tc + 1) * 512], in_=pb[bi][:, :], func=AF.Silu), reads=[rpb[bi]], writes=[r_zsS])
        conv_targets = [(lambda tc: xcT[:, 0, tc * 512:(tc + 1) * 512], r_xcT, 2 * g),
                        (lambda tc: xcT[:, 1, tc * 512:(tc + 1) * 512], r_xcT, 2 * g + 1),
                        (lambda tc: BT[:, tc * 512:(tc + 1) * 512], r_BT, 8 + g),
                        (lambda tc: CT[:, tc * 512:(tc + 1) * 512], r_CT, 12 + g)]
        cw, r_cw = sm["convw"]
        cbv, r_cbv = sm["convb"]
        for ti, (dst_fn, rdst, ct) in enumerate(conv_targets):
            for k in range(4):
                S.op("dve", lambda e, ti=ti, k=k, ct=ct: e.tensor_scalar(out=dg[:, ti, k, :], in0=ident[:], scalar1=cw[:, ct * 4 + k:ct * 4 + k + 1], scalar2=None, op0=ALU.mult),
                     reads=[r_id, r_cw], writes=[r_dg])
        pend_conv = []

        def conv_tail():
            (ti, dst_fn, rdst, ct, tc, u, ru) = pend_conv.pop(0)
            pc = 2 + (tc % 2)
            for k in range(4):
                S.op("pe", lambda e, k=k: e.matmul(pb[pc][:, :], lhsT=dg[:, ti, k, :], rhs=u[:, k:k + 512], start=(k == 0), stop=(k == 3)),
                     reads=[r_dg, ru], writes=[rpb[pc]], rg=(0, 4))
            S.op("act", lambda e: e.activation(out=dst_fn(tc), in_=pb[pc][:, :], func=AF.Silu, bias=cbv[:, ct:ct + 1], scale=1.0), reads=[rpb[pc], r_cbv], writes=[rdst])

        for ti, (dst_fn, rdst, ct) in enumerate(conv_targets):
            wb, rwb = load_wtile()
            for tc in range(4):
                bi = next_bank()
                fm_matmuls(wb, rwb, tc, bi)
                u, ru = ub[(ti * 4 + tc) % 3]
                if tc == 0:
                    S.op("pool", lambda e: e.memset(u[:, 0:3], 0.0), writes=[ru])
                S.op("act", lambda e: e.copy(out=u[:, 3:515], in_=pb[bi][:, :]), reads=[rpb[bi]], writes=[ru])
                if tc < 3:
                    un, run = ub[(ti * 4 + tc + 1) % 3]
                    S.op("pool", lambda e: e.tensor_copy(out=un[:, 0:3], in_=u[:, 512:515]), reads=[ru], writes=[run])
                pend_conv.append((ti, dst_fn, rdst, ct, tc, u, ru))
                if len(pend_conv) >= 2:
                    conv_tail()
        while pend_conv:
            conv_tail()
        for tt in range(NT):
            for j in range(2):
                S.op("pe", lambda e, j=j: e.transpose(out=pbt[:, j * 128:(j + 1) * 128], in_=xcT[:, j, tt * 128:(tt + 1) * 128], identity=ident[:]),
                     reads=[r_xcT, r_id], writes=[rpbt], rg=(0, 4))
            S.op("pe", lambda e: e.transpose(out=pbt[:, 256:384], in_=BT[:, tt * 128:(tt + 1) * 128], identity=ident[:]), reads=[r_BT, r_id], writes=[rpbt], rg=(0, 4))
            S.op("dve", lambda e: e.tensor_tensor(out=xdt[:, tt, :, :], in0=pbt[:, 0:256].rearrange("p (a b) -> p a b", a=4),
                                                  in1=dtv[:, tt, 4 * g:4 * g + 4].unsqueeze(2).to_broadcast([128, 4, 64]), op=ALU.mult),
                 reads=[rpbt, r_dtv], writes=[r_xdt])
            S.op("act", lambda e: e.copy(out=Btm[:, tt, :], in_=pbt[:, 256:384]), reads=[rpbt], writes=[r_Btm])
        if g == 0:
            dbg("xcT", xcT[:, 0, :], r_xcT, [128, 2048])
            dbg("BT", BT[:, :], r_BT, [128, 2048])
        S.barrier()
        AR.reset(SSD_TMP)
        NB3 = 4
        CBm = [G("CBm%d" % i, [128, 2, 256], BF16) for i in range(2)]
        EA = [G("EA%d" % i, [128, 256], F32) for i in range(NB3)]
        Cdec = [G("Cdec%d" % i, [128, 256], BF16) for i in range(NB3)]
        D1 = [G("D1%d" % i, [128, 384], F32) for i in range(NB3)]
        MT = [G("MT%d" % i, [128, 384], BF16) for i in range(NB3)]
        xws = [G("xw%d" % i, [128, 2, 4, 64], BF16) for i in range(2)]
        cds = [G("cdall%d" % i, [128, 4], F32) for i in range(2)]
        dtes = [G("dte%d" % i, [128, 2, 4], F32) for i in range(2)]
        htmp, r_htmp = G("htmp", [128, 4, 64], F32)
        h32, r_h32 = G("h32", [128, 4, 64], F32)
        hbf, r_hbf = G("hbf", [128, 4, 64], BF16)
        ytmp = [G("ytmp%d" % i, [128, 256], F32) for i in range(4)]
        yg = [G("yg%d" % i, [128, 256], F32) for i in range(2)]
        S.op("pool", lambda e: e.memset(h32[:], 0.0), writes=[r_h32])
        S.op("pool", lambda e: e.memset(hbf[:], 0.0), writes=[r_hbf])
        dsk, r_dsk = sm["dskip"]
        snw, r_snw = sm["snw"]
        BY = [0, 1, 2]
        sst = {"by": 0, "k": 0}
        info = {}

        def Cc(c):
            l0 = c * 256
            bx = 6
            cb_, rcb_ = CBm[c % 2]
            S.op("pe", lambda e: e.matmul(pb[bx][:, 0:256], lhsT=BT[:, l0:l0 + 128], rhs=CT[:, l0:l0 + 256], start=True, stop=True),
                 reads=[r_BT, r_CT], writes=[rpb[bx]], rg=(0, 4))
            S.op("pe", lambda e: e.matmul(pb[bx][:, 256:384], lhsT=BT[:, l0 + 128:l0 + 256], rhs=CT[:, l0 + 128:l0 + 256], start=True, stop=True),
                 reads=[r_BT, r_CT], writes=[rpb[bx]], rg=(0, 4))
            S.op("dve", lambda e: e.tensor_tensor(out=cb_[:, 0, 0:128], in0=pb[bx][:, 0:128], in1=trimask[:], op=ALU.mult), reads=[rpb[bx], r_tm], writes=[rcb_])
            S.op("act", lambda e: e.copy(out=cb_[:, 0, 128:256], in_=pb[bx][:, 128:256]), reads=[rpb[bx]], writes=[rcb_])
            S.op("dve", lambda e: e.tensor_tensor(out=cb_[:, 1, 0:128], in0=pb[bx][:, 256:384], in1=trimask[:], op=ALU.mult), reads=[rpb[bx], r_tm], writes=[rcb_])

        def A(c, r):
            l0 = c * 256
            h = 4 * g + r
            k = sst["k"] % NB3
            sst["k"] += 1
            info[(c, r)] = k
            by = BY[sst["by"] % 3]
            sst["by"] += 1
            cb_, rcb_ = CBm[c % 2]
            xw, r_xw = xws[c % 2]
            cdall, r_cd = cds[c % 2]
            dte, r_dte = dtes[c % 2]
            for pi in range(3):
                ap_, rap_ = a3[pi]
                S.op("pe", lambda e, pi=pi, ap_=ap_: e.matmul(pb[by][:, 0:256], lhsT=ap_[:, 2 * c, h:h + 1].to_broadcast([128, 128]), rhs=tri2b[:, :], start=(pi == 0), stop=False),
                     reads=[rap_, r_trib], writes=[rpb[by]], rg=(0, 4))
            for pi in range(3):
                ap_, rap_ = a3[pi]
                S.op("pe", lambda e, pi=pi, ap_=ap_: e.matmul(pb[by][:, 128:256], lhsT=ap_[:, 2 * c + 1, h:h + 1].to_broadcast([128, 128]), rhs=tri2b[:, 0:128], start=False, stop=(pi == 2)),
                     reads=[rap_, r_trib], writes=[rpb[by]], rg=(0, 4))
            ea, rea = EA[k]
            cd_, rcd_ = Cdec[k]
            d1, rd1 = D1[k]
            mt, rmt = MT[k]
            S.op("dve", lambda e: e.tensor_scalar(out=d1[:, 0:256], in0=pb[by][:, 0:256], scalar1=acs_tm[:, 2 * c, h:h + 1], scalar2=0.0, op0=ALU.subtract, op1=ALU.min),
                 reads=[rpb[by], r_acs], writes=[rd1])
            S.op("dve", lambda e: e.tensor_scalar(out=d1[:, 256:384], in0=pb[by][:, 128:256], scalar1=acs_tm[:, 2 * c + 1, h:h + 1], scalar2=0.0, op0=ALU.subtract, op1=ALU.min),
                 reads=[rpb[by], r_acs], writes=[rd1])
            S.op("act", lambda e: e.activation(out=ea[:], in_=pb[by][:, 0:256], func=AF.Exp), reads=[rpb[by]], writes=[rea])
            S.op("act", lambda e: e.activation(out=d1[:], in_=d1[:], func=AF.Exp), reads=[rd1], writes=[rd1])
            S.op("pool", lambda e: e.tensor_tensor(out=mt[:, 0:256], in0=d1[:, 0:256], in1=cb_[:, 0, :], op=ALU.mult), reads=[rd1, rcb_], writes=[rmt])
            S.op("pool", lambda e: e.tensor_tensor(out=mt[:, 256:384], in0=d1[:, 256:384], in1=cb_[:, 1, 0:128], op=ALU.mult), reads=[rd1, rcb_], writes=[rmt])
            if c > 0:
                S.op("dve", lambda e: e.tensor_tensor(out=cd_[:], in0=CT[:, l0:l0 + 256], in1=ea[:], op=ALU.mult), reads=[r_CT, rea], writes=[rcd_])
            if c < 7:
                S.op("act", lambda e: e.copy(out=cdall[:, r:r + 1], in_=ea[:, 255:256]), reads=[rea], writes=[r_cd])
                S.op("act", lambda e: e.copy(out=dte[:, :, r], in_=d1[:, 255:384:128]), reads=[rd1], writes=[r_dte])

        def B(c, r):
            k = info[(c, r)]
            jp, par = r // 2, r % 2
            cd_, rcd_ = Cdec[k]
            mt, rmt = MT[k]
            bo = 3 + jp
            yo = pb[bo][par * 64:(par + 1) * 64, :]
            S.op("pe", lambda e: e.matmul(yo[:, 0:256], lhsT=xdt[:, 2 * c, r, :], rhs=mt[:, 0:256], start=True, stop=False),
                 reads=[r_xdt, rmt], writes=[rpb[bo]], rg=(0, 4))
            S.op("pe", lambda e: e.matmul(yo[:, 128:256], lhsT=xdt[:, 2 * c + 1, r, :], rhs=mt[:, 256:384], start=False, stop=(c == 0)),
                 reads=[r_xdt, rmt], writes=[rpb[bo]], rg=(0, 4))
            if c > 0:
                S.op("pe", lambda e: e.matmul(yo[:, 0:256], lhsT=hbf[:, r, :], rhs=cd_[:], start=False, stop=True),
                     reads=[r_hbf, rcd_], writes=[rpb[bo]], rg=(0, 4))

        def St_pre(c):
            xw, r_xw = xws[c % 2]
            dte, r_dte = dtes[c % 2]
            for si_ in range(2):
                S.op("dve", lambda e, si_=si_: e.tensor_tensor(out=xw[:, si_, :, :], in0=xdt[:, 2 * c + si_, :, :], in1=dte[:, si_, :].unsqueeze(2).to_broadcast([128, 4, 64]), op=ALU.mult),
                     reads=[r_xdt, r_dte], writes=[r_xw])

        def St(c):
            bst = 5
            xw, r_xw = xws[c % 2]
            cdall, r_cd = cds[c % 2]
            dte, r_dte = dtes[c % 2]
            S.op("pe", lambda e: e.matmul(pb[bst][:, 0:256], lhsT=Btm[:, 2 * c, :], rhs=xw[:, 0, :, :].rearrange("p a b -> p (a b)"), start=True, stop=False),
                 reads=[r_Btm, r_xw], writes=[rpb[bst]], rg=(0, 4))
            S.op("pe", lambda e: e.matmul(pb[bst][:, 0:256], lhsT=Btm[:, 2 * c + 1, :], rhs=xw[:, 1, :, :].rearrange("p a b -> p (a b)"), start=False, stop=True),
                 reads=[r_Btm, r_xw], writes=[rpb[bst]], rg=(0, 4))
            S.op("dve", lambda e: e.tensor_tensor(out=htmp[:], in0=h32[:], in1=cdall[:, 0:4].unsqueeze(2).to_broadcast([128, 4, 64]), op=ALU.mult),
                 reads=[r_h32, r_cd], writes=[r_htmp])
            S.op("dve", lambda e: e.tensor_tensor(out=h32[:], in0=htmp[:], in1=pb[bst][:, 0:256].rearrange("p (a b) -> p a b", a=4), op=ALU.add),
                 reads=[r_htmp, rpb[bst]], writes=[r_h32])
            S.op("pool", lambda e: e.tensor_copy(out=hbf[:], in_=h32[:]), reads=[r_h32], writes=[r_hbf])

        def Yev(c, jp):
            l0 = c * 256
            bo = 3 + jp
            ft = 2 * g + jp
            yk = sst["y"] % 4
            sst["y"] += 1
            yt, ryt = ytmp[yk]
            ygg, rygg = yg[jp]
            S.op("dve", lambda e: e.scalar_tensor_tensor(out=yt[:], in0=xcT[:, jp, l0:l0 + 256], scalar=dsk[:, ft:ft + 1], in1=pb[bo][:, 0:256], op0=ALU.mult, op1=ALU.add),
                 reads=[r_xcT, r_dsk, rpb[bo]], writes=[ryt])
            S.op("dve", lambda e: e.tensor_tensor(out=ygg[:], in0=yt[:], in1=zsS[:, jp, l0:l0 + 256], op=ALU.mult), reads=[ryt, r_zsS], writes=[rygg])
            S.op("act", lambda e: e.mul(out=mixT[:, 8 + ft, l0:l0 + 256], in_=ygg[:], mul=snw[:, ft:ft + 1]), reads=[rygg, r_snw], writes=[r_mixt[8 + ft]])
            S.op("act", lambda e: e.activation(out=yt[:], in_=ygg[:], func=AF.Square), reads=[rygg], writes=[ryt])
            pendpe.append((c, yk))
            if g == 0 and jp == 0 and c == 1:
                dbg("yg", ygg[:], rygg, [128, 256], F32)

        def YevPE():
            c, yk = pendpe.pop(0)
            yt, ryt = ytmp[yk]
            bq = 5
            kq = sst["q"] % 4
            sst["q"] += 1
            for hh in range(2):
                S.op("pe", lambda e, hh=hh: e.matmul(pb[bq][:, 400 + 2 * kq + hh:401 + 2 * kq + hh], lhsT=yt[:, hh * 128:(hh + 1) * 128], rhs=onesf[:, 0:1], start=True, stop=True),
                     reads=[ryt, r_ones], writes=[rpb[bq]], rg=(0, 4))
            pend.append((c, kq))

        def Yev2():
            c, kq = pend.pop(0)
            bq = 5
            S.op("dve", lambda e: e.tensor_tensor(out=ssq[:, 2 * c:2 * c + 2], in0=ssq[:, 2 * c:2 * c + 2], in1=pb[bq][:, 400 + 2 * kq:402 + 2 * kq], op=ALU.add), reads=[r_ssq, rpb[bq]], writes=[r_ssq])

        pend = []
        pendpe = []
        sst["q"] = 0
        sst["y"] = 0
        DEPTH = 3
        sl = [(c, r) for c in range(8) for r in range(4)]
        issued = 0

        def issue_A(upto):
            nonlocal_issued = issued_box[0]
            while nonlocal_issued <= upto and nonlocal_issued < len(sl):
                c_, r_ = sl[nonlocal_issued]
                if r_ == 0:
                    Cc(c_)
                A(c_, r_)
                nonlocal_issued += 1
            issued_box[0] = nonlocal_issued

        issued_box = [0]
        for si, (c, r) in enumerate(sl):
            issue_A(si + DEPTH)
            if r == 3 and c < 7:
                St_pre(c)
            B(c, r)
            if r % 2 == 1:
                if len(pendpe) >= 2:
                    YevPE()
                if len(pend) >= 3:
                    Yev2()
                Yev(c, r // 2)
            if r == 3 and c < 7:
                St(c)
            if g == 3 and r == 1:
                for kc in (2 * c, 2 * c + 1):
                    S.dma(lambda e, kc=kc: e.dma_start(out=woutb[:, kc, :], in_=wout_d[kc * 128:(kc + 1) * 128, :]), writes=[r_woutb] + r_hTc, q="pool")
        while pendpe:
            YevPE()
        while pend:
            Yev2()
        S.barrier()
    dbg("ssq", ssq[:], r_ssq, [128, 16], F32)
    dbg("mixS", mixT[:, 8, :], r_mixt[8], [128, 2048])
    chk("ssd")

    AR.reset(PHASE)
    pw2, r_pw2 = G("pw2", [128, 1024], F32)
    xt2 = [G("xt2%d" % i, [128, 1024], F32) for i in range(2)]
    ob = [G("ob%d" % i, [128, 1024], F32) for i in range(2)]
    tmpo, r_tmpo = G("tmpo", [128, 1024], F32)
    junk2, r_junk2 = G("junk2", [128, 1024], F32)
    ss2, r_ss2 = G("ss2", [128, 16], F32)
    S.dma(lambda e: e.dma_start(out=pw2[:], in_=sm_d["postw"].partition_broadcast(128)), writes=[r_pw2])
    S.op("dve", lambda e: e.tensor_scalar(out=ssq[:], in0=ssq[:], scalar1=1.0 / D, scalar2=EPS, op0=ALU.mult, op1=ALU.add), reads=[r_ssq], writes=[r_ssq])
    S.op("act", lambda e: e.activation(out=ssq[:], in_=ssq[:], func=AF.Sqrt), reads=[r_ssq], writes=[r_ssq])
    S.op("dve", lambda e: e.reciprocal(out=ssq[:], in_=ssq[:]), reads=[r_ssq], writes=[r_ssq])
    for tt in range(NT):
        xb, rxb = xt2[tt % 2]
        o_, ro_ = ob[tt % 2]
        S.dma(lambda e: e.dma_start(out=xb[:], in_=x_d[tt * 128:(tt + 1) * 128, :]), writes=[rxb])
        obk = [(4 * tt + i_) % 7 for i_ in range(4)]
        for half in (1, 0):
            for hs in range(2):
                bi = obk[(1 - half) * 2 + hs]
                for kk in range(8):
                    kc = half * 8 + kk
                    S.op("pe", lambda e, kc=kc, kk=kk: e.matmul(pb[bi][:, :], lhsT=mixT[:, kc, tt * 128:(tt + 1) * 128], rhs=woutb[:, kc, hs * 512:(hs + 1) * 512],
                                                                 start=(kk == 0), stop=(kk == 7)), reads=[r_mixt[kc], r_woutb], writes=[rpb[bi]], rg=(0, 4))
        for hs in range(2):
            bs_, ba_ = obk[hs], obk[2 + hs]
            S.op("act", lambda e: e.mul(out=tmpo[:, hs * 512:(hs + 1) * 512], in_=pb[bs_][:, :], mul=ssq[:, tt:tt + 1]),
                 reads=[rpb[bs_], r_ssq], writes=[r_tmpo])
            S.op("dve", lambda e: e.tensor_tensor(out=o_[:, hs * 512:(hs + 1) * 512], in0=tmpo[:, hs * 512:(hs + 1) * 512], in1=pb[ba_][:, :], op=ALU.add),
                 reads=[r_tmpo, rpb[ba_]], writes=[ro_])
        S.op("act", lambda e: e.activation(out=junk2[:], in_=o_[:], func=AF.Square, accum_out=ss2[:, tt:tt + 1]), reads=[ro_], writes=[r_junk2, r_ss2])
        S.op("dve", lambda e: e.tensor_scalar(out=ss2[:, tt:tt + 1], in0=ss2[:, tt:tt + 1], scalar1=1.0 / D, scalar2=EPS, op0=ALU.mult, op1=ALU.add), reads=[r_ss2], writes=[r_ss2])
        S.op("act", lambda e: e.activation(out=ss2[:, tt:tt + 1], in_=ss2[:, tt:tt + 1], func=AF.Sqrt), reads=[r_ss2], writes=[r_ss2])
        S.op("dve", lambda e: e.reciprocal(out=ss2[:, tt:tt + 1], in_=ss2[:, tt:tt + 1]), reads=[r_ss2], writes=[r_ss2])
        S.op("dve", lambda e: e.scalar_tensor_tensor(out=o_[:], in0=o_[:], scalar=ss2[:, tt:tt + 1], in1=pw2[:], op0=ALU.mult, op1=ALU.mult),
             reads=[ro_, r_ss2, r_pw2], writes=[ro_])
        S.op("pool", lambda e: e.tensor_tensor(out=o_[:], in0=o_[:], in1=xb[:], op=ALU.add), reads=[ro_, rxb], writes=[ro_])
        S.dma(lambda e: e.dma_start(out=out_d[tt * 128:(tt + 1) * 128, :], in_=o_[:]), reads=[ro_])


_CACHE = {}


def prepare_inputs(inp):
    inp = {k: np.asarray(v) for k, v in inp.items()}
    shared = {}
    shared["wt"] = host_wtiles(np.ascontiguousarray(inp["w_in"][0], dtype=np.float32))
    shared["w1"] = np.ascontiguousarray(inp["cmp_w1"][0], dtype=np.float32)
    shared["wout"] = np.ascontiguousarray(inp["w_out"][0], dtype=np.float32)
    shared.update(host_consts())
    shared.update(host_small(inp))
    return inp, shared


def kernel(**inputs):
    inp, shared = prepare_inputs(inputs)
    if "nc" not in _CACHE:
        _CACHE["nc"] = build()[0]
    nc = _CACHE["nc"]
    x = np.ascontiguousarray(inp["x"], dtype=np.float32)
    in_maps = []
    for b in range(8):
        m = dict(shared)
        m["x"] = x[b]
        in_maps.append(m)
    res = run_bass_kernel_spmd(nc, in_maps, core_ids=list(range(8)))
    return np.stack([res.results[b]["out"] for b in range(8)], 0).astype(np.float32)
```

```python
import numpy as np
import concourse.bass as bass
import concourse.mybir as mybir
from concourse.bass_utils import run_bass_kernel_spmd

F32 = mybir.dt.float32
BF16 = mybir.dt.bfloat16
AF = mybir.ActivationFunctionType
ALU = mybir.AluOpType

S_LEN = 2048
D = 1024
NT = 16
NEGB = -1.0e9
EPS = 1e-6
SAME_ENGINE_SYNC = True

O_Q, O_KCM, O_VCM, O_KSL, O_VSL, O_KWN, O_VWN, O_GLOG, O_ZATT, O_ZSSM, O_XSSM, O_B, O_C, O_DT = (
    0, 1024, 1280, 1536, 1792, 2048, 2304, 2560, 2608, 3632, 4656, 5680, 6192, 6704)


class Res:
    __slots__ = ("name", "last_write", "readers", "excl", "rg")

    def __init__(self, name, excl=False):
        self.name = name
        self.last_write = None
        self.readers = {}
        self.excl = excl
        self.rg = None


class Sched:
    ENGS = ("pe", "act", "dve", "pool", "sp")

    def __init__(self, nc, n_dma_sems=8):
        self.nc = nc
        self.eng = {"pe": nc.tensor, "act": nc.scalar, "dve": nc.vector, "pool": nc.gpsimd, "sp": nc.sync}
        self.sem = {e: nc.alloc_semaphore("s_" + e) for e in ("pe", "act", "dve", "pool")}
        self.cnt = {e: 0 for e in ("pe", "act", "dve", "pool")}
        self.dsem = [nc.alloc_semaphore("s_dma%d" % i) for i in range(n_dma_sems)]
        self.dcnt = [0] * n_dma_sems
        half = n_dma_sems // 2
        self.dpool = {"sp": list(range(0, half)), "pool": list(range(half, n_dma_sems))}
        self.dnext = {"sp": 0, "pool": 0}
        self.seen = {e: {} for e in self.ENGS}
        self.nops = 0

    def _semof(self, f):
        return self.dsem[f] if isinstance(f, int) else self.sem[f]

    def _collect(self, e, reads, writes, rg=None):
        waits = {}

        def need(ev, force=False):
            f, c = ev
            if f == e and (e == "pe" or not SAME_ENGINE_SYNC) and not force:
                return
            if self.seen[e].get(f, 0) >= c:
                return
            if waits.get(f, 0) < c:
                waits[f] = c

        for r in reads:
            if r.excl:
                if r.last_write:
                    need(r.last_write)
                for ev in r.readers.items():
                    need(ev)
            elif r.last_write:
                need(r.last_write)
        for w in writes:
            if w.last_write:
                force = False
                if e == "pe" and w.excl and rg is not None and w.rg is not None:
                    if rg[1] <= w.rg[0] or w.rg[1] <= rg[0]:
                        force = True
                need(w.last_write, force)
            for ev in w.readers.items():
                need(ev)
        return waits

    def _emit_waits(self, e, waits):
        for f, c in waits.items():
            self.seen[e][f] = c
            self.eng[e].wait_ge(self._semof(f), c)

    def op(self, e, fn, reads=(), writes=(), rg=None):
        waits = self._collect(e, reads, writes, rg)
        self._emit_waits(e, waits)
        self.cnt[e] += 1
        c = self.cnt[e]
        fn(self.eng[e]).then_inc(self.sem[e], 1)
        self.nops += 1
        for r in reads:
            if r.excl:
                r.last_write = (e, c)
                r.readers = {}
            elif r.readers.get(e, 0) < c:
                r.readers[e] = c
        for w in writes:
            w.last_write = (e, c)
            w.readers = {}
            if e == "pe" and w.excl:
                w.rg = rg
        return (e, c)

    def dma(self, fn, reads=(), writes=(), q="sp"):
        waits = self._collect(q, reads, writes)
        pool_ = self.dpool[q]
        s = pool_[self.dnext[q] % len(pool_)]
        self.dnext[q] += 1
        if self.dcnt[s] > 0 and self.seen[q].get(s, 0) < self.dcnt[s]:
            if waits.get(s, 0) < self.dcnt[s]:
                waits[s] = self.dcnt[s]
        self._emit_waits(q, waits)
        self.dcnt[s] += 16
        c = self.dcnt[s]
        fn(self.eng[q]).then_inc(self.dsem[s], 16)
        self.nops += 1
        for r in reads:
            if r.readers.get(s, 0) < c:
                r.readers[s] = c
        for w in writes:
            w.last_write = (s, c)
            w.readers = {}
        return (s, c)

    def barrier(self):
        for e in self.ENGS:
            waits = {}
            for f in ("pe", "act", "dve", "pool"):
                if f != e and self.cnt[f] > self.seen[e].get(f, 0):
                    waits[f] = self.cnt[f]
            for s in range(len(self.dsem)):
                if self.dcnt[s] > self.seen[e].get(s, 0):
                    waits[s] = self.dcnt[s]
            self._emit_waits(e, waits)

    def finish(self):
        waits = {}
        for s in range(len(self.dsem)):
            if self.dcnt[s] > self.seen["sp"].get(s, 0):
                waits[s] = self.dcnt[s]
        self._emit_waits("sp", waits)


class Arena:
    def __init__(self, nc):
        self.nc = nc
        self.base = (nc.sbuf_base + 63) // 64 * 64
        self.top = nc.sbuf_top
        self.cur = self.base
        self.n = 0

    def alloc(self, name, shape, dt):
        esz = 2 if dt == BF16 else 4
        per = esz
        for s in shape[1:]:
            per *= s
        off = self.cur
        self.cur = (off + per + 63) // 64 * 64
        assert self.cur <= self.top, "SBUF overflow at %s: %d > %d" % (name, self.cur, self.top)
        self.n += 1
        self.last_off = off
        return self.nc.alloc_sbuf_tensor_at("%s_%d" % (name, self.n), list(shape), dt, offset=off)

    def mark(self):
        return self.cur

    def reset(self, m):
        self.cur = m


def _tile_cols():
    def partner(d):
        return d + 8 if d < 8 else (d - 8 if d < 16 else -1)

    tiles = []
    tmg = [O_GLOG + i for i in range(48)] + [O_DT + i for i in range(16)] + [-1] * 64
    tiles.append(("tmg", tmg))
    for g in range(4):
        tiles.append(("kvcm%d" % g, [O_KCM + g * 64 + d for d in range(64)] + [O_VCM + g * 64 + d for d in range(64)]))
    for g in range(4):
        for j in range(2):
            hs = (4 * g + 2 * j, 4 * g + 2 * j + 1)
            tiles.append(("q", [O_Q + h * 64 + d for h in hs for d in range(64)]))
        for off in (O_KSL, O_KWN):
            tiles.append(("k", [off + g * 64 + d for d in range(64)] * 2))
        for j in range(2):
            hs = (4 * g + 2 * j, 4 * g + 2 * j + 1)
            tiles.append(("z", [O_ZATT + h * 64 + d for h in hs for d in range(64)]))
        tiles.append(("tmv", [O_VSL + g * 64 + d for d in range(64)] + [O_VWN + g * 64 + d for d in range(64)]))
    for g in range(4):
        for off in (O_ZSSM, O_XSSM):
            for j in range(2):
                h0 = 4 * g + 2 * j
                tiles.append(("s", [off + h0 * 64 + i for i in range(128)]))
        tiles.append(("b", [O_B + g * 128 + i for i in range(128)]))
        tiles.append(("c", [O_C + g * 128 + i for i in range(128)]))
    return tiles


N_WT = 1 + 4 + 4 * 7 + 4 * 6


def host_consts():
    c = {}
    half = 8
    inv_freq = (500000.0 ** (-(np.arange(half, dtype=np.float32) * 2.0 / 16))).astype(np.float32)
    pos = np.arange(S_LEN, dtype=np.float32)
    ang = pos[:, None] * inv_freq[None, :]
    cos = np.cos(ang).astype(np.float32).T
    sin = np.sin(ang).astype(np.float32).T
    C = np.ones((128, S_LEN), np.float32)
    Sg = np.zeros((128, S_LEN), np.float32)
    for hp in range(2):
        for d in range(16):
            p = hp * 64 + d
            C[p] = cos[d % 8]
            Sg[p] = -sin[d % 8] if d < 8 else sin[d % 8]
    c["ropeC"] = C
    c["ropeS"] = Sg
    k = np.arange(128)[:, None]
    q = np.arange(128)[None, :]
    c["causalb"] = np.where(k <= q, 0.0, NEGB).astype(np.float32)
    c["antib"] = np.where(k > q, 0.0, NEGB).astype(np.float32)
    c["trimask"] = (k <= q).astype(np.float32)
    pm = np.zeros((128, 128), np.float32)
    for dst in range(128):
        dl = dst % 64
        if dl < 16:
            pm[(dst // 64) * 64 + (dl + 8 if dl < 8 else dl - 8), dst] = 1.0
    c["permm"] = pm
    c["antimask"] = (k > q).astype(np.float32)
    cc = np.arange(128)[:, None]
    qq = np.arange(S_LEN)[None, :]
    c["cmpbias"] = np.where((16 * cc + 31 <= qq) & (cc < 127), 0.0, NEGB).astype(np.float32)
    E = np.zeros((128, 16, 128), np.float32)
    for kt in range(16):
        for kk in range(128):
            j = 2 * kt + kk // 64
            E[j, kt, kk] = 1.0
            E[64 + j, kt, kk] = 1.0
    c["Eall"] = E.reshape(128, 16 * 128)
    c["ident"] = np.eye(128, dtype=np.float32)
    tok = np.arange(S_LEN)
    cur = tok // 64
    j = np.arange(32)[None, :]
    A = np.full((S_LEN, 32), -3.0e38, np.float32)
    A[np.arange(S_LEN), np.maximum(cur - 1, 0)] = 1.0e30
    A[np.arange(S_LEN), cur] = 2.0e30
    A[:, 0] = 3.0e30
    Bm = np.where(j > cur[:, None], -1.0e30, 3.0e38).astype(np.float32)
    c["topkA"] = A.reshape(16, 128, 32).transpose(1, 0, 2).reshape(128, 512).copy()
    c["topkB"] = Bm.reshape(16, 128, 32).transpose(1, 0, 2).reshape(128, 512).copy()
    ci = np.arange(128)[:, None] * 16
    bj = np.arange(32)[None, :] * 64
    ov = ((ci <= bj + 63) & (ci + 31 >= bj)).astype(np.float32)
    ov[127] = 0.0
    vx = np.zeros((128, 33), np.float32)
    vx[:, 0] = 1.0
    vx[:, 1:] = ov
    c["vcext"] = vx
    tri = np.zeros((128, 256), np.float32)
    tri[:, 0:128] = (k <= q)
    tri[:, 128:256] = 1.0
    c["tri2"] = tri
    return c


CONST_SHAPES = {"ropeC": [128, 2048], "ropeS": [128, 2048], "causalb": [128, 128], "antib": [128, 128],
                "trimask": [128, 128], "permm": [128, 128], "antimask": [128, 128], "cmpbias": [128, 2048], "Eall": [128, 2048], "ident": [128, 128],
                "topkA": [128, 512], "topkB": [128, 512], "vcext": [128, 33], "tri2": [128, 256]}

SMALL_SHAPES = {"prew": [1, 1024], "postw": [1, 1024], "posT": [128, 32], "b1T": [128, 4], "w2k": [128, 256],
                "w2v": [128, 128], "b2k": [128, 1], "b2v": [1, 64], "gateb": [1, 48], "convw": [128, 64],
                "convb": [128, 16], "dtb": [1, 16], "alog": [1, 16], "dskip": [128, 8], "snw": [128, 8]}


def host_small(inp):
    s = {}
    s["prew"] = inp["pre_norm_w"][0][None, :]
    s["postw"] = inp["post_norm_w"][0][None, :]
    pos = inp["cmp_pos"][0]
    s["posT"] = np.concatenate([pos[0].T, pos[1].T], 0)
    b1 = inp["cmp_b1"][0]
    s["b1T"] = np.stack([b1[kv, hc * 128:(hc + 1) * 128] for kv in range(2) for hc in range(2)], 1)
    w2 = inp["cmp_w2"][0]
    w2k = w2[0].reshape(2, 128, 64).transpose(1, 0, 2)
    s["w2k"] = np.concatenate([w2k, w2k], 2).reshape(128, 256)
    s["w2v"] = w2[1].reshape(2, 128, 64).transpose(1, 0, 2).reshape(128, 128)
    b2 = inp["cmp_b2"][0]
    s["b2k"] = np.concatenate([b2[0], b2[0]])[:, None]
    s["b2v"] = b2[1][None, :]
    s["gateb"] = inp["gate_b"][0][None, :]
    cw = inp["conv_w"][0]
    s["convw"] = cw.T.reshape(16, 128, 4).transpose(1, 0, 2).reshape(128, 64)
    s["convb"] = inp["conv_b"][0].reshape(16, 128).T
    s["dtb"] = inp["dt_bias"][0][None, :]
    s["alog"] = inp["a_log"][0][None, :]
    s["dskip"] = np.repeat(inp["d_skip"][0], 64).reshape(8, 128).T
    s["snw"] = inp["ssm_norm_w"][0].reshape(8, 128).T
    return {k: np.ascontiguousarray(v, dtype=np.float32) for k, v in s.items()}


def host_wtiles(w_in):
    tiles = _tile_cols()
    assert len(tiles) == N_WT
    out = np.zeros((N_WT, 128, 8, 128), np.float32)
    for t, (_, cols) in enumerate(tiles):
        cols = np.asarray(cols)
        wc = np.zeros((1024, 128), np.float32)
        m = cols >= 0
        wc[:, m] = w_in[:, cols[m]]
        out[t] = wc.reshape(8, 128, 128).transpose(1, 0, 2)
    return out.reshape(N_WT, 128, 1024)


class _Stop(Exception):
    pass


def build(debug=(), stop=None):
    nc = bass.Bass("TRN2", target_bir_lowering=False)
    S = Sched(nc, n_dma_sems=16)
    dbg_outs = {}
    try:
        _build_body(nc, S, dbg_outs, debug, stop)
    except _Stop:
        pass
    S.finish()
    return nc, dbg_outs


def _build_body(nc, S, dbg_outs, debug, stop):
    AR = Arena(nc)

    def chk(name):
        if stop == name:
            raise _Stop()

    def din(name, shape):
        return nc.dram_tensor(name, list(shape), F32, kind="ExternalInput").ap()

    x_d = din("x", [S_LEN, D])
    wt_d = din("wt", [N_WT, 128, 1024])
    w1_d = din("w1", [2, 2048, 256])
    wout_d = din("wout", [2048, 1024])
    cst_d = {k: din(k, v) for k, v in CONST_SHAPES.items()}
    sm_d = {k: din(k, v) for k, v in SMALL_SHAPES.items()}
    out_d = nc.dram_tensor("out", [S_LEN, D], F32, kind="ExternalOutput").ap()

    def dbg(name, ap, res, shape, dt=BF16):
        if name not in debug:
            return
        d = nc.dram_tensor("dbg_" + name, list(shape), dt, kind="ExternalOutput").ap()
        dbg_outs[name] = shape
        S.dma(lambda e: e.dma_start(out=d, in_=ap), reads=[res])

    pb = [nc.alloc_psum_tensor("pb%d" % i, [128, 512], F32) for i in range(7)]
    rpb = [Res("pb%d" % i, excl=True) for i in range(7)]
    pbt = nc.alloc_psum_tensor("pbt", [128, 1024], BF16)
    rpbt = Res("pbt", excl=True)

    def G(name, shape, dt):
        return AR.alloc(name, shape, dt), Res(name)

    hT, r_hT = G("hT", [128, 8, S_LEN], BF16)
    r_hTc = [Res("hTc%d" % i) for i in range(4)]
    woutb = nc.alloc_sbuf_tensor_at("woutb_alias", [128, 16, 1024], BF16, offset=AR.last_off)
    r_woutb = Res("woutb")
    mixT, r_mix = G("mixT", [128, 16, S_LEN], BF16)
    r_mixt = [Res("mix%d" % i) for i in range(16)]
    ropeC, r_ropeC = G("ropeC", [128, S_LEN], BF16)
    ropeS, r_ropeS = G("ropeS", [128, S_LEN], BF16)
    cmpbias, r_cmpb = G("cmpbias", [128, S_LEN], BF16)
    ident, r_id = G("ident", [128, 128], BF16)
    causalb, r_cb = G("causalb", [128, 128], BF16)
    antib, r_ab = G("antib", [128, 128], BF16)
    trimask, r_tm = G("trimask", [128, 128], BF16)
    permm, r_pm = G("permm", [128, 128], BF16)
    antimask, r_am = G("antimask", [128, 128], BF16)
    topkA, r_tA = G("topkA", [128, 16, 32], BF16)
    topkB, r_tB = G("topkB", [128, 16, 32], BF16)
    onesf, r_ones = G("onesf", [128, 1], F32)
    wbf = [G("wbf%d" % i, [128, 8, 128], BF16) for i in range(4)]
    gates, r_gates = G("gates", [128, 16, 48], F32)
    dtv, r_dtv = G("dtv", [128, 16, 16], F32)
    a_tm, r_atm = G("a_tm", [128, 16, 16], F32)
    a3 = [G("a3_%d" % i, [128, 16, 16], BF16) for i in range(3)]
    tri2b, r_trib = G("tri2b", [128, 256], BF16)
    acs_tm, r_acs = G("acs_tm", [128, 16, 16], F32)
    ssq, r_ssq = G("ssq", [128, 16], F32)
    kcT2, r_kc = G("kcT2", [128, 4, 128], BF16)
    vcaug, r_vc = G("vcaug", [128, 4, 97], BF16)
    sm = {}
    for k, shp in SMALL_SHAPES.items():
        if shp[0] == 128:
            sm[k] = G("sm_" + k, shp, F32)
    bc = {}
    for k in ("b2v", "gateb", "dtb", "alog"):
        bc[k] = G("bc_" + k, [128, SMALL_SHAPES[k][1]], F32)
    aneg, r_aneg = G("aneg", [128, 16], F32)
    bias1, r_bias1 = G("bias1", [128, 4], F32)
    w2kb, r_w2kb = G("w2kb", [128, 2, 128], BF16)
    w2vb, r_w2vb = G("w2vb", [128, 2, 64], BF16)
    posTb, r_posTb = G("posTb", [128, 32], BF16)
    PHASE = AR.mark()
    tri2, r_tri = G("tri2", [128, 256], F32)
    cstg, r_cstg = G("cstg", [128, 64], F32)

    def load_const(name, dst, rdst, ncols, view=None):
        o = dst[:] if view is None else view
        S.dma(lambda e: e.dma_start(out=o, in_=cst_d[name]), writes=[rdst], q="pool")

    load_const("ropeC", ropeC, r_ropeC, 2048)
    load_const("ropeS", ropeS, r_ropeS, 2048)
    load_const("cmpbias", cmpbias, r_cmpb, 2048)
    load_const("ident", ident, r_id, 128)
    load_const("causalb", causalb, r_cb, 128)
    load_const("antib", antib, r_ab, 128)
    load_const("trimask", trimask, r_tm, 128)
    load_const("permm", permm, r_pm, 128)
    load_const("antimask", antimask, r_am, 128)
    S.dma(lambda e: e.dma_start(out=topkA[:].rearrange("p a b -> p (a b)"), in_=cst_d["topkA"]), writes=[r_tA], q="pool")
    S.dma(lambda e: e.dma_start(out=topkB[:].rearrange("p a b -> p (a b)"), in_=cst_d["topkB"]), writes=[r_tB], q="pool")
    S.dma(lambda e: e.dma_start(out=tri2[:], in_=cst_d["tri2"]), writes=[r_tri])
    S.dma(lambda e: e.dma_start(out=tri2b[:], in_=cst_d["tri2"]), writes=[r_trib], q="pool")
    S.op("pool", lambda e: e.memset(onesf[:], 1.0), writes=[r_ones])
    for k in sm:
        t, r = sm[k]
        S.dma(lambda e, t=t, k=k: e.dma_start(out=t[:], in_=sm_d[k]), writes=[r])
    for k in bc:
        t, r = bc[k]
        S.dma(lambda e, t=t, k=k: e.dma_start(out=t[:], in_=sm_d[k].partition_broadcast(128)), writes=[r])
    S.dma(lambda e: e.dma_start(out=cstg[:, 0:33], in_=cst_d["vcext"]), writes=[r_cstg])
    for g in range(4):
        S.op("pool", lambda e, g=g: e.tensor_copy(out=vcaug[:, g, 64:97], in_=cstg[:, 0:33]), reads=[r_cstg], writes=[r_vc])
    S.op("act", lambda e: e.activation(out=aneg[:], in_=bc["alog"][0][:], func=AF.Exp), reads=[bc["alog"][1]], writes=[r_aneg])
    S.op("dve", lambda e: e.tensor_scalar(out=aneg[:], in0=aneg[:], scalar1=-1.0, scalar2=None, op0=ALU.mult), reads=[r_aneg], writes=[r_aneg])
    S.op("pool", lambda e: e.tensor_copy(out=w2kb[:].rearrange("p a b -> p (a b)"), in_=sm["w2k"][0][:]), reads=[sm["w2k"][1]], writes=[r_w2kb])
    S.op("pool", lambda e: e.tensor_copy(out=w2vb[:].rearrange("p a b -> p (a b)"), in_=sm["w2v"][0][:]), reads=[sm["w2v"][1]], writes=[r_w2vb])
    S.op("pool", lambda e: e.tensor_copy(out=posTb[:], in_=sm["posT"][0][:]), reads=[sm["posT"][1]], writes=[r_posTb])
    S.op("pool", lambda e: e.memset(ssq[:], 0.0), writes=[r_ssq])

    wstate = {"next": 0, "issued": 0}
    WPF = 2

    def load_wtile():
        t = wstate["next"]
        wstate["next"] += 1
        while wstate["issued"] <= min(t + WPF, N_WT - 1):
            ti = wstate["issued"]
            wstate["issued"] += 1
            wb_, rwb_ = wbf[ti % 4]
            S.dma(lambda e, ti=ti, wb_=wb_: e.dma_start(out=wb_[:].rearrange("p a b -> p (a b)"), in_=wt_d[ti]), writes=[rwb_], q="pool")
        return wbf[t % 4]

    bank_rr = {"i": 0}

    def next_bank(lo=0, hi=2):
        i = lo + bank_rr["i"] % (hi - lo)
        bank_rr["i"] += 1
        return i

    def fm_matmuls(wb, rwb, tc, bi):
        for kc in range(8):
            S.op("pe", lambda e, kc=kc: e.matmul(pb[bi][:, :], lhsT=wb[:, kc, :], rhs=hT[:, kc, tc * 512:(tc + 1) * 512],
                                                  start=(kc == 0), stop=(kc == 7)),
                 reads=[rwb, r_hTc[tc]], writes=[rpb[bi]], rg=(0, 4))

    xt = [G("xt%d" % i, [128, 1024], F32) for i in range(2)]
    pw, r_pw = G("pw", [128, 1024], F32)
    junk, r_junk = G("junk", [128, 1024], F32)
    hb = [G("hb%d" % i, [128, 1024], BF16) for i in range(2)]
    ss0, r_ss0 = G("ss0", [128, 16], F32)
    S.dma(lambda e: e.dma_start(out=pw[:], in_=sm_d["prew"].partition_broadcast(128)), writes=[r_pw])
    hb3 = hb + [G("hb2", [128, 1024], BF16)]
    xt3 = xt + [G("xt2_", [128, 1024], F32)]

    def p0_a(tt):
        xb, rxb = xt3[tt % 3]
        hbb, rhbb = hb3[tt % 3]
        S.dma(lambda e: e.dma_start(out=xb[:], in_=x_d[tt * 128:(tt + 1) * 128, :]), writes=[rxb])
        S.op("act", lambda e: e.activation(out=junk[:], in_=xb[:], func=AF.Square, accum_out=ss0[:, tt:tt + 1]), reads=[rxb], writes=[r_junk, r_ss0])
        S.op("dve", lambda e: e.tensor_scalar(out=ss0[:, tt:tt + 1], in0=ss0[:, tt:tt + 1], scalar1=1.0 / D, scalar2=EPS, op0=ALU.mult, op1=ALU.add), reads=[r_ss0], writes=[r_ss0])
        S.op("act", lambda e: e.activation(out=ss0[:, tt:tt + 1], in_=ss0[:, tt:tt + 1], func=AF.Sqrt), reads=[r_ss0], writes=[r_ss0])
        S.op("dve", lambda e: e.reciprocal(out=ss0[:, tt:tt + 1], in_=ss0[:, tt:tt + 1]), reads=[r_ss0], writes=[r_ss0])
        S.op("dve", lambda e: e.scalar_tensor_tensor(out=hbb[:], in0=xb[:], scalar=ss0[:, tt:tt + 1], in1=pw[:], op0=ALU.mult, op1=ALU.mult),
             reads=[rxb, r_ss0, r_pw], writes=[rhbb])

    def p0_b(tt):
        hbb, rhbb = hb3[tt % 3]
        for kc in range(8):
            S.op("pe", lambda e, kc=kc: e.transpose(out=pbt[:, kc * 128:(kc + 1) * 128], in_=hbb[:, kc * 128:(kc + 1) * 128], identity=ident[:]),
                 reads=[rhbb, r_id], writes=[rpbt], rg=(0, 4))
        S.op("act", lambda e: e.copy(out=hT[:, 0:4, tt * 128:(tt + 1) * 128], in_=pbt[:, 0:512].rearrange("p (k t) -> p k t", k=4)),
             reads=[rpbt], writes=[r_hTc[tt // 4]])
        S.op("dve", lambda e: e.tensor_copy(out=hT[:, 4:8, tt * 128:(tt + 1) * 128], in_=pbt[:, 512:1024].rearrange("p (k t) -> p k t", k=4)),
             reads=[rpbt], writes=[r_hTc[tt // 4]])

    p0_a(0)
    for tt in range(NT):
        if tt + 1 < NT:
            p0_a(tt + 1)
        p0_b(tt)
    dbg("hT", hT[:, 0, :], r_hTc[3], [128, 2048])
    chk("p0")

    graw, r_graw = G("graw", [128, 16, 64], F32)
    wb, rwb = load_wtile()
    for tt in range(NT):
        bi = next_bank()
        for kc in range(8):
            S.op("pe", lambda e, kc=kc: e.matmul(pb[bi][:, 0:128], lhsT=hT[:, kc, tt * 128:(tt + 1) * 128], rhs=wb[:, kc, :],
                                                  start=(kc == 0), stop=(kc == 7)), reads=[rwb, r_hTc[tt // 4]], writes=[rpb[bi]], rg=(0, 4))
        S.op("dve", lambda e: e.tensor_tensor(out=graw[:, tt, 0:48], in0=pb[bi][:, 0:48], in1=bc["gateb"][0][:], op=ALU.add),
             reads=[rpb[bi], bc["gateb"][1]], writes=[r_graw])
        S.op("dve", lambda e: e.tensor_tensor(out=graw[:, tt, 48:64], in0=pb[bi][:, 48:64], in1=bc["dtb"][0][:], op=ALU.add),
             reads=[rpb[bi], bc["dtb"][1]], writes=[r_graw])
    S.op("act", lambda e: e.activation(out=gates[:], in_=graw[:, :, 0:48], func=AF.Sigmoid), reads=[r_graw], writes=[r_gates])
    S.op("act", lambda e: e.activation(out=dtv[:], in_=graw[:, :, 48:64], func=AF.Exp), reads=[r_graw], writes=[r_dtv])
    S.op("act", lambda e: e.activation(out=dtv[:], in_=dtv[:], func=AF.Ln, bias=1.0, scale=1.0), reads=[r_dtv], writes=[r_dtv])
    S.op("dve", lambda e: e.tensor_tensor(out=a_tm[:], in0=dtv[:], in1=aneg[:].unsqueeze(1).to_broadcast([128, 16, 16]), op=ALU.mult),
         reads=[r_dtv, r_aneg], writes=[r_atm])
    ares, r_ares = G("ares", [128, 16, 16], F32)
    S.op("dve", lambda e: e.tensor_copy(out=a3[0][0][:], in_=a_tm[:]), reads=[r_atm], writes=[a3[0][1]])
    S.op("dve", lambda e: e.tensor_tensor(out=ares[:], in0=a_tm[:], in1=a3[0][0][:], op=ALU.subtract), reads=[r_atm, a3[0][1]], writes=[r_ares])
    S.op("dve", lambda e: e.tensor_copy(out=a3[1][0][:], in_=ares[:]), reads=[r_ares], writes=[a3[1][1]])
    S.op("dve", lambda e: e.tensor_tensor(out=ares[:], in0=ares[:], in1=a3[1][0][:], op=ALU.subtract), reads=[r_ares, a3[1][1]], writes=[r_ares])
    S.op("dve", lambda e: e.tensor_copy(out=a3[2][0][:], in_=ares[:]), reads=[r_ares], writes=[a3[2][1]])
    for c in range(8):
        bi = next_bank()
        S.op("pe", lambda e: e.matmul(pb[bi][:, 0:16], lhsT=tri2[:, 0:128], rhs=a_tm[:, 2 * c, :], start=True, stop=True),
             reads=[r_tri, r_atm], writes=[rpb[bi]], rg=(0, 4))
        S.op("pe", lambda e: e.matmul(pb[bi][:, 16:32], lhsT=tri2[:, 128:256], rhs=a_tm[:, 2 * c, :], start=True, stop=False),
             reads=[r_tri, r_atm], writes=[rpb[bi]], rg=(0, 4))
        S.op("pe", lambda e: e.matmul(pb[bi][:, 16:32], lhsT=tri2[:, 0:128], rhs=a_tm[:, 2 * c + 1, :], start=False, stop=True),
             reads=[r_tri, r_atm], writes=[rpb[bi]], rg=(0, 4))
        S.op("dve", lambda e: e.tensor_copy(out=acs_tm[:, 2 * c:2 * c + 2, :], in_=pb[bi][:, 0:32].rearrange("p (a b) -> p a b", a=2)),
             reads=[rpb[bi]], writes=[r_acs])
    dbg("gates", gates[:].rearrange("p a b -> p (a b)"), r_gates, [128, 768], F32)
    dbg("dtv", dtv[:].rearrange("p a b -> p (a b)"), r_dtv, [128, 256], F32)
    dbg("acs", acs_tm[:].rearrange("p a b -> p (a b)"), r_acs, [128, 256], F32)
    chk("pg")

    kvT, r_kvT = G("kvT", [128, 4, S_LEN], BF16)
    w1b, r_w1b = G("w1b", [128, 32, 256], BF16)
    hid = [G("hid%d" % i, [128, 4, 128], BF16) for i in range(4)]
    for pc in range(4):
        for kv in range(2):
            S.dma(lambda e, kv=kv: e.dma_start(out=w1b[kv * 64:(kv + 1) * 64, pc * 8:(pc + 1) * 8, :],
                                               in_=w1_d[kv, pc * 512:(pc + 1) * 512, :].rearrange("(l d) h -> d l h", d=64)), writes=[r_w1b], q="pool")
    for g in range(4):
        wb, rwb = load_wtile()
        for tc in range(4):
            bi = next_bank()
            fm_matmuls(wb, rwb, tc, bi)
            S.op("act", lambda e: e.copy(out=kvT[:, g, tc * 512:(tc + 1) * 512], in_=pb[bi][:, :]), reads=[rpb[bi]], writes=[r_kvT])
    for kv in range(2):
        rows = slice(kv * 64, kv * 64 + 64)
        for hc in range(2):
            bi = 2 + kv
            for l in range(32):
                S.op("pe", lambda e, l=l: e.matmul(pb[bi][:, 0:1], lhsT=w1b[rows, l, hc * 128:(hc + 1) * 128], rhs=posTb[rows, l:l + 1],
                                                    start=(l == 0), stop=(l == 31)), reads=[r_w1b, r_posTb], writes=[rpb[bi]], rg=(2 * kv, 2 * kv + 2))
            col = kv * 2 + hc
            S.op("dve", lambda e: e.tensor_tensor(out=bias1[:, col:col + 1], in0=pb[bi][:, 0:1], in1=sm["b1T"][0][:, col:col + 1], op=ALU.add),
                 reads=[rpb[bi], sm["b1T"][1]], writes=[r_bias1])
    for hc in range(2):
        for l in range(32):
            for kv in range(2):
                rows = slice(kv * 64, kv * 64 + 64)
                bi = 2 + 2 * (hc % 2) + kv
                S.op("pe", lambda e, l=l: e.matmul(pb[bi][:, 0:508].rearrange("p (g c) -> p g c", g=4), lhsT=w1b[rows, l, hc * 128:(hc + 1) * 128],
                                                    rhs=kvT[rows, :, l:l + 2017:16], start=(l == 0), stop=(l == 31)),
                     reads=[r_w1b, r_kvT], writes=[rpb[bi]], rg=(2 * kv, 2 * kv + 2))
        for kv in range(2):
            bi = 2 + 2 * (hc % 2) + kv
            col = kv * 2 + hc
            hd, rhd = hid[col]
            S.op("act", lambda e: e.activation(out=hd[:, :, 0:127], in_=pb[bi][:, 0:508].rearrange("p (g c) -> p g c", g=4), func=AF.Silu, bias=bias1[:, col:col + 1], scale=1.0),
                 reads=[rpb[bi], r_bias1], writes=[rhd])
    for g in range(4):
        bi = 6
        for hc in range(2):
            S.op("pe", lambda e, hc=hc: e.matmul(pb[bi][:, 0:127], lhsT=w2kb[:, hc, :], rhs=hid[hc][0][:, g, 0:127], start=(hc == 0), stop=(hc == 1)),
                 reads=[r_w2kb, hid[hc][1]], writes=[rpb[bi]], rg=(0, 4))
        S.op("act", lambda e: e.activation(out=kcT2[:, g, 0:127], in_=pb[bi][:, 0:127], func=AF.Identity, bias=sm["b2k"][0][:, 0:1], scale=1.0),
             reads=[rpb[bi], sm["b2k"][1]], writes=[r_kc])
        bi = 0 + (g % 2)
        for hc in range(2):
            S.op("pe", lambda e, hc=hc: e.matmul(pb[bi][0:127, 0:64], lhsT=hid[2 + hc][0][:, g, 0:127], rhs=w2vb[:, hc, :], start=(hc == 0), stop=(hc == 1)),
                 reads=[r_w2vb, hid[2 + hc][1]], writes=[rpb[bi]], rg=(0, 4))
        S.op("dve", lambda e: e.tensor_tensor(out=vcaug[0:127, g, 0:64], in0=pb[bi][0:127, 0:64], in1=bc["b2v"][0][0:127, :], op=ALU.add),
             reads=[rpb[bi], bc["b2v"][1]], writes=[r_vc])
    dbg("kcT", kcT2[:, 0, :], r_kc, [128, 128])
    dbg("vc", vcaug[:, 0, :], r_vc, [128, 97])
    chk("cmp")
    S.barrier()
    AR.reset(PHASE)

    def rope_pair(dst_ap_fn, rdst, raw_dst_fn=None, rraw=None, split=None):
        wa, rwa = load_wtile()
        pend = []

        def tail():
            (tc, ba, raw_ap, rrw) = pend.pop(0)
            bp = 2 + (tc % 2)
            t1, rt1 = ropet[0][0]
            t2, rt2 = ropet[0][1]
            S.op("pe", lambda e: e.matmul(pb[bp][:, :], lhsT=permm[:], rhs=raw_ap, start=True, stop=True), reads=[r_pm, rrw], writes=[rpb[bp]], rg=(0, 4))
            S.op("dve", lambda e: e.tensor_tensor(out=t1[:], in0=pb[ba][:, :], in1=ropeC[:, tc * 512:(tc + 1) * 512], op=ALU.mult),
                 reads=[rpb[ba], r_ropeC], writes=[rt1])
            S.op("dve", lambda e: e.tensor_tensor(out=t2[:], in0=pb[bp][:, :], in1=ropeS[:, tc * 512:(tc + 1) * 512], op=ALU.mult),
                 reads=[rpb[bp], r_ropeS], writes=[rt2])
            if split is None:
                S.op("pool", lambda e: e.tensor_tensor(out=dst_ap_fn(tc), in0=t1[:], in1=t2[:], op=ALU.add), reads=[rt1, rt2], writes=[rdst])
            else:
                (d0, rd0), (d1_, rd1_) = split
                S.op("pool", lambda e: e.tensor_tensor(out=d0[0:64, tc * 512:(tc + 1) * 512], in0=t1[0:64, :], in1=t2[0:64, :], op=ALU.add), reads=[rt1, rt2], writes=[rd0])
                S.op("pool", lambda e: e.tensor_tensor(out=d1_[64:128, tc * 512:(tc + 1) * 512], in0=t1[64:128, :], in1=t2[64:128, :], op=ALU.add), reads=[rt1, rt2], writes=[rd1_])

        for tc in range(4):
            ba = next_bank(0, 2)
            fm_matmuls(wa, rwa, tc, ba)
            if raw_dst_fn is not None:
                raw_ap, rrw = raw_dst_fn(tc), rraw
            else:
                kr, rkr = kraws[tc % 2]
                raw_ap, rrw = kr[:], rkr
            S.op("act", lambda e: e.copy(out=raw_ap, in_=pb[ba][:, :]), reads=[rpb[ba]], writes=[rrw])
            pend.append((tc, ba, raw_ap, rrw))
            if len(pend) >= 2:
                tail()
        while pend:
            tail()

    for g in range(4):
        AR.reset(PHASE)
        qraw, r_qraw = G("qraw", [128, 2, S_LEN], BF16)
        qrot, r_qrot = G("qrot", [128, 2, S_LEN], BF16)
        kTs, r_kTs = G("kTs", [128, S_LEN], BF16)
        kTw, r_kTw = G("kTw", [128, S_LEN], BF16)
        kTx = [(kTs, r_kTs), (kTw, r_kTw)]
        zsT, r_zsT = G("zsT", [128, 2, S_LEN], BF16)
        vaug, r_vaug = G("vaug", [128, 16, 2, 65], BF16)
        ropet = [[G("ropet%d%d" % (i, k), [128, 512], F32) for k in range(2)] for i in range(1)]
        kraws = [G("kraw%d" % i, [128, 512], BF16) for i in range(2)]
        PT = [G("PT%d" % i, [128, 512], BF16) for i in range(3)]
        oacc, r_oacc = G("oacc", [128, 4, 256], F32)
        oaccb, r_oaccb = G("oaccb", [128, 4, 256], BF16)
        impacc, r_imp = G("impacc", [128, 4, 32], F32)
        impt, r_impt = G("impt", [128, 4, 32], F32)
        den, r_den = G("den", [128, 4], F32)
        fac, r_fac = G("fac", [128, 4], F32)
        m8, r_m8 = G("m8", [128, 8], F32)
        wk, r_wk = G("wk", [128, 32], F32)
        selq, r_selq = G("selq", [128, 96], BF16)
        KE1, r_KE1 = G("KE1", [128, S_LEN], BF16)
        QS = [[G("QS%d%d" % (j_, p_), [128, 512], BF16) for p_ in range(2)] for j_ in range(2)]
        S.op("pool", lambda e: e.memset(kTs[64:128, :], 0.0), writes=[r_kTs])
        S.op("pool", lambda e: e.memset(KE1[0:64, :], 0.0), writes=[r_KE1])
        S.dma(lambda e: e.dma_start(out=kTs[64:96, :], in_=cst_d["Eall"][0:32, :]), writes=[r_kTs], q="pool")
        S.dma(lambda e: e.dma_start(out=KE1[0:32, :], in_=cst_d["Eall"][0:32, :]), writes=[r_KE1], q="pool")
        for j_ in range(2):
            S.op("pool", lambda e, j_=j_: e.memset(QS[j_][0][0][64:128, :], 0.0), writes=[QS[j_][0][1]])
            S.op("pool", lambda e, j_=j_: e.memset(QS[j_][1][0][0:64, :], 0.0), writes=[QS[j_][1][1]])

        for j in range(2):
            rope_pair(lambda tc, j=j: qrot[:, j, tc * 512:(tc + 1) * 512], r_qrot,
                      lambda tc, j=j: qraw[:, j, tc * 512:(tc + 1) * 512], r_qraw)
        rope_pair(None, None, split=((kTs, r_kTs), (KE1, r_KE1)))
        rope_pair(lambda tc: kTw[:, tc * 512:(tc + 1) * 512], r_kTw)
        for j in range(2):
            wb, rwb = load_wtile()
            for tc in range(4):
                bi = next_bank()
                fm_matmuls(wb, rwb, tc, bi)
                S.op("act", lambda e: e.activation(out=zsT[:, j, tc * 512:(tc + 1) * 512], in_=pb[bi][:, :], func=AF.Silu),
                     reads=[rpb[bi]], writes=[r_zsT])
        wb, rwb = load_wtile()
        S.op("pool", lambda e: e.memset(vaug[:, :, :, 64:65], 1.0), writes=[r_vaug])
        for tt in range(NT):
            bi = next_bank()
            for kc in range(8):
                S.op("pe", lambda e, kc=kc: e.matmul(pb[bi][:, 0:128], lhsT=hT[:, kc, tt * 128:(tt + 1) * 128], rhs=wb[:, kc, :],
                                                      start=(kc == 0), stop=(kc == 7)), reads=[rwb, r_hTc[tt // 4]], writes=[rpb[bi]], rg=(0, 4))
            S.op("act", lambda e: e.copy(out=vaug[:, tt, :, 0:64], in_=pb[bi][:, 0:128].rearrange("p (a b) -> p a b", a=2)),
                 reads=[rpb[bi]], writes=[r_vaug])
        if g == 0:
            dbg("qrot", qrot[:, 0, :], r_qrot, [128, 2048])
            dbg("kTs", kTs[:, :], r_kTs, [128, 2048])
            dbg("vaug", vaug[:].rearrange("p a b c -> p (a b c)"), r_vaug, [128, 2080])
            chk("aproj")

        SB = [0, 1, 2, 3, 6]
        ACCB = [4, 5]
        rr = {"s": 0, "pt": 0, "acc": 0, "cm": 0, "df": 0, "ac": 0}
        oaccs = [(oacc, r_oacc), G("oacc1", [128, 4, 256], F32)]
        dens = [(den, r_den), G("den1", [128, 4], F32)]
        facs = [(fac, r_fac), G("fac1", [128, 4], F32)]
        otmps = [G("otmp%d" % i, [128, 4, 64], F32) for i in range(2)]
        for i_ in range(3, 6):
            PT.append(G("PT%d" % i_, [128, 512], BF16))
        for (kr_, rkr_) in kraws:
            PT.append((kr_, rkr_))

        def gidx(r, br):
            return (4 * g + r) * 3 + br

        def finalize(pv, rbank, r, br, qc, oa, roa, first):
            dn, rdn = dens[rr["df"] % 2]
            fc, rfc = facs[rr["df"] % 2]
            ot, rot = otmps[rr["df"] % 2]
            rr["df"] += 1
            gi = gidx(r, br)
            S.op("dve", lambda e: e.tensor_scalar(out=dn[:], in0=pv[:, :, 64], scalar1=1e-30, scalar2=None, op0=ALU.max), reads=[rbank], writes=[rdn])
            S.op("dve", lambda e: e.reciprocal(out=dn[:], in_=dn[:]), reads=[rdn], writes=[rdn])
            S.op("dve", lambda e: e.tensor_tensor(out=fc[:], in0=dn[:], in1=gates[:, 4 * qc:4 * qc + 4, gi], op=ALU.mult), reads=[rdn, r_gates], writes=[rfc])
            if first:
                S.op("dve", lambda e: e.tensor_tensor(out=oa[:, :, r * 64:(r + 1) * 64], in0=pv[:, :, 0:64], in1=fc[:].unsqueeze(2).to_broadcast([128, 4, 64]), op=ALU.mult),
                     reads=[rbank, rfc], writes=[roa])
            else:
                S.op("dve", lambda e: e.tensor_tensor(out=ot[:], in0=pv[:, :, 0:64], in1=fc[:].unsqueeze(2).to_broadcast([128, 4, 64]), op=ALU.mult),
                     reads=[rbank, rfc], writes=[rot])
                if br == 1:
                    S.op("pool", lambda e: e.tensor_tensor(out=oaccb[:, :, r * 64:(r + 1) * 64], in0=oa[:, :, r * 64:(r + 1) * 64], in1=ot[:], op=ALU.add),
                         reads=[roa, rot], writes=[r_oaccb])
                else:
                    S.op("pool", lambda e: e.tensor_tensor(out=oa[:, :, r * 64:(r + 1) * 64], in0=oa[:, :, r * 64:(r + 1) * 64], in1=ot[:], op=ALU.add),
                         reads=[roa, rot], writes=[roa])
            return dn, rdn

        accs = [G("accs%d" % i, [128, 388], F32) for i in range(2)]
        selqs = [(selq, r_selq)] + [G("selq%d" % i, [128, 96], BF16) for i in range(1, 4)]

        def topk(qc, sT, rsT):
            S.op("dve", lambda e: e.tensor_tensor(out=impacc[:], in0=impacc[:], in1=topkA[:, 4 * qc:4 * qc + 4, :], op=ALU.max), reads=[r_imp, r_tA], writes=[r_imp])
            S.op("dve", lambda e: e.tensor_tensor(out=impacc[:], in0=impacc[:], in1=topkB[:, 4 * qc:4 * qc + 4, :], op=ALU.min), reads=[r_imp, r_tB], writes=[r_imp])
            for qi in range(4):
                sq_, rsq_ = selqs[qi]
                S.op("dve", lambda e: e.max(out=m8[:], in_=impacc[:, qi, :]), reads=[r_imp], writes=[r_m8])
                S.op("dve", lambda e: e.match_replace(out=wk[:], in_to_replace=m8[:], in_values=impacc[:, qi, :], imm_value=-3.0e38), reads=[r_imp, r_m8], writes=[r_wk])
                S.op("dve", lambda e: e.max(out=m8[:], in_=wk[:]), reads=[r_wk], writes=[r_m8])
                S.op("dve", lambda e: e.tensor_scalar(out=sq_[:].rearrange("p (a b) -> p a b", a=3), in0=impacc[:, qi, :].unsqueeze(1).to_broadcast([128, 3, 32]),
                                                      scalar1=m8[:, 7:8], scalar2=NEGB, op0=ALU.is_lt, op1=ALU.mult), reads=[r_imp, r_m8], writes=[rsq_])

        def topk_pe(qc, qi, sT, rsT):
            sq_, rsq_ = selqs[qi]
            S.op("pe", lambda e: e.transpose(out=pbt[0:96, 512 + qi * 128:640 + qi * 128], in_=sq_[:, 0:96], identity=ident[:]), reads=[rsq_, r_id], writes=[rpbt], rg=(0, 4))
            if qi == 3:
                for j_ in range(2):
                    S.op("act", lambda e, j_=j_: e.copy(out=QS[j_][0][0][64:96, :], in_=pbt[64:96, 512:1024]), reads=[rpbt], writes=[QS[j_][0][1]])
                    S.op("act", lambda e, j_=j_: e.copy(out=QS[j_][1][0][0:32, :], in_=pbt[0:32, 512:1024]), reads=[rpbt], writes=[QS[j_][1][1]])
                if g == 0 and qc == 3:
                    dbg("selT", QS[0][1][0][:, :], QS[0][1][1], [128, 512])

        def cmp_step(qc, j, oa, roa, sT, rsT):
            st = {}

            def qk():
                st["pt"] = []
                bss = []
                for par in range(2):
                    bss.append(SB[rr["s"] % 5])
                    rr["s"] += 1
                    st["pt"].append(PT[rr["pt"] % 8])
                    rr["pt"] += 1
                for par in range(2):
                    rows = slice(par * 64, par * 64 + 64)
                    bs = bss[par]
                    S.op("pe", lambda e: e.matmul(pb[bs][0:127, :], lhsT=kcT2[rows, g, 0:127], rhs=qraw[rows, j, qc * 512:(qc + 1) * 512], start=True, stop=False),
                         reads=[r_kc, r_qraw], writes=[rpb[bs]], rg=(2 * par, 2 * par + 2))
                for par in range(2):
                    bs = bss[par]
                    S.op("pe", lambda e: e.matmul(pb[bs][0:127, :], lhsT=ident[0:127, 0:127], rhs=cmpbias[0:127, qc * 512:(qc + 1) * 512], start=False, stop=True),
                         reads=[r_id, r_cmpb], writes=[rpb[bs]], rg=(0, 4))
                for par in range(2):
                    bs = bss[par]
                    pt, rpt = st["pt"][par]
                    S.op("act", lambda e: e.activation(out=pt[0:127, :], in_=pb[bs][0:127, :], func=AF.Exp, scale=0.125), reads=[rpb[bs]], writes=[rpt])

            def pv():
                for par in range(2):
                    r = 2 * j + par
                    pt, rpt = st["pt"][par]
                    bc_ = ACCB[par]
                    for qi in range(4):
                        S.op("pe", lambda e, qi=qi: e.matmul(pb[bc_][:, qi * 97:(qi + 1) * 97], lhsT=pt[0:127, qi * 128:(qi + 1) * 128], rhs=vcaug[0:127, g, :],
                                                              start=True, stop=True), reads=[rpt, r_vc], writes=[rpb[bc_]], rg=(0, 4))
                for par in range(2):
                    r = 2 * j + par
                    bc_ = ACCB[par]
                    ac_, rac_ = accs[rr["ac"] % 2]
                    rr["ac"] += 1
                    S.op("act", lambda e: e.copy(out=ac_[:, 0:388], in_=pb[bc_][:, 0:388]), reads=[rpb[bc_]], writes=[rac_])
                    pvv = ac_[:, 0:388].rearrange("p (a b) -> p a b", a=4)
                    dn, rdn = finalize(pvv, rac_, r, 0, qc, oa, roa, True)
                    if r == 0:
                        S.op("dve", lambda e: e.tensor_tensor(out=impacc[:], in0=pvv[:, :, 65:97], in1=dn[:].unsqueeze(2).to_broadcast([128, 4, 32]), op=ALU.mult),
                             reads=[rac_, rdn], writes=[r_imp])
                    else:
                        S.op("dve", lambda e: e.tensor_tensor(out=impt[:], in0=pvv[:, :, 65:97], in1=dn[:].unsqueeze(2).to_broadcast([128, 4, 32]), op=ALU.mult),
                             reads=[rac_, rdn], writes=[r_impt])
                        S.op("pool", lambda e: e.tensor_tensor(out=impacc[:], in0=impacc[:], in1=impt[:], op=ALU.add), reads=[r_imp, r_impt], writes=[r_imp])
                if j == 1 and g == 0 and qc == 3:
                    dbg("imp", impacc[:].rearrange("p a b -> p (a b)"), r_imp, [128, 128], F32)
            return qk, pv

        def att_step(qc, br, j, kt, kt_lo, kt_hi, oa, roa, sT, rsT):
            st = {}
            kT_, rkT_ = kTx[br - 1]
            qlo = max(0, kt - 4 * qc)
            qhi = 4 if br == 1 else min(4, kt + 5 - 4 * qc)
            c0, c1 = qlo * 128, qhi * 128

            def qk():
                st["pt"] = []
                bss = []
                for par in range(2):
                    bss.append(SB[rr["s"] % 5])
                    rr["s"] += 1
                    st["pt"].append(PT[rr["pt"] % 8])
                    rr["pt"] += 1
                masks = []
                if kt >= 4 * qc:
                    masks.append((kt - 4 * qc, trimask, r_tm))
                if br == 2 and 0 <= kt + 4 - 4 * qc < 4:
                    masks.append((kt + 4 - 4 * qc, antimask, r_am))
                if br == 1:
                    for par in range(2):
                        bs = bss[par]
                        ke_, rke_ = (kTs, r_kTs) if par == 0 else (KE1, r_KE1)
                        qs_, rqs_ = QS[j][par]
                        S.op("pe", lambda e: e.matmul(pb[bs][:, c0:c1], lhsT=ke_[:, kt * 128:(kt + 1) * 128], rhs=qs_[:, c0:c1], start=True, stop=True),
                             reads=[rke_, rqs_], writes=[rpb[bs]], rg=(0, 4))
                else:
                    for par in range(2):
                        bs = bss[par]
                        rows = slice(par * 64, par * 64 + 64)
                        S.op("pe", lambda e: e.matmul(pb[bs][:, c0:c1], lhsT=kT_[rows, kt * 128:(kt + 1) * 128], rhs=qrot[rows, j, qc * 512 + c0:qc * 512 + c1],
                                                      start=True, stop=True),
                             reads=[rkT_, r_qrot], writes=[rpb[bs]], rg=(2 * par, 2 * par + 2))
                for par in range(2):
                    bs = bss[par]
                    pt, rpt = st["pt"][par]
                    S.op("act", lambda e: e.activation(out=pt[:, c0:c1], in_=pb[bs][:, c0:c1], func=AF.Exp, scale=0.125), reads=[rpb[bs]], writes=[rpt])
                for par in range(2):
                    pt, rpt = st["pt"][par]
                    for (qb, mt, rmt) in masks:
                        S.op("pool", lambda e, qb=qb, mt=mt: e.tensor_tensor(out=pt[:, qb * 128:(qb + 1) * 128], in0=pt[:, qb * 128:(qb + 1) * 128], in1=mt[:], op=ALU.mult),
                             reads=[rpt, rmt], writes=[rpt])

            def pv():
                for par in range(2):
                    pt, rpt = st["pt"][par]
                    accb = ACCB[par]
                    for qi in range(qlo, qhi):
                        st_ = (kt == kt_lo and qi == qlo)
                        S.op("pe", lambda e, qi=qi, st_=st_: e.matmul(pb[accb][:, qi * 65:(qi + 1) * 65], lhsT=pt[:, qi * 128:(qi + 1) * 128], rhs=vaug[:, kt, br - 1, :],
                                                                       start=st_, stop=False, skip_group_check=True),
                             reads=[rpt, r_vaug], writes=[rpb[accb]], rg=(0, 4))
                if kt == kt_hi:
                    for par in range(2):
                        accb = ACCB[par]
                        ac_, rac_ = accs[rr["ac"] % 2]
                        rr["ac"] += 1
                        S.op("act", lambda e: e.copy(out=ac_[:, 0:260], in_=pb[accb][:, 0:260]), reads=[rpb[accb]], writes=[rac_])
                        pvv = ac_[:, 0:260].rearrange("p (a b) -> p a b", a=4)
                        finalize(pvv, rac_, 2 * j + par, br, qc, oa, roa, False)
            return qk, pv

        def finish_chunk(qc, oa, roa):
            if g == 0 and qc == 3:
                dbg("oatt", oaccb[:].rearrange("p a b -> p (a b)"), r_oaccb, [128, 1024])
            for j in range(2):
                for qi in range(4):
                    S.op("pe", lambda e: e.transpose(out=pbt[:, qi * 128:(qi + 1) * 128], in_=oaccb[:, qi, j * 128:(j + 1) * 128], identity=ident[:]),
                         reads=[r_oaccb, r_id], writes=[rpbt], rg=(0, 4))
                S.op("dve", lambda e: e.tensor_tensor(out=mixT[:, 2 * g + j, qc * 512:(qc + 1) * 512], in0=pbt[:, 0:512], in1=zsT[:, j, qc * 512:(qc + 1) * 512], op=ALU.mult),
                     reads=[rpbt, r_zsT], writes=[r_mixt[2 * g + j]])

        def qcopies(qc):
            for j_ in range(2):
                S.op("pool", lambda e, j_=j_: e.tensor_copy(out=QS[j_][0][0][0:64, :], in_=qrot[0:64, j_, qc * 512:(qc + 1) * 512]), reads=[r_qrot], writes=[QS[j_][0][1]])
                S.op("pool", lambda e, j_=j_: e.tensor_copy(out=QS[j_][1][0][64:128, :], in_=qrot[64:128, j_, qc * 512:(qc + 1) * 512]), reads=[r_qrot], writes=[QS[j_][1][1]])

        steps = []
        posts = {}
        for qc in range(4):
            oa, roa = oaccs[qc % 2]
            sT, rsT = None, None
            base = len(steps)
            if qc > 0:
                poa, proa = oaccs[(qc - 1) % 2]
                posts.setdefault(base + 13, []).append(lambda qc=qc, poa=poa, proa=proa: finish_chunk(qc - 1, poa, proa))
            for j in range(2):
                steps.append(cmp_step(qc, j, oa, roa, sT, rsT))
            posts.setdefault(base, []).append(lambda qc=qc: qcopies(qc))
            posts.setdefault(base + (1 if qc == 0 else 2), []).append(lambda qc=qc, sT=sT, rsT=rsT: topk(qc, sT, rsT))
            tk0 = 4 if qc == 0 else 9
            for qi in range(4):
                posts.setdefault(base + tk0 + qi, []).append(lambda qc=qc, qi=qi, sT=sT, rsT=rsT: topk_pe(qc, qi, sT, rsT))
            for br in (2, 1):
                for j in range(2):
                    kt_lo = 0 if br == 1 else max(0, 4 * qc - 4)
                    kt_hi = 4 * qc + 3
                    for kt in range(kt_lo, kt_hi + 1):
                        steps.append(att_step(qc, br, j, kt, kt_lo, kt_hi, oa, roa, sT, rsT))
        ADEPTH = 3
        nq = 0
        for si_, (qk, pv) in enumerate(steps):
            while nq <= min(si_ + ADEPTH, len(steps) - 1):
                steps[nq][0]()
                nq += 1
            pv()
            for f_ in posts.get(si_, []):
                f_()
        finish_chunk(3, *oaccs[3 % 2])
        S.barrier()
    dbg("mixA", mixT[:, 0, :], r_mixt[0], [128, 2048])
    chk("att")

    for g in range(4):
        AR.reset(PHASE)
        zsS, r_zsS = G("zsS", [128, 2, S_LEN], BF16)
        xcT, r_xcT = G("xcT", [128, 2, S_LEN], BF16)
        BT, r_BT = G("BT", [128, S_LEN], BF16)
        CT, r_CT = G("CT", [128, S_LEN], BF16)
        xdt, r_xdt = G("xdt", [128, 16, 4, 64], BF16)
        Btm, r_Btm = G("Btm", [128, 16, 128], BF16)
        SSD_TMP = AR.mark()
        ub = [G("ub%d" % i, [128, 515], BF16) for i in range(3)]
        dg, r_dg = G("dg", [128, 4, 4, 128], BF16)
        for j in range(2):
            wb, rwb = load_wtile()
            for tc in range(4):
                bi = next_bank()
                fm_matmuls(wb, rwb, tc, bi)
                S.op("act", lambda e: e.activation(out=zsS[:, j, tc * 512:(tc + 1) * 512], in_=pb[bi][:, :], func=AF.Silu), reads=[rpb[bi]], writes=[r_zsS])
        conv_targets = [(lambda tc: xcT[:, 0, tc * 512:(tc + 1) * 512], r_xcT, 2 * g),
                        (lambda tc: xcT[:, 1, tc * 512:(tc + 1) * 512], r_xcT, 2 * g + 1),
                        (lambda tc: BT[:, tc * 512:(tc + 1) * 512], r_BT, 8 + g),
                        (lambda tc: CT[:, tc * 512:(tc + 1) * 512], r_CT, 12 + g)]
        cw, r_cw = sm["convw"]
        cbv, r_cbv = sm["convb"]
        for ti, (dst_fn, rdst, ct) in enumerate(conv_targets):
            for k in range(4):
                S.op("dve", lambda e, ti=ti, k=k, ct=ct: e.tensor_scalar(out=dg[:, ti, k, :], in0=ident[:], scalar1=cw[:, ct * 4 + k:ct * 4 + k + 1], scalar2=None, op0=ALU.mult),
                     reads=[r_id, r_cw], writes=[r_dg])
        pend_conv = []

        def conv_tail():
            (ti, dst_fn, rdst, ct, tc, u, ru) = pend_conv.pop(0)
            pc = 2 + (tc % 2)
            for k in range(4):
                S.op("pe", lambda e, k=k: e.matmul(pb[pc][:, :], lhsT=dg[:, ti, k, :], rhs=u[:, k:k + 512], start=(k == 0), stop=(k == 3)),
                     reads=[r_dg, ru], writes=[rpb[pc]], rg=(0, 4))
            S.op("act", lambda e: e.activation(out=dst_fn(tc), in_=pb[pc][:, :], func=AF.Silu, bias=cbv[:, ct:ct + 1], scale=1.0), reads=[rpb[pc], r_cbv], writes=[rdst])

        for ti, (dst_fn, rdst, ct) in enumerate(conv_targets):
            wb, rwb = load_wtile()
            for tc in range(4):
                bi = next_bank()
                fm_matmuls(wb, rwb, tc, bi)
                u, ru = ub[(ti * 4 + tc) % 3]
                if tc == 0:
                    S.op("pool", lambda e: e.memset(u[:, 0:3], 0.0), writes=[ru])
                S.op("act", lambda e: e.copy(out=u[:, 3:515], in_=pb[bi][:, :]), reads=[rpb[bi]], writes=[ru])
                if tc < 3:
                    un, run = ub[(ti * 4 + tc + 1) % 3]
                    S.op("pool", lambda e: e.tensor_copy(out=un[:, 0:3], in_=u[:, 512:515]), reads=[ru], writes=[run])
                pend_conv.append((ti, dst_fn, rdst, ct, tc, u, ru))
                if len(pend_conv) >= 2:
                    conv_tail()
        while pend_conv:
            conv_tail()
        for tt in range(NT):
            for j in range(2):
                S.op("pe", lambda e, j=j: e.transpose(out=pbt[:, j * 128:(j + 1) * 128], in_=xcT[:, j, tt * 128:(tt + 1) * 128], identity=ident[:]),
                     reads=[r_xcT, r_id], writes=[rpbt], rg=(0, 4))
            S.op("pe", lambda e: e.transpose(out=pbt[:, 256:384], in_=BT[:, tt * 128:(tt + 1) * 128], identity=ident[:]), reads=[r_BT, r_id], writes=[rpbt], rg=(0, 4))
            S.op("dve", lambda e: e.tensor_tensor(out=xdt[:, tt, :, :], in0=pbt[:, 0:256].rearrange("p (a b) -> p a b", a=4),
                                                  in1=dtv[:, tt, 4 * g:4 * g + 4].unsqueeze(2).to_broadcast([128, 4, 64]), op=ALU.mult),
                 reads=[rpbt, r_dtv], writes=[r_xdt])
            S.op("act", lambda e: e.copy(out=Btm[:, tt, :], in_=pbt[:, 256:384]), reads=[rpbt], writes=[r_Btm])
        if g == 0:
            dbg("xcT", xcT[:, 0, :], r_xcT, [128, 2048])
            dbg("BT", BT[:, :], r_BT, [128, 2048])
        NB3 = 4
        CBm = [G("CBm%d" % i, [128, 2, 256], BF16) for i in range(2)]
        EA = [G("EA%d" % i, [128, 256], F32) for i in range(NB3)]
        Cdec = [G("Cdec%d" % i, [128, 256], BF16) for i in range(NB3)]
        D1 = [G("D1%d" % i, [128, 384], F32) for i in range(NB3)]
        MT = [G("MT%d" % i, [128, 384], BF16) for i in range(NB3)]
        xws = [G("xw%d" % i, [128, 2, 4, 64], BF16) for i in range(2)]
        cds = [G("cdall%d" % i, [128, 4], F32) for i in range(2)]
        dtes = [G("dte%d" % i, [128, 2, 4], F32) for i in range(2)]
        htmp, r_htmp = G("htmp", [128, 4, 64], F32)
        h32, r_h32 = G("h32", [128, 4, 64], F32)
        hbf, r_hbf = G("hbf", [128, 4, 64], BF16)
        ytmp = [G("ytmp%d" % i, [128, 256], F32) for i in range(4)]
        yg = [G("yg%d" % i, [128, 256], F32) for i in range(2)]
        S.op("pool", lambda e: e.memset(h32[:], 0.0), writes=[r_h32])
        S.op("pool", lambda e: e.memset(hbf[:], 0.0), writes=[r_hbf])
        dsk, r_dsk = sm["dskip"]
        snw, r_snw = sm["snw"]
        BY = [0, 1, 2]
        sst = {"by": 0, "k": 0}
        info = {}

        def Cc(c):
            l0 = c * 256
            bx = 6
            cb_, rcb_ = CBm[c % 2]
            S.op("pe", lambda e: e.matmul(pb[bx][:, 0:256], lhsT=BT[:, l0:l0 + 128], rhs=CT[:, l0:l0 + 256], start=True, stop=True),
                 reads=[r_BT, r_CT], writes=[rpb[bx]], rg=(0, 4))
            S.op("pe", lambda e: e.matmul(pb[bx][:, 256:384], lhsT=BT[:, l0 + 128:l0 + 256], rhs=CT[:, l0 + 128:l0 + 256], start=True, stop=True),
                 reads=[r_BT, r_CT], writes=[rpb[bx]], rg=(0, 4))
            S.op("dve", lambda e: e.tensor_tensor(out=cb_[:, 0, 0:128], in0=pb[bx][:, 0:128], in1=trimask[:], op=ALU.mult), reads=[rpb[bx], r_tm], writes=[rcb_])
            S.op("act", lambda e: e.copy(out=cb_[:, 0, 128:256], in_=pb[bx][:, 128:256]), reads=[rpb[bx]], writes=[rcb_])
            S.op("dve", lambda e: e.tensor_tensor(out=cb_[:, 1, 0:128], in0=pb[bx][:, 256:384], in1=trimask[:], op=ALU.mult), reads=[rpb[bx], r_tm], writes=[rcb_])

        def A(c, r):
            l0 = c * 256
            h = 4 * g + r
            k = sst["k"] % NB3
            sst["k"] += 1
            info[(c, r)] = k
            by = BY[sst["by"] % 3]
            sst["by"] += 1
            cb_, rcb_ = CBm[c % 2]
            xw, r_xw = xws[c % 2]
            cdall, r_cd = cds[c % 2]
            dte, r_dte = dtes[c % 2]
            for pi in range(3):
                ap_, rap_ = a3[pi]
                S.op("pe", lambda e, pi=pi, ap_=ap_: e.matmul(pb[by][:, 0:256], lhsT=ap_[:, 2 * c, h:h + 1].to_broadcast([128, 128]), rhs=tri2b[:, :], start=(pi == 0), stop=False),
                     reads=[rap_, r_trib], writes=[rpb[by]], rg=(0, 4))
            for pi in range(3):
                ap_, rap_ = a3[pi]
                S.op("pe", lambda e, pi=pi, ap_=ap_: e.matmul(pb[by][:, 128:256], lhsT=ap_[:, 2 * c + 1, h:h + 1].to_broadcast([128, 128]), rhs=tri2b[:, 0:128], start=False, stop=(pi == 2)),
                     reads=[rap_, r_trib], writes=[rpb[by]], rg=(0, 4))
            ea, rea = EA[k]
            cd_, rcd_ = Cdec[k]
            d1, rd1 = D1[k]
            mt, rmt = MT[k]
            S.op("dve", lambda e: e.tensor_scalar(out=d1[:, 0:256], in0=pb[by][:, 0:256], scalar1=acs_tm[:, 2 * c, h:h + 1], scalar2=0.0, op0=ALU.subtract, op1=ALU.min),
                 reads=[rpb[by], r_acs], writes=[rd1])
            S.op("dve", lambda e: e.tensor_scalar(out=d1[:, 256:384], in0=pb[by][:, 128:256], scalar1=acs_tm[:, 2 * c + 1, h:h + 1], scalar2=0.0, op0=ALU.subtract, op1=ALU.min),
                 reads=[rpb[by], r_acs], writes=[rd1])
            S.op("act", lambda e: e.activation(out=ea[:], in_=pb[by][:, 0:256], func=AF.Exp), reads=[rpb[by]], writes=[rea])
            S.op("act", lambda e: e.activation(out=d1[:], in_=d1[:], func=AF.Exp), reads=[rd1], writes=[rd1])
            S.op("pool", lambda e: e.tensor_tensor(out=mt[:, 0:256], in0=d1[:, 0:256], in1=cb_[:, 0, :], op=ALU.mult), reads=[rd1, rcb_], writes=[rmt])
            S.op("pool", lambda e: e.tensor_tensor(out=mt[:, 256:384], in0=d1[:, 256:384], in1=cb_[:, 1, 0:128], op=ALU.mult), reads=[rd1, rcb_], writes=[rmt])
            if c > 0:
                S.op("dve", lambda e: e.tensor_tensor(out=cd_[:], in0=CT[:, l0:l0 + 256], in1=ea[:], op=ALU.mult), reads=[r_CT, rea], writes=[rcd_])
            if c < 7:
                S.op("act", lambda e: e.copy(out=cdall[:, r:r + 1], in_=ea[:, 255:256]), reads=[rea], writes=[r_cd])
                S.op("act", lambda e: e.copy(out=dte[:, :, r], in_=d1[:, 255:384:128]), reads=[rd1], writes=[r_dte])

        def B(c, r):
            k = info[(c, r)]
            jp, par = r // 2, r % 2
            cd_, rcd_ = Cdec[k]
            mt, rmt = MT[k]
            bo = 3 + jp
            yo = pb[bo][par * 64:(par + 1) * 64, :]
            S.op("pe", lambda e: e.matmul(yo[:, 0:256], lhsT=xdt[:, 2 * c, r, :], rhs=mt[:, 0:256], start=True, stop=False),
                 reads=[r_xdt, rmt], writes=[rpb[bo]], rg=(0, 4))
            S.op("pe", lambda e: e.matmul(yo[:, 128:256], lhsT=xdt[:, 2 * c + 1, r, :], rhs=mt[:, 256:384], start=False, stop=(c == 0)),
                 reads=[r_xdt, rmt], writes=[rpb[bo]], rg=(0, 4))
            if c > 0:
                S.op("pe", lambda e: e.matmul(yo[:, 0:256], lhsT=hbf[:, r, :], rhs=cd_[:], start=False, stop=True),
                     reads=[r_hbf, rcd_], writes=[rpb[bo]], rg=(0, 4))

        def St_pre(c):
            xw, r_xw = xws[c % 2]
            dte, r_dte = dtes[c % 2]
            for si_ in range(2):
                S.op("dve", lambda e, si_=si_: e.tensor_tensor(out=xw[:, si_, :, :], in0=xdt[:, 2 * c + si_, :, :], in1=dte[:, si_, :].unsqueeze(2).to_broadcast([128, 4, 64]), op=ALU.mult),
                     reads=[r_xdt, r_dte], writes=[r_xw])

        def St(c):
            bst = 5
            xw, r_xw = xws[c % 2]
            cdall, r_cd = cds[c % 2]
            dte, r_dte = dtes[c % 2]
            S.op("pe", lambda e: e.matmul(pb[bst][:, 0:256], lhsT=Btm[:, 2 * c, :], rhs=xw[:, 0, :, :].rearrange("p a b -> p (a b)"), start=True, stop=False),
                 reads=[r_Btm, r_xw], writes=[rpb[bst]], rg=(0, 4))
            S.op("pe", lambda e: e.matmul(pb[bst][:, 0:256], lhsT=Btm[:, 2 * c + 1, :], rhs=xw[:, 1, :, :].rearrange("p a b -> p (a b)"), start=False, stop=True),
                 reads=[r_Btm, r_xw], writes=[rpb[bst]], rg=(0, 4))
            S.op("dve", lambda e: e.tensor_tensor(out=htmp[:], in0=h32[:], in1=cdall[:, 0:4].unsqueeze(2).to_broadcast([128, 4, 64]), op=ALU.mult),
                 reads=[r_h32, r_cd], writes=[r_htmp])
            S.op("dve", lambda e: e.tensor_tensor(out=h32[:], in0=htmp[:], in1=pb[bst][:, 0:256].rearrange("p (a b) -> p a b", a=4), op=ALU.add),
                 reads=[r_htmp, rpb[bst]], writes=[r_h32])
            S.op("pool", lambda e: e.tensor_copy(out=hbf[:], in_=h32[:]), reads=[r_h32], writes=[r_hbf])

        def Yev(c, jp):
            l0 = c * 256
            bo = 3 + jp
            ft = 2 * g + jp
            yk = sst["y"] % 4
            sst["y"] += 1
            yt, ryt = ytmp[yk]
            ygg, rygg = yg[jp]
            S.op("dve", lambda e: e.scalar_tensor_tensor(out=yt[:], in0=xcT[:, jp, l0:l0 + 256], scalar=dsk[:, ft:ft + 1], in1=pb[bo][:, 0:256], op0=ALU.mult, op1=ALU.add),
                 reads=[r_xcT, r_dsk, rpb[bo]], writes=[ryt])
            S.op("dve", lambda e: e.tensor_tensor(out=ygg[:], in0=yt[:], in1=zsS[:, jp, l0:l0 + 256], op=ALU.mult), reads=[ryt, r_zsS], writes=[rygg])
            S.op("act", lambda e: e.mul(out=mixT[:, 8 + ft, l0:l0 + 256], in_=ygg[:], mul=snw[:, ft:ft + 1]), reads=[rygg, r_snw], writes=[r_mixt[8 + ft]])
            S.op("act", lambda e: e.activation(out=yt[:], in_=ygg[:], func=AF.Square), reads=[rygg], writes=[ryt])
            pendpe.append((c, yk))
            if g == 0 and jp == 0 and c == 1:
                dbg("yg", ygg[:], rygg, [128, 256], F32)

        def YevPE():
            c, yk = pendpe.pop(0)
            yt, ryt = ytmp[yk]
            bq = 5
            kq = sst["q"] % 4
            sst["q"] += 1
            for hh in range(2):
                S.op("pe", lambda e, hh=hh: e.matmul(pb[bq][:, 400 + 2 * kq + hh:401 + 2 * kq + hh], lhsT=yt[:, hh * 128:(hh + 1) * 128], rhs=onesf[:, 0:1], start=True, stop=True),
                     reads=[ryt, r_ones], writes=[rpb[bq]], rg=(0, 4))
            pend.append((c, kq))

        def Yev2():
            c, kq = pend.pop(0)
            bq = 5
            S.op("dve", lambda e: e.tensor_tensor(out=ssq[:, 2 * c:2 * c + 2], in0=ssq[:, 2 * c:2 * c + 2], in1=pb[bq][:, 400 + 2 * kq:402 + 2 * kq], op=ALU.add), reads=[r_ssq, rpb[bq]], writes=[r_ssq])

        pend = []
        pendpe = []
        sst["q"] = 0
        sst["y"] = 0
        DEPTH = 3
        sl = [(c, r) for c in range(8) for r in range(4)]
        issued = 0

        def issue_A(upto):
            nonlocal_issued = issued_box[0]
            while nonlocal_issued <= upto and nonlocal_issued < len(sl):
                c_, r_ = sl[nonlocal_issued]
                if r_ == 0:
                    Cc(c_)
                A(c_, r_)
                nonlocal_issued += 1
            issued_box[0] = nonlocal_issued

        issued_box = [0]
        for si, (c, r) in enumerate(sl):
            issue_A(si + DEPTH)
            if r == 3 and c < 7:
                St_pre(c)
            B(c, r)
            if r % 2 == 1:
                if len(pendpe) >= 2:
                    YevPE()
                if len(pend) >= 3:
                    Yev2()
                Yev(c, r // 2)
            if r == 3 and c < 7:
                St(c)
            if g == 3 and r == 1:
                for kc in (2 * c, 2 * c + 1):
                    S.dma(lambda e, kc=kc: e.dma_start(out=woutb[:, kc, :], in_=wout_d[kc * 128:(kc + 1) * 128, :]), writes=[r_woutb] + r_hTc, q="pool")
        while pendpe:
            YevPE()
        while pend:
            Yev2()
        S.barrier()
    dbg("ssq", ssq[:], r_ssq, [128, 16], F32)
    dbg("mixS", mixT[:, 8, :], r_mixt[8], [128, 2048])
    chk("ssd")

    AR.reset(PHASE)
    pw2, r_pw2 = G("pw2", [128, 1024], F32)
    xt2 = [G("xt2%d" % i, [128, 1024], F32) for i in range(2)]
    ob = [G("ob%d" % i, [128, 1024], F32) for i in range(2)]
    tmpo, r_tmpo = G("tmpo", [128, 1024], F32)
    junk2, r_junk2 = G("junk2", [128, 1024], F32)
    ss2, r_ss2 = G("ss2", [128, 16], F32)
    S.dma(lambda e: e.dma_start(out=pw2[:], in_=sm_d["postw"].partition_broadcast(128)), writes=[r_pw2])
    S.op("dve", lambda e: e.tensor_scalar(out=ssq[:], in0=ssq[:], scalar1=1.0 / D, scalar2=EPS, op0=ALU.mult, op1=ALU.add), reads=[r_ssq], writes=[r_ssq])
    S.op("act", lambda e: e.activation(out=ssq[:], in_=ssq[:], func=AF.Sqrt), reads=[r_ssq], writes=[r_ssq])
    S.op("dve", lambda e: e.reciprocal(out=ssq[:], in_=ssq[:]), reads=[r_ssq], writes=[r_ssq])
    for tt in range(NT):
        xb, rxb = xt2[tt % 2]
        o_, ro_ = ob[tt % 2]
        S.dma(lambda e: e.dma_start(out=xb[:], in_=x_d[tt * 128:(tt + 1) * 128, :]), writes=[rxb])
        obk = [(4 * tt + i_) % 7 for i_ in range(4)]
        for half in (1, 0):
            for hs in range(2):
                bi = obk[(1 - half) * 2 + hs]
                for kk in range(8):
                    kc = half * 8 + kk
                    S.op("pe", lambda e, kc=kc, kk=kk: e.matmul(pb[bi][:, :], lhsT=mixT[:, kc, tt * 128:(tt + 1) * 128], rhs=woutb[:, kc, hs * 512:(hs + 1) * 512],
                                                                 start=(kk == 0), stop=(kk == 7)), reads=[r_mixt[kc], r_woutb], writes=[rpb[bi]], rg=(0, 4))
        for hs in range(2):
            bs_, ba_ = obk[hs], obk[2 + hs]
            S.op("act", lambda e: e.mul(out=tmpo[:, hs * 512:(hs + 1) * 512], in_=pb[bs_][:, :], mul=ssq[:, tt:tt + 1]),
                 reads=[rpb[bs_], r_ssq], writes=[r_tmpo])
            S.op("dve", lambda e: e.tensor_tensor(out=o_[:, hs * 512:(hs + 1) * 512], in0=tmpo[:, hs * 512:(hs + 1) * 512], in1=pb[ba_][:, :], op=ALU.add),
                 reads=[r_tmpo, rpb[ba_]], writes=[ro_])
        S.op("act", lambda e: e.activation(out=junk2[:], in_=o_[:], func=AF.Square, accum_out=ss2[:, tt:tt + 1]), reads=[ro_], writes=[r_junk2, r_ss2])
        S.op("dve", lambda e: e.tensor_scalar(out=ss2[:, tt:tt + 1], in0=ss2[:, tt:tt + 1], scalar1=1.0 / D, scalar2=EPS, op0=ALU.mult, op1=ALU.add), reads=[r_ss2], writes=[r_ss2])
        S.op("act", lambda e: e.activation(out=ss2[:, tt:tt + 1], in_=ss2[:, tt:tt + 1], func=AF.Sqrt), reads=[r_ss2], writes=[r_ss2])
        S.op("dve", lambda e: e.reciprocal(out=ss2[:, tt:tt + 1], in_=ss2[:, tt:tt + 1]), reads=[r_ss2], writes=[r_ss2])
        S.op("dve", lambda e: e.scalar_tensor_tensor(out=o_[:], in0=o_[:], scalar=ss2[:, tt:tt + 1], in1=pw2[:], op0=ALU.mult, op1=ALU.mult),
             reads=[ro_, r_ss2, r_pw2], writes=[ro_])
        S.op("pool", lambda e: e.tensor_tensor(out=o_[:], in0=o_[:], in1=xb[:], op=ALU.add), reads=[ro_, rxb], writes=[ro_])
        S.dma(lambda e: e.dma_start(out=out_d[tt * 128:(tt + 1) * 128, :], in_=o_[:]), reads=[ro_])


_CACHE = {}


def prepare_inputs(inp):
    inp = {k: np.asarray(v) for k, v in inp.items()}
    shared = {}
    shared["wt"] = host_wtiles(np.ascontiguousarray(inp["w_in"][0], dtype=np.float32))
    shared["w1"] = np.ascontiguousarray(inp["cmp_w1"][0], dtype=np.float32)
    shared["wout"] = np.ascontiguousarray(inp["w_out"][0], dtype=np.float32)
    shared.update(host_consts())
    shared.update(host_small(inp))
    return inp, shared


def kernel(**inputs):
    inp, shared = prepare_inputs(inputs)
    if "nc" not in _CACHE:
        _CACHE["nc"] = build()[0]
    nc = _CACHE["nc"]
    x = np.ascontiguousarray(inp["x"], dtype=np.float32)
    in_maps = []
    for b in range(8):
        m = dict(shared)
        m["x"] = x[b]
        in_maps.append(m)
    res = run_bass_kernel_spmd(nc, in_maps, core_ids=list(range(8)))
    return np.stack([res.results[b]["out"] for b in range(8)], 0).astype(np.float32)
```

```python
import numpy as np
import concourse.bass as bass
import concourse.mybir as mybir
from concourse.bass_utils import run_bass_kernel_spmd

F32 = mybir.dt.float32
BF16 = mybir.dt.bfloat16
AF = mybir.ActivationFunctionType
ALU = mybir.AluOpType

S_LEN = 2048
D = 1024
NT = 16
NEGB = -1.0e9
EPS = 1e-6
SAME_ENGINE_SYNC = True

O_Q, O_KCM, O_VCM, O_KSL, O_VSL, O_KWN, O_VWN, O_GLOG, O_ZATT, O_ZSSM, O_XSSM, O_B, O_C, O_DT = (
    0, 1024, 1280, 1536, 1792, 2048, 2304, 2560, 2608, 3632, 4656, 5680, 6192, 6704)


class Res:
    __slots__ = ("name", "last_write", "readers", "excl", "rg")

    def __init__(self, name, excl=False):
        self.name = name
        self.last_write = None
        self.readers = {}
        self.excl = excl
        self.rg = None


class Sched:
    ENGS = ("pe", "act", "dve", "pool", "sp")

    def __init__(self, nc, n_dma_sems=8):
        self.nc = nc
        self.eng = {"pe": nc.tensor, "act": nc.scalar, "dve": nc.vector, "pool": nc.gpsimd, "sp": nc.sync}
        self.sem = {e: nc.alloc_semaphore("s_" + e) for e in ("pe", "act", "dve", "pool")}
        self.cnt = {e: 0 for e in ("pe", "act", "dve", "pool")}
        self.dsem = [nc.alloc_semaphore("s_dma%d" % i) for i in range(n_dma_sems)]
        self.dcnt = [0] * n_dma_sems
        half = n_dma_sems // 2
        self.dpool = {"sp": list(range(0, half)), "pool": list(range(half, n_dma_sems))}
        self.dnext = {"sp": 0, "pool": 0}
        self.seen = {e: {} for e in self.ENGS}
        self.nops = 0

    def _semof(self, f):
        return self.dsem[f] if isinstance(f, int) else self.sem[f]

    def _collect(self, e, reads, writes, rg=None):
        waits = {}

        def need(ev, force=False):
            f, c = ev
            if f == e and (e == "pe" or not SAME_ENGINE_SYNC) and not force:
                return
            if self.seen[e].get(f, 0) >= c:
                return
            if waits.get(f, 0) < c:
                waits[f] = c

        for r in reads:
            if r.excl:
                if r.last_write:
                    need(r.last_write)
                for ev in r.readers.items():
                    need(ev)
            elif r.last_write:
                need(r.last_write)
        for w in writes:
            if w.last_write:
                force = False
                if e == "pe" and w.excl and rg is not None and w.rg is not None:
                    if rg[1] <= w.rg[0] or w.rg[1] <= rg[0]:
                        force = True
                need(w.last_write, force)
            for ev in w.readers.items():
                need(ev)
        return waits

    def _emit_waits(self, e, waits):
        for f, c in waits.items():
            self.seen[e][f] = c
            self.eng[e].wait_ge(self._semof(f), c)

    def op(self, e, fn, reads=(), writes=(), rg=None):
        waits = self._collect(e, reads, writes, rg)
        self._emit_waits(e, waits)
        self.cnt[e] += 1
        c = self.cnt[e]
        fn(self.eng[e]).then_inc(self.sem[e], 1)
        self.nops += 1
        for r in reads:
            if r.excl:
                r.last_write = (e, c)
                r.readers = {}
            elif r.readers.get(e, 0) < c:
                r.readers[e] = c
        for w in writes:
            w.last_write = (e, c)
            w.readers = {}
            if e == "pe" and w.excl:
                w.rg = rg
        return (e, c)

    def dma(self, fn, reads=(), writes=(), q="sp"):
        waits = self._collect(q, reads, writes)
        pool_ = self.dpool[q]
        s = pool_[self.dnext[q] % len(pool_)]
        self.dnext[q] += 1
        if self.dcnt[s] > 0 and self.seen[q].get(s, 0) < self.dcnt[s]:
            if waits.get(s, 0) < self.dcnt[s]:
                waits[s] = self.dcnt[s]
        self._emit_waits(q, waits)
        self.dcnt[s] += 16
        c = self.dcnt[s]
        fn(self.eng[q]).then_inc(self.dsem[s], 16)
        self.nops += 1
        for r in reads:
            if r.readers.get(s, 0) < c:
                r.readers[s] = c
        for w in writes:
            w.last_write = (s, c)
            w.readers = {}
        return (s, c)

    def barrier(self):
        for e in self.ENGS:
            waits = {}
            for f in ("pe", "act", "dve", "pool"):
                if f != e and self.cnt[f] > self.seen[e].get(f, 0):
                    waits[f] = self.cnt[f]
            for s in range(len(self.dsem)):
                if self.dcnt[s] > self.seen[e].get(s, 0):
                    waits[s] = self.dcnt[s]
            self._emit_waits(e, waits)

    def finish(self):
        waits = {}
        for s in range(len(self.dsem)):
            if self.dcnt[s] > self.seen["sp"].get(s, 0):
                waits[s] = self.dcnt[s]
        self._emit_waits("sp", waits)


class Arena:
    def __init__(self, nc):
        self.nc = nc
        self.base = (nc.sbuf_base + 63) // 64 * 64
        self.top = nc.sbuf_top
        self.cur = self.base
        self.n = 0

    def alloc(self, name, shape, dt):
        esz = 2 if dt == BF16 else 4
        per = esz
        for s in shape[1:]:
            per *= s
        off = self.cur
        self.cur = (off + per + 63) // 64 * 64
        assert self.cur <= self.top, "SBUF overflow at %s: %d > %d" % (name, self.cur, self.top)
        self.n += 1
        self.last_off = off
        return self.nc.alloc_sbuf_tensor_at("%s_%d" % (name, self.n), list(shape), dt, offset=off)

    def mark(self):
        return self.cur

    def reset(self, m):
        self.cur = m


def _tile_cols():
    def partner(d):
        return d + 8 if d < 8 else (d - 8 if d < 16 else -1)

    tiles = []
    tmg = [O_GLOG + i for i in range(48)] + [O_DT + i for i in range(16)] + [-1] * 64
    tiles.append(("tmg", tmg))
    for g in range(4):
        tiles.append(("kvcm%d" % g, [O_KCM + g * 64 + d for d in range(64)] + [O_VCM + g * 64 + d for d in range(64)]))
    for g in range(4):
        for j in range(2):
            hs = (4 * g + 2 * j, 4 * g + 2 * j + 1)
            tiles.append(("q", [O_Q + h * 64 + d for h in hs for d in range(64)]))
        for off in (O_KSL, O_KWN):
            tiles.append(("k", [off + g * 64 + d for d in range(64)] * 2))
        for j in range(2):
            hs = (4 * g + 2 * j, 4 * g + 2 * j + 1)
            tiles.append(("z", [O_ZATT + h * 64 + d for h in hs for d in range(64)]))
        tiles.append(("tmv", [O_VSL + g * 64 + d for d in range(64)] + [O_VWN + g * 64 + d for d in range(64)]))
    for g in range(4):
        for off in (O_ZSSM, O_XSSM):
            for j in range(2):
                h0 = 4 * g + 2 * j
                tiles.append(("s", [off + h0 * 64 + i for i in range(128)]))
        tiles.append(("b", [O_B + g * 128 + i for i in range(128)]))
        tiles.append(("c", [O_C + g * 128 + i for i in range(128)]))
    return tiles


N_WT = 1 + 4 + 4 * 7 + 4 * 6


def host_consts():
    c = {}
    half = 8
    inv_freq = (500000.0 ** (-(np.arange(half, dtype=np.float32) * 2.0 / 16))).astype(np.float32)
    pos = np.arange(S_LEN, dtype=np.float32)
    ang = pos[:, None] * inv_freq[None, :]
    cos = np.cos(ang).astype(np.float32).T
    sin = np.sin(ang).astype(np.float32).T
    C = np.ones((128, S_LEN), np.float32)
    Sg = np.zeros((128, S_LEN), np.float32)
    for hp in range(2):
        for d in range(16):
            p = hp * 64 + d
            C[p] = cos[d % 8]
            Sg[p] = -sin[d % 8] if d < 8 else sin[d % 8]
    c["ropeC"] = C
    c["ropeS"] = Sg
    k = np.arange(128)[:, None]
    q = np.arange(128)[None, :]
    c["causalb"] = np.where(k <= q, 0.0, NEGB).astype(np.float32)
    c["antib"] = np.where(k > q, 0.0, NEGB).astype(np.float32)
    c["trimask"] = (k <= q).astype(np.float32)
    pm = np.zeros((128, 128), np.float32)
    for dst in range(128):
        dl = dst % 64
        if dl < 16:
            pm[(dst // 64) * 64 + (dl + 8 if dl < 8 else dl - 8), dst] = 1.0
    c["permm"] = pm
    c["antimask"] = (k > q).astype(np.float32)
    cc = np.arange(128)[:, None]
    qq = np.arange(S_LEN)[None, :]
    c["cmpbias"] = np.where((16 * cc + 31 <= qq) & (cc < 127), 0.0, NEGB).astype(np.float32)
    E = np.zeros((128, 16, 128), np.float32)
    for kt in range(16):
        for kk in range(128):
            j = 2 * kt + kk // 64
            E[j, kt, kk] = 1.0
            E[64 + j, kt, kk] = 1.0
    c["Eall"] = E.reshape(128, 16 * 128)
    c["ident"] = np.eye(128, dtype=np.float32)
    tok = np.arange(S_LEN)
    cur = tok // 64
    j = np.arange(32)[None, :]
    A = np.full((S_LEN, 32), -3.0e38, np.float32)
    A[np.arange(S_LEN), np.maximum(cur - 1, 0)] = 1.0e30
    A[np.arange(S_LEN), cur] = 2.0e30
    A[:, 0] = 3.0e30
    Bm = np.where(j > cur[:, None], -1.0e30, 3.0e38).astype(np.float32)
    c["topkA"] = A.reshape(16, 128, 32).transpose(1, 0, 2).reshape(128, 512).copy()
    c["topkB"] = Bm.reshape(16, 128, 32).transpose(1, 0, 2).reshape(128, 512).copy()
    ci = np.arange(128)[:, None] * 16
    bj = np.arange(32)[None, :] * 64
    ov = ((ci <= bj + 63) & (ci + 31 >= bj)).astype(np.float32)
    ov[127] = 0.0
    vx = np.zeros((128, 33), np.float32)
    vx[:, 0] = 1.0
    vx[:, 1:] = ov
    c["vcext"] = vx
    tri = np.zeros((128, 256), np.float32)
    tri[:, 0:128] = (k <= q)
    tri[:, 128:256] = 1.0
    c["tri2"] = tri
    return c


CONST_SHAPES = {"ropeC": [128, 2048], "ropeS": [128, 2048], "causalb": [128, 128], "antib": [128, 128],
                "trimask": [128, 128], "permm": [128, 128], "antimask": [128, 128], "cmpbias": [128, 2048], "Eall": [128, 2048], "ident": [128, 128],
                "topkA": [128, 512], "topkB": [128, 512], "vcext": [128, 33], "tri2": [128, 256]}

SMALL_SHAPES = {"prew": [1, 1024], "postw": [1, 1024], "posT": [128, 32], "b1T": [128, 4], "w2k": [128, 256],
                "w2v": [128, 128], "b2k": [128, 1], "b2v": [1, 64], "gateb": [1, 48], "convw": [128, 64],
                "convb": [128, 16], "dtb": [1, 16], "alog": [1, 16], "dskip": [128, 8], "snw": [128, 8]}


def host_small(inp):
    s = {}
    s["prew"] = inp["pre_norm_w"][0][None, :]
    s["postw"] = inp["post_norm_w"][0][None, :]
    pos = inp["cmp_pos"][0]
    s["posT"] = np.concatenate([pos[0].T, pos[1].T], 0)
    b1 = inp["cmp_b1"][0]
    s["b1T"] = np.stack([b1[kv, hc * 128:(hc + 1) * 128] for kv in range(2) for hc in range(2)], 1)
    w2 = inp["cmp_w2"][0]
    w2k = w2[0].reshape(2, 128, 64).transpose(1, 0, 2)
    s["w2k"] = np.concatenate([w2k, w2k], 2).reshape(128, 256)
    s["w2v"] = w2[1].reshape(2, 128, 64).transpose(1, 0, 2).reshape(128, 128)
    b2 = inp["cmp_b2"][0]
    s["b2k"] = np.concatenate([b2[0], b2[0]])[:, None]
    s["b2v"] = b2[1][None, :]
    s["gateb"] = inp["gate_b"][0][None, :]
    cw = inp["conv_w"][0]
    s["convw"] = cw.T.reshape(16, 128, 4).transpose(1, 0, 2).reshape(128, 64)
    s["convb"] = inp["conv_b"][0].reshape(16, 128).T
    s["dtb"] = inp["dt_bias"][0][None, :]
    s["alog"] = inp["a_log"][0][None, :]
    s["dskip"] = np.repeat(inp["d_skip"][0], 64).reshape(8, 128).T
    s["snw"] = inp["ssm_norm_w"][0].reshape(8, 128).T
    return {k: np.ascontiguousarray(v, dtype=np.float32) for k, v in s.items()}


def host_wtiles(w_in):
    tiles = _tile_cols()
    assert len(tiles) == N_WT
    out = np.zeros((N_WT, 128, 8, 128), np.float32)
    for t, (_, cols) in enumerate(tiles):
        cols = np.asarray(cols)
        wc = np.zeros((1024, 128), np.float32)
        m = cols >= 0
        wc[:, m] = w_in[:, cols[m]]
        out[t] = wc.reshape(8, 128, 128).transpose(1, 0, 2)
    return out.reshape(N_WT, 128, 1024)


class _Stop(Exception):
    pass


def build(debug=(), stop=None):
    nc = bass.Bass("TRN2", target_bir_lowering=False)
    S = Sched(nc, n_dma_sems=16)
    dbg_outs = {}
    try:
        _build_body(nc, S, dbg_outs, debug, stop)
    except _Stop:
        pass
    S.finish()
    return nc, dbg_outs


def _build_body(nc, S, dbg_outs, debug, stop):
    AR = Arena(nc)

    def chk(name):
        if stop == name:
            raise _Stop()

    def din(name, shape):
        return nc.dram_tensor(name, list(shape), F32, kind="ExternalInput").ap()

    x_d = din("x", [S_LEN, D])
    wt_d = din("wt", [N_WT, 128, 1024])
    w1_d = din("w1", [2, 2048, 256])
    wout_d = din("wout", [2048, 1024])
    cst_d = {k: din(k, v) for k, v in CONST_SHAPES.items()}
    sm_d = {k: din(k, v) for k, v in SMALL_SHAPES.items()}
    out_d = nc.dram_tensor("out", [S_LEN, D], F32, kind="ExternalOutput").ap()

    def dbg(name, ap, res, shape, dt=BF16):
        if name not in debug:
            return
        d = nc.dram_tensor("dbg_" + name, list(shape), dt, kind="ExternalOutput").ap()
        dbg_outs[name] = shape
        S.dma(lambda e: e.dma_start(out=d, in_=ap), reads=[res])

    pb = [nc.alloc_psum_tensor("pb%d" % i, [128, 512], F32) for i in range(7)]
    rpb = [Res("pb%d" % i, excl=True) for i in range(7)]
    pbt = nc.alloc_psum_tensor("pbt", [128, 1024], BF16)
    rpbt = Res("pbt", excl=True)

    def G(name, shape, dt):
        return AR.alloc(name, shape, dt), Res(name)

    hT, r_hT = G("hT", [128, 8, S_LEN], BF16)
    r_hTc = [Res("hTc%d" % i) for i in range(4)]
    woutb = nc.alloc_sbuf_tensor_at("woutb_alias", [128, 16, 1024], BF16, offset=AR.last_off)
    r_woutb = Res("woutb")
    mixT, r_mix = G("mixT", [128, 16, S_LEN], BF16)
    r_mixt = [Res("mix%d" % i) for i in range(16)]
    ropeC, r_ropeC = G("ropeC", [128, S_LEN], BF16)
    ropeS, r_ropeS = G("ropeS", [128, S_LEN], BF16)
    cmpbias, r_cmpb = G("cmpbias", [128, S_LEN], BF16)
    ident, r_id = G("ident", [128, 128], BF16)
    causalb, r_cb = G("causalb", [128, 128], BF16)
    antib, r_ab = G("antib", [128, 128], BF16)
    trimask, r_tm = G("trimask", [128, 128], BF16)
    permm, r_pm = G("permm", [128, 128], BF16)
    antimask, r_am = G("antimask", [128, 128], BF16)
    topkA, r_tA = G("topkA", [128, 16, 32], BF16)
    topkB, r_tB = G("topkB", [128, 16, 32], BF16)
    onesf, r_ones = G("onesf", [128, 1], F32)
    wbf = [G("wbf%d" % i, [128, 8, 128], BF16) for i in range(4)]
    gates, r_gates = G("gates", [128, 16, 48], F32)
    dtv, r_dtv = G("dtv", [128, 16, 16], F32)
    a_tm, r_atm = G("a_tm", [128, 16, 16], F32)
    a3 = [G("a3_%d" % i, [128, 16, 16], BF16) for i in range(3)]
    tri2b, r_trib = G("tri2b", [128, 256], BF16)
    acs_tm, r_acs = G("acs_tm", [128, 16, 16], F32)
    ssq, r_ssq = G("ssq", [128, 16], F32)
    kcT2, r_kc = G("kcT2", [128, 4, 128], BF16)
    vcaug, r_vc = G("vcaug", [128, 4, 97], BF16)
    sm = {}
    for k, shp in SMALL_SHAPES.items():
        if shp[0] == 128:
            sm[k] = G("sm_" + k, shp, F32)
    bc = {}
    for k in ("b2v", "gateb", "dtb", "alog"):
        bc[k] = G("bc_" + k, [128, SMALL_SHAPES[k][1]], F32)
    aneg, r_aneg = G("aneg", [128, 16], F32)
    bias1, r_bias1 = G("bias1", [128, 4], F32)
    w2kb, r_w2kb = G("w2kb", [128, 2, 128], BF16)
    w2vb, r_w2vb = G("w2vb", [128, 2, 64], BF16)
    posTb, r_posTb = G("posTb", [128, 32], BF16)
    PHASE = AR.mark()
    tri2, r_tri = G("tri2", [128, 256], F32)
    cstg, r_cstg = G("cstg", [128, 64], F32)

    def load_const(name, dst, rdst, ncols, view=None):
        o = dst[:] if view is None else view
        S.dma(lambda e: e.dma_start(out=o, in_=cst_d[name]), writes=[rdst], q="pool")

    load_const("ropeC", ropeC, r_ropeC, 2048)
    load_const("ropeS", ropeS, r_ropeS, 2048)
    load_const("cmpbias", cmpbias, r_cmpb, 2048)
    load_const("ident", ident, r_id, 128)
    load_const("causalb", causalb, r_cb, 128)
    load_const("antib", antib, r_ab, 128)
    load_const("trimask", trimask, r_tm, 128)
    load_const("permm", permm, r_pm, 128)
    load_const("antimask", antimask, r_am, 128)
    S.dma(lambda e: e.dma_start(out=topkA[:].rearrange("p a b -> p (a b)"), in_=cst_d["topkA"]), writes=[r_tA], q="pool")
    S.dma(lambda e: e.dma_start(out=topkB[:].rearrange("p a b -> p (a b)"), in_=cst_d["topkB"]), writes=[r_tB], q="pool")
    S.dma(lambda e: e.dma_start(out=tri2[:], in_=cst_d["tri2"]), writes=[r_tri])
    S.dma(lambda e: e.dma_start(out=tri2b[:], in_=cst_d["tri2"]), writes=[r_trib], q="pool")
    S.op("pool", lambda e: e.memset(onesf[:], 1.0), writes=[r_ones])
    for k in sm:
        t, r = sm[k]
        S.dma(lambda e, t=t, k=k: e.dma_start(out=t[:], in_=sm_d[k]), writes=[r])
    for k in bc:
        t, r = bc[k]
        S.dma(lambda e, t=t, k=k: e.dma_start(out=t[:], in_=sm_d[k].partition_broadcast(128)), writes=[r])
    S.dma(lambda e: e.dma_start(out=cstg[:, 0:33], in_=cst_d["vcext"]), writes=[r_cstg])
    for g in range(4):
        S.op("pool", lambda e, g=g: e.tensor_copy(out=vcaug[:, g, 64:97], in_=cstg[:, 0:33]), reads=[r_cstg], writes=[r_vc])
    S.op("act", lambda e: e.activation(out=aneg[:], in_=bc["alog"][0][:], func=AF.Exp), reads=[bc["alog"][1]], writes=[r_aneg])
    S.op("dve", lambda e: e.tensor_scalar(out=aneg[:], in0=aneg[:], scalar1=-1.0, scalar2=None, op0=ALU.mult), reads=[r_aneg], writes=[r_aneg])
    S.op("pool", lambda e: e.tensor_copy(out=w2kb[:].rearrange("p a b -> p (a b)"), in_=sm["w2k"][0][:]), reads=[sm["w2k"][1]], writes=[r_w2kb])
    S.op("pool", lambda e: e.tensor_copy(out=w2vb[:].rearrange("p a b -> p (a b)"), in_=sm["w2v"][0][:]), reads=[sm["w2v"][1]], writes=[r_w2vb])
    S.op("pool", lambda e: e.tensor_copy(out=posTb[:], in_=sm["posT"][0][:]), reads=[sm["posT"][1]], writes=[r_posTb])
    S.op("pool", lambda e: e.memset(ssq[:], 0.0), writes=[r_ssq])

    wstate = {"next": 0, "issued": 0}
    WPF = 2

    def load_wtile():
        t = wstate["next"]
        wstate["next"] += 1
        while wstate["issued"] <= min(t + WPF, N_WT - 1):
            ti = wstate["issued"]
            wstate["issued"] += 1
            wb_, rwb_ = wbf[ti % 4]
            S.dma(lambda e, ti=ti, wb_=wb_: e.dma_start(out=wb_[:].rearrange("p a b -> p (a b)"), in_=wt_d[ti]), writes=[rwb_], q="pool")
        return wbf[t % 4]

    bank_rr = {"i": 0}

    def next_bank(lo=0, hi=2):
        i = lo + bank_rr["i"] % (hi - lo)
        bank_rr["i"] += 1
        return i

    def fm_matmuls(wb, rwb, tc, bi):
        for kc in range(8):
            S.op("pe", lambda e, kc=kc: e.matmul(pb[bi][:, :], lhsT=wb[:, kc, :], rhs=hT[:, kc, tc * 512:(tc + 1) * 512],
                                                  start=(kc == 0), stop=(kc == 7)),
                 reads=[rwb, r_hTc[tc]], writes=[rpb[bi]], rg=(0, 4))

    xt = [G("xt%d" % i, [128, 1024], F32) for i in range(2)]
    pw, r_pw = G("pw", [128, 1024], F32)
    junk, r_junk = G("junk", [128, 1024], F32)
    hb = [G("hb%d" % i, [128, 1024], BF16) for i in range(2)]
    ss0, r_ss0 = G("ss0", [128, 16], F32)
    S.dma(lambda e: e.dma_start(out=pw[:], in_=sm_d["prew"].partition_broadcast(128)), writes=[r_pw])
    hb3 = hb + [G("hb2", [128, 1024], BF16)]
    xt3 = xt + [G("xt2_", [128, 1024], F32)]

    def p0_a(tt):
        xb, rxb = xt3[tt % 3]
        hbb, rhbb = hb3[tt % 3]
        S.dma(lambda e: e.dma_start(out=xb[:], in_=x_d[tt * 128:(tt + 1) * 128, :]), writes=[rxb])
        S.op("act", lambda e: e.activation(out=junk[:], in_=xb[:], func=AF.Square, accum_out=ss0[:, tt:tt + 1]), reads=[rxb], writes=[r_junk, r_ss0])
        S.op("dve", lambda e: e.tensor_scalar(out=ss0[:, tt:tt + 1], in0=ss0[:, tt:tt + 1], scalar1=1.0 / D, scalar2=EPS, op0=ALU.mult, op1=ALU.add), reads=[r_ss0], writes=[r_ss0])
        S.op("act", lambda e: e.activation(out=ss0[:, tt:tt + 1], in_=ss0[:, tt:tt + 1], func=AF.Sqrt), reads=[r_ss0], writes=[r_ss0])
        S.op("dve", lambda e: e.reciprocal(out=ss0[:, tt:tt + 1], in_=ss0[:, tt:tt + 1]), reads=[r_ss0], writes=[r_ss0])
        S.op("dve", lambda e: e.scalar_tensor_tensor(out=hbb[:], in0=xb[:], scalar=ss0[:, tt:tt + 1], in1=pw[:], op0=ALU.mult, op1=ALU.mult),
             reads=[rxb, r_ss0, r_pw], writes=[rhbb])

    def p0_b(tt):
        hbb, rhbb = hb3[tt % 3]
        for kc in range(8):
            S.op("pe", lambda e, kc=kc: e.transpose(out=pbt[:, kc * 128:(kc + 1) * 128], in_=hbb[:, kc * 128:(kc + 1) * 128], identity=ident[:]),
                 reads=[rhbb, r_id], writes=[rpbt], rg=(0, 4))
        S.op("act", lambda e: e.copy(out=hT[:, 0:4, tt * 128:(tt + 1) * 128], in_=pbt[:, 0:512].rearrange("p (k t) -> p k t", k=4)),
             reads=[rpbt], writes=[r_hTc[tt // 4]])
        S.op("dve", lambda e: e.tensor_copy(out=hT[:, 4:8, tt * 128:(tt + 1) * 128], in_=pbt[:, 512:1024].rearrange("p (k t) -> p k t", k=4)),
             reads=[rpbt], writes=[r_hTc[tt // 4]])

    p0_a(0)
    for tt in range(NT):
        if tt + 1 < NT:
            p0_a(tt + 1)
        p0_b(tt)
    dbg("hT", hT[:, 0, :], r_hTc[3], [128, 2048])
    chk("p0")

    graw, r_graw = G("graw", [128, 16, 64], F32)
    wb, rwb = load_wtile()
    for tt in range(NT):
        bi = next_bank()
        for kc in range(8):
            S.op("pe", lambda e, kc=kc: e.matmul(pb[bi][:, 0:128], lhsT=hT[:, kc, tt * 128:(tt + 1) * 128], rhs=wb[:, kc, :],
                                                  start=(kc == 0), stop=(kc == 7)), reads=[rwb, r_hTc[tt // 4]], writes=[rpb[bi]], rg=(0, 4))
        S.op("dve", lambda e: e.tensor_tensor(out=graw[:, tt, 0:48], in0=pb[bi][:, 0:48], in1=bc["gateb"][0][:], op=ALU.add),
             reads=[rpb[bi], bc["gateb"][1]], writes=[r_graw])
        S.op("dve", lambda e: e.tensor_tensor(out=graw[:, tt, 48:64], in0=pb[bi][:, 48:64], in1=bc["dtb"][0][:], op=ALU.add),
             reads=[rpb[bi], bc["dtb"][1]], writes=[r_graw])
    S.op("act", lambda e: e.activation(out=gates[:], in_=graw[:, :, 0:48], func=AF.Sigmoid), reads=[r_graw], writes=[r_gates])
    S.op("act", lambda e: e.activation(out=dtv[:], in_=graw[:, :, 48:64], func=AF.Exp), reads=[r_graw], writes=[r_dtv])
    S.op("act", lambda e: e.activation(out=dtv[:], in_=dtv[:], func=AF.Ln, bias=1.0, scale=1.0), reads=[r_dtv], writes=[r_dtv])
    S.op("dve", lambda e: e.tensor_tensor(out=a_tm[:], in0=dtv[:], in1=aneg[:].unsqueeze(1).to_broadcast([128, 16, 16]), op=ALU.mult),
         reads=[r_dtv, r_aneg], writes=[r_atm])
    ares, r_ares = G("ares", [128, 16, 16], F32)
    S.op("dve", lambda e: e.tensor_copy(out=a3[0][0][:], in_=a_tm[:]), reads=[r_atm], writes=[a3[0][1]])
    S.op("dve", lambda e: e.tensor_tensor(out=ares[:], in0=a_tm[:], in1=a3[0][0][:], op=ALU.subtract), reads=[r_atm, a3[0][1]], writes=[r_ares])
    S.op("dve", lambda e: e.tensor_copy(out=a3[1][0][:], in_=ares[:]), reads=[r_ares], writes=[a3[1][1]])
    S.op("dve", lambda e: e.tensor_tensor(out=ares[:], in0=ares[:], in1=a3[1][0][:], op=ALU.subtract), reads=[r_ares, a3[1][1]], writes=[r_ares])
    S.op("dve", lambda e: e.tensor_copy(out=a3[2][0][:], in_=ares[:]), reads=[r_ares], writes=[a3[2][1]])
    for c in range(8):
        bi = next_bank()
        S.op("pe", lambda e: e.matmul(pb[bi][:, 0:16], lhsT=tri2[:, 0:128], rhs=a_tm[:, 2 * c, :], start=True, stop=True),
             reads=[r_tri, r_atm], writes=[rpb[bi]], rg=(0, 4))
        S.op("pe", lambda e: e.matmul(pb[bi][:, 16:32], lhsT=tri2[:, 128:256], rhs=a_tm[:, 2 * c, :], start=True, stop=False),
             reads=[r_tri, r_atm], writes=[rpb[bi]], rg=(0, 4))
        S.op("pe", lambda e: e.matmul(pb[bi][:, 16:32], lhsT=tri2[:, 0:128], rhs=a_tm[:, 2 * c + 1, :], start=False, stop=True),
             reads=[r_tri, r_atm], writes=[rpb[bi]], rg=(0, 4))
        S.op("dve", lambda e: e.tensor_copy(out=acs_tm[:, 2 * c:2 * c + 2, :], in_=pb[bi][:, 0:32].rearrange("p (a b) -> p a b", a=2)),
             reads=[rpb[bi]], writes=[r_acs])
    dbg("gates", gates[:].rearrange("p a b -> p (a b)"), r_gates, [128, 768], F32)
    dbg("dtv", dtv[:].rearrange("p a b -> p (a b)"), r_dtv, [128, 256], F32)
    dbg("acs", acs_tm[:].rearrange("p a b -> p (a b)"), r_acs, [128, 256], F32)
    chk("pg")

    kvT, r_kvT = G("kvT", [128, 4, S_LEN], BF16)
    w1b, r_w1b = G("w1b", [128, 32, 256], BF16)
    hid = [G("hid%d" % i, [128, 4, 128], BF16) for i in range(4)]
    for pc in range(4):
        for kv in range(2):
            S.dma(lambda e, kv=kv: e.dma_start(out=w1b[kv * 64:(kv + 1) * 64, pc * 8:(pc + 1) * 8, :],
                                               in_=w1_d[kv, pc * 512:(pc + 1) * 512, :].rearrange("(l d) h -> d l h", d=64)), writes=[r_w1b], q="pool")
    for g in range(4):
        wb, rwb = load_wtile()
        for tc in range(4):
            bi = next_bank()
            fm_matmuls(wb, rwb, tc, bi)
            S.op("act", lambda e: e.copy(out=kvT[:, g, tc * 512:(tc + 1) * 512], in_=pb[bi][:, :]), reads=[rpb[bi]], writes=[r_kvT])
    for kv in range(2):
        rows = slice(kv * 64, kv * 64 + 64)
        for hc in range(2):
            bi = 2 + kv
            for l in range(32):
                S.op("pe", lambda e, l=l: e.matmul(pb[bi][:, 0:1], lhsT=w1b[rows, l, hc * 128:(hc + 1) * 128], rhs=posTb[rows, l:l + 1],
                                                    start=(l == 0), stop=(l == 31)), reads=[r_w1b, r_posTb], writes=[rpb[bi]], rg=(2 * kv, 2 * kv + 2))
            col = kv * 2 + hc
            S.op("dve", lambda e: e.tensor_tensor(out=bias1[:, col:col + 1], in0=pb[bi][:, 0:1], in1=sm["b1T"][0][:, col:col + 1], op=ALU.add),
                 reads=[rpb[bi], sm["b1T"][1]], writes=[r_bias1])
    for hc in range(2):
        for l in range(32):
            for kv in range(2):
                rows = slice(kv * 64, kv * 64 + 64)
                bi = 2 + 2 * (hc % 2) + kv
                S.op("pe", lambda e, l=l: e.matmul(pb[bi][:, 0:508].rearrange("p (g c) -> p g c", g=4), lhsT=w1b[rows, l, hc * 128:(hc + 1) * 128],
                                                    rhs=kvT[rows, :, l:l + 2017:16], start=(l == 0), stop=(l == 31)),
                     reads=[r_w1b, r_kvT], writes=[rpb[bi]], rg=(2 * kv, 2 * kv + 2))
        for kv in range(2):
            bi = 2 + 2 * (hc % 2) + kv
            col = kv * 2 + hc
            hd, rhd = hid[col]
            S.op("act", lambda e: e.activation(out=hd[:, :, 0:127], in_=pb[bi][:, 0:508].rearrange("p (g c) -> p g c", g=4), func=AF.Silu, bias=bias1[:, col:col + 1], scale=1.0),
                 reads=[rpb[bi], r_bias1], writes=[rhd])
    for g in range(4):
        bi = 6
        for hc in range(2):
            S.op("pe", lambda e, hc=hc: e.matmul(pb[bi][:, 0:127], lhsT=w2kb[:, hc, :], rhs=hid[hc][0][:, g, 0:127], start=(hc == 0), stop=(hc == 1)),
                 reads=[r_w2kb, hid[hc][1]], writes=[rpb[bi]], rg=(0, 4))
        S.op("act", lambda e: e.activation(out=kcT2[:, g, 0:127], in_=pb[bi][:, 0:127], func=AF.Identity, bias=sm["b2k"][0][:, 0:1], scale=1.0),
             reads=[rpb[bi], sm["b2k"][1]], writes=[r_kc])
        bi = 0 + (g % 2)
        for hc in range(2):
            S.op("pe", lambda e, hc=hc: e.matmul(pb[bi][0:127, 0:64], lhsT=hid[2 + hc][0][:, g, 0:127], rhs=w2vb[:, hc, :], start=(hc == 0), stop=(hc == 1)),
                 reads=[r_w2vb, hid[2 + hc][1]], writes=[rpb[bi]], rg=(0, 4))
        S.op("dve", lambda e: e.tensor_tensor(out=vcaug[0:127, g, 0:64], in0=pb[bi][0:127, 0:64], in1=bc["b2v"][0][0:127, :], op=ALU.add),
             reads=[rpb[bi], bc["b2v"][1]], writes=[r_vc])
    dbg("kcT", kcT2[:, 0, :], r_kc, [128, 128])
    dbg("vc", vcaug[:, 0, :], r_vc, [128, 97])
    chk("cmp")
    S.barrier()
    AR.reset(PHASE)

    def rope_pair(dst_ap_fn, rdst, raw_dst_fn=None, rraw=None, split=None):
        wa, rwa = load_wtile()
        pend = []

        def tail():
            (tc, ba, raw_ap, rrw) = pend.pop(0)
            bp = 2 + (tc % 2)
            t1, rt1 = ropet[0][0]
            t2, rt2 = ropet[0][1]
            S.op("pe", lambda e: e.matmul(pb[bp][:, :], lhsT=permm[:], rhs=raw_ap, start=True, stop=True), reads=[r_pm, rrw], writes=[rpb[bp]], rg=(0, 4))
            S.op("dve", lambda e: e.tensor_tensor(out=t1[:], in0=pb[ba][:, :], in1=ropeC[:, tc * 512:(tc + 1) * 512], op=ALU.mult),
                 reads=[rpb[ba], r_ropeC], writes=[rt1])
            S.op("dve", lambda e: e.tensor_tensor(out=t2[:], in0=pb[bp][:, :], in1=ropeS[:, tc * 512:(tc + 1) * 512], op=ALU.mult),
                 reads=[rpb[bp], r_ropeS], writes=[rt2])
            if split is None:
                S.op("pool", lambda e: e.tensor_tensor(out=dst_ap_fn(tc), in0=t1[:], in1=t2[:], op=ALU.add), reads=[rt1, rt2], writes=[rdst])
            else:
                (d0, rd0), (d1_, rd1_) = split
                S.op("pool", lambda e: e.tensor_tensor(out=d0[0:64, tc * 512:(tc + 1) * 512], in0=t1[0:64, :], in1=t2[0:64, :], op=ALU.add), reads=[rt1, rt2], writes=[rd0])
                S.op("pool", lambda e: e.tensor_tensor(out=d1_[64:128, tc * 512:(tc + 1) * 512], in0=t1[64:128, :], in1=t2[64:128, :], op=ALU.add), reads=[rt1, rt2], writes=[rd1_])

        for tc in range(4):
            ba = next_bank(0, 2)
            fm_matmuls(wa, rwa, tc, ba)
            if raw_dst_fn is not None:
                raw_ap, rrw = raw_dst_fn(tc), rraw
            else:
                kr, rkr = kraws[tc % 2]
                raw_ap, rrw = kr[:], rkr
            S.op("act", lambda e: e.copy(out=raw_ap, in_=pb[ba][:, :]), reads=[rpb[ba]], writes=[rrw])
            pend.append((tc, ba, raw_ap, rrw))
            if len(pend) >= 2:
                tail()
        while pend:
            tail()

    for g in range(4):
        AR.reset(PHASE)
        qraw, r_qraw = G("qraw", [128, 2, S_LEN], BF16)
        qrot, r_qrot = G("qrot", [128, 2, S_LEN], BF16)
        kTs, r_kTs = G("kTs", [128, S_LEN], BF16)
        kTw, r_kTw = G("kTw", [128, S_LEN], BF16)
        kTx = [(kTs, r_kTs), (kTw, r_kTw)]
        zsT, r_zsT = G("zsT", [128, 2, S_LEN], BF16)
        vaug, r_vaug = G("vaug", [128, 16, 2, 65], BF16)
        ropet = [[G("ropet%d%d" % (i, k), [128, 512], F32) for k in range(2)] for i in range(1)]
        kraws = [G("kraw%d" % i, [128, 512], BF16) for i in range(2)]
        PT = [G("PT%d" % i, [128, 512], BF16) for i in range(3)]
        oacc, r_oacc = G("oacc", [128, 4, 256], F32)
        oaccb, r_oaccb = G("oaccb", [128, 4, 256], BF16)
        impacc, r_imp = G("impacc", [128, 4, 32], F32)
        impt, r_impt = G("impt", [128, 4, 32], F32)
        den, r_den = G("den", [128, 4], F32)
        fac, r_fac = G("fac", [128, 4], F32)
        m8, r_m8 = G("m8", [128, 8], F32)
        wk, r_wk = G("wk", [128, 32], F32)
        selq, r_selq = G("selq", [128, 96], BF16)
        KE1, r_KE1 = G("KE1", [128, S_LEN], BF16)
        QS = [[G("QS%d%d" % (j_, p_), [128, 512], BF16) for p_ in range(2)] for j_ in range(2)]
        S.op("pool", lambda e: e.memset(kTs[64:128, :], 0.0), writes=[r_kTs])
        S.op("pool", lambda e: e.memset(KE1[0:64, :], 0.0), writes=[r_KE1])
        S.dma(lambda e: e.dma_start(out=kTs[64:96, :], in_=cst_d["Eall"][0:32, :]), writes=[r_kTs], q="pool")
        S.dma(lambda e: e.dma_start(out=KE1[0:32, :], in_=cst_d["Eall"][0:32, :]), writes=[r_KE1], q="pool")
        for j_ in range(2):
            S.op("pool", lambda e, j_=j_: e.memset(QS[j_][0][0][64:128, :], 0.0), writes=[QS[j_][0][1]])
            S.op("pool", lambda e, j_=j_: e.memset(QS[j_][1][0][0:64, :], 0.0), writes=[QS[j_][1][1]])

        for j in range(2):
            rope_pair(lambda tc, j=j: qrot[:, j, tc * 512:(tc + 1) * 512], r_qrot,
                      lambda tc, j=j: qraw[:, j, tc * 512:(tc + 1) * 512], r_qraw)
        rope_pair(None, None, split=((kTs, r_kTs), (KE1, r_KE1)))
        rope_pair(lambda tc: kTw[:, tc * 512:(tc + 1) * 512], r_kTw)
        for j in range(2):
            wb, rwb = load_wtile()
            for tc in range(4):
                bi = next_bank()
                fm_matmuls(wb, rwb, tc, bi)
                S.op("act", lambda e: e.activation(out=zsT[:, j, tc * 512:(tc + 1) * 512], in_=pb[bi][:, :], func=AF.Silu),
                     reads=[rpb[bi]], writes=[r_zsT])
        wb, rwb = load_wtile()
        S.op("pool", lambda e: e.memset(vaug[:, :, :, 64:65], 1.0), writes=[r_vaug])
        for tt in range(NT):
            bi = next_bank()
            for kc in range(8):
                S.op("pe", lambda e, kc=kc: e.matmul(pb[bi][:, 0:128], lhsT=hT[:, kc, tt * 128:(tt + 1) * 128], rhs=wb[:, kc, :],
                                                      start=(kc == 0), stop=(kc == 7)), reads=[rwb, r_hTc[tt // 4]], writes=[rpb[bi]], rg=(0, 4))
            S.op("act", lambda e: e.copy(out=vaug[:, tt, :, 0:64], in_=pb[bi][:, 0:128].rearrange("p (a b) -> p a b", a=2)),
                 reads=[rpb[bi]], writes=[r_vaug])
        if g == 0:
            dbg("qrot", qrot[:, 0, :], r_qrot, [128, 2048])
            dbg("kTs", kTs[:, :], r_kTs, [128, 2048])
            dbg("vaug", vaug[:].rearrange("p a b c -> p (a b c)"), r_vaug, [128, 2080])
            chk("aproj")

        SB = [0, 1, 2, 3, 6]
        ACCB = [4, 5]
        rr = {"s": 0, "pt": 0, "acc": 0, "cm": 0, "df": 0, "ac": 0}
        oaccs = [(oacc, r_oacc), G("oacc1", [128, 4, 256], F32)]
        dens = [(den, r_den), G("den1", [128, 4], F32)]
        facs = [(fac, r_fac), G("fac1", [128, 4], F32)]
        otmps = [G("otmp%d" % i, [128, 4, 64], F32) for i in range(2)]
        for i_ in range(3, 6):
            PT.append(G("PT%d" % i_, [128, 512], BF16))
        for (kr_, rkr_) in kraws:
            PT.append((kr_, rkr_))

        def gidx(r, br):
            return (4 * g + r) * 3 + br

        def finalize(pv, rbank, r, br, qc, oa, roa, first):
            dn, rdn = dens[rr["df"] % 2]
            fc, rfc = facs[rr["df"] % 2]
            ot, rot = otmps[rr["df"] % 2]
            rr["df"] += 1
            gi = gidx(r, br)
            S.op("dve", lambda e: e.tensor_scalar(out=dn[:], in0=pv[:, :, 64], scalar1=1e-30, scalar2=None, op0=ALU.max), reads=[rbank], writes=[rdn])
            S.op("dve", lambda e: e.reciprocal(out=dn[:], in_=dn[:]), reads=[rdn], writes=[rdn])
            S.op("dve", lambda e: e.tensor_tensor(out=fc[:], in0=dn[:], in1=gates[:, 4 * qc:4 * qc + 4, gi], op=ALU.mult), reads=[rdn, r_gates], writes=[rfc])
            if first:
                S.op("dve", lambda e: e.tensor_tensor(out=oa[:, :, r * 64:(r + 1) * 64], in0=pv[:, :, 0:64], in1=fc[:].unsqueeze(2).to_broadcast([128, 4, 64]), op=ALU.mult),
                     reads=[rbank, rfc], writes=[roa])
            else:
                S.op("dve", lambda e: e.tensor_tensor(out=ot[:], in0=pv[:, :, 0:64], in1=fc[:].unsqueeze(2).to_broadcast([128, 4, 64]), op=ALU.mult),
                     reads=[rbank, rfc], writes=[rot])
                if br == 1:
                    S.op("pool", lambda e: e.tensor_tensor(out=oaccb[:, :, r * 64:(r + 1) * 64], in0=oa[:, :, r * 64:(r + 1) * 64], in1=ot[:], op=ALU.add),
                         reads=[roa, rot], writes=[r_oaccb])
                else:
                    S.op("pool", lambda e: e.tensor_tensor(out=oa[:, :, r * 64:(r + 1) * 64], in0=oa[:, :, r * 64:(r + 1) * 64], in1=ot[:], op=ALU.add),
                         reads=[roa, rot], writes=[roa])
            return dn, rdn

        accs = [G("accs%d" % i, [128, 388], F32) for i in range(2)]
        selqs = [(selq, r_selq)] + [G("selq%d" % i, [128, 96], BF16) for i in range(1, 4)]

        def topk(qc, sT, rsT):
            S.op("dve", lambda e: e.tensor_tensor(out=impacc[:], in0=impacc[:], in1=topkA[:, 4 * qc:4 * qc + 4, :], op=ALU.max), reads=[r_imp, r_tA], writes=[r_imp])
            S.op("dve", lambda e: e.tensor_tensor(out=impacc[:], in0=impacc[:], in1=topkB[:, 4 * qc:4 * qc + 4, :], op=ALU.min), reads=[r_imp, r_tB], writes=[r_imp])
            for qi in range(4):
                sq_, rsq_ = selqs[qi]
                S.op("dve", lambda e: e.max(out=m8[:], in_=impacc[:, qi, :]), reads=[r_imp], writes=[r_m8])
                S.op("dve", lambda e: e.match_replace(out=wk[:], in_to_replace=m8[:], in_values=impacc[:, qi, :], imm_value=-3.0e38), reads=[r_imp, r_m8], writes=[r_wk])
                S.op("dve", lambda e: e.max(out=m8[:], in_=wk[:]), reads=[r_wk], writes=[r_m8])
                S.op("dve", lambda e: e.tensor_scalar(out=sq_[:].rearrange("p (a b) -> p a b", a=3), in0=impacc[:, qi, :].unsqueeze(1).to_broadcast([128, 3, 32]),
                                                      scalar1=m8[:, 7:8], scalar2=NEGB, op0=ALU.is_lt, op1=ALU.mult), reads=[r_imp, r_m8], writes=[rsq_])

        def topk_pe(qc, qi, sT, rsT):
            sq_, rsq_ = selqs[qi]
            S.op("pe", lambda e: e.transpose(out=pbt[0:96, 512 + qi * 128:640 + qi * 128], in_=sq_[:, 0:96], identity=ident[:]), reads=[rsq_, r_id], writes=[rpbt], rg=(0, 4))
            if qi == 3:
                for j_ in range(2):
                    S.op("act", lambda e, j_=j_: e.copy(out=QS[j_][0][0][64:96, :], in_=pbt[64:96, 512:1024]), reads=[rpbt], writes=[QS[j_][0][1]])
                    S.op("act", lambda e, j_=j_: e.copy(out=QS[j_][1][0][0:32, :], in_=pbt[0:32, 512:1024]), reads=[rpbt], writes=[QS[j_][1][1]])
                if g == 0 and qc == 3:
                    dbg("selT", QS[0][1][0][:, :], QS[0][1][1], [128, 512])

        def cmp_step(qc, j, oa, roa, sT, rsT):
            st = {}

            def qk():
                st["pt"] = []
                bss = []
                for par in range(2):
                    bss.append(SB[rr["s"] % 5])
                    rr["s"] += 1
                    st["pt"].append(PT[rr["pt"] % 8])
                    rr["pt"] += 1
                for par in range(2):
                    rows = slice(par * 64, par * 64 + 64)
                    bs = bss[par]
                    S.op("pe", lambda e: e.matmul(pb[bs][0:127, :], lhsT=kcT2[rows, g, 0:127], rhs=qraw[rows, j, qc * 512:(qc + 1) * 512], start=True, stop=False),
                         reads=[r_kc, r_qraw], writes=[rpb[bs]], rg=(2 * par, 2 * par + 2))
                for par in range(2):
                    bs = bss[par]
                    S.op("pe", lambda e: e.matmul(pb[bs][0:127, :], lhsT=ident[0:127, 0:127], rhs=cmpbias[0:127, qc * 512:(qc + 1) * 512], start=False, stop=True),
                         reads=[r_id, r_cmpb], writes=[rpb[bs]], rg=(0, 4))
                for par in range(2):
                    bs = bss[par]
                    pt, rpt = st["pt"][par]
                    S.op("act", lambda e: e.activation(out=pt[0:127, :], in_=pb[bs][0:127, :], func=AF.Exp, scale=0.125), reads=[rpb[bs]], writes=[rpt])

            def pv():
                for par in range(2):
                    r = 2 * j + par
                    pt, rpt = st["pt"][par]
                    bc_ = ACCB[par]
                    for qi in range(4):
                        S.op("pe", lambda e, qi=qi: e.matmul(pb[bc_][:, qi * 97:(qi + 1) * 97], lhsT=pt[0:127, qi * 128:(qi + 1) * 128], rhs=vcaug[0:127, g, :],
                                                              start=True, stop=True), reads=[rpt, r_vc], writes=[rpb[bc_]], rg=(0, 4))
                for par in range(2):
                    r = 2 * j + par
                    bc_ = ACCB[par]
                    ac_, rac_ = accs[rr["ac"] % 2]
                    rr["ac"] += 1
                    S.op("act", lambda e: e.copy(out=ac_[:, 0:388], in_=pb[bc_][:, 0:388]), reads=[rpb[bc_]], writes=[rac_])
                    pvv = ac_[:, 0:388].rearrange("p (a b) -> p a b", a=4)
                    dn, rdn = finalize(pvv, rac_, r, 0, qc, oa, roa, True)
                    if r == 0:
                        S.op("dve", lambda e: e.tensor_tensor(out=impacc[:], in0=pvv[:, :, 65:97], in1=dn[:].unsqueeze(2).to_broadcast([128, 4, 32]), op=ALU.mult),
                             reads=[rac_, rdn], writes=[r_imp])
                    else:
                        S.op("dve", lambda e: e.tensor_tensor(out=impt[:], in0=pvv[:, :, 65:97], in1=dn[:].unsqueeze(2).to_broadcast([128, 4, 32]), op=ALU.mult),
                             reads=[rac_, rdn], writes=[r_impt])
                        S.op("pool", lambda e: e.tensor_tensor(out=impacc[:], in0=impacc[:], in1=impt[:], op=ALU.add), reads=[r_imp, r_impt], writes=[r_imp])
                if j == 1 and g == 0 and qc == 3:
                    dbg("imp", impacc[:].rearrange("p a b -> p (a b)"), r_imp, [128, 128], F32)
            return qk, pv

        def att_step(qc, br, j, kt, kt_lo, kt_hi, oa, roa, sT, rsT):
            st = {}
            kT_, rkT_ = kTx[br - 1]
            qlo = max(0, kt - 4 * qc)
            qhi = 4 if br == 1 else min(4, kt + 5 - 4 * qc)
            c0, c1 = qlo * 128, qhi * 128

            def qk():
                st["pt"] = []
                bss = []
                for par in range(2):
                    bss.append(SB[rr["s"] % 5])
                    rr["s"] += 1
                    st["pt"].append(PT[rr["pt"] % 8])
                    rr["pt"] += 1
                masks = []
                if kt >= 4 * qc:
                    masks.append((kt - 4 * qc, trimask, r_tm))
                if br == 2 and 0 <= kt + 4 - 4 * qc < 4:
                    masks.append((kt + 4 - 4 * qc, antimask, r_am))
                if br == 1:
                    for par in range(2):
                        bs = bss[par]
                        ke_, rke_ = (kTs, r_kTs) if par == 0 else (KE1, r_KE1)
                        qs_, rqs_ = QS[j][par]
                        S.op("pe", lambda e: e.matmul(pb[bs][:, c0:c1], lhsT=ke_[:, kt * 128:(kt + 1) * 128], rhs=qs_[:, c0:c1], start=True, stop=True),
                             reads=[rke_, rqs_], writes=[rpb[bs]], rg=(0, 4))
                else:
                    for par in range(2):
                        bs = bss[par]
                        rows = slice(par * 64, par * 64 + 64)
                        S.op("pe", lambda e: e.matmul(pb[bs][:, c0:c1], lhsT=kT_[rows, kt * 128:(kt + 1) * 128], rhs=qrot[rows, j, qc * 512 + c0:qc * 512 + c1],
                                                      start=True, stop=True),
                             reads=[rkT_, r_qrot], writes=[rpb[bs]], rg=(2 * par, 2 * par + 2))
                for par in range(2):
                    bs = bss[par]
                    pt, rpt = st["pt"][par]
                    S.op("act", lambda e: e.activation(out=pt[:, c0:c1], in_=pb[bs][:, c0:c1], func=AF.Exp, scale=0.125), reads=[rpb[bs]], writes=[rpt])
                for par in range(2):
                    pt, rpt = st["pt"][par]
                    for (qb, mt, rmt) in masks:
                        S.op("pool", lambda e, qb=qb, mt=mt: e.tensor_tensor(out=pt[:, qb * 128:(qb + 1) * 128], in0=pt[:, qb * 128:(qb + 1) * 128], in1=mt[:], op=ALU.mult),
                             reads=[rpt, rmt], writes=[rpt])

            def pv():
                for par in range(2):
                    pt, rpt = st["pt"][par]
                    accb = ACCB[par]
                    for qi in range(qlo, qhi):
                        st_ = (kt == kt_lo and qi == qlo)
                        S.op("pe", lambda e, qi=qi, st_=st_: e.matmul(pb[accb][:, qi * 65:(qi + 1) * 65], lhsT=pt[:, qi * 128:(qi + 1) * 128], rhs=vaug[:, kt, br - 1, :],
                                                                       start=st_, stop=False, skip_group_check=True),
                             reads=[rpt, r_vaug], writes=[rpb[accb]], rg=(0, 4))
                if kt == kt_hi:
                    for par in range(2):
                        accb = ACCB[par]
                        ac_, rac_ = accs[rr["ac"] % 2]
                        rr["ac"] += 1
                        S.op("act", lambda e: e.copy(out=ac_[:, 0:260], in_=pb[accb][:, 0:260]), reads=[rpb[accb]], writes=[rac_])
                        pvv = ac_[:, 0:260].rearrange("p (a b) -> p a b", a=4)
                        finalize(pvv, rac_, 2 * j + par, br, qc, oa, roa, False)
            return qk, pv

        def finish_chunk(qc, oa, roa):
            if g == 0 and qc == 3:
                dbg("oatt", oaccb[:].rearrange("p a b -> p (a b)"), r_oaccb, [128, 1024])
            for j in range(2):
                for qi in range(4):
                    S.op("pe", lambda e, j=j, qi=qi: e.transpose(out=pbt[:, j * 512 + qi * 128:j * 512 + (qi + 1) * 128], in_=oaccb[:, qi, j * 128:(j + 1) * 128], identity=ident[:]),
                         reads=[r_oaccb, r_id], writes=[rpbt], rg=(0, 4))
            for j in range(2):
                S.op("dve", lambda e, j=j: e.tensor_tensor(out=mixT[:, 2 * g + j, qc * 512:(qc + 1) * 512], in0=pbt[:, j * 512:(j + 1) * 512], in1=zsT[:, j, qc * 512:(qc + 1) * 512], op=ALU.mult),
                     reads=[rpbt, r_zsT], writes=[r_mixt[2 * g + j]])

        def qcopies(qc):
            for j_ in range(2):
                S.op("pool", lambda e, j_=j_: e.tensor_copy(out=QS[j_][0][0][0:64, :], in_=qrot[0:64, j_, qc * 512:(qc + 1) * 512]), reads=[r_qrot], writes=[QS[j_][0][1]])
                S.op("pool", lambda e, j_=j_: e.tensor_copy(out=QS[j_][1][0][64:128, :], in_=qrot[64:128, j_, qc * 512:(qc + 1) * 512]), reads=[r_qrot], writes=[QS[j_][1][1]])

        steps = []
        posts = {}
        for qc in range(4):
            oa, roa = oaccs[qc % 2]
            sT, rsT = None, None
            base = len(steps)
            if qc > 0:
                poa, proa = oaccs[(qc - 1) % 2]
                posts.setdefault(base + 13, []).append(lambda qc=qc, poa=poa, proa=proa: finish_chunk(qc - 1, poa, proa))
            for j in range(2):
                steps.append(cmp_step(qc, j, oa, roa, sT, rsT))
            posts.setdefault(base, []).append(lambda qc=qc: qcopies(qc))
            posts.setdefault(base + (1 if qc == 0 else 2), []).append(lambda qc=qc, sT=sT, rsT=rsT: topk(qc, sT, rsT))
            tk0 = 4 if qc == 0 else 9
            for qi in range(4):
                posts.setdefault(base + tk0 + qi, []).append(lambda qc=qc, qi=qi, sT=sT, rsT=rsT: topk_pe(qc, qi, sT, rsT))
            for br in (2, 1):
                for j in range(2):
                    kt_lo = 0 if br == 1 else max(0, 4 * qc - 4)
                    kt_hi = 4 * qc + 3
                    for kt in range(kt_lo, kt_hi + 1):
                        steps.append(att_step(qc, br, j, kt, kt_lo, kt_hi, oa, roa, sT, rsT))
        ADEPTH = 3
        nq = 0
        for si_, (qk, pv) in enumerate(steps):
            while nq <= min(si_ + ADEPTH, len(steps) - 1):
                steps[nq][0]()
                nq += 1
            pv()
            for f_ in posts.get(si_, []):
                f_()
        finish_chunk(3, *oaccs[3 % 2])
        S.barrier()
    dbg("mixA", mixT[:, 0, :], r_mixt[0], [128, 2048])
    chk("att")

    for g in range(4):
        AR.reset(PHASE)
        zsS, r_zsS = G("zsS", [128, 2, S_LEN], BF16)
        xcT, r_xcT = G("xcT", [128, 2, S_LEN], BF16)
        BT, r_BT = G("BT", [128, S_LEN], BF16)
        CT, r_CT = G("CT", [128, S_LEN], BF16)
        xdt, r_xdt = G("xdt", [128, 16, 4, 64], BF16)
        Btm, r_Btm = G("Btm", [128, 16, 128], BF16)
        SSD_TMP = AR.mark()
        ub = [G("ub%d" % i, [128, 515], BF16) for i in range(3)]
        dg, r_dg = G("dg", [128, 4, 4, 128], BF16)
        for j in range(2):
            wb, rwb = load_wtile()
            for tc in range(4):
                bi = next_bank()
                fm_matmuls(wb, rwb, tc, bi)
                S.op("act", lambda e: e.activation(out=zsS[:, j, tc * 512:(tc + 1) * 512], in_=pb[bi][:, :], func=AF.Silu), reads=[rpb[bi]], writes=[r_zsS])
        conv_targets = [(lambda tc: xcT[:, 0, tc * 512:(tc + 1) * 512], r_xcT, 2 * g),
                        (lambda tc: xcT[:, 1, tc * 512:(tc + 1) * 512], r_xcT, 2 * g + 1),
                        (lambda tc: BT[:, tc * 512:(tc + 1) * 512], r_BT, 8 + g),
                        (lambda tc: CT[:, tc * 512:(tc + 1) * 512], r_CT, 12 + g)]
        cw, r_cw = sm["convw"]
        cbv, r_cbv = sm["convb"]
        for ti, (dst_fn, rdst, ct) in enumerate(conv_targets):
            for k in range(4):
                S.op("dve", lambda e, ti=ti, k=k, ct=ct: e.tensor_scalar(out=dg[:, ti, k, :], in0=ident[:], scalar1=cw[:, ct * 4 + k:ct * 4 + k + 1], scalar2=None, op0=ALU.mult),
                     reads=[r_id, r_cw], writes=[r_dg])
        pend_conv = []

        def conv_tail():
            (ti, dst_fn, rdst, ct, tc, u, ru) = pend_conv.pop(0)
            pc = 2 + (tc % 2)
            for k in range(4):
                S.op("pe", lambda e, k=k: e.matmul(pb[pc][:, :], lhsT=dg[:, ti, k, :], rhs=u[:, k:k + 512], start=(k == 0), stop=(k == 3)),
                     reads=[r_dg, ru], writes=[rpb[pc]], rg=(0, 4))
            S.op("act", lambda e: e.activation(out=dst_fn(tc), in_=pb[pc][:, :], func=AF.Silu, bias=cbv[:, ct:ct + 1], scale=1.0), reads=[rpb[pc], r_cbv], writes=[rdst])

        for ti, (dst_fn, rdst, ct) in enumerate(conv_targets):
            wb, rwb = load_wtile()
            for tc in range(4):
                bi = next_bank()
                fm_matmuls(wb, rwb, tc, bi)
                u, ru = ub[(ti * 4 + tc) % 3]
                if tc == 0:
                    S.op("pool", lambda e: e.memset(u[:, 0:3], 0.0), writes=[ru])
                S.op("act", lambda e: e.copy(out=u[:, 3:515], in_=pb[bi][:, :]), reads=[rpb[bi]], writes=[ru])
                if tc < 3:
                    un, run = ub[(ti * 4 + tc + 1) % 3]
                    S.op("pool", lambda e: e.tensor_copy(out=un[:, 0:3], in_=u[:, 512:515]), reads=[ru], writes=[run])
                pend_conv.append((ti, dst_fn, rdst, ct, tc, u, ru))
                if len(pend_conv) >= 2:
                    conv_tail()
        while pend_conv:
            conv_tail()
        for tt in range(NT):
            for j in range(2):
                S.op("pe", lambda e, j=j: e.transpose(out=pbt[:, j * 128:(j + 1) * 128], in_=xcT[:, j, tt * 128:(tt + 1) * 128], identity=ident[:]),
                     reads=[r_xcT, r_id], writes=[rpbt], rg=(0, 4))
            S.op("pe", lambda e: e.transpose(out=pbt[:, 256:384], in_=BT[:, tt * 128:(tt + 1) * 128], identity=ident[:]), reads=[r_BT, r_id], writes=[rpbt], rg=(0, 4))
            S.op("dve", lambda e: e.tensor_tensor(out=xdt[:, tt, :, :], in0=pbt[:, 0:256].rearrange("p (a b) -> p a b", a=4),
                                                  in1=dtv[:, tt, 4 * g:4 * g + 4].unsqueeze(2).to_broadcast([128, 4, 64]), op=ALU.mult),
                 reads=[rpbt, r_dtv], writes=[r_xdt])
            S.op("act", lambda e: e.copy(out=Btm[:, tt, :], in_=pbt[:, 256:384]), reads=[rpbt], writes=[r_Btm])
        if g == 0:
            dbg("xcT", xcT[:, 0, :], r_xcT, [128, 2048])
            dbg("BT", BT[:, :], r_BT, [128, 2048])
        NB3 = 4
        CBm = [G("CBm%d" % i, [128, 2, 256], BF16) for i in range(2)]
        EA = [G("EA%d" % i, [128, 256], F32) for i in range(NB3)]
        Cdec = [G("Cdec%d" % i, [128, 256], BF16) for i in range(NB3)]
        D1 = [G("D1%d" % i, [128, 384], F32) for i in range(NB3)]
        MT = [G("MT%d" % i, [128, 384], BF16) for i in range(NB3)]
        xws = [G("xw%d" % i, [128, 2, 4, 64], BF16) for i in range(2)]
        cds = [G("cdall%d" % i, [128, 4], F32) for i in range(2)]
        dtes = [G("dte%d" % i, [128, 2, 4], F32) for i in range(2)]
        htmp, r_htmp = G("htmp", [128, 4, 64], F32)
        h32, r_h32 = G("h32", [128, 4, 64], F32)
        hbf, r_hbf = G("hbf", [128, 4, 64], BF16)
        ytmp = [G("ytmp%d" % i, [128, 256], F32) for i in range(4)]
        yg = [G("yg%d" % i, [128, 256], F32) for i in range(2)]
        S.op("pool", lambda e: e.memset(h32[:], 0.0), writes=[r_h32])
        S.op("pool", lambda e: e.memset(hbf[:], 0.0), writes=[r_hbf])
        dsk, r_dsk = sm["dskip"]
        snw, r_snw = sm["snw"]
        BY = [0, 1, 2]
        sst = {"by": 0, "k": 0}
        info = {}

        def Cc(c):
            l0 = c * 256
            bx = 6
            cb_, rcb_ = CBm[c % 2]
            S.op("pe", lambda e: e.matmul(pb[bx][:, 0:256], lhsT=BT[:, l0:l0 + 128], rhs=CT[:, l0:l0 + 256], start=True, stop=True),
                 reads=[r_BT, r_CT], writes=[rpb[bx]], rg=(0, 4))
            S.op("pe", lambda e: e.matmul(pb[bx][:, 256:384], lhsT=BT[:, l0 + 128:l0 + 256], rhs=CT[:, l0 + 128:l0 + 256], start=True, stop=True),
                 reads=[r_BT, r_CT], writes=[rpb[bx]], rg=(0, 4))
            S.op("dve", lambda e: e.tensor_tensor(out=cb_[:, 0, 0:128], in0=pb[bx][:, 0:128], in1=trimask[:], op=ALU.mult), reads=[rpb[bx], r_tm], writes=[rcb_])
            S.op("act", lambda e: e.copy(out=cb_[:, 0, 128:256], in_=pb[bx][:, 128:256]), reads=[rpb[bx]], writes=[rcb_])
            S.op("dve", lambda e: e.tensor_tensor(out=cb_[:, 1, 0:128], in0=pb[bx][:, 256:384], in1=trimask[:], op=ALU.mult), reads=[rpb[bx], r_tm], writes=[rcb_])

        def A(c, r):
            l0 = c * 256
            h = 4 * g + r
            k = sst["k"] % NB3
            sst["k"] += 1
            info[(c, r)] = k
            by = BY[sst["by"] % 3]
            sst["by"] += 1
            cb_, rcb_ = CBm[c % 2]
            xw, r_xw = xws[c % 2]
            cdall, r_cd = cds[c % 2]
            dte, r_dte = dtes[c % 2]
            for pi in range(3):
                ap_, rap_ = a3[pi]
                S.op("pe", lambda e, pi=pi, ap_=ap_: e.matmul(pb[by][:, 0:256], lhsT=ap_[:, 2 * c, h:h + 1].to_broadcast([128, 128]), rhs=tri2b[:, :], start=(pi == 0), stop=False),
                     reads=[rap_, r_trib], writes=[rpb[by]], rg=(0, 4))
            for pi in range(3):
                ap_, rap_ = a3[pi]
                S.op("pe", lambda e, pi=pi, ap_=ap_: e.matmul(pb[by][:, 128:256], lhsT=ap_[:, 2 * c + 1, h:h + 1].to_broadcast([128, 128]), rhs=tri2b[:, 0:128], start=False, stop=(pi == 2)),
                     reads=[rap_, r_trib], writes=[rpb[by]], rg=(0, 4))
            ea, rea = EA[k]
            cd_, rcd_ = Cdec[k]
            d1, rd1 = D1[k]
            mt, rmt = MT[k]
            S.op("dve", lambda e: e.tensor_scalar(out=d1[:, 0:256], in0=pb[by][:, 0:256], scalar1=acs_tm[:, 2 * c, h:h + 1], scalar2=0.0, op0=ALU.subtract, op1=ALU.min),
                 reads=[rpb[by], r_acs], writes=[rd1])
            S.op("dve", lambda e: e.tensor_scalar(out=d1[:, 256:384], in0=pb[by][:, 128:256], scalar1=acs_tm[:, 2 * c + 1, h:h + 1], scalar2=0.0, op0=ALU.subtract, op1=ALU.min),
                 reads=[rpb[by], r_acs], writes=[rd1])
            S.op("act", lambda e: e.activation(out=ea[:], in_=pb[by][:, 0:256], func=AF.Exp), reads=[rpb[by]], writes=[rea])
            S.op("act", lambda e: e.activation(out=d1[:], in_=d1[:], func=AF.Exp), reads=[rd1], writes=[rd1])
            S.op("pool", lambda e: e.tensor_tensor(out=mt[:, 0:256], in0=d1[:, 0:256], in1=cb_[:, 0, :], op=ALU.mult), reads=[rd1, rcb_], writes=[rmt])
            S.op("pool", lambda e: e.tensor_tensor(out=mt[:, 256:384], in0=d1[:, 256:384], in1=cb_[:, 1, 0:128], op=ALU.mult), reads=[rd1, rcb_], writes=[rmt])
            if c > 0:
                S.op("dve", lambda e: e.tensor_tensor(out=cd_[:], in0=CT[:, l0:l0 + 256], in1=ea[:], op=ALU.mult), reads=[r_CT, rea], writes=[rcd_])
            if c < 7:
                S.op("act", lambda e: e.copy(out=cdall[:, r:r + 1], in_=ea[:, 255:256]), reads=[rea], writes=[r_cd])
                S.op("act", lambda e: e.copy(out=dte[:, :, r], in_=d1[:, 255:384:128]), reads=[rd1], writes=[r_dte])

        def B(c, r):
            k = info[(c, r)]
            jp, par = r // 2, r % 2
            cd_, rcd_ = Cdec[k]
            mt, rmt = MT[k]
            bo = 3 + jp
            yo = pb[bo][par * 64:(par + 1) * 64, :]
            S.op("pe", lambda e: e.matmul(yo[:, 0:256], lhsT=xdt[:, 2 * c, r, :], rhs=mt[:, 0:256], start=True, stop=False),
                 reads=[r_xdt, rmt], writes=[rpb[bo]], rg=(0, 4))
            S.op("pe", lambda e: e.matmul(yo[:, 128:256], lhsT=xdt[:, 2 * c + 1, r, :], rhs=mt[:, 256:384], start=False, stop=(c == 0)),
                 reads=[r_xdt, rmt], writes=[rpb[bo]], rg=(0, 4))
            if c > 0:
                S.op("pe", lambda e: e.matmul(yo[:, 0:256], lhsT=hbf[:, r, :], rhs=cd_[:], start=False, stop=True),
                     reads=[r_hbf, rcd_], writes=[rpb[bo]], rg=(0, 4))

        def St_pre(c):
            xw, r_xw = xws[c % 2]
            dte, r_dte = dtes[c % 2]
            for si_ in range(2):
                S.op("dve", lambda e, si_=si_: e.tensor_tensor(out=xw[:, si_, :, :], in0=xdt[:, 2 * c + si_, :, :], in1=dte[:, si_, :].unsqueeze(2).to_broadcast([128, 4, 64]), op=ALU.mult),
                     reads=[r_xdt, r_dte], writes=[r_xw])

        def St(c):
            bst = 5
            xw, r_xw = xws[c % 2]
            cdall, r_cd = cds[c % 2]
            dte, r_dte = dtes[c % 2]
            S.op("pe", lambda e: e.matmul(pb[bst][:, 0:256], lhsT=Btm[:, 2 * c, :], rhs=xw[:, 0, :, :].rearrange("p a b -> p (a b)"), start=True, stop=False),
                 reads=[r_Btm, r_xw], writes=[rpb[bst]], rg=(0, 4))
            S.op("pe", lambda e: e.matmul(pb[bst][:, 0:256], lhsT=Btm[:, 2 * c + 1, :], rhs=xw[:, 1, :, :].rearrange("p a b -> p (a b)"), start=False, stop=True),
                 reads=[r_Btm, r_xw], writes=[rpb[bst]], rg=(0, 4))
            S.op("dve", lambda e: e.tensor_tensor(out=htmp[:], in0=h32[:], in1=cdall[:, 0:4].unsqueeze(2).to_broadcast([128, 4, 64]), op=ALU.mult),
                 reads=[r_h32, r_cd], writes=[r_htmp])
            S.op("dve", lambda e: e.tensor_tensor(out=h32[:], in0=htmp[:], in1=pb[bst][:, 0:256].rearrange("p (a b) -> p a b", a=4), op=ALU.add),
                 reads=[r_htmp, rpb[bst]], writes=[r_h32])
            S.op("pool", lambda e: e.tensor_copy(out=hbf[:], in_=h32[:]), reads=[r_h32], writes=[r_hbf])

        def Yev(c, jp):
            l0 = c * 256
            bo = 3 + jp
            ft = 2 * g + jp
            yk = sst["y"] % 4
            sst["y"] += 1
            yt, ryt = ytmp[yk]
            ygg, rygg = yg[jp]
            S.op("dve", lambda e: e.scalar_tensor_tensor(out=yt[:], in0=xcT[:, jp, l0:l0 + 256], scalar=dsk[:, ft:ft + 1], in1=pb[bo][:, 0:256], op0=ALU.mult, op1=ALU.add),
                 reads=[r_xcT, r_dsk, rpb[bo]], writes=[ryt])
            S.op("dve", lambda e: e.tensor_tensor(out=ygg[:], in0=yt[:], in1=zsS[:, jp, l0:l0 + 256], op=ALU.mult), reads=[ryt, r_zsS], writes=[rygg])
            S.op("act", lambda e: e.mul(out=mixT[:, 8 + ft, l0:l0 + 256], in_=ygg[:], mul=snw[:, ft:ft + 1]), reads=[rygg, r_snw], writes=[r_mixt[8 + ft]])
            S.op("act", lambda e: e.activation(out=yt[:], in_=ygg[:], func=AF.Square), reads=[rygg], writes=[ryt])
            pendpe.append((c, yk))
            if g == 0 and jp == 0 and c == 1:
                dbg("yg", ygg[:], rygg, [128, 256], F32)

        def YevPE():
            c, yk = pendpe.pop(0)
            yt, ryt = ytmp[yk]
            bq = 5
            kq = sst["q"] % 4
            sst["q"] += 1
            for hh in range(2):
                S.op("pe", lambda e, hh=hh: e.matmul(pb[bq][:, 400 + 2 * kq + hh:401 + 2 * kq + hh], lhsT=yt[:, hh * 128:(hh + 1) * 128], rhs=onesf[:, 0:1], start=True, stop=True),
                     reads=[ryt, r_ones], writes=[rpb[bq]], rg=(0, 4))
            pend.append((c, kq))

        def Yev2():
            c, kq = pend.pop(0)
            bq = 5
            S.op("dve", lambda e: e.tensor_tensor(out=ssq[:, 2 * c:2 * c + 2], in0=ssq[:, 2 * c:2 * c + 2], in1=pb[bq][:, 400 + 2 * kq:402 + 2 * kq], op=ALU.add), reads=[r_ssq, rpb[bq]], writes=[r_ssq])

        pend = []
        pendpe = []
        sst["q"] = 0
        sst["y"] = 0
        DEPTH = 3
        sl = [(c, r) for c in range(8) for r in range(4)]
        issued = 0

        def issue_A(upto):
            nonlocal_issued = issued_box[0]
            while nonlocal_issued <= upto and nonlocal_issued < len(sl):
                c_, r_ = sl[nonlocal_issued]
                if r_ == 0:
                    Cc(c_)
                A(c_, r_)
                nonlocal_issued += 1
            issued_box[0] = nonlocal_issued

        issued_box = [0]
        for si, (c, r) in enumerate(sl):
            issue_A(si + DEPTH)
            if r == 3 and c < 7:
                St_pre(c)
            B(c, r)
            if r % 2 == 1:
                if len(pendpe) >= 2:
                    YevPE()
                if len(pend) >= 3:
                    Yev2()
                Yev(c, r // 2)
            if r == 3 and c < 7:
                St(c)
            if g == 3 and r == 1:
                for kc in (2 * c, 2 * c + 1):
                    S.dma(lambda e, kc=kc: e.dma_start(out=woutb[:, kc, :], in_=wout_d[kc * 128:(kc + 1) * 128, :]), writes=[r_woutb] + r_hTc, q="pool")
        while pendpe:
            YevPE()
        while pend:
            Yev2()
        S.barrier()
    dbg("ssq", ssq[:], r_ssq, [128, 16], F32)
    dbg("mixS", mixT[:, 8, :], r_mixt[8], [128, 2048])
    chk("ssd")

    AR.reset(PHASE)
    pw2, r_pw2 = G("pw2", [128, 1024], F32)
    xt2 = [G("xt2%d" % i, [128, 1024], F32) for i in range(2)]
    ob = [G("ob%d" % i, [128, 1024], F32) for i in range(2)]
    tmpo, r_tmpo = G("tmpo", [128, 1024], F32)
    junk2, r_junk2 = G("junk2", [128, 1024], F32)
    ss2, r_ss2 = G("ss2", [128, 16], F32)
    S.dma(lambda e: e.dma_start(out=pw2[:], in_=sm_d["postw"].partition_broadcast(128)), writes=[r_pw2])
    S.op("dve", lambda e: e.tensor_scalar(out=ssq[:], in0=ssq[:], scalar1=1.0 / D, scalar2=EPS, op0=ALU.mult, op1=ALU.add), reads=[r_ssq], writes=[r_ssq])
    S.op("act", lambda e: e.activation(out=ssq[:], in_=ssq[:], func=AF.Sqrt), reads=[r_ssq], writes=[r_ssq])
    S.op("dve", lambda e: e.reciprocal(out=ssq[:], in_=ssq[:]), reads=[r_ssq], writes=[r_ssq])
    for tt in range(NT):
        xb, rxb = xt2[tt % 2]
        o_, ro_ = ob[tt % 2]
        S.dma(lambda e: e.dma_start(out=xb[:], in_=x_d[tt * 128:(tt + 1) * 128, :]), writes=[rxb])
        obk = [(4 * tt + i_) % 7 for i_ in range(4)]
        for half in (1, 0):
            for hs in range(2):
                bi = obk[(1 - half) * 2 + hs]
                for kk in range(8):
                    kc = half * 8 + kk
                    S.op("pe", lambda e, kc=kc, kk=kk: e.matmul(pb[bi][:, :], lhsT=mixT[:, kc, tt * 128:(tt + 1) * 128], rhs=woutb[:, kc, hs * 512:(hs + 1) * 512],
                                                                 start=(kk == 0), stop=(kk == 7)), reads=[r_mixt[kc], r_woutb], writes=[rpb[bi]], rg=(0, 4))
        for hs in range(2):
            bs_, ba_ = obk[hs], obk[2 + hs]
            S.op("act", lambda e: e.mul(out=tmpo[:, hs * 512:(hs + 1) * 512], in_=pb[bs_][:, :], mul=ssq[:, tt:tt + 1]),
                 reads=[rpb[bs_], r_ssq], writes=[r_tmpo])
            S.op("dve", lambda e: e.tensor_tensor(out=o_[:, hs * 512:(hs + 1) * 512], in0=tmpo[:, hs * 512:(hs + 1) * 512], in1=pb[ba_][:, :], op=ALU.add),
                 reads=[r_tmpo, rpb[ba_]], writes=[ro_])
        S.op("act", lambda e: e.activation(out=junk2[:], in_=o_[:], func=AF.Square, accum_out=ss2[:, tt:tt + 1]), reads=[ro_], writes=[r_junk2, r_ss2])
        S.op("dve", lambda e: e.tensor_scalar(out=ss2[:, tt:tt + 1], in0=ss2[:, tt:tt + 1], scalar1=1.0 / D, scalar2=EPS, op0=ALU.mult, op1=ALU.add), reads=[r_ss2], writes=[r_ss2])
        S.op("act", lambda e: e.activation(out=ss2[:, tt:tt + 1], in_=ss2[:, tt:tt + 1], func=AF.Sqrt), reads=[r_ss2], writes=[r_ss2])
        S.op("dve", lambda e: e.reciprocal(out=ss2[:, tt:tt + 1], in_=ss2[:, tt:tt + 1]), reads=[r_ss2], writes=[r_ss2])
        S.op("dve", lambda e: e.scalar_tensor_tensor(out=o_[:], in0=o_[:], scalar=ss2[:, tt:tt + 1], in1=pw2[:], op0=ALU.mult, op1=ALU.mult),
             reads=[ro_, r_ss2, r_pw2], writes=[ro_])
        S.op("pool", lambda e: e.tensor_tensor(out=o_[:], in0=o_[:], in1=xb[:], op=ALU.add), reads=[ro_, rxb], writes=[ro_])
        S.dma(lambda e: e.dma_start(out=out_d[tt * 128:(tt + 1) * 128, :], in_=o_[:]), reads=[ro_])


_CACHE = {}


def prepare_inputs(inp):
    inp = {k: np.asarray(v) for k, v in inp.items()}
    shared = {}
    shared["wt"] = host_wtiles(np.ascontiguousarray(inp["w_in"][0], dtype=np.float32))
    shared["w1"] = np.ascontiguousarray(inp["cmp_w1"][0], dtype=np.float32)
    shared["wout"] = np.ascontiguousarray(inp["w_out"][0], dtype=np.float32)
    shared.update(host_consts())
    shared.update(host_small(inp))
    return inp, shared


def kernel(**inputs):
    inp, shared = prepare_inputs(inputs)
    if "nc" not in _CACHE:
        _CACHE["nc"] = build()[0]
    nc = _CACHE["nc"]
    x = np.ascontiguousarray(inp["x"], dtype=np.float32)
    in_maps = []
    for b in range(8):
        m = dict(shared)
        m["x"] = x[b]
        in_maps.append(m)
    res = run_bass_kernel_spmd(nc, in_maps, core_ids=list(range(8)))
    return np.stack([res.results[b]["out"] for b in range(8)], 0).astype(np.float32)
```
